# Optimizing a Trainium2 kernel written in Bass

```python
import jax
import jax.numpy as jnp
from jax import lax
import numpy as np

D_MODEL = 2048
BATCH = 2
SEQ = 8192
DEPTH = 4

N_MIXERS = 3
D_FF = 5632
HALF_STEP = 0.5
NORM_EPS = 1e-6
NEG_INF = -1e30
ROPE_THETA = 500000.0

A_HEADS = 16
A_HEAD_DIM = 128
A_ROT_DIM = A_HEAD_DIM // 4
A_PATTERNS = ((128, 1), (512, 4), (2048, 16))
A_GROUPS = len(A_PATTERNS)
A_BLOCK = 128

B_HEADS = 16
B_Q_RANK = 1536
B_KV_RANK = 512
B_NOPE_DIM = 128
B_ROPE_DIM = 64
B_V_DIM = 128
B_BLOCK = 128

C_WIDTH = D_MODEL
C_CONV = 3

kernel_name = 'hybrid_dilated_mla_shortconv_macaron'


def rmsnorm(x, gain):
    xf = x.astype(jnp.float32)
    y = xf * lax.rsqrt(jnp.mean(xf * xf, axis=-1, keepdims=True) + NORM_EPS)
    return (y * gain.astype(jnp.float32)).astype(x.dtype)


def swiglu(x, w13, w2):
    gate, up = jnp.split(x @ w13, 2, axis=-1)
    return (jax.nn.silu(gate) * up) @ w2


def rope(t, positions, rot_dim):
    half = rot_dim // 2
    inv_freq = ROPE_THETA ** (-jnp.arange(half, dtype=jnp.float32) / half)
    ang = positions.astype(jnp.float32)[:, :, None, None] * inv_freq
    cos, sin = jnp.cos(ang), jnp.sin(ang)
    tf = t.astype(jnp.float32)
    t1, t2, rest = tf[..., :half], tf[..., half:rot_dim], tf[..., rot_dim:]
    out = jnp.concatenate([t1 * cos - t2 * sin, t2 * cos + t1 * sin, rest], axis=-1)
    return out.astype(t.dtype)


def dilated_branch(q, k, v, span, dilation):
    bsz, seq, heads, dh = q.shape
    sub = seq // dilation
    nb = -(-sub // A_BLOCK)
    pad = nb * A_BLOCK - sub

    def to_blocks(a):
        a = a.reshape(bsz, sub, dilation, heads, dh).transpose(0, 2, 1, 3, 4)
        a = jnp.pad(a, ((0, 0), (0, 0), (0, pad), (0, 0), (0, 0)))
        return a.reshape(bsz, dilation, nb, A_BLOCK, heads, dh)

    def with_prev(a):
        prev = jnp.pad(a, ((0, 0), (0, 0), (1, 0), (0, 0), (0, 0), (0, 0)))[:, :, :-1]
        return jnp.concatenate([prev, a], axis=3)

    qb = to_blocks(q)
    kw = with_prev(to_blocks(k))
    vw = with_prev(to_blocks(v))
    s = jnp.einsum('bcnqhd,bcnkhd->bcnhqk', qb, kw).astype(jnp.float32) * (dh ** -0.5)
    qi = jnp.arange(A_BLOCK)[:, None] + A_BLOCK
    ki = jnp.arange(2 * A_BLOCK)[None, :]
    dist = qi - ki
    band = (dist >= 0) & (dist <= span)
    has_prev = (jnp.arange(nb) > 0)[:, None, None] | (ki >= A_BLOCK)[None]
    mask = band[None] & has_prev
    s = jnp.where(mask[:, None], s, NEG_INF)
    m = jnp.max(s, axis=-1, keepdims=True)
    p = jnp.exp(s - m)
    l = jnp.sum(p, axis=-1, keepdims=True)
    o = jnp.einsum('bcnhqk,bcnkhd->bcnqhd', p / l, vw.astype(jnp.float32))
    lse = (m + jnp.log(l))[..., 0]

    o = o.reshape(bsz, dilation, nb * A_BLOCK, heads, dh)[:, :, :sub]
    o = o.transpose(0, 2, 1, 3, 4).reshape(bsz, seq, heads, dh)
    lse = lse.transpose(0, 1, 2, 4, 3).reshape(bsz, dilation, nb * A_BLOCK, heads)[:, :, :sub]
    lse = lse.transpose(0, 2, 1, 3).reshape(bsz, seq, heads)
    return o, lse


def dilated_mixer(h, positions, w_qkv, w_o):
    bsz, seq, _ = h.shape
    qkv = (h @ w_qkv).reshape(bsz, seq, A_GROUPS, 3, A_HEADS, A_HEAD_DIM)
    outs, lses = [], []
    for g, (window, dilation) in enumerate(A_PATTERNS):
        q = rope(qkv[:, :, g, 0], positions, A_ROT_DIM)
        k = rope(qkv[:, :, g, 1], positions, A_ROT_DIM)
        o, lse = dilated_branch(q, k, qkv[:, :, g, 2], window // dilation, dilation)
        outs.append(o)
        lses.append(lse)
    wts = jax.nn.softmax(jnp.stack(lses), axis=0)[..., None]
    o = jnp.sum(wts * jnp.stack(outs), axis=0)
    return o.reshape(bsz, seq, A_HEADS * A_HEAD_DIM).astype(h.dtype) @ w_o


def mla_mixer(h, positions, w_in, q_norm, w_uq, kv_norm, w_ukv, w_o):
    bsz, seq, _ = h.shape
    c_q, c_kv, k_rope = jnp.split(h @ w_in, [B_Q_RANK, B_Q_RANK + B_KV_RANK], axis=-1)
    q = (rmsnorm(c_q, q_norm) @ w_uq).reshape(bsz, seq, B_HEADS, B_NOPE_DIM + B_ROPE_DIM)
    q_nope = q[..., :B_NOPE_DIM]
    q_rope = rope(q[..., B_NOPE_DIM:], positions, B_ROPE_DIM)
    k_rope = rope(k_rope[:, :, None, :], positions, B_ROPE_DIM)[:, :, 0]
    kv = (rmsnorm(c_kv, kv_norm) @ w_ukv).reshape(bsz, seq, B_HEADS, B_NOPE_DIM + B_V_DIM)
    k_nope, v = kv[..., :B_NOPE_DIM], kv[..., B_NOPE_DIM:]
    vf = v.astype(jnp.float32)
    scale = (B_NOPE_DIM + B_ROPE_DIM) ** -0.5
    nq = seq // B_BLOCK
    key_pos = jnp.arange(seq)

    def blocks(a):
        return a.reshape((bsz, nq, B_BLOCK) + a.shape[2:]).swapaxes(0, 1)

    def attend(args):
        start, qn, qr = args
        s = (jnp.einsum('bqhd,bkhd->bhqk', qn, k_nope)
             + jnp.einsum('bqhd,bkd->bhqk', qr, k_rope)).astype(jnp.float32) * scale
        causal = (start + jnp.arange(B_BLOCK))[:, None] >= key_pos[None, :]
        p = jax.nn.softmax(jnp.where(causal, s, NEG_INF), axis=-1)
        return jnp.einsum('bhqk,bkhd->bqhd', p, vf)

    o = lax.map(attend, (jnp.arange(nq) * B_BLOCK, blocks(q_nope), blocks(q_rope)))
    o = o.swapaxes(0, 1).reshape(bsz, seq, B_HEADS * B_V_DIM)
    return o.astype(h.dtype) @ w_o


def short_conv_mixer(h, w_in, conv_w, w_out):
    b_gate, c_gate, u = jnp.split(h @ w_in, 3, axis=-1)
    z = lax.conv_general_dilated(c_gate * u, conv_w[:, None, :], window_strides=(1,),
                                 padding=[(C_CONV - 1, 0)],
                                 dimension_numbers=('NWC', 'WIO', 'NWC'),
                                 feature_group_count=C_WIDTH)
    return (b_gate * z) @ w_out


def setup_inputs(seed: int = 0) -> dict:
    key = jax.random.key(seed)
    keys = jax.random.split(key, 128)
    counter = [0]

    def nk():
        counter[0] += 1
        return keys[counter[0] - 1]

    def dense(fan_in, fan_out):
        return jax.random.normal(nk(), (fan_in, fan_out), jnp.float32) * fan_in ** -0.5

    def gain(n):
        return 1.0 + 0.02 * jax.random.normal(nk(), (n,), jnp.float32)

    inp = {}
    inp['x'] = jax.random.normal(nk(), (BATCH, SEQ, D_MODEL), jnp.float32)
    offset = jax.random.randint(nk(), (BATCH, 1), 0, 4096, dtype=jnp.int32)
    inp['positions'] = offset + jnp.arange(SEQ, dtype=jnp.int32)[None, :]
    for i in range(DEPTH):
        p = 'l%d_' % i
        inp[p + 'ffn1_norm'] = gain(D_MODEL)
        inp[p + 'ffn1_w13'] = dense(D_MODEL, 2 * D_FF)
        inp[p + 'ffn1_w2'] = dense(D_FF, D_MODEL)
        inp[p + 'mix_norm'] = gain(D_MODEL)
        kind = i % N_MIXERS
        if kind == 0:
            inp[p + 'a_w_qkv'] = dense(D_MODEL, A_GROUPS * 3 * A_HEADS * A_HEAD_DIM)
            inp[p + 'a_w_o'] = dense(A_HEADS * A_HEAD_DIM, D_MODEL)
        elif kind == 1:
            inp[p + 'b_w_in'] = dense(D_MODEL, B_Q_RANK + B_KV_RANK + B_ROPE_DIM)
            inp[p + 'b_q_norm'] = gain(B_Q_RANK)
            inp[p + 'b_w_uq'] = dense(B_Q_RANK, B_HEADS * (B_NOPE_DIM + B_ROPE_DIM))
            inp[p + 'b_kv_norm'] = gain(B_KV_RANK)
            inp[p + 'b_w_ukv'] = dense(B_KV_RANK, B_HEADS * (B_NOPE_DIM + B_V_DIM))
            inp[p + 'b_w_o'] = dense(B_HEADS * B_V_DIM, D_MODEL)
        else:
            inp[p + 'c_w_in'] = dense(D_MODEL, 3 * C_WIDTH)
            inp[p + 'c_conv_w'] = jax.random.normal(nk(), (C_CONV, C_WIDTH), jnp.float32) * C_CONV ** -0.5
            inp[p + 'c_w_out'] = dense(C_WIDTH, D_MODEL)
        inp[p + 'ffn2_norm'] = gain(D_MODEL)
        inp[p + 'ffn2_w13'] = dense(D_MODEL, 2 * D_FF)
        inp[p + 'ffn2_w2'] = dense(D_FF, D_MODEL)
    inp['final_norm'] = gain(D_MODEL)
    return inp


def reference(x, positions,
              l0_ffn1_norm, l0_ffn1_w13, l0_ffn1_w2, l0_mix_norm, l0_a_w_qkv, l0_a_w_o,
              l0_ffn2_norm, l0_ffn2_w13, l0_ffn2_w2,
              l1_ffn1_norm, l1_ffn1_w13, l1_ffn1_w2, l1_mix_norm, l1_b_w_in, l1_b_q_norm,
              l1_b_w_uq, l1_b_kv_norm, l1_b_w_ukv, l1_b_w_o,
              l1_ffn2_norm, l1_ffn2_w13, l1_ffn2_w2,
              l2_ffn1_norm, l2_ffn1_w13, l2_ffn1_w2, l2_mix_norm, l2_c_w_in, l2_c_conv_w, l2_c_w_out,
              l2_ffn2_norm, l2_ffn2_w13, l2_ffn2_w2,
              l3_ffn1_norm, l3_ffn1_w13, l3_ffn1_w2, l3_mix_norm, l3_a_w_qkv, l3_a_w_o,
              l3_ffn2_norm, l3_ffn2_w13, l3_ffn2_w2,
              final_norm):
    ffn = [
        (l0_ffn1_norm, l0_ffn1_w13, l0_ffn1_w2, l0_mix_norm, l0_ffn2_norm, l0_ffn2_w13, l0_ffn2_w2),
        (l1_ffn1_norm, l1_ffn1_w13, l1_ffn1_w2, l1_mix_norm, l1_ffn2_norm, l1_ffn2_w13, l1_ffn2_w2),
        (l2_ffn1_norm, l2_ffn1_w13, l2_ffn1_w2, l2_mix_norm, l2_ffn2_norm, l2_ffn2_w13, l2_ffn2_w2),
        (l3_ffn1_norm, l3_ffn1_w13, l3_ffn1_w2, l3_mix_norm, l3_ffn2_norm, l3_ffn2_w13, l3_ffn2_w2),
    ]
    mixers = [
        lambda t: dilated_mixer(t, positions, l0_a_w_qkv, l0_a_w_o),
        lambda t: mla_mixer(t, positions, l1_b_w_in, l1_b_q_norm, l1_b_w_uq,
                            l1_b_kv_norm, l1_b_w_ukv, l1_b_w_o),
        lambda t: short_conv_mixer(t, l2_c_w_in, l2_c_conv_w, l2_c_w_out),
        lambda t: dilated_mixer(t, positions, l3_a_w_qkv, l3_a_w_o),
    ]
    h = x
    for i in range(DEPTH):
        n1, w13a, w2a, nm, n2, w13b, w2b = ffn[i]
        h = h + HALF_STEP * swiglu(rmsnorm(h, n1), w13a, w2a)
        h = h + mixers[i](rmsnorm(h, nm))
        h = h + HALF_STEP * swiglu(rmsnorm(h, n2), w13b, w2b)
    return rmsnorm(h, final_norm)
```

```python
import contextlib
import numpy as np
import concourse.bass as bass
import concourse.mybir as mybir
from concourse.bass_utils import run_bass_kernel_spmd

F32 = mybir.dt.float32
BF16 = mybir.dt.bfloat16
I32 = mybir.dt.int32
AF = mybir.ActivationFunctionType
ALU = mybir.AluOpType

D_MODEL = 2048
BATCH = 2
SEQ = 8192
DEPTH = 4
D_FF = 5632
ROPE_THETA = 500000.0
A_HEADS = 16
A_HEAD_DIM = 128
A_ROT = 32
A_DIL = (1, 4, 16)
B_HEADS = 16
B_Q_RANK = 1536
B_KV_RANK = 512
B_NOPE = 128
B_ROPE = 64
B_V = 128
NCORES = 8
TOK = BATCH * SEQ // NCORES
CPB = NCORES // BATCH
EPS = 1e-6


class Buf:
    __slots__ = ("w", "r")

    def __init__(self):
        self.w = None
        self.r = {}


def bufs(n):
    return [Buf() for _ in range(n)]


class DSem:
    __slots__ = ("sem", "cnt")

    def __init__(self, sem):
        self.sem = sem
        self.cnt = 0


class Sched:
    ENG = ("pe", "act", "dve", "pool", "sp")

    def __init__(self, nc, stack):
        self.nc = nc
        self.stack = stack
        self.q = {e: [] for e in self.ENG}
        self.cur = {}
        self.cnt = {}
        self.known = {e: {} for e in self.ENG}
        self.nsem = 0
        self.ds_all = []
        self.ds_hw = []
        self.ds_sw = []
        self.ds_free = []
        self.ds_free_sw = []
        for e in self.ENG:
            self.cur[e] = self._new_sem("e_" + e)
            self.cnt[e] = 0

    def _new_sem(self, name):
        self.nsem += 1
        return self.stack.enter_context(self.nc.semaphore("s%d_%s" % (self.nsem, name)))

    def dsem(self, name="d", sw=False):
        free = self.ds_free_sw if sw else self.ds_free
        if free:
            return free.pop()
        d = DSem(self._new_sem(name))
        (self.ds_sw if sw else self.ds_hw).append(d)
        self.ds_all.append(d)
        return d

    def _deps(self, reads, writes):
        deps = []
        for b in reads:
            if b.w is not None:
                deps.append(b.w)
        for b in writes:
            if b.w is not None:
                deps.append(b.w)
            deps.extend(b.r.values())
        return deps

    def _waits(self, eng, deps, skip=None):
        kn = self.known[eng]
        best = {}
        for (sem, val) in deps:
            if sem is skip:
                continue
            k = id(sem)
            if kn.get(k, 0) >= val:
                continue
            if k not in best or best[k][1] < val:
                best[k] = (sem, val)
        for k, (sem, val) in best.items():
            kn[k] = val
            self.q[eng].append(("wait", sem, val))

    def _mark(self, tok, reads, writes):
        k = id(tok[0])
        for b in reads:
            if k not in b.r or b.r[k][1] < tok[1]:
                b.r[k] = tok
        for b in writes:
            b.w = tok
            b.r = {}

    def _sig(self, eng):
        self.cnt[eng] += 1
        return (self.cur[eng], self.cnt[eng])

    def op(self, eng, fn, reads=(), writes=()):
        self._waits(eng, self._deps(reads, writes))
        tok = self._sig(eng)
        self.q[eng].append(("op", fn, tok[0]))
        self._mark(tok, reads, writes)
        return tok

    def pe_group(self, fns, reads=(), writes=()):
        self._waits("pe", self._deps(reads, writes), skip=self.cur["pe"])
        tok = self._sig("pe")
        for f in fns[:-1]:
            self.q["pe"].append(("op", f, None))
        self.q["pe"].append(("op", fns[-1], tok[0]))
        self._mark(tok, reads, writes)
        return tok

    def dma(self, queue, out, in_, dsem, reads=(), writes=(), group_final=None, **kw):
        deps = self._deps(reads, writes)
        if group_final is not None:
            deps = [d_ for d_ in deps if not (d_[0] is dsem.sem and d_[1] == group_final)]
        self._waits(queue, deps)
        dsem.cnt += 16
        tok = (dsem.sem, dsem.cnt if group_final is None else group_final)
        self.q[queue].append(("dma", out, in_, dsem.sem, kw))
        self._mark(tok, reads, writes)
        return tok

    def coll(self, kind, groups, src, dst, dsem, reads=(), writes=()):
        self._waits("pool", self._deps(reads, writes))
        dsem.cnt += 1
        tok = (dsem.sem, dsem.cnt)
        self.q["pool"].append(("coll", kind, groups, src, dst, dsem.sem))
        self._mark(tok, reads, writes)
        return tok

    def call(self, eng, fn, reads=(), writes=()):
        self._waits(eng, self._deps(reads, writes))
        self.q[eng].append(("call", fn))

    def wait_tok(self, eng, tok):
        self._waits(eng, [tok])

    def barrier(self, recycle=True):
        for e in self.ENG:
            if e != "sp" and self.cnt[e] > 0:
                self.wait_tok("sp", (self.cur[e], self.cnt[e]))
        for d in self.ds_all:
            if d.cnt > 0:
                self.wait_tok("sp", (d.sem, d.cnt))
        tok = self._sig("sp")
        self.q["sp"].append(("inc", tok[0]))
        for e in self.ENG:
            if e != "sp":
                self.wait_tok(e, tok)
        if recycle:
            self.ds_free = list(self.ds_hw)
            self.ds_free_sw = list(self.ds_sw)

    def emit(self):
        def run(e, items):
            for it in items:
                if it[0] == "wait":
                    e.wait_ge(it[1], it[2])
                elif it[0] == "op":
                    ins = it[1](e)
                    if it[2] is not None:
                        ins.then_inc(it[2], 1)
                elif it[0] == "inc":
                    e.sem_inc(it[1], 1)
                elif it[0] == "call":
                    it[1](e)
                elif it[0] == "coll":
                    _, kind, groups, src, dst, sem = it
                    e.collective_compute(kind, ALU.bypass, replica_groups=groups, ins=[src.opt()], outs=[dst.opt()]).then_inc(sem, 1)
                else:
                    _, out, in_, sem, kw = it
                    if callable(in_):
                        in_ = in_()
                    e.dma_start(out=out, in_=in_, **kw).then_inc(sem, 16)

        with self.nc.Block() as block:
            @block.tensor
            def _(e):
                run(e, self.q["pe"])

            @block.scalar
            def _(e):
                run(e, self.q["act"])

            @block.vector
            def _(e):
                run(e, self.q["dve"])

            @block.gpsimd
            def _(e):
                run(e, self.q["pool"])

            @block.sync
            def _(e):
                run(e, self.q["sp"])


class Stream:
    def __init__(self, S, queue, slots):
        self.S = S
        self.queue = queue
        self.slots = slots
        self.items = []
        self.issued = 0

    def add(self, fn):
        self.items.append(fn)
        return len(self.items) - 1

    def ensure(self, upto):
        upto = min(upto, len(self.items) - 1)
        while self.issued <= upto:
            i = self.issued
            t, buf, ds = self.slots[i % len(self.slots)]
            o, a, kw = self.items[i](t)
            self.S.dma(self.queue, o, a, ds, writes=[buf], **kw)
            self.issued += 1

    def get(self, i):
        self.ensure(i + len(self.slots) - 1)
        return self.slots[i % len(self.slots)]


class Ctx:
    def __init__(self, nc, gstack):
        self.nc = nc
        self.gstack = gstack
        self.stack = gstack
        self.S = Sched(nc, gstack)
        self.n = 0
        self.K = {}

    def sb(self, shape, dt, name="sb"):
        self.n += 1
        return self.stack.enter_context(self.nc.sbuf_tensor("%s_%d" % (name, self.n), list(shape), dt))

    def ps(self, shape, dt=F32, name="ps"):
        self.n += 1
        return self.stack.enter_context(self.nc.psum_tensor("%s_%d" % (name, self.n), list(shape), dt))

    @contextlib.contextmanager
    def phase(self):
        with contextlib.ExitStack() as st:
            self.stack = st
            yield
            self.S.barrier()
        self.stack = self.gstack


def MM(out, lhsT, rhs, start, stop):
    return lambda e: e.matmul(out, lhsT, rhs, start=start, stop=stop)


def ACT(out, in_, func, **kw):
    return lambda e: e.activation(out=out, in_=in_, func=func, **kw)


def TT(out, in0, in1, op):
    return lambda e: e.tensor_tensor(out=out, in0=in0, in1=in1, op=op)


def TS(out, in0, s1, s2, op0, op1=None):
    if op1 is None:
        return lambda e: e.tensor_scalar(out=out, in0=in0, scalar1=s1, scalar2=None, op0=op0)
    return lambda e: e.tensor_scalar(out=out, in0=in0, scalar1=s1, scalar2=s2, op0=op0, op1=op1)


def STT(out, in0, scalar, in1, op0, op1):
    return lambda e: e.scalar_tensor_tensor(out=out, in0=in0, scalar=scalar, in1=in1, op0=op0, op1=op1)


def CP(out, in_):
    return lambda e: e.tensor_copy(out=out, in_=in_)


def MEMSET(ap, v):
    return lambda e: e.memset(ap, v)


def RECIP(out, in_):
    return lambda e: e.reciprocal(out, in_)


CAST_KW = dict(max_dma_last_dim=4096)


def emit_consts(C, ident_d):
    S = C.S
    K = C.K
    K["ones"] = C.sb([128, 128], BF16, "ones")
    K["ones_b"] = Buf()
    S.op("pool", MEMSET(K["ones"][:, :], 1.0), writes=[K["ones_b"]])
    K["eps"] = C.sb([128, 1], F32, "eps")
    K["eps_b"] = Buf()
    S.op("pool", MEMSET(K["eps"][:, :], EPS), writes=[K["eps_b"]])
    K["ident"] = C.sb([128, 128], F32, "ident")
    K["ident_b"] = Buf()
    ds = S.dsem()
    S.dma("sp", K["ident"][:, :], ident_d[:, :], ds, writes=[K["ident_b"]])


def norm_res(C, KC, NT, nslot=2):
    S = C.S
    R = {}
    R["h"] = [(C.sb([128, KC, NT], F32, "h"), bufs(KC), S.dsem("h")) for _ in range(nslot)]
    R["ssum"] = (C.ps([128, NT], F32, "ssum"), Buf())
    R["sq"] = [(C.sb([128, NT], BF16, "sq"), Buf()) for _ in range(3)]
    R["rs"] = (C.sb([128, NT], F32, "rs"), Buf())
    return R


def emit_rstd(C, R, src, srcb, KC, NT, Dn):
    S = C.S
    K = C.K
    ssum, ssum_b = R["ssum"]
    for k in range(KC):
        sq, sq_b = R["sq"][k % len(R["sq"])]
        S.op("act", ACT(sq[:, :NT], src(k), AF.Square), reads=[srcb[k]], writes=[sq_b])
        S.pe_group([MM(ssum[:, :NT], K["ones"][:, :], sq[:, :NT], k == 0, k == KC - 1)],
                   reads=[sq_b, K["ones_b"]], writes=[ssum_b])
    rs, rs_b = R["rs"]
    S.op("act", ACT(rs[:, :NT], ssum[:, :NT], AF.Sqrt, scale=1.0 / Dn, bias=K["eps"][:, :]),
         reads=[ssum_b, K["eps_b"]], writes=[rs_b])
    S.op("dve", RECIP(rs[:, :NT], rs[:, :NT]), reads=[rs_b], writes=[rs_b])
    return rs, rs_b


def emit_norm_full(C, hT_d, gain, gain_b, xn, xnb, KC, T, d=1, NT=256):
    S = C.S
    R = norm_res(C, KC, NT)
    for t in range(T // NT):
        h, hb, hds = R["h"][t % 2]
        tsl = slice(t * NT, (t + 1) * NT)
        S.dma("sp", h[:, :, :], hT_d[:, :, tsl].rearrange("k p n -> p k n"), hds, writes=hb)
        rs, rs_b = emit_rstd(C, R, lambda k, h=h: h[:, k, :], hb, KC, NT, KC * 128)
        for k in range(KC):
            if d == 1:
                o = xn[:, k, tsl]
                i0 = h[:, k, :]
                i1 = rs[:, :]
            else:
                m0 = t * NT // d
                o = xn[:, k, :].rearrange("p (r m) -> p m r", r=d)[:, m0:m0 + NT // d, :]
                i0 = h[:, k, :].rearrange("p (m r) -> p m r", r=d)
                i1 = rs[:, :].rearrange("p (m r) -> p m r", r=d)
            S.op("dve", STT(o, i0, gain[:, k:k + 1], i1, ALU.mult, ALU.mult),
                 reads=[hb[k], rs_b, gain_b], writes=[xnb[k]])


def emit_ffn(C, hT_d, gain_d, w13r, w2r, D=D_MODEL, DFF=D_FF, T=TOK, NT=512):
    S = C.S
    K = C.K
    KC = D // 128
    NJ = DFF // 128
    NTILE = T // NT
    with C.phase():
        w13s = [(C.sb([128, KC * 256], BF16, "w13"), Buf(), S.dsem("w13", sw=True)) for _ in range(2)]
        w2s = [(C.sb([128, NJ * 128], BF16, "w2"), Buf(), S.dsem("w2", sw=True)) for _ in range(2)]
        gain = C.sb([128, KC], F32, "gain")
        gain_b = Buf()
        S.dma("sp", gain[:, :], gain_d[:, :], S.dsem(), writes=[gain_b])
        R = norm_res(C, KC, NT)
        hst = [S.dsem("hst") for _ in range(2)]
        xn = C.sb([128, KC, NT], BF16, "xn")
        xb = bufs(KC)
        hid = C.sb([128, NJ, NT], BF16, "hid")
        hidb = bufs(NJ)
        G = [(C.ps([128, NT], F32, "G"), Buf()) for _ in range(2)]
        U = [(C.ps([128, NT], F32, "U"), Buf()) for _ in range(2)]
        Y = [(C.ps([128, NT], F32, "Y"), Buf()) for _ in range(2)]
        sgs = [(C.sb([128, NT], F32, "sg"), Buf()) for _ in range(2)]

        st13 = Stream(S, "pool", w13s)
        st2 = Stream(S, "pool", w2s)
        for t in range(NTILE):
            for j in range(NJ):
                st13.add(lambda tt, j=j: (tt[:, :], w13r[j, :, :], CAST_KW))
            for m in range(KC):
                st2.add(lambda tt, m=m: (tt[:, :], w2r[m, :, :], CAST_KW))

        for t in range(NTILE):
            hs = t % 2
            h, hb, hds = R["h"][hs]
            tsl = slice(t * NT, (t + 1) * NT)
            S.dma("sp", h[:, :, :], hT_d[:, :, tsl].rearrange("k p n -> p k n"), hds, writes=hb)
            st13.ensure(t * NJ + 1)
            rs, rs_b = emit_rstd(C, R, lambda k, h=h: h[:, k, :], hb, KC, NT, D)
            for k in range(KC):
                S.op("dve", STT(xn[:, k, :], h[:, k, :], gain[:, k:k + 1], rs[:, :], ALU.mult, ALU.mult),
                     reads=[hb[k], rs_b, gain_b], writes=[xb[k]])
            for j in range(NJ):
                w, wb, _ = st13.get(t * NJ + j)
                if j == NJ - 4:
                    st2.ensure(t * KC + 1)
                g_, gb = G[j % 2]
                u_, ub = U[j % 2]
                S.pe_group([MM(g_[:, :], w[:, (2 * k) * 128:(2 * k + 1) * 128], xn[:, k, :], k == 0, k == KC - 1)
                            for k in range(KC)], reads=[wb] + xb, writes=[gb])
                S.pe_group([MM(u_[:, :], w[:, (2 * k + 1) * 128:(2 * k + 2) * 128], xn[:, k, :], k == 0, k == KC - 1)
                            for k in range(KC)], reads=[wb] + xb, writes=[ub])
                sg, sgb = sgs[j % 2]
                S.op("act", ACT(sg[:, :], g_[:, :], AF.Silu), reads=[gb], writes=[sgb])
                S.op("dve", TT(hid[:, j, :], sg[:, :], u_[:, :], ALU.mult), reads=[sgb, ub], writes=[hidb[j]])
            for m in range(KC):
                w, wb, _ = st2.get(t * KC + m)
                if m == KC - 2 and t + 1 < NTILE:
                    st13.ensure((t + 1) * NJ + 1)
                y_, yb = Y[m % 2]
                S.pe_group([MM(y_[:, :], w[:, j * 128:(j + 1) * 128], hid[:, j, :], j == 0, j == NJ - 1)
                            for j in range(NJ)], reads=[wb] + hidb, writes=[yb])
                S.op("dve", STT(h[:, m, :], y_[:, :], 0.5, h[:, m, :], ALU.mult, ALU.add),
                     reads=[yb, hb[m]], writes=[hb[m]])
                if m == 0:
                    fin = hst[hs].cnt + 16 * KC
                S.dma("sp", hT_d[m, :, tsl], h[:, m, :], hst[hs], reads=[hb[m]], group_final=fin)


def emit_pin(C, x_d, hT_d, D=D_MODEL, T=TOK):
    S = C.S
    K = C.K
    KC = D // 128
    with C.phase():
        xs = [(C.sb([128, D], F32, "x"), Buf(), S.dsem("x")) for _ in range(8)]
        hst = [(C.sb([128, KC, 512], F32, "hst"), bufs(KC), S.dsem("hst")) for _ in range(2)]
        ps = [(C.ps([128, 512], F32, "tp"), Buf()) for _ in range(2)]
        n = 0
        for grp in range(T // 512):
            xt = []
            for i in range(4):
                x, xb, xds = xs[(grp % 2) * 4 + i]
                t0 = grp * 512 + i * 128
                S.dma("sp", x[:, :], x_d[t0:t0 + 128, :], xds, writes=[xb])
                xt.append((x, xb))
            hs, hsb, hds = hst[grp % 2]
            for k in range(KC):
                p, pb = ps[n % 2]
                S.pe_group([lambda e, p=p, x=xt[i][0], i=i, k=k: e.transpose(
                    p[:, i * 128:(i + 1) * 128], x[:, k * 128:(k + 1) * 128], K["ident"][:, :]) for i in range(4)],
                    reads=[b for (_, b) in xt] + [K["ident_b"]], writes=[pb])
                eng = "act" if n % 2 == 0 else "dve"
                if eng == "act":
                    S.op("act", ACT(hs[:, k, :], p[:, :], AF.Copy), reads=[pb], writes=[hsb[k]])
                else:
                    S.op("dve", CP(hs[:, k, :], p[:, :]), reads=[pb], writes=[hsb[k]])
                n += 1
            S.dma("sp", hT_d[:, :, grp * 512:(grp + 1) * 512].rearrange("k p n -> p k n"), hs[:, :, :], hds, reads=hsb)


def emit_final(C, hT_d, gain_d, out_d, D=D_MODEL, T=TOK, NT=512):
    S = C.S
    K = C.K
    KC = D // 128
    with C.phase():
        gain = C.sb([128, KC], F32, "gain")
        gain_b = Buf()
        S.dma("sp", gain[:, :], gain_d[:, :], S.dsem(), writes=[gain_b])
        R = norm_res(C, KC, NT)
        xn = C.sb([128, KC, NT], F32, "xn32")
        xb = bufs(KC)
        osb = [(C.sb([128, D], F32, "osb"), Buf(), S.dsem("o")) for _ in range(2)]
        ps = [(C.ps([128, 512], F32, "tp"), Buf()) for _ in range(2)]
        n = 0
        no = 0
        last = []
        for t in range(T // NT):
            h, hb, hds = R["h"][t % 2]
            tsl = slice(t * NT, (t + 1) * NT)
            S.dma("sp", h[:, :, :], hT_d[:, :, tsl].rearrange("k p n -> p k n"), hds, writes=hb)
            rs, rs_b = emit_rstd(C, R, lambda k, h=h: h[:, k, :], hb, KC, NT, D)
            for k in range(KC):
                S.op("dve", STT(xn[:, k, :], h[:, k, :], gain[:, k:k + 1], rs[:, :], ALU.mult, ALU.mult),
                     reads=[hb[k], rs_b, gain_b], writes=[xb[k]])
            for i in range(NT // 128):
                o, ob, ods = osb[no % 2]
                no += 1
                for k0 in range(0, KC, 4):
                    p, pb = ps[n % 2]
                    S.pe_group([lambda e, p=p, kk=kk, k0=k0, i=i: e.transpose(
                        p[:, kk * 128:(kk + 1) * 128], xn[:, k0 + kk, i * 128:(i + 1) * 128], K["ident"][:, :])
                        for kk in range(4)], reads=xb[k0:k0 + 4] + [K["ident_b"]], writes=[pb])
                    if n % 2 == 0:
                        S.op("act", ACT(o[:, k0 * 128:(k0 + 4) * 128], p[:, :], AF.Copy), reads=[pb], writes=[ob])
                    else:
                        S.op("dve", CP(o[:, k0 * 128:(k0 + 4) * 128], p[:, :]), reads=[pb], writes=[ob])
                    n += 1
                t0 = t * NT + i * 128
                last.append(S.dma("sp", out_d[t0:t0 + 128, :], o[:, :], ods, reads=[ob]))
        for tk in last[-2:]:
            S.wait_tok("sp", tk)


def emit_oproj(C, g, gb, wr_d, hT_d, KCin, T=TOK, KCout=D_MODEL // 128, NT=512, scale=1.0):
    S = C.S
    ws = [(C.sb([128, KCin * 128], BF16, "wo"), Buf(), S.dsem("wo", sw=True)) for _ in range(2)]
    hr = [(C.sb([128, T], F32, "hrow"), Buf(), S.dsem("hr"), S.dsem("hrs")) for _ in range(2)]
    Y = [(C.ps([128, NT], F32, "Y"), Buf()) for _ in range(2)]
    st = Stream(S, "pool", ws)
    for m in range(KCout):
        st.add(lambda tt, m=m: (tt[:, :], wr_d[m, :, :], CAST_KW))
    n = 0
    for m in range(KCout):
        w, wb, _ = st.get(m)
        h, hb, hds, hss = hr[m % 2]
        S.dma("sp", h[:, :], hT_d[m, :, :], hds, writes=[hb])
        for t in range(T // NT):
            y_, yb = Y[n % 2]
            n += 1
            tsl = slice(t * NT, (t + 1) * NT)
            S.pe_group([MM(y_[:, :], w[:, k * 128:(k + 1) * 128], g[:, k, tsl], k == 0, k == KCin - 1)
                        for k in range(KCin)], reads=[wb] + list(gb), writes=[yb])
            S.op("dve", STT(h[:, tsl], y_[:, :], scale, h[:, tsl], ALU.mult, ALU.add), reads=[yb, hb], writes=[hb])
        S.dma("sp", hT_d[m, :, :], h[:, :], hss, reads=[hb])


def emit_mixC_1(C, hT_d, gain_d, winr, cuT_d, bT_d, cux_d, D=D_MODEL, T=TOK, NT=512, gather=None):
    S = C.S
    KC = D // 128
    with C.phase():
        gain = C.sb([128, KC], F32, "gain")
        gain_b = Buf()
        S.dma("sp", gain[:, :], gain_d[:, :], S.dsem(), writes=[gain_b])
        xn = C.sb([128, KC, T], BF16, "xnf")
        xb = bufs(KC)
        emit_norm_full(C, hT_d, gain, gain_b, xn, xb, KC, T)
        ws = [(C.sb([128, 3 * KC * 128], BF16, "win"), Buf(), S.dsem("win", sw=True)) for _ in range(2)]
        st = Stream(S, "pool", ws)
        for m in range(KC):
            st.add(lambda tt, m=m: (tt[:, :], winr[m, :, :], CAST_KW))
        P = [[(C.ps([128, NT], F32, "P"), Buf()) for _ in range(2)] for _ in range(3)]
        csb = [(C.sb([128, NT], F32, "c"), Buf()) for _ in range(2)]
        cur = [(C.sb([128, T], F32, "cu"), Buf(), S.dsem("cu")) for _ in range(2)]
        br = [(C.sb([128, T], F32, "b"), Buf(), S.dsem("b")) for _ in range(2)]
        xds = [S.dsem("cux") for _ in range(2)]
        n = 0
        for m in range(KC):
            w, wb, _ = st.get(m)
            cu, cub, cuds = cur[m % 2]
            b_, bb, bds = br[m % 2]
            for t in range(T // NT):
                tsl = slice(t * NT, (t + 1) * NT)
                pp = [P[s][n % 2] for s in range(3)]
                for s in range(3):
                    S.pe_group([MM(pp[s][0][:, :], w[:, (s * KC + k) * 128:(s * KC + k + 1) * 128], xn[:, k, tsl],
                                   k == 0, k == KC - 1) for k in range(KC)], reads=[wb] + xb, writes=[pp[s][1]])
                c_, cb = csb[n % 2]
                n += 1
                S.op("act", ACT(b_[:, tsl], pp[0][0][:, :], AF.Copy), reads=[pp[0][1]], writes=[bb])
                S.op("act", ACT(c_[:, :], pp[1][0][:, :], AF.Copy), reads=[pp[1][1]], writes=[cb])
                S.op("dve", TT(cu[:, tsl], c_[:, :], pp[2][0][:, :], ALU.mult), reads=[cb, pp[2][1]], writes=[cub])
            S.dma("sp", cuT_d[m, :, :], cu[:, :], cuds, reads=[cub])
            S.dma("sp", bT_d[m, :, :], b_[:, :], bds, reads=[bb])
            S.dma("sp", cux_d[0][:, 2 * m:2 * m + 2], cu[:, T - 2:T], xds[m % 2], reads=[cub])
        if gather is not None:
            S.barrier(recycle=False)
            emit_gather(C, cux_d, gather, 1)


def emit_mixC_2(C, hT_d, cuT_d, bT_d, cuh_d, convw_d, hmask_d, woutr, D=D_MODEL, T=TOK):
    S = C.S
    KC = D // 128
    with C.phase():
        cw = C.sb([128, KC, 3], F32, "convw")
        cwb = Buf()
        S.dma("sp", cw[:, :, :], convw_d[:, :, :], S.dsem(), writes=[cwb])
        hm = C.sb([128, 1], F32, "hmask")
        hmb = Buf()
        S.dma("sp", hm[:, :], hmask_d[:, :], S.dsem(), writes=[hmb])
        g = C.sb([128, KC, T], BF16, "g")
        gb = bufs(KC)
        cur = [(C.sb([128, T + 2], F32, "cu"), Buf(), Buf(), S.dsem("cu"), S.dsem("cuh")) for _ in range(2)]
        br = [(C.sb([128, T], F32, "b"), Buf(), S.dsem("b")) for _ in range(2)]
        zs = [(C.sb([128, T], F32, "z"), Buf()) for _ in range(2)]
        for m in range(KC):
            cu, cub, chb, cuds, chds = cur[m % 2]
            b_, bb, bds = br[m % 2]
            z, zb = zs[m % 2]
            S.dma("sp", cu[:, 2:T + 2], cuT_d[m, :, :], cuds, writes=[cub])
            S.dma("sp", cu[:, 0:2], cuh_d[0][:, 2 * m:2 * m + 2], chds, writes=[chb])
            S.dma("sp", b_[:, :], bT_d[m, :, :], bds, writes=[bb])
            S.op("dve", TS(cu[:, 0:2], cu[:, 0:2], hm[:, 0:1], None, ALU.mult), reads=[hmb], writes=[chb])
            S.op("dve", TS(z[:, :], cu[:, 2:T + 2], cw[:, m, 2:3], None, ALU.mult), reads=[cub, cwb], writes=[zb])
            S.op("dve", STT(z[:, :], cu[:, 1:T + 1], cw[:, m, 1:2], z[:, :], ALU.mult, ALU.add),
                 reads=[cub, chb, cwb, zb], writes=[zb])
            S.op("dve", STT(z[:, :], cu[:, 0:T], cw[:, m, 0:1], z[:, :], ALU.mult, ALU.add),
                 reads=[cub, chb, cwb, zb], writes=[zb])
            S.op("pool", TT(g[:, m, :], z[:, :], b_[:, :], ALU.mult), reads=[zb, bb], writes=[gb[m]])
        emit_oproj(C, g, gb, woutr, hT_d, KC, T, KCout=KC)


def emit_dram_copy(C, dst, src):
    S = C.S
    ds = S.dsem("cp")
    fin = ds.cnt + 16 * dst.shape[0]
    for k in range(dst.shape[0]):
        S.dma("sp", dst[k, :, :], src[k, :, :], ds, group_final=fin)
    S.barrier()


TWO_PI = 6.283185307179586
CW1 = 6.28125
CW2 = float(np.float32(np.frombuffer(np.uint32(np.frombuffer(np.float32(TWO_PI - CW1).tobytes(), np.uint32)[0] & 0xFFFFF000).tobytes(), np.float32)[0]))
CW3 = float(np.float32(TWO_PI - CW1 - CW2))
PI_LO = 3.1415925


def _wrap(S, r, rb, m, mb):
    S.op("dve", TS(m, r, float(np.pi), -TWO_PI, ALU.is_gt, ALU.mult), reads=[rb], writes=[mb])
    S.op("dve", TT(r, r, m, ALU.add), reads=[rb, mb], writes=[rb])
    S.op("dve", TS(m, r, -float(np.pi), TWO_PI, ALU.is_lt, ALU.mult), reads=[rb], writes=[mb])
    S.op("dve", TT(r, r, m, ALU.add), reads=[rb, mb], writes=[rb])
    S.op("dve", TS(r, r, PI_LO, -PI_LO, ALU.min, ALU.max), reads=[rb], writes=[rb])


def emit_tables(C, pos_d, frq_d, tab_d, dils, T=TOK):
    S = C.S
    with C.phase():
        pos_i = C.sb([128, T], I32, "posi")
        pos_f = C.sb([128, T], F32, "posf")
        pb = Buf()
        S.dma("sp", pos_i[:, :], pos_d[0:1, :].partition_broadcast(128), S.dsem(), writes=[pb])
        S.op("dve", CP(pos_f[:, :], pos_i[:, :]), reads=[pb], writes=[pb])
        x = C.sb([128, T], F32, "ang")
        r = C.sb([128, T], F32, "red")
        m = C.sb([128, T], F32, "msk")
        kf = C.sb([128, T], F32, "kf")
        ki = C.sb([128, T], I32, "ki")
        sn = C.sb([128, T], F32, "sin")
        cs = C.sb([128, T], F32, "cos")
        xb, rb, mb, kb, snb, csb = bufs(6)
        for i, d in enumerate(dils):
            fr = C.sb([128, 2], F32, "frq")
            fb = Buf()
            S.dma("sp", fr[:, :], frq_d[i, :, :], S.dsem(), writes=[fb])
            if d == 1:
                S.op("dve", TS(x[:, :], pos_f[:, :], fr[:, 0:1], None, ALU.mult), reads=[pb, fb], writes=[xb])
            else:
                S.op("dve", TS(x[:, :].rearrange("p (r m) -> p m r", r=d), pos_f[:, :].rearrange("p (m r) -> p m r", r=d),
                               fr[:, 0:1], None, ALU.mult), reads=[pb, fb], writes=[xb])
            S.op("dve", TS(ki[:, :], x[:, :], 1.0 / TWO_PI, None, ALU.mult), reads=[xb], writes=[kb])
            S.op("dve", CP(kf[:, :], ki[:, :]), reads=[kb], writes=[kb])
            S.op("dve", STT(r[:, :], kf[:, :], -CW1, x[:, :], ALU.mult, ALU.add), reads=[kb, xb], writes=[rb])
            S.op("dve", STT(r[:, :], kf[:, :], -CW2, r[:, :], ALU.mult, ALU.add), reads=[kb, rb], writes=[rb])
            S.op("dve", STT(r[:, :], kf[:, :], -CW3, r[:, :], ALU.mult, ALU.add), reads=[kb, rb], writes=[rb])
            _wrap(S, r[:, :], rb, m[:, :], mb)
            S.op("act", ACT(sn[:, :], r[:, :], AF.Sin), reads=[rb], writes=[snb])
            S.op("dve", TS(sn[:, :], sn[:, :], fr[:, 1:2], None, ALU.mult), reads=[snb, fb], writes=[snb])
            S.op("dve", TS(r[:, :], r[:, :], float(np.pi / 2), None, ALU.add), reads=[rb, snb], writes=[rb])
            _wrap(S, r[:, :], rb, m[:, :], mb)
            S.op("act", ACT(cs[:, :], r[:, :], AF.Sin), reads=[rb], writes=[csb])
            S.dma("sp", tab_d[i, 0, :, :], cs[:, :], S.dsem(), reads=[csb])
            S.dma("sp", tab_d[i, 1, :, :], sn[:, :], S.dsem(), reads=[snb])


def emit_rope_evac(S, p, pb, st_out, stb, ct, sn, tb_, q32, q32b, tA, tAb, tB, tBb, h):
    hi = 32 + h
    if p is not None:
        S.op("act", ACT(q32[:, :], p[:, :], AF.Copy), reads=[pb], writes=[q32b])
    S.op("pool", CP(st_out, q32[:, :]), reads=[q32b], writes=[stb])
    S.op("dve", TT(tA[0:hi, :], q32[0:hi, :], ct[0:hi, :], ALU.mult), reads=[q32b, tb_], writes=[tAb])
    S.op("dve", TT(tB[0:h, :], q32[32:hi, :], sn[32:hi, :], ALU.mult), reads=[q32b, tb_], writes=[tBb])
    S.op("dve", TT(tB[32:hi, :], q32[0:h, :], sn[0:h, :], ALU.mult), reads=[q32b, tb_], writes=[tBb])
    S.op("dve", TT(st_out[0:hi, :], tA[0:hi, :], tB[0:hi, :], ALU.add), reads=[tAb, tBb], writes=[stb])


def halo_off(dils):
    off = [0]
    for d in dils:
        off.append(off[-1] + d)
    return off


def emit_mixA_1(C, hT_d, gain_d, wqkv_r, tab_d, QT_d, KT_d, V_d, KTx_d, Vx_d, dils=A_DIL, H=A_HEADS,
                D=D_MODEL, T=TOK, NT=512, gather=None):
    S = C.S
    KC = D // 128
    HB = H // 4
    NB = T // 128
    off = halo_off(dils)
    with C.phase():
        gain = C.sb([128, KC], F32, "gain")
        gain_b = Buf()
        S.dma("sp", gain[:, :], gain_d[:, :], S.dsem(), writes=[gain_b])
        xn = C.sb([128, KC, T], BF16, "xnf")
        xb = bufs(KC)
        ws = [(C.sb([128, 4 * KC * 128], BF16, "wqkv"), Buf(), S.dsem("wqkv", sw=True)) for _ in range(2)]
        st = Stream(S, "pool", ws)
        for i in range(len(dils) * 3 * HB):
            st.add(lambda tt, i=i: (tt[:, :], wqkv_r[i, :, :], CAST_KW))
        stg = [(C.sb([128, 4, T], BF16, "stg"), Buf(), S.dsem("stg"), S.dsem("stgx")) for _ in range(2)]
        P = [(C.ps([128, NT], F32, "P"), Buf()) for _ in range(2)]
        ct = C.sb([128, T], F32, "ct")
        sn = C.sb([128, T], F32, "sn")
        tb_ = Buf()
        tA = [(C.sb([128, NT], F32, "tA"), Buf()) for _ in range(2)]
        tB = [(C.sb([128, NT], F32, "tB"), Buf()) for _ in range(2)]
        q32s = [(C.sb([128, NT], F32, "q32"), Buf()) for _ in range(2)]
        for (t_, b_) in tB:
            S.op("pool", MEMSET(t_[:, :], 0.0), writes=[b_])
        n = 0
        ns = 0
        wi = 0
        for gi, d in enumerate(dils):
            nb = NB // d
            with contextlib.ExitStack() as nst:
                old = C.stack
                C.stack = nst
                emit_norm_full(C, hT_d, gain, gain_b, xn, xb, KC, T, d=d)
                C.stack = old
                S.barrier(recycle=False)
            tds = S.dsem()
            fin = tds.cnt + 32
            S.dma("sp", ct[:, :], tab_d[gi, 0, :, :], tds, writes=[tb_], group_final=fin)
            S.dma("sp", sn[:, :], tab_d[gi, 1, :, :], tds, writes=[tb_], group_final=fin)
            for s in range(3):
                for hb in range(HB):
                    w, wb, _ = st.get(wi)
                    wi += 1
                    sg, sgb, sds, sxs = stg[ns % 2]
                    ns += 1
                    if s < 2:
                        for hh in range(4):
                            for t in range(T // NT):
                                tsl = slice(t * NT, (t + 1) * NT)
                                p, pb = P[n % 2]
                                a_, ab = tA[n % 2]
                                b2, bb = tB[n % 2]
                                n += 1
                                S.pe_group([MM(p[:, :], w[:, (hh * KC + k) * 128:(hh * KC + k + 1) * 128], xn[:, k, tsl],
                                               k == 0, k == KC - 1) for k in range(KC)], reads=[wb] + xb, writes=[pb])
                                q32, q32b = q32s[n % 2]
                                emit_rope_evac(S, p, pb, sg[:, hh, tsl], sgb, ct[:, tsl], sn[:, tsl], tb_,
                                               q32, q32b, a_, ab, b2, bb, 16)
                        dst = (QT_d if s == 0 else KT_d)[gi, 4 * hb:4 * hb + 4, :, :].rearrange("h p t -> p h t")
                        S.dma("sp", dst, sg[:, :, :], sds, reads=[sgb])
                        if s == 1:
                            fin = sxs.cnt + 16 * d
                            for r in range(d):
                                c0 = (r * nb + nb - 1) * 128
                                S.dma("sp", KTx_d[4 * hb:4 * hb + 4, :, (off[gi] + r) * 128:(off[gi] + r + 1) * 128]
                                      .rearrange("h p c -> p h c"), sg[:, :, c0:c0 + 128], sxs, reads=[sgb], group_final=fin)
                    else:
                        sg4 = sg[:, :, :].rearrange("p h (b c) -> p h b c", c=128)
                        for bi in range(NB):
                            p, pb = P[n % 2]
                            n += 1
                            S.pe_group([MM(p[:, :], xn[:, k, bi * 128:(bi + 1) * 128], w[:, k * 512:(k + 1) * 512],
                                           k == 0, k == KC - 1) for k in range(KC)], reads=[wb] + xb, writes=[pb])
                            src = p[:, :].rearrange("p (h c) -> p h c", h=4)
                            if n % 2 == 0:
                                S.op("act", ACT(sg4[:, :, bi, :], src, AF.Copy), reads=[pb], writes=[sgb])
                            else:
                                S.op("dve", CP(sg4[:, :, bi, :], src), reads=[pb], writes=[sgb])
                        S.dma("sp", V_d[gi, 4 * hb:4 * hb + 4, :, :].rearrange("h p x -> p h x"), sg[:, :, :], sds, reads=[sgb])
                        fin = sxs.cnt + 16 * d
                        for r in range(d):
                            bi = r * nb + nb - 1
                            S.dma("sp", Vx_d[4 * hb:4 * hb + 4, :, off[gi] + r, :].rearrange("h p c -> p h c"),
                                  sg4[:, :, bi, :], sxs, reads=[sgb], group_final=fin)
        if gather is not None:
            S.barrier(recycle=False)
            emit_gather(C, KTx_d, gather[0], H)
            emit_gather(C, Vx_d, gather[1], H)


def emit_mixA_2(C, hT_d, QT_d, KT_d, V_d, KTh_d, Vh_d, hmask_d, mask_d, wo_r, dils=A_DIL, H=A_HEADS,
                D=D_MODEL, T=TOK):
    S = C.S
    K = C.K
    KC = D // 128
    NB = T // 128
    off = halo_off(dils)
    nhmax = max(dils)
    scale = float(A_HEAD_DIM) ** -0.5
    with C.phase():
        hm = C.sb([128, 1], F32, "hmask")
        hmb = Buf()
        S.dma("sp", hm[:, :], hmask_d[:, :], S.dsem(), writes=[hmb])
        mk32 = C.sb([128, 256], F32, "mk32")
        mkb = Buf()
        S.dma("sp", mk32[:, :], mask_d[:, :], S.dsem(), writes=[mkb])
        M = C.sb([128, 256], BF16, "M")
        Mh = C.sb([128, 256], BF16, "Mh")
        Mb, Mhb = Buf(), Buf()
        S.op("dve", CP(M[:, :], mk32[:, :]), reads=[mkb], writes=[Mb])
        S.op("dve", CP(Mh[:, 128:256], mk32[:, 128:256]), reads=[mkb], writes=[Mhb])
        S.op("dve", TS(Mh[:, 0:128], mk32[:, 0:128], hm[:, 0:1], None, ALU.mult), reads=[mkb, hmb], writes=[Mhb])
        oT = C.sb([128, H, T], BF16, "oT")
        oTb = bufs(H)
        slots = []
        for _ in range(2):
            slots.append(dict(q=C.sb([128, T], BF16, "q"), k=C.sb([128, (nhmax + NB) * 128], BF16, "k"),
                              v=C.sb([128, nhmax + NB, 128], BF16, "v"), b=Buf(), ds=S.dsem("qkv")))
        Oa = C.sb([128, T], F32, "Oacc")
        La = C.sb([128, T], F32, "Lacc")
        Oab, Lab = Buf(), Buf()
        pe_ = [(C.sb([128, 256], BF16, "pe"), Buf()) for _ in range(2)]
        pm_ = [(C.sb([128, 256], BF16, "pm"), Buf()) for _ in range(2)]
        pS = [(C.ps([128, 512], F32, "pS"), Buf()) for _ in range(2)]
        pO = [(C.ps([128, 512], F32, "pO"), Buf()) for _ in range(2)]
        pL = [(C.ps([128, 512], F32, "pL"), Buf()) for _ in range(2)]
        items = [(h, gi) for h in range(H) for gi in range(len(dils))]

        def load(i):
            h, gi = items[i]
            d = dils[gi]
            sl = slots[i % 2]
            fin = sl["ds"].cnt + 16 * 5
            kw = dict(writes=[sl["b"]], group_final=fin)
            S.dma("sp", sl["q"][:, :], QT_d[gi, h, :, :], sl["ds"], **kw)
            S.dma("sp", sl["k"][:, 0:d * 128], KTh_d[h, :, off[gi] * 128:(off[gi] + d) * 128], sl["ds"], **kw)
            S.dma("sp", sl["k"][:, d * 128:(d + NB) * 128], KT_d[gi, h, :, :], sl["ds"], **kw)
            S.dma("sp", sl["v"][:, 0:d, :], Vh_d[h, :, off[gi]:off[gi] + d, :], sl["ds"], **kw)
            S.dma("sp", sl["v"][:, d:d + NB, :].rearrange("p b c -> p (b c)"), V_d[gi, h, :, :], sl["ds"], **kw)

        load(0)
        n = 0
        for i, (h, gi) in enumerate(items):
            if i + 1 < len(items):
                load(i + 1)
            d = dils[gi]
            nb = NB // d
            sl = slots[i % 2]
            q, k_, v, slb = sl["q"], sl["k"], sl["v"], sl["b"]
            for bi in range(NB):
                r, a = bi // nb, bi % nb
                cur = d + bi
                prev = (d + bi - 1) if a > 0 else r
                msk, mskb = (M, Mb) if a > 0 else (Mh, Mhb)
                ps, psb = pS[n % 2]
                po, pob = pO[n % 2]
                pl, plb = pL[n % 2]
                e_, eb = pe_[n % 2]
                m_, mb = pm_[n % 2]
                n += 1
                qs = q[:, bi * 128:(bi + 1) * 128]
                S.pe_group([MM(ps[:, 0:128], k_[:, prev * 128:(prev + 1) * 128], qs, True, True),
                            MM(ps[:, 128:256], k_[:, cur * 128:(cur + 1) * 128], qs, True, True)],
                           reads=[slb], writes=[psb])
                S.op("act", ACT(e_[:, :], ps[:, 0:256], AF.Exp, scale=scale), reads=[psb], writes=[eb])
                S.op("pool", TT(m_[:, :], e_[:, :], msk[:, :], ALU.mult), reads=[eb, mskb], writes=[mb])
                S.pe_group([MM(po[:, 0:128], v[:, prev, :], m_[:, 0:128], True, False),
                            MM(po[:, 0:128], v[:, cur, :], m_[:, 128:256], False, True)],
                           reads=[slb, mb], writes=[pob])
                S.pe_group([MM(pl[:, 0:128], K["ones"][:, :], m_[:, 0:128], True, False),
                            MM(pl[:, 0:128], K["ones"][:, :], m_[:, 128:256], False, True)],
                           reads=[K["ones_b"], mb], writes=[plb])
                if d == 1:
                    oc = Oa[:, bi * 128:(bi + 1) * 128]
                    lc = La[:, bi * 128:(bi + 1) * 128]
                else:
                    oc = Oa[:, :].rearrange("p (a i r) -> p a r i", i=128, r=d)[:, a, r, :]
                    lc = La[:, :].rearrange("p (a i r) -> p a r i", i=128, r=d)[:, a, r, :]
                if gi == 0:
                    S.op("act", ACT(oc, po[:, 0:128], AF.Copy), reads=[pob], writes=[Oab])
                    S.op("dve", CP(lc, pl[:, 0:128]), reads=[plb], writes=[Lab])
                else:
                    S.op("dve", TT(oc, po[:, 0:128], oc, ALU.add), reads=[pob, Oab], writes=[Oab])
                    S.op("dve", TT(lc, pl[:, 0:128], lc, ALU.add), reads=[plb, Lab], writes=[Lab])
            if gi == len(dils) - 1:
                S.op("dve", RECIP(La[:, :], La[:, :]), reads=[Lab], writes=[Lab])
                S.op("pool", TT(oT[:, h, :], Oa[:, :], La[:, :], ALU.mult), reads=[Oab, Lab], writes=[oTb[h]])
        emit_oproj(C, oT, oTb, wo_r, hT_d, H, T, KCout=KC)


def emit_mixB_1a(C, hT_d, gain_d, gq_d, gkv_d, winr, tab_d, cqn_d, kvx_d, krx_d,
                 QR=B_Q_RANK, KVR=B_KV_RANK, D=D_MODEL, T=TOK, NT=512, gather=None):
    S = C.S
    KC = D // 128
    NQ = QR // 128
    NKV = KVR // 128
    NCH = NQ + NKV + 1
    with C.phase():
        gain = C.sb([128, KC], F32, "gain")
        gq = C.sb([128, NQ], F32, "gq")
        gkv = C.sb([128, NKV], F32, "gkv")
        gain_b, gqb, gkvb = bufs(3)
        S.dma("sp", gain[:, :], gain_d[:, :], S.dsem(), writes=[gain_b])
        S.dma("sp", gq[:, :], gq_d[:, :], S.dsem(), writes=[gqb])
        S.dma("sp", gkv[:, :], gkv_d[:, :], S.dsem(), writes=[gkvb])
        xn = C.sb([128, KC, T], BF16, "xnf")
        xb = bufs(KC)
        with contextlib.ExitStack() as nst:
            old = C.stack
            C.stack = nst
            emit_norm_full(C, hT_d, gain, gain_b, xn, xb, KC, T)
            C.stack = old
            S.barrier(recycle=False)
        ct = C.sb([128, T], F32, "ct")
        sn = C.sb([128, T], F32, "sn")
        tb_ = Buf()
        tds = S.dsem()
        fin = tds.cnt + 32
        S.dma("sp", ct[:, :], tab_d[0, :, :], tds, writes=[tb_], group_final=fin)
        S.dma("sp", sn[:, :], tab_d[1, :, :], tds, writes=[tb_], group_final=fin)
        ws = [(C.sb([128, KC * 128], BF16, "win"), Buf(), S.dsem("win", sw=True)) for _ in range(3)]
        st = Stream(S, "pool", ws)
        for t in range(T // NT):
            for m in range(NCH):
                st.add(lambda tt, m=m: (tt[:, :], winr[m, :, :], CAST_KW))
        c32 = C.sb([128, NCH, NT], F32, "c32")
        cb = bufs(NCH)
        P = [(C.ps([128, NT], F32, "P"), Buf()) for _ in range(2)]
        Rq = norm_res(C, 1, NT, nslot=0)
        cq_st = [(C.sb([128, NQ, NT], BF16, "cqst"), Buf(), S.dsem("cqst")) for _ in range(2)]
        kv_st = [(C.sb([128, NKV, NT], BF16, "kvst"), Buf(), S.dsem("kvst")) for _ in range(2)]
        kr_st = [(C.sb([128, NT], BF16, "krst"), Buf(), S.dsem("krst")) for _ in range(2)]
        tA = (C.sb([128, NT], F32, "tA"), Buf())
        tB = (C.sb([128, NT], F32, "tB"), Buf())
        n = 0
        for t in range(T // NT):
            tsl = slice(t * NT, (t + 1) * NT)
            for m in range(NCH):
                w, wb, _ = st.get(t * NCH + m)
                p, pb = P[n % 2]
                n += 1
                S.pe_group([MM(p[:, :], w[:, k * 128:(k + 1) * 128], xn[:, k, tsl], k == 0, k == KC - 1)
                            for k in range(KC)], reads=[wb] + xb, writes=[pb])
                S.op("act", ACT(c32[:, m, :], p[:, :], AF.Copy), reads=[pb], writes=[cb[m]])
            cq, cqb, cqd = cq_st[t % 2]
            kv, kvb, kvd = kv_st[t % 2]
            kr, krb, krd = kr_st[t % 2]
            rs, rs_b = emit_rstd(C, Rq, lambda k: c32[:, k, :], cb[0:NQ], NQ, NT, QR)
            for k in range(NQ):
                S.op("dve", STT(cq[:, k, :], c32[:, k, :], gq[:, k:k + 1], rs[:, :], ALU.mult, ALU.mult),
                     reads=[cb[k], rs_b, gqb], writes=[cqb])
            S.dma("sp", cqn_d[:, :, tsl].rearrange("k p n -> p k n"), cq[:, :, :], cqd, reads=[cqb])
            rs, rs_b = emit_rstd(C, Rq, lambda k: c32[:, NQ + k, :], cb[NQ:NQ + NKV], NKV, NT, KVR)
            for k in range(NKV):
                S.op("dve", STT(kv[:, k, :], c32[:, NQ + k, :], gkv[:, k:k + 1], rs[:, :], ALU.mult, ALU.mult),
                     reads=[cb[NQ + k], rs_b, gkvb], writes=[kvb])
            S.dma("sp", kvx_d[:, :, tsl].rearrange("k p n -> p k n"), kv[:, :, :], kvd, reads=[kvb])
            emit_rope_evac(S, None, None, kr[:, :], krb, ct[:, tsl], sn[:, tsl], tb_,
                           c32[:, NCH - 1, :], cb[NCH - 1], tA[0], tA[1], tB[0], tB[1], 32)
            S.dma("sp", krx_d[0][:, tsl], kr[:, :], krd, reads=[krb])
        if gather is not None:
            S.barrier(recycle=False)
            emit_gather(C, kvx_d, gather[0], NKV)
            emit_gather(C, krx_d, gather[1], 1)


def emit_mixB_1b(C, cqn_d, wuq_r, tab_d, QB_d, H=B_HEADS, QR=B_Q_RANK, T=TOK, NT=512):
    S = C.S
    NQ = QR // 128
    with C.phase():
        cq = C.sb([128, NQ, T], BF16, "cqn")
        cqb = Buf()
        S.dma("sp", cq[:, :, :], cqn_d[:, :, :].rearrange("k p n -> p k n"), S.dsem(), writes=[cqb])
        ct = C.sb([128, T], F32, "ct")
        sn = C.sb([128, T], F32, "sn")
        tb_ = Buf()
        tds = S.dsem()
        fin = tds.cnt + 32
        S.dma("sp", ct[:, :], tab_d[0, :, :], tds, writes=[tb_], group_final=fin)
        S.dma("sp", sn[:, :], tab_d[1, :, :], tds, writes=[tb_], group_final=fin)
        ws = [(C.sb([128, 2 * NQ * 128], BF16, "wuq"), Buf(), S.dsem("wuq", sw=True)) for _ in range(2)]
        st = Stream(S, "pool", ws)
        for h in range(H):
            st.add(lambda tt, h=h: (tt[:, :], wuq_r[h, :, :], CAST_KW))
        stg = [(C.sb([128, 2, T], BF16, "qst"), Buf(), S.dsem("qst")) for _ in range(2)]
        P = [(C.ps([128, NT], F32, "P"), Buf()) for _ in range(2)]
        tA = [(C.sb([128, NT], F32, "tA"), Buf()) for _ in range(2)]
        tB = [(C.sb([128, NT], F32, "tB"), Buf()) for _ in range(2)]
        q32s = [(C.sb([128, NT], F32, "q32"), Buf()) for _ in range(2)]
        n = 0
        for h in range(H):
            w, wb, _ = st.get(h)
            sg, sgb, sds = stg[h % 2]
            for t in range(T // NT):
                tsl = slice(t * NT, (t + 1) * NT)
                for part in range(2):
                    p, pb = P[n % 2]
                    a_, ab = tA[n % 2]
                    b2, bb = tB[n % 2]
                    q32, q32b = q32s[n % 2]
                    n += 1
                    S.pe_group([MM(p[:, :], w[:, (part * NQ + k) * 128:(part * NQ + k + 1) * 128], cq[:, k, tsl],
                                   k == 0, k == NQ - 1) for k in range(NQ)], reads=[wb, cqb], writes=[pb])
                    if part == 0:
                        S.op("act", ACT(sg[:, 0, tsl], p[:, :], AF.Copy), reads=[pb], writes=[sgb])
                    else:
                        emit_rope_evac(S, p, pb, sg[:, 1, tsl], sgb, ct[:, tsl], sn[:, tsl], tb_,
                                       q32, q32b, a_, ab, b2, bb, 32)
            S.dma("sp", QB_d[h, :, :, :].rearrange("s p t -> p s t"), sg[:, :, :], sds, reads=[sgb])


def emit_mixB_2(C, QB_d, kvg_d, krg_d, wukv_r, qidx_d, kidx_d, oT_d, H=B_HEADS, KVR=B_KV_RANK,
                T=TOK, SK=SEQ, NT=512, gathered=False):
    S = C.S
    K = C.K
    NKV = KVR // 128
    NKT = SK // 128
    scale = float(B_NOPE + B_ROPE) ** -0.5
    with C.phase():
        lat = C.sb([128, NKV, SK], BF16, "lat")
        kr = C.sb([128, SK], BF16, "kr")
        latb, krb = Buf(), Buf()
        if gathered:
            lds = S.dsem()
            fin = lds.cnt + 16 * NKV
            for k in range(NKV):
                S.dma("sp", lat[:, k, :].rearrange("p (s n) -> p s n", n=T), kvg_d[k].rearrange("s p n -> p s n"), lds,
                      writes=[latb], group_final=fin)
            S.dma("sp", kr[:, :].rearrange("p (s n) -> p s n", n=T), krg_d[0].rearrange("s p n -> p s n"), S.dsem(), writes=[krb])
        else:
            S.dma("sp", lat[:, :, :], kvg_d[:, :, :].rearrange("k p n -> p k n"), S.dsem(), writes=[latb])
            S.dma("sp", kr[:, :], krg_d[:, :], S.dsem(), writes=[krb])
        qi = C.sb([128, T], F32, "qidx")
        ki = C.sb([128, NKT], F32, "kidx")
        qib, kib = Buf(), Buf()
        S.dma("sp", qi[:, :], qidx_d[0:1, :].partition_broadcast(128), S.dsem(), writes=[qib])
        S.dma("sp", ki[:, :], kidx_d[:, :], S.dsem(), writes=[kib])
        ws = [(C.sb([128, 2 * NKV * 128], BF16, "wukv"), Buf(), S.dsem("wukv", sw=True)) for _ in range(2)]
        st = Stream(S, "pool", ws)
        for h in range(H):
            st.add(lambda tt, h=h: (tt[:, :], wukv_r[h, :, :], CAST_KW))
        Kn = C.sb([128, SK], BF16, "Kn")
        Knb = Buf()
        V = C.sb([128, NKT, 128], BF16, "V")
        Vb = Buf()
        qs = [(C.sb([128, 2, T], BF16, "q"), Buf(), S.dsem("q")) for _ in range(2)]
        es = [(C.sb([128, NT], BF16, "e"), Buf()) for _ in range(2)]
        pms = [(C.sb([128, NT], BF16, "pm"), Buf()) for _ in range(2)]
        ost = [(C.sb([128, T], BF16, "ost"), Buf(), S.dsem("ost")) for _ in range(2)]
        rl = (C.sb([128, NT], F32, "rl"), Buf())
        pP = [(C.ps([128, NT], F32, "pP"), Buf()) for _ in range(2)]
        pS = [(C.ps([128, NT], F32, "pS"), Buf()) for _ in range(2)]
        pO = [(C.ps([128, NT], F32, "pO"), Buf()) for _ in range(2)]
        pL = [(C.ps([128, NT], F32, "pL"), Buf()) for _ in range(2)]

        def loadq(h):
            q, qb, qd = qs[h % 2]
            S.dma("sp", q[:, :, :], QB_d[h, :, :, :].rearrange("s p t -> p s t"), qd, writes=[qb])

        loadq(0)
        n = 0
        ne = 0
        for h in range(H):
            if h + 1 < H:
                loadq(h + 1)
            w, wb, _ = st.get(h)
            q, qb, _ = qs[h % 2]
            o_, ob, od = ost[h % 2]
            for tt in range(SK // NT):
                p, pb = pP[n % 2]
                n += 1
                S.pe_group([MM(p[:, :], w[:, k * 128:(k + 1) * 128], lat[:, k, tt * NT:(tt + 1) * NT], k == 0, k == NKV - 1)
                            for k in range(NKV)], reads=[wb, latb], writes=[pb])
                if n % 2 == 0:
                    S.op("act", ACT(Kn[:, tt * NT:(tt + 1) * NT], p[:, :], AF.Copy), reads=[pb], writes=[Knb])
                else:
                    S.op("dve", CP(Kn[:, tt * NT:(tt + 1) * NT], p[:, :]), reads=[pb], writes=[Knb])
            for k4 in range(NKT // 4):
                p, pb = pP[n % 2]
                n += 1
                fns = []
                for j in range(4):
                    kt = k4 * 4 + j
                    for k in range(NKV):
                        fns.append(MM(p[:, j * 128:(j + 1) * 128], lat[:, k, kt * 128:(kt + 1) * 128],
                                      w[:, (NKV + k) * 128:(NKV + k + 1) * 128], k == 0, k == NKV - 1))
                S.pe_group(fns, reads=[wb, latb], writes=[pb])
                dst = V[:, k4 * 4:k4 * 4 + 4, :].rearrange("p b c -> p (b c)")
                if n % 2 == 0:
                    S.op("act", ACT(dst, p[:, :], AF.Copy), reads=[pb], writes=[Vb])
                else:
                    S.op("dve", CP(dst, p[:, :]), reads=[pb], writes=[Vb])
            for qg in range(T // NT):
                qsl = slice(qg * NT, (qg + 1) * NT)
                po, pob = pO[qg % 2]
                pl, plb = pL[qg % 2]
                for kt in range(NKT):
                    ps, psb = pS[ne % 2]
                    e_, eb = es[ne % 2]
                    m_, mb = pms[ne % 2]
                    ne += 1
                    ksl = slice(kt * 128, (kt + 1) * 128)
                    S.pe_group([MM(ps[:, :], Kn[:, ksl], q[:, 0, qsl], True, False),
                                MM(ps[:, :], kr[0:64, ksl], q[0:64, 1, qsl], False, True)],
                               reads=[Knb, krb, qb], writes=[psb])
                    S.op("act", ACT(e_[:, :], ps[:, :], AF.Exp, scale=scale), reads=[psb], writes=[eb])
                    S.op("dve", STT(m_[:, :], qi[:, qsl], ki[:, kt:kt + 1], e_[:, :], ALU.is_ge, ALU.mult),
                         reads=[eb, qib, kib], writes=[mb])
                    S.pe_group([MM(po[:, :], V[:, kt, :], m_[:, :], kt == 0, kt == NKT - 1)],
                               reads=[Vb, mb], writes=[pob])
                    S.pe_group([MM(pl[:, :], K["ones"][:, :], m_[:, :], kt == 0, kt == NKT - 1)],
                               reads=[K["ones_b"], mb], writes=[plb])
                r_, rb = rl
                S.op("dve", RECIP(r_[:, :], pl[:, :]), reads=[plb], writes=[rb])
                S.op("dve", TT(o_[:, qsl], po[:, :], r_[:, :], ALU.mult), reads=[pob, rb], writes=[ob])
            S.dma("sp", oT_d[h, :, :], o_[:, :], od, reads=[ob])


def emit_mixB_3(C, hT_d, oT_d, wo_r, H=B_HEADS, D=D_MODEL, T=TOK):
    S = C.S
    with C.phase():
        oT = C.sb([128, H, T], BF16, "oT")
        ob = Buf()
        S.dma("sp", oT[:, :, :], oT_d[:, :, :].rearrange("h p t -> p h t"), S.dsem(), writes=[ob])
        emit_oproj(C, oT, [ob], wo_r, hT_d, H, T, KCout=D // 128)


A_PERM = np.array(list(range(0, 16)) + list(range(32, 48)) + list(range(16, 32)) + list(range(48, 128)))


def fm_w(W, chunks, group=1):
    Din = W.shape[0]
    KC = Din // 128
    M = len(chunks[0])
    nb = len(chunks) // group
    out = np.empty((nb, 128, group, KC, M), np.float32)
    Wp = np.concatenate([W, np.zeros((Din, 1), W.dtype)], axis=1)
    for b in range(nb):
        for g in range(group):
            cols = np.asarray(chunks[b * group + g])
            out[b, :, g] = Wp[:, cols].reshape(KC, 128, M).transpose(1, 0, 2)
    return out.reshape(nb, 128, group * KC * M)


def tm_w(W, blocks):
    KC = W.shape[0] // 128
    out = np.empty((len(blocks), 128, KC, len(blocks[0])), np.float32)
    for b, cols in enumerate(blocks):
        out[b] = W[:, cols].reshape(KC, 128, len(cols)).transpose(1, 0, 2)
    return out.reshape(len(blocks), 128, -1)


def chunks_of(n0, n):
    return [n0 + np.arange(m * 128, (m + 1) * 128) for m in range(n // 128)]


def gain_l(g):
    return np.ascontiguousarray(np.asarray(g, np.float32).reshape(-1, 128).T)


def lay_w13(w13):
    D, F2 = w13.shape
    DFF = F2 // 2
    KC, NJ = D // 128, DFF // 128
    a = w13.reshape(KC, 128, 2, NJ, 128)
    return np.ascontiguousarray(a.transpose(3, 1, 0, 2, 4)).reshape(NJ, 128, KC * 256)


def lay_w2(w2):
    DFF, D = w2.shape
    KC, NJ = D // 128, DFF // 128
    a = w2.reshape(NJ, 128, KC, 128)
    return np.ascontiguousarray(a.transpose(2, 1, 0, 3)).reshape(KC, 128, NJ * 128)


def lay_sq(w):
    return fm_w(w, chunks_of(0, w.shape[1]))


def lay_wqkv(W, ng, H):
    HB = H // 4
    blks = []
    for g in range(ng):
        for s in range(3):
            for hb in range(HB):
                base = lambda h: ((g * 3 + s) * H + h) * 128
                if s < 2:
                    blks.append(fm_w(W, [base(4 * hb + hh) + A_PERM for hh in range(4)], group=4)[0])
                else:
                    blks.append(tm_w(W, [base(4 * hb) + np.arange(512)])[0])
    return np.stack(blks)


def lay_win_c(w, D):
    ch = []
    for m in range(D // 128):
        for s in range(3):
            ch.append(s * D + np.arange(m * 128, (m + 1) * 128))
    return fm_w(w, ch, group=3)


def _pad128(a):
    return np.concatenate([a, -np.ones(128 - len(a), int)])


def lay_win_b(w, QR, KVR):
    return fm_w(w, chunks_of(0, QR + KVR) + [_pad128(QR + KVR + np.arange(64))])


def lay_wuq(w, H):
    return fm_w(w, sum([[h * 192 + np.arange(128), _pad128(h * 192 + 128 + np.arange(64))] for h in range(H)], []), group=2)


def lay_wukv(w, H):
    return fm_w(w, sum([[h * 256 + np.arange(128), h * 256 + 128 + np.arange(128)] for h in range(H)], []), group=2)


def rope_frq(half, rows_a, rows_b):
    f = (np.float32(ROPE_THETA) ** (-np.arange(half, dtype=np.float32) / np.float32(half))).astype(np.float32)
    o = np.zeros((128, 2), np.float32)
    o[rows_a, 0] = f
    o[rows_a, 1] = 1.0
    o[rows_b, 0] = f
    o[rows_b, 1] = -1.0
    return o


def band_mask():
    k = np.arange(128)[:, None]
    q = np.arange(128)[None, :]
    return np.concatenate([(k >= q), (k <= q)], axis=1).astype(np.float32)


T_ = TOK
KC_ = D_MODEL // 128
NJ_ = D_FF // 128
NQ_ = B_Q_RANK // 128
NKV_ = B_KV_RANK // 128
NHALO = sum(A_DIL)
TAB_DILS = A_DIL + (1,)

SPEC = {
    "ident": ((128, 128), F32), "x": ((T_, D_MODEL), F32), "pos": ((1, T_), I32), "frq": ((4, 128, 2), F32),
    "mask": ((128, 256), F32), "hmask": ((128, 1), F32), "qidx": ((1, T_), F32), "kidx": ((128, SEQ // 128), F32),
    "gf": ((128, KC_), F32), "out": ((T_, D_MODEL), F32),
    "hT": ((KC_, 128, T_), F32), "hT_in": ((KC_, 128, T_), F32), "tab": ((4, 2, 128, T_), F32),
    "QT": ((3, A_HEADS, 128, T_), BF16), "KT": ((3, A_HEADS, 128, T_), BF16), "V": ((3, A_HEADS, 128, T_), BF16),
    "KTx": ((A_HEADS, 128, NHALO * 128), BF16), "Vx": ((A_HEADS, 128, NHALO, 128), BF16),
    "KTh": ((A_HEADS, 128, NHALO * 128), BF16), "Vh": ((A_HEADS, 128, NHALO, 128), BF16),
    "cqn": ((NQ_, 128, T_), BF16), "kvx": ((NKV_, 128, T_), BF16), "krx": ((128, T_), BF16),
    "kvg": ((NKV_, 128, SEQ), BF16), "krg": ((128, SEQ), BF16), "QB": ((B_HEADS, 2, 128, T_), BF16),
    "oT": ((B_HEADS, 128, T_), BF16),
    "cuT": ((KC_, 128, T_), F32), "bT": ((KC_, 128, T_), F32), "cux": ((KC_, 128, 2), F32), "cuh": ((KC_, 128, 2), F32),
}
for _l in range(DEPTH):
    for _s in ("1", "2"):
        SPEC["l%d_g%s" % (_l, _s)] = ((128, KC_), F32)
        SPEC["l%d_w13_%s" % (_l, _s)] = ((NJ_, 128, KC_ * 256), F32)
        SPEC["l%d_w2_%s" % (_l, _s)] = ((KC_, 128, NJ_ * 128), F32)
    SPEC["l%d_gm" % _l] = ((128, KC_), F32)
for _l in (0, 3):
    SPEC["l%d_wqkv" % _l] = ((36, 128, 4 * KC_ * 128), F32)
    SPEC["l%d_wo" % _l] = ((KC_, 128, A_HEADS * 128), F32)
SPEC["l1_gq"] = ((128, NQ_), F32)
SPEC["l1_gkv"] = ((128, NKV_), F32)
SPEC["l1_win"] = ((NQ_ + NKV_ + 1, 128, KC_ * 128), F32)
SPEC["l1_wuq"] = ((B_HEADS, 128, 2 * NQ_ * 128), F32)
SPEC["l1_wukv"] = ((B_HEADS, 128, 2 * NKV_ * 128), F32)
SPEC["l1_wo"] = ((KC_, 128, B_HEADS * 128), F32)
SPEC["l2_win"] = ((KC_, 128, 3 * KC_ * 128), F32)
SPEC["l2_convw"] = ((128, KC_, 3), F32)
SPEC["l2_wout"] = ((KC_, 128, KC_ * 128), F32)


def ffn_names(l, s):
    return ["l%d_g%s" % (l, s), "l%d_w13_%s" % (l, s), "l%d_w2_%s" % (l, s)]


def do_ffn(C, d, l, s):
    emit_ffn(C, d["hT"], d["l%d_g%s" % (l, s)], d["l%d_w13_%s" % (l, s)], d["l%d_w2_%s" % (l, s)])


def do_A1(C, d, l):
    emit_mixA_1(C, d["hT"], d["l%d_gm" % l], d["l%d_wqkv" % l], d["tab"], d["QT"], d["KT"], d["V"], d["KTx"], d["Vx"])


def do_A2(C, d, l):
    emit_mixA_2(C, d["hT"], d["QT"], d["KT"], d["V"], d["KTh"], d["Vh"], d["hmask"], d["mask"], d["l%d_wo" % l])


def do_B1(C, d):
    emit_mixB_1a(C, d["hT"], d["l1_gm"], d["l1_gq"], d["l1_gkv"], d["l1_win"], d["tab"][3], d["cqn"], d["kvx"], d["krx"])
    emit_mixB_1b(C, d["cqn"], d["l1_wuq"], d["tab"][3], d["QB"])


def do_B2(C, d):
    emit_mixB_2(C, d["QB"], d["kvg"], d["krg"], d["l1_wukv"], d["qidx"], d["kidx"], d["oT"])
    emit_mixB_3(C, d["hT"], d["oT"], d["l1_wo"])


def do_tables(C, d):
    emit_tables(C, d["pos"], d["frq"], d["tab"], TAB_DILS)


def L1(C, d):
    do_tables(C, d)
    emit_pin(C, d["x"], d["hT"])
    do_ffn(C, d, 0, "1")
    do_A1(C, d, 0)


def L2(C, d):
    emit_dram_copy(C, d["hT"], d["hT_in"])
    do_A2(C, d, 0)
    do_ffn(C, d, 0, "2")
    do_ffn(C, d, 1, "1")
    do_tables(C, d)
    do_B1(C, d)


def L3(C, d):
    emit_dram_copy(C, d["hT"], d["hT_in"])
    do_B2(C, d)
    do_ffn(C, d, 1, "2")
    do_ffn(C, d, 2, "1")
    emit_mixC_1(C, d["hT"], d["l2_gm"], d["l2_win"], d["cuT"], d["bT"], d["cux"])


def L4(C, d):
    emit_dram_copy(C, d["hT"], d["hT_in"])
    emit_mixC_2(C, d["hT"], d["cuT"], d["bT"], d["cuh"], d["l2_convw"], d["hmask"], d["l2_wout"])
    do_ffn(C, d, 2, "2")
    do_ffn(C, d, 3, "1")
    do_tables(C, d)
    do_A1(C, d, 3)


def L5(C, d):
    emit_dram_copy(C, d["hT"], d["hT_in"])
    do_A2(C, d, 3)
    do_ffn(C, d, 3, "2")
    emit_final(C, d["hT"], d["gf"], d["out"])


LAUNCHES = [
    ("L1", L1, ["ident", "x", "pos", "frq"] + ffn_names(0, "1") + ["l0_gm", "l0_wqkv"],
     ["hT", "QT", "KT", "V", "KTx", "Vx"], ["tab"]),
    ("L2", L2, ["ident", "hT_in", "QT", "KT", "V", "KTh", "Vh", "hmask", "mask", "l0_wo"] + ffn_names(0, "2") + ffn_names(1, "1")
     + ["pos", "frq", "l1_gm", "l1_gq", "l1_gkv", "l1_win", "l1_wuq"],
     ["hT", "kvx", "krx", "QB"], ["tab", "cqn"]),
    ("L3", L3, ["ident", "hT_in", "QB", "kvg", "krg", "l1_wukv", "qidx", "kidx", "l1_wo"] + ffn_names(1, "2") + ffn_names(2, "1")
     + ["l2_gm", "l2_win"],
     ["hT", "cuT", "bT", "cux"], ["oT"]),
    ("L4", L4, ["ident", "hT_in", "cuT", "bT", "cuh", "l2_convw", "hmask", "l2_wout"] + ffn_names(2, "2") + ffn_names(3, "1")
     + ["pos", "frq", "l3_gm", "l3_wqkv"],
     ["hT", "QT", "KT", "V", "KTx", "Vx"], ["tab"]),
    ("L5", L5, ["ident", "hT_in", "QT", "KT", "V", "KTh", "Vh", "hmask", "mask", "l3_wo"] + ffn_names(3, "2") + ["gf"],
     ["out"], ["hT"]),
]

_PROGS = {}


def build_prog(name, fn, ins, outs, scratch):
    if name in _PROGS:
        return _PROGS[name]
    nc = bass.Bass("TRN2", target_bir_lowering=False)
    d = {}
    for n in ins:
        d[n] = nc.dram_tensor(n, list(SPEC[n][0]), SPEC[n][1], kind="ExternalInput").ap()
    for n in outs:
        d[n] = nc.dram_tensor(n, list(SPEC[n][0]), SPEC[n][1], kind="ExternalOutput").ap()
    for n in scratch:
        d[n] = nc.dram_tensor(n, list(SPEC[n][0]), SPEC[n][1]).ap()
    with contextlib.ExitStack() as st:
        C = Ctx(nc, st)
        emit_consts(C, d["ident"])
        fn(C, d)
        C.S.emit()
    _PROGS[name] = nc
    return nc


def host_layout(inp):
    W = {}
    f = lambda k: np.asarray(inp[k], np.float32)
    for l in range(DEPTH):
        p = "l%d_" % l
        for s, nm in (("1", "ffn1"), ("2", "ffn2")):
            W[p + "g" + s] = gain_l(f(p + nm + "_norm"))
            W[p + "w13_" + s] = lay_w13(f(p + nm + "_w13"))
            W[p + "w2_" + s] = lay_w2(f(p + nm + "_w2"))
        W[p + "gm"] = gain_l(f(p + "mix_norm"))
    for l in (0, 3):
        p = "l%d_" % l
        W[p + "wqkv"] = lay_wqkv(f(p + "a_w_qkv"), 3, A_HEADS)
        W[p + "wo"] = lay_sq(f(p + "a_w_o"))
    W["l1_gq"] = gain_l(f("l1_b_q_norm"))
    W["l1_gkv"] = gain_l(f("l1_b_kv_norm"))
    W["l1_win"] = lay_win_b(f("l1_b_w_in"), B_Q_RANK, B_KV_RANK)
    W["l1_wuq"] = lay_wuq(f("l1_b_w_uq"), B_HEADS)
    W["l1_wukv"] = lay_wukv(f("l1_b_w_ukv"), B_HEADS)
    W["l1_wo"] = lay_sq(f("l1_b_w_o"))
    W["l2_win"] = lay_win_c(f("l2_c_w_in"), D_MODEL)
    W["l2_convw"] = np.ascontiguousarray(f("l2_c_conv_w").T.reshape(KC_, 128, 3).transpose(1, 0, 2))
    W["l2_wout"] = lay_sq(f("l2_c_w_out"))
    W["gf"] = gain_l(f("final_norm"))
    W["ident"] = np.eye(128, dtype=np.float32)
    fa = rope_frq(A_ROT // 2, np.arange(0, 16), np.arange(32, 48))
    fb = rope_frq(B_ROPE // 2, np.arange(0, 32), np.arange(32, 64))
    W["frq"] = np.stack([fa, fa, fa, fb])
    W["mask"] = band_mask()
    W["kidx"] = (np.arange(SEQ // 128)[None, :] * 128 + np.arange(128)[:, None]).astype(np.float32)
    x = f("x")
    pos = np.asarray(inp["positions"], np.int32)
    per_core = []
    for c in range(NCORES):
        b, j = c // CPB, c % CPB
        per_core.append({
            "x": np.ascontiguousarray(x[b, j * T_:(j + 1) * T_]),
            "pos": np.ascontiguousarray(pos[b:b + 1, j * T_:(j + 1) * T_]),
            "hmask": np.full((128, 1), float(j > 0), np.float32),
            "qidx": (j * T_ + np.arange(T_, dtype=np.float32))[None],
        })
    return W, per_core


def kernel_unfused(**inp):
    import ml_dtypes
    W, pc = host_layout(inp)
    state = [dict(p) for p in pc]
    out = None
    for (name, fn, ins, outs, scratch) in LAUNCHES:
        nc = build_prog(name, fn, ins, outs, scratch)
        in_maps = []
        for c in range(NCORES):
            im = {}
            for n in ins:
                im[n] = state[c][n] if n in state[c] else W[n]
            in_maps.append(im)
        res = run_bass_kernel_spmd(nc, in_maps, core_ids=list(range(NCORES))).results
        for c in range(NCORES):
            for n in outs:
                state[c]["hT_in" if n == "hT" else n] = np.asarray(res[c][n])
        for c in range(NCORES):
            j = c % CPB
            if "KTx" in outs:
                state[c]["KTh"] = state[c - 1]["KTx"] if j > 0 else np.zeros_like(state[c]["KTx"])
                state[c]["Vh"] = state[c - 1]["Vx"] if j > 0 else np.zeros_like(state[c]["Vx"])
            if "kvx" in outs:
                b0 = (c // CPB) * CPB
                state[c]["kvg"] = np.concatenate([state[b0 + i]["kvx"] for i in range(CPB)], axis=2)
                state[c]["krg"] = np.concatenate([state[b0 + i]["krx"][0] for i in range(CPB)], axis=1)
            if "cux" in outs:
                state[c]["cuh"] = state[c - 1]["cux"] if j > 0 else np.zeros_like(state[c]["cux"])
    out = np.stack([np.concatenate([np.asarray(state[b * CPB + j]["out"]) for j in range(CPB)], axis=0) for b in range(BATCH)])
    return out.astype(np.float32)


def kernel(**inputs):
    return kernel_fused(**inputs)


GROUPS = [list(range(b * CPB, (b + 1) * CPB)) for b in range(BATCH)]


def emit_gather(C, src, dst, n):
    S = C.S
    cs = S.dsem("cc", sw=True)
    for i in range(n):
        S.coll("AllGather", GROUPS, src[i], dst[i], cs)


def emit_select(C, src_g, dst, oh_d, n, X, dt):
    S = C.S
    with C.phase():
        oh = C.sb([128, CPB], F32, "oh")
        ohb = Buf()
        S.dma("sp", oh[:, :], oh_d[:, :], S.dsem(), writes=[ohb])
        cand = [(C.sb([128, CPB, X], dt, "cand"), Buf(), S.dsem("cand")) for _ in range(2)]
        acc = [(C.sb([128, X], F32, "acc"), Buf()) for _ in range(2)]
        outs = [(C.sb([128, X], dt, "sel"), Buf(), S.dsem("sel")) for _ in range(2)]
        for i in range(n):
            c, cb, cds = cand[i % 2]
            a, ab = acc[i % 2]
            o, ob, ods = outs[i % 2]
            S.dma("sp", c[:, :, :], src_g[i].rearrange("s p x -> p s x"), cds, writes=[cb])
            S.op("dve", TS(a[:, :], c[:, 0, :], oh[:, 0:1], None, ALU.mult), reads=[cb, ohb], writes=[ab])
            for s in range(1, CPB):
                last = s == CPB - 1
                S.op("dve", STT(o[:, :] if last else a[:, :], c[:, s, :], oh[:, s:s + 1], a[:, :], ALU.mult, ALU.add),
                     reads=[cb, ohb, ab], writes=[ob] if last else [ab])
            S.dma("sp", dst[i], o[:, :], ods, reads=[ob])


def emit_select_c(C, cug, cuh, oh_d, KC=KC_):
    S = C.S
    X = 2 * KC
    with C.phase():
        oh = C.sb([128, CPB], F32, "oh")
        ohb = Buf()
        S.dma("sp", oh[:, :], oh_d[:, :], S.dsem(), writes=[ohb])
        c = C.sb([128, CPB, X], F32, "cand")
        cb = Buf()
        S.dma("sp", c[:, :, :], cug[:, :, 0:X].rearrange("s p x -> p s x"), S.dsem(), writes=[cb])
        a = C.sb([128, X], F32, "acc")
        ab = Buf()
        S.op("dve", TS(a[:, :], c[:, 0, :], oh[:, 0:1], None, ALU.mult), reads=[cb, ohb], writes=[ab])
        for s in range(1, CPB):
            S.op("dve", STT(a[:, :], c[:, s, :], oh[:, s:s + 1], a[:, :], ALU.mult, ALU.add),
                 reads=[cb, ohb, ab], writes=[ab])
        S.dma("sp", cuh[:, 0:X], a[:, :], S.dsem(), reads=[ab])


def FUSED(C, d):
    def A_layer(l):
        sfx = str(l)
        emit_mixA_1(C, d["hT"], d["l%d_gm" % l], d["l%d_wqkv" % l], d["tab"], d["QT" + sfx], d["KT" + sfx], d["V" + sfx],
                    d["KTx" + sfx], d["Vx" + sfx], gather=(d["KTg" + sfx], d["Vg" + sfx]))
        emit_select(C, d["KTg" + sfx], d["KTh" + sfx], d["oh"], A_HEADS, NHALO * 128, BF16)
        emit_select(C, d["Vg" + sfx].rearrange("h s p b c -> h s p (b c)"), d["Vh" + sfx].rearrange("h p b c -> h p (b c)"),
                    d["oh"], A_HEADS, NHALO * 128, BF16)
        emit_mixA_2(C, d["hT"], d["QT" + sfx], d["KT" + sfx], d["V" + sfx], d["KTh" + sfx], d["Vh" + sfx], d["hmask"], d["mask"],
                    d["l%d_wo" % l])

    do_tables(C, d)
    emit_pin(C, d["x"], d["hT"])
    do_ffn(C, d, 0, "1")
    A_layer(0)
    do_ffn(C, d, 0, "2")
    do_ffn(C, d, 1, "1")
    emit_mixB_1a(C, d["hT"], d["l1_gm"], d["l1_gq"], d["l1_gkv"], d["l1_win"], d["tab"][3], d["cqn"], d["kvx"], d["krx"],
                 gather=(d["kvgs"], d["krgs"]))
    emit_mixB_1b(C, d["cqn"], d["l1_wuq"], d["tab"][3], d["QB"])
    emit_mixB_2(C, d["QB"], d["kvgs"], d["krgs"], d["l1_wukv"], d["qidx"], d["kidx"], d["oT"], gathered=True)
    emit_mixB_3(C, d["hT"], d["oT"], d["l1_wo"])
    do_ffn(C, d, 1, "2")
    do_ffn(C, d, 2, "1")
    emit_mixC_1(C, d["hT"], d["l2_gm"], d["l2_win"], d["cuT"], d["bT"], d["cux"], gather=d["cug"])
    emit_select_c(C, d["cug"][0], d["cuh"][0], d["oh"])
    emit_mixC_2(C, d["hT"], d["cuT"], d["bT"], d["cuh"], d["l2_convw"], d["hmask"], d["l2_wout"])
    do_ffn(C, d, 2, "2")
    do_ffn(C, d, 3, "1")
    A_layer(3)
    do_ffn(C, d, 3, "2")
    emit_final(C, d["hT"], d["gf"], d["out"])


for _l in ("0", "3"):
    for _n in ("QT", "KT", "V", "KTx", "Vx", "KTh", "Vh"):
        SPEC[_n + _l] = SPEC[_n]
    SPEC["KTg" + _l] = ((A_HEADS, CPB, 128, NHALO * 128), BF16)
    SPEC["Vg" + _l] = ((A_HEADS, CPB, 128, NHALO, 128), BF16)
SPEC["kvgs"] = ((NKV_, CPB, 128, T_), BF16)
SPEC["krgs"] = ((1, CPB, 128, T_), BF16)
SPEC["cux"] = ((1, 128, 512), F32)
SPEC["cug"] = ((1, CPB, 128, 512), F32)
SPEC["cuh"] = ((1, 128, 512), F32)
SPEC["oh"] = ((128, CPB), F32)
SPEC["krx"] = ((1, 128, T_), BF16)

FUSED_INS = (["ident", "x", "pos", "frq", "mask", "hmask", "oh", "qidx", "kidx", "gf"]
             + sum([ffn_names(l, s) for l in range(DEPTH) for s in ("1", "2")], [])
             + ["l%d_gm" % l for l in range(DEPTH)]
             + ["l0_wqkv", "l0_wo", "l3_wqkv", "l3_wo", "l1_gq", "l1_gkv", "l1_win", "l1_wuq", "l1_wukv", "l1_wo",
                "l2_win", "l2_convw", "l2_wout"])
FUSED_SCRATCH = (["hT", "tab", "cqn", "kvx", "krx", "kvgs", "krgs", "QB", "oT", "cuT", "bT", "cux", "cug", "cuh"]
                 + [n + l for l in ("0", "3") for n in ("QT", "KT", "V", "KTx", "Vx", "KTh", "Vh", "KTg", "Vg")])


def kernel_fused(**inp):
    W, pc = host_layout(inp)
    nc = build_prog("FUSED", FUSED, FUSED_INS, ["out"], FUSED_SCRATCH)
    in_maps = []
    for c in range(NCORES):
        j = c % CPB
        oh = np.zeros((128, CPB), np.float32)
        if j > 0:
            oh[:, j - 1] = 1.0
        pcc = dict(pc[c], oh=oh)
        in_maps.append({n: (pcc[n] if n in pcc else W[n]) for n in FUSED_INS})
    res = run_bass_kernel_spmd(nc, in_maps, core_ids=list(range(NCORES))).results
    out = np.stack([np.concatenate([np.asarray(res[b * CPB + j]["out"]) for j in range(CPB)], axis=0) for b in range(BATCH)])
    return out.astype(np.float32)
```

```python
import contextlib
import numpy as np
import concourse.bass as bass
import concourse.mybir as mybir
from concourse.bass_utils import run_bass_kernel_spmd

F32 = mybir.dt.float32
BF16 = mybir.dt.bfloat16
I32 = mybir.dt.int32
AF = mybir.ActivationFunctionType
ALU = mybir.AluOpType

D_MODEL = 2048
BATCH = 2
SEQ = 8192
DEPTH = 4
D_FF = 5632
ROPE_THETA = 500000.0
A_HEADS = 16
A_HEAD_DIM = 128
A_ROT = 32
A_DIL = (1, 4, 16)
B_HEADS = 16
B_Q_RANK = 1536
B_KV_RANK = 512
B_NOPE = 128
B_ROPE = 64
B_V = 128
NCORES = 8
TOK = BATCH * SEQ // NCORES
CPB = NCORES // BATCH
EPS = 1e-6


class Buf:
    __slots__ = ("w", "r")

    def __init__(self):
        self.w = None
        self.r = {}


def bufs(n):
    return [Buf() for _ in range(n)]


class DSem:
    __slots__ = ("sem", "cnt")

    def __init__(self, sem):
        self.sem = sem
        self.cnt = 0


class Sched:
    ENG = ("pe", "act", "dve", "pool", "sp")

    def __init__(self, nc, stack):
        self.nc = nc
        self.stack = stack
        self.q = {e: [] for e in self.ENG}
        self.cur = {}
        self.cnt = {}
        self.known = {e: {} for e in self.ENG}
        self.nsem = 0
        self.ds_all = []
        self.ds_hw = []
        self.ds_sw = []
        self.ds_free = []
        self.ds_free_sw = []
        for e in self.ENG:
            self.cur[e] = self._new_sem("e_" + e)
            self.cnt[e] = 0

    def _new_sem(self, name):
        self.nsem += 1
        return self.stack.enter_context(self.nc.semaphore("s%d_%s" % (self.nsem, name)))

    def dsem(self, name="d", sw=False):
        free = self.ds_free_sw if sw else self.ds_free
        if free:
            return free.pop()
        d = DSem(self._new_sem(name))
        (self.ds_sw if sw else self.ds_hw).append(d)
        self.ds_all.append(d)
        return d

    def _deps(self, reads, writes):
        deps = []
        for b in reads:
            if b.w is not None:
                deps.append(b.w)
        for b in writes:
            if b.w is not None:
                deps.append(b.w)
            deps.extend(b.r.values())
        return deps

    def _waits(self, eng, deps, skip=None):
        kn = self.known[eng]
        best = {}
        for (sem, val) in deps:
            if sem is skip:
                continue
            k = id(sem)
            if kn.get(k, 0) >= val:
                continue
            if k not in best or best[k][1] < val:
                best[k] = (sem, val)
        for k, (sem, val) in best.items():
            kn[k] = val
            self.q[eng].append(("wait", sem, val))

    def _mark(self, tok, reads, writes):
        k = id(tok[0])
        for b in reads:
            if k not in b.r or b.r[k][1] < tok[1]:
                b.r[k] = tok
        for b in writes:
            b.w = tok
            b.r = {}

    def _sig(self, eng):
        self.cnt[eng] += 1
        return (self.cur[eng], self.cnt[eng])

    def op(self, eng, fn, reads=(), writes=()):
        self._waits(eng, self._deps(reads, writes))
        tok = self._sig(eng)
        self.q[eng].append(("op", fn, tok[0]))
        self._mark(tok, reads, writes)
        return tok

    def pe_group(self, fns, reads=(), writes=()):
        self._waits("pe", self._deps(reads, writes), skip=self.cur["pe"])
        tok = self._sig("pe")
        for f in fns[:-1]:
            self.q["pe"].append(("op", f, None))
        self.q["pe"].append(("op", fns[-1], tok[0]))
        self._mark(tok, reads, writes)
        return tok

    def dma(self, queue, out, in_, dsem, reads=(), writes=(), group_final=None, **kw):
        deps = self._deps(reads, writes)
        if group_final is not None:
            deps = [d_ for d_ in deps if not (d_[0] is dsem.sem and d_[1] == group_final)]
        self._waits(queue, deps)
        dsem.cnt += 16
        tok = (dsem.sem, dsem.cnt if group_final is None else group_final)
        self.q[queue].append(("dma", out, in_, dsem.sem, kw))
        self._mark(tok, reads, writes)
        return tok

    def coll(self, kind, groups, src, dst, dsem, reads=(), writes=()):
        self._waits("pool", self._deps(reads, writes))
        dsem.cnt += 1
        tok = (dsem.sem, dsem.cnt)
        self.q["pool"].append(("coll", kind, groups, src, dst, dsem.sem))
        self._mark(tok, reads, writes)
        return tok

    def call(self, eng, fn, reads=(), writes=()):
        self._waits(eng, self._deps(reads, writes))
        self.q[eng].append(("call", fn))

    def wait_tok(self, eng, tok):
        self._waits(eng, [tok])

    def barrier(self, recycle=True):
        for e in self.ENG:
            if e != "sp" and self.cnt[e] > 0:
                self.wait_tok("sp", (self.cur[e], self.cnt[e]))
        for d in self.ds_all:
            if d.cnt > 0:
                self.wait_tok("sp", (d.sem, d.cnt))
        tok = self._sig("sp")
        self.q["sp"].append(("inc", tok[0]))
        for e in self.ENG:
            if e != "sp":
                self.wait_tok(e, tok)
        if recycle:
            self.ds_free = list(self.ds_hw)
            self.ds_free_sw = list(self.ds_sw)

    def emit(self):
        def run(e, items):
            for it in items:
                if it[0] == "wait":
                    e.wait_ge(it[1], it[2])
                elif it[0] == "op":
                    ins = it[1](e)
                    if it[2] is not None:
                        ins.then_inc(it[2], 1)
                elif it[0] == "inc":
                    e.sem_inc(it[1], 1)
                elif it[0] == "call":
                    it[1](e)
                elif it[0] == "coll":
                    _, kind, groups, src, dst, sem = it
                    e.collective_compute(kind, ALU.bypass, replica_groups=groups, ins=[src.opt()], outs=[dst.opt()]).then_inc(sem, 1)
                else:
                    _, out, in_, sem, kw = it
                    if callable(in_):
                        in_ = in_()
                    e.dma_start(out=out, in_=in_, **kw).then_inc(sem, 16)

        with self.nc.Block() as block:
            @block.tensor
            def _(e):
                run(e, self.q["pe"])

            @block.scalar
            def _(e):
                run(e, self.q["act"])

            @block.vector
            def _(e):
                run(e, self.q["dve"])

            @block.gpsimd
            def _(e):
                run(e, self.q["pool"])

            @block.sync
            def _(e):
                run(e, self.q["sp"])


class Stream:
    def __init__(self, S, queue, slots):
        self.S = S
        self.queue = queue
        self.slots = slots
        self.items = []
        self.issued = 0

    def add(self, fn):
        self.items.append(fn)
        return len(self.items) - 1

    def ensure(self, upto):
        upto = min(upto, len(self.items) - 1)
        while self.issued <= upto:
            i = self.issued
            t, buf, ds = self.slots[i % len(self.slots)]
            o, a, kw = self.items[i](t)
            self.S.dma(self.queue, o, a, ds, writes=[buf], **kw)
            self.issued += 1

    def get(self, i):
        self.ensure(i + len(self.slots) - 1)
        return self.slots[i % len(self.slots)]


class Ctx:
    def __init__(self, nc, gstack):
        self.nc = nc
        self.gstack = gstack
        self.stack = gstack
        self.S = Sched(nc, gstack)
        self.n = 0
        self.K = {}

    def sb(self, shape, dt, name="sb"):
        self.n += 1
        return self.stack.enter_context(self.nc.sbuf_tensor("%s_%d" % (name, self.n), list(shape), dt))

    def ps(self, shape, dt=F32, name="ps"):
        self.n += 1
        return self.stack.enter_context(self.nc.psum_tensor("%s_%d" % (name, self.n), list(shape), dt))

    @contextlib.contextmanager
    def phase(self):
        with contextlib.ExitStack() as st:
            self.stack = st
            yield
            self.S.barrier()
        self.stack = self.gstack


def MM(out, lhsT, rhs, start, stop):
    return lambda e: e.matmul(out, lhsT, rhs, start=start, stop=stop)


def ACT(out, in_, func, **kw):
    return lambda e: e.activation(out=out, in_=in_, func=func, **kw)


def TT(out, in0, in1, op):
    return lambda e: e.tensor_tensor(out=out, in0=in0, in1=in1, op=op)


def TS(out, in0, s1, s2, op0, op1=None):
    if op1 is None:
        return lambda e: e.tensor_scalar(out=out, in0=in0, scalar1=s1, scalar2=None, op0=op0)
    return lambda e: e.tensor_scalar(out=out, in0=in0, scalar1=s1, scalar2=s2, op0=op0, op1=op1)


def STT(out, in0, scalar, in1, op0, op1):
    return lambda e: e.scalar_tensor_tensor(out=out, in0=in0, scalar=scalar, in1=in1, op0=op0, op1=op1)


def CP(out, in_):
    return lambda e: e.tensor_copy(out=out, in_=in_)


def MEMSET(ap, v):
    return lambda e: e.memset(ap, v)


def RECIP(out, in_):
    return lambda e: e.reciprocal(out, in_)


CAST_KW = dict(max_dma_last_dim=4096)
SKEW = 2


def emit_consts(C, ident_d):
    S = C.S
    K = C.K
    K["ones"] = C.sb([128, 128], BF16, "ones")
    K["ones_b"] = Buf()
    S.op("pool", MEMSET(K["ones"][:, :], 1.0), writes=[K["ones_b"]])
    K["eps"] = C.sb([128, 1], F32, "eps")
    K["eps_b"] = Buf()
    S.op("pool", MEMSET(K["eps"][:, :], EPS), writes=[K["eps_b"]])
    K["ident"] = C.sb([128, 128], F32, "ident")
    K["ident_b"] = Buf()
    ds = S.dsem()
    S.dma("sp", K["ident"][:, :], ident_d[:, :], ds, writes=[K["ident_b"]])


def norm_res(C, KC, NT, nslot=2):
    S = C.S
    R = {}
    R["h"] = [(C.sb([128, KC, NT], F32, "h"), bufs(KC), S.dsem("h")) for _ in range(nslot)]
    R["ssum"] = (C.ps([128, NT], F32, "ssum"), Buf())
    R["sq"] = [(C.sb([128, NT], BF16, "sq"), Buf()) for _ in range(3)]
    R["rs"] = (C.sb([128, NT], F32, "rs"), Buf())
    return R


def emit_rstd(C, R, src, srcb, KC, NT, Dn):
    S = C.S
    K = C.K
    ssum, ssum_b = R["ssum"]
    for k in range(KC):
        sq, sq_b = R["sq"][k % len(R["sq"])]
        S.op("act", ACT(sq[:, :NT], src(k), AF.Square), reads=[srcb[k]], writes=[sq_b])
        S.pe_group([MM(ssum[:, :NT], K["ones"][:, :], sq[:, :NT], k == 0, k == KC - 1)],
                   reads=[sq_b, K["ones_b"]], writes=[ssum_b])
    rs, rs_b = R["rs"]
    S.op("act", ACT(rs[:, :NT], ssum[:, :NT], AF.Sqrt, scale=1.0 / Dn, bias=K["eps"][:, :]),
         reads=[ssum_b, K["eps_b"]], writes=[rs_b])
    S.op("dve", RECIP(rs[:, :NT], rs[:, :NT]), reads=[rs_b], writes=[rs_b])
    return rs, rs_b


def emit_norm_full(C, hT_d, gain, gain_b, xn, xnb, KC, T, d=1, NT=256):
    S = C.S
    R = norm_res(C, KC, NT)
    for t in range(T // NT):
        h, hb, hds = R["h"][t % 2]
        tsl = slice(t * NT, (t + 1) * NT)
        S.dma("sp", h[:, :, :], hT_d[:, :, tsl].rearrange("k p n -> p k n"), hds, writes=hb)
        rs, rs_b = emit_rstd(C, R, lambda k, h=h: h[:, k, :], hb, KC, NT, KC * 128)
        for k in range(KC):
            if d == 1:
                o = xn[:, k, tsl]
                i0 = h[:, k, :]
                i1 = rs[:, :]
            else:
                m0 = t * NT // d
                o = xn[:, k, :].rearrange("p (r m) -> p m r", r=d)[:, m0:m0 + NT // d, :]
                i0 = h[:, k, :].rearrange("p (m r) -> p m r", r=d)
                i1 = rs[:, :].rearrange("p (m r) -> p m r", r=d)
            S.op("dve", STT(o, i0, gain[:, k:k + 1], i1, ALU.mult, ALU.mult),
                 reads=[hb[k], rs_b, gain_b], writes=[xnb[k]])


def emit_ffn(C, hT_d, gain_d, w13r, w2r, D=D_MODEL, DFF=D_FF, T=TOK, NT=512):
    S = C.S
    K = C.K
    KC = D // 128
    NJ = DFF // 128
    NTILE = T // NT
    with C.phase():
        w13s = [(C.sb([128, KC * 256], BF16, "w13"), Buf(), S.dsem("w13", sw=True)) for _ in range(2)]
        w2s = [(C.sb([128, NJ * 128], BF16, "w2"), Buf(), S.dsem("w2", sw=True)) for _ in range(2)]
        gain = C.sb([128, KC], F32, "gain")
        gain_b = Buf()
        S.dma("sp", gain[:, :], gain_d[:, :], S.dsem(), writes=[gain_b])
        R = norm_res(C, KC, NT)
        hst = [S.dsem("hst") for _ in range(2)]
        xn = C.sb([128, KC, NT], BF16, "xn")
        xb = bufs(KC)
        hid = C.sb([128, NJ, NT], BF16, "hid")
        hidb = bufs(NJ)
        G = [(C.ps([128, NT], F32, "G"), Buf()) for _ in range(2)]
        U = [(C.ps([128, NT], F32, "U"), Buf()) for _ in range(2)]
        Y = [(C.ps([128, NT], F32, "Y"), Buf()) for _ in range(2)]
        sgs = [(C.sb([128, NT], F32, "sg"), Buf()) for _ in range(2)]

        st13 = Stream(S, "pool", w13s)
        st2 = Stream(S, "pool", w2s)
        for t in range(NTILE):
            for j in range(NJ):
                st13.add(lambda tt, j=j: (tt[:, :], w13r[j, :, :], CAST_KW))
            for m in range(KC):
                st2.add(lambda tt, m=m: (tt[:, :], w2r[m, :, :], CAST_KW))

        for t in range(NTILE):
            hs = t % 2
            h, hb, hds = R["h"][hs]
            tsl = slice(t * NT, (t + 1) * NT)
            S.dma("sp", h[:, :, :], hT_d[:, :, tsl].rearrange("k p n -> p k n"), hds, writes=hb)
            st13.ensure(t * NJ + 1)
            rs, rs_b = emit_rstd(C, R, lambda k, h=h: h[:, k, :], hb, KC, NT, D)
            for k in range(KC):
                S.op("dve", STT(xn[:, k, :], h[:, k, :], gain[:, k:k + 1], rs[:, :], ALU.mult, ALU.mult),
                     reads=[hb[k], rs_b, gain_b], writes=[xb[k]])
            for j in range(NJ):
                w, wb, _ = st13.get(t * NJ + j)
                if j == NJ - 4:
                    st2.ensure(t * KC + 1)
                g_, gb = G[j % 2]
                u_, ub = U[j % 2]
                S.pe_group([MM(g_[:, :], w[:, (2 * k) * 128:(2 * k + 1) * 128], xn[:, k, :], k == 0, k == KC - 1)
                            for k in range(KC)], reads=[wb] + xb, writes=[gb])
                S.pe_group([MM(u_[:, :], w[:, (2 * k + 1) * 128:(2 * k + 2) * 128], xn[:, k, :], k == 0, k == KC - 1)
                            for k in range(KC)], reads=[wb] + xb, writes=[ub])
                sg, sgb = sgs[j % 2]
                S.op("act", ACT(sg[:, :], g_[:, :], AF.Silu), reads=[gb], writes=[sgb])
                S.op("dve", TT(hid[:, j, :], sg[:, :], u_[:, :], ALU.mult), reads=[sgb, ub], writes=[hidb[j]])
            for m in range(KC):
                w, wb, _ = st2.get(t * KC + m)
                if m == KC - 2 and t + 1 < NTILE:
                    st13.ensure((t + 1) * NJ + 1)
                y_, yb = Y[m % 2]
                S.pe_group([MM(y_[:, :], w[:, j * 128:(j + 1) * 128], hid[:, j, :], j == 0, j == NJ - 1)
                            for j in range(NJ)], reads=[wb] + hidb, writes=[yb])
                S.op("dve", STT(h[:, m, :], y_[:, :], 0.5, h[:, m, :], ALU.mult, ALU.add),
                     reads=[yb, hb[m]], writes=[hb[m]])
                if m == 0:
                    fin = hst[hs].cnt + 16 * KC
                S.dma("sp", hT_d[m, :, tsl], h[:, m, :], hst[hs], reads=[hb[m]], group_final=fin)


def emit_pin(C, x_d, hT_d, D=D_MODEL, T=TOK):
    S = C.S
    K = C.K
    KC = D // 128
    with C.phase():
        xs = [(C.sb([128, D], F32, "x"), Buf(), S.dsem("x")) for _ in range(8)]
        hst = [(C.sb([128, KC, 512], F32, "hst"), bufs(KC), S.dsem("hst")) for _ in range(2)]
        ps = [(C.ps([128, 512], F32, "tp"), Buf()) for _ in range(2)]
        n = 0
        for grp in range(T // 512):
            xt = []
            for i in range(4):
                x, xb, xds = xs[(grp % 2) * 4 + i]
                t0 = grp * 512 + i * 128
                S.dma("sp", x[:, :], x_d[t0:t0 + 128, :], xds, writes=[xb])
                xt.append((x, xb))
            hs, hsb, hds = hst[grp % 2]
            for k in range(KC):
                p, pb = ps[n % 2]
                S.pe_group([lambda e, p=p, x=xt[i][0], i=i, k=k: e.transpose(
                    p[:, i * 128:(i + 1) * 128], x[:, k * 128:(k + 1) * 128], K["ident"][:, :]) for i in range(4)],
                    reads=[b for (_, b) in xt] + [K["ident_b"]], writes=[pb])
                eng = "act" if n % 2 == 0 else "dve"
                if eng == "act":
                    S.op("act", ACT(hs[:, k, :], p[:, :], AF.Copy), reads=[pb], writes=[hsb[k]])
                else:
                    S.op("dve", CP(hs[:, k, :], p[:, :]), reads=[pb], writes=[hsb[k]])
                n += 1
            S.dma("sp", hT_d[:, :, grp * 512:(grp + 1) * 512].rearrange("k p n -> p k n"), hs[:, :, :], hds, reads=hsb)


def emit_final(C, hT_d, gain_d, out_d, D=D_MODEL, T=TOK, NT=512):
    S = C.S
    K = C.K
    KC = D // 128
    with C.phase():
        gain = C.sb([128, KC], F32, "gain")
        gain_b = Buf()
        S.dma("sp", gain[:, :], gain_d[:, :], S.dsem(), writes=[gain_b])
        R = norm_res(C, KC, NT)
        xn = C.sb([128, KC, NT], F32, "xn32")
        xb = bufs(KC)
        osb = [(C.sb([128, D], F32, "osb"), Buf(), S.dsem("o")) for _ in range(2)]
        ps = [(C.ps([128, 512], F32, "tp"), Buf()) for _ in range(2)]
        n = 0
        no = 0
        last = []
        for t in range(T // NT):
            h, hb, hds = R["h"][t % 2]
            tsl = slice(t * NT, (t + 1) * NT)
            S.dma("sp", h[:, :, :], hT_d[:, :, tsl].rearrange("k p n -> p k n"), hds, writes=hb)
            rs, rs_b = emit_rstd(C, R, lambda k, h=h: h[:, k, :], hb, KC, NT, D)
            for k in range(KC):
                S.op("dve", STT(xn[:, k, :], h[:, k, :], gain[:, k:k + 1], rs[:, :], ALU.mult, ALU.mult),
                     reads=[hb[k], rs_b, gain_b], writes=[xb[k]])
            for i in range(NT // 128):
                o, ob, ods = osb[no % 2]
                no += 1
                for k0 in range(0, KC, 4):
                    p, pb = ps[n % 2]
                    S.pe_group([lambda e, p=p, kk=kk, k0=k0, i=i: e.transpose(
                        p[:, kk * 128:(kk + 1) * 128], xn[:, k0 + kk, i * 128:(i + 1) * 128], K["ident"][:, :])
                        for kk in range(4)], reads=xb[k0:k0 + 4] + [K["ident_b"]], writes=[pb])
                    if n % 2 == 0:
                        S.op("act", ACT(o[:, k0 * 128:(k0 + 4) * 128], p[:, :], AF.Copy), reads=[pb], writes=[ob])
                    else:
                        S.op("dve", CP(o[:, k0 * 128:(k0 + 4) * 128], p[:, :]), reads=[pb], writes=[ob])
                    n += 1
                t0 = t * NT + i * 128
                last.append(S.dma("sp", out_d[t0:t0 + 128, :], o[:, :], ods, reads=[ob]))
        for tk in last[-2:]:
            S.wait_tok("sp", tk)


def emit_oproj(C, g, gb, wr_d, hT_d, KCin, T=TOK, KCout=D_MODEL // 128, NT=512, scale=1.0):
    S = C.S
    ws = [(C.sb([128, KCin * 128], BF16, "wo"), Buf(), S.dsem("wo", sw=True)) for _ in range(2)]
    hr = [(C.sb([128, T], F32, "hrow"), Buf(), S.dsem("hr"), S.dsem("hrs")) for _ in range(2)]
    Y = [(C.ps([128, NT], F32, "Y"), Buf()) for _ in range(2)]
    st = Stream(S, "pool", ws)
    for m in range(KCout):
        st.add(lambda tt, m=m: (tt[:, :], wr_d[m, :, :], CAST_KW))
    n = 0
    for m in range(KCout):
        w, wb, _ = st.get(m)
        h, hb, hds, hss = hr[m % 2]
        S.dma("sp", h[:, :], hT_d[m, :, :], hds, writes=[hb])
        for t in range(T // NT):
            y_, yb = Y[n % 2]
            n += 1
            tsl = slice(t * NT, (t + 1) * NT)
            S.pe_group([MM(y_[:, :], w[:, k * 128:(k + 1) * 128], g[:, k, tsl], k == 0, k == KCin - 1)
                        for k in range(KCin)], reads=[wb] + list(gb), writes=[yb])
            S.op("dve", STT(h[:, tsl], y_[:, :], scale, h[:, tsl], ALU.mult, ALU.add), reads=[yb, hb], writes=[hb])
        S.dma("sp", hT_d[m, :, :], h[:, :], hss, reads=[hb])


def emit_mixC_1(C, hT_d, gain_d, winr, cuT_d, bT_d, cux_d, D=D_MODEL, T=TOK, NT=512, gather=None):
    S = C.S
    KC = D // 128
    with C.phase():
        gain = C.sb([128, KC], F32, "gain")
        gain_b = Buf()
        S.dma("sp", gain[:, :], gain_d[:, :], S.dsem(), writes=[gain_b])
        xn = C.sb([128, KC, T], BF16, "xnf")
        xb = bufs(KC)
        emit_norm_full(C, hT_d, gain, gain_b, xn, xb, KC, T)
        ws = [(C.sb([128, 3 * KC * 128], BF16, "win"), Buf(), S.dsem("win", sw=True)) for _ in range(2)]
        st = Stream(S, "pool", ws)
        for m in range(KC):
            st.add(lambda tt, m=m: (tt[:, :], winr[m, :, :], CAST_KW))
        P = [[(C.ps([128, NT], F32, "P"), Buf()) for _ in range(2)] for _ in range(3)]
        csb = [(C.sb([128, NT], F32, "c"), Buf()) for _ in range(2)]
        cur = [(C.sb([128, T], F32, "cu"), Buf(), S.dsem("cu")) for _ in range(2)]
        br = [(C.sb([128, T], F32, "b"), Buf(), S.dsem("b")) for _ in range(2)]
        xds = [S.dsem("cux") for _ in range(2)]
        n = 0
        for m in range(KC):
            w, wb, _ = st.get(m)
            cu, cub, cuds = cur[m % 2]
            b_, bb, bds = br[m % 2]
            for t in range(T // NT):
                tsl = slice(t * NT, (t + 1) * NT)
                pp = [P[s][n % 2] for s in range(3)]
                for s in range(3):
                    S.pe_group([MM(pp[s][0][:, :], w[:, (s * KC + k) * 128:(s * KC + k + 1) * 128], xn[:, k, tsl],
                                   k == 0, k == KC - 1) for k in range(KC)], reads=[wb] + xb, writes=[pp[s][1]])
                c_, cb = csb[n % 2]
                n += 1
                S.op("act", ACT(b_[:, tsl], pp[0][0][:, :], AF.Copy), reads=[pp[0][1]], writes=[bb])
                S.op("act", ACT(c_[:, :], pp[1][0][:, :], AF.Copy), reads=[pp[1][1]], writes=[cb])
                S.op("dve", TT(cu[:, tsl], c_[:, :], pp[2][0][:, :], ALU.mult), reads=[cb, pp[2][1]], writes=[cub])
            S.dma("sp", cuT_d[m, :, :], cu[:, :], cuds, reads=[cub])
            S.dma("sp", bT_d[m, :, :], b_[:, :], bds, reads=[bb])
            S.dma("sp", cux_d[0][:, 2 * m:2 * m + 2], cu[:, T - 2:T], xds[m % 2], reads=[cub])
        if gather is not None:
            S.barrier(recycle=False)
            emit_gather(C, cux_d, gather, 1)


def emit_mixC_2(C, hT_d, cuT_d, bT_d, cuh_d, convw_d, hmask_d, woutr, D=D_MODEL, T=TOK):
    S = C.S
    KC = D // 128
    with C.phase():
        cw = C.sb([128, KC, 3], F32, "convw")
        cwb = Buf()
        S.dma("sp", cw[:, :, :], convw_d[:, :, :], S.dsem(), writes=[cwb])
        hm = C.sb([128, 1], F32, "hmask")
        hmb = Buf()
        S.dma("sp", hm[:, :], hmask_d[:, :], S.dsem(), writes=[hmb])
        g = C.sb([128, KC, T], BF16, "g")
        gb = bufs(KC)
        cur = [(C.sb([128, T + 2], F32, "cu"), Buf(), Buf(), S.dsem("cu"), S.dsem("cuh")) for _ in range(2)]
        br = [(C.sb([128, T], F32, "b"), Buf(), S.dsem("b")) for _ in range(2)]
        zs = [(C.sb([128, T], F32, "z"), Buf()) for _ in range(2)]
        for m in range(KC):
            cu, cub, chb, cuds, chds = cur[m % 2]
            b_, bb, bds = br[m % 2]
            z, zb = zs[m % 2]
            S.dma("sp", cu[:, 2:T + 2], cuT_d[m, :, :], cuds, writes=[cub])
            S.dma("sp", cu[:, 0:2], cuh_d[0][:, 2 * m:2 * m + 2], chds, writes=[chb])
            S.dma("sp", b_[:, :], bT_d[m, :, :], bds, writes=[bb])
            S.op("dve", TS(cu[:, 0:2], cu[:, 0:2], hm[:, 0:1], None, ALU.mult), reads=[hmb], writes=[chb])
            S.op("dve", TS(z[:, :], cu[:, 2:T + 2], cw[:, m, 2:3], None, ALU.mult), reads=[cub, cwb], writes=[zb])
            S.op("dve", STT(z[:, :], cu[:, 1:T + 1], cw[:, m, 1:2], z[:, :], ALU.mult, ALU.add),
                 reads=[cub, chb, cwb, zb], writes=[zb])
            S.op("dve", STT(z[:, :], cu[:, 0:T], cw[:, m, 0:1], z[:, :], ALU.mult, ALU.add),
                 reads=[cub, chb, cwb, zb], writes=[zb])
            S.op("pool", TT(g[:, m, :], z[:, :], b_[:, :], ALU.mult), reads=[zb, bb], writes=[gb[m]])
        emit_oproj(C, g, gb, woutr, hT_d, KC, T, KCout=KC)


def emit_dram_copy(C, dst, src):
    S = C.S
    ds = S.dsem("cp")
    fin = ds.cnt + 16 * dst.shape[0]
    for k in range(dst.shape[0]):
        S.dma("sp", dst[k, :, :], src[k, :, :], ds, group_final=fin)
    S.barrier()


TWO_PI = 6.283185307179586
CW1 = 6.28125
CW2 = float(np.float32(np.frombuffer(np.uint32(np.frombuffer(np.float32(TWO_PI - CW1).tobytes(), np.uint32)[0] & 0xFFFFF000).tobytes(), np.float32)[0]))
CW3 = float(np.float32(TWO_PI - CW1 - CW2))
PI_LO = 3.1415925


def _wrap(S, r, rb, m, mb):
    S.op("dve", TS(m, r, float(np.pi), -TWO_PI, ALU.is_gt, ALU.mult), reads=[rb], writes=[mb])
    S.op("dve", TT(r, r, m, ALU.add), reads=[rb, mb], writes=[rb])
    S.op("dve", TS(m, r, -float(np.pi), TWO_PI, ALU.is_lt, ALU.mult), reads=[rb], writes=[mb])
    S.op("dve", TT(r, r, m, ALU.add), reads=[rb, mb], writes=[rb])
    S.op("dve", TS(r, r, PI_LO, -PI_LO, ALU.min, ALU.max), reads=[rb], writes=[rb])


def emit_tables(C, pos_d, frq_d, tab_d, dils, T=TOK):
    S = C.S
    with C.phase():
        pos_i = C.sb([128, T], I32, "posi")
        pos_f = C.sb([128, T], F32, "posf")
        pb = Buf()
        S.dma("sp", pos_i[:, :], pos_d[0:1, :].partition_broadcast(128), S.dsem(), writes=[pb])
        S.op("dve", CP(pos_f[:, :], pos_i[:, :]), reads=[pb], writes=[pb])
        x = C.sb([128, T], F32, "ang")
        r = C.sb([128, T], F32, "red")
        m = C.sb([128, T], F32, "msk")
        kf = C.sb([128, T], F32, "kf")
        ki = C.sb([128, T], I32, "ki")
        sn = C.sb([128, T], F32, "sin")
        cs = C.sb([128, T], F32, "cos")
        xb, rb, mb, kb, snb, csb = bufs(6)
        for i, d in enumerate(dils):
            fr = C.sb([128, 2], F32, "frq")
            fb = Buf()
            S.dma("sp", fr[:, :], frq_d[i, :, :], S.dsem(), writes=[fb])
            if d == 1:
                S.op("dve", TS(x[:, :], pos_f[:, :], fr[:, 0:1], None, ALU.mult), reads=[pb, fb], writes=[xb])
            else:
                S.op("dve", TS(x[:, :].rearrange("p (r m) -> p m r", r=d), pos_f[:, :].rearrange("p (m r) -> p m r", r=d),
                               fr[:, 0:1], None, ALU.mult), reads=[pb, fb], writes=[xb])
            S.op("dve", TS(ki[:, :], x[:, :], 1.0 / TWO_PI, None, ALU.mult), reads=[xb], writes=[kb])
            S.op("dve", CP(kf[:, :], ki[:, :]), reads=[kb], writes=[kb])
            S.op("dve", STT(r[:, :], kf[:, :], -CW1, x[:, :], ALU.mult, ALU.add), reads=[kb, xb], writes=[rb])
            S.op("dve", STT(r[:, :], kf[:, :], -CW2, r[:, :], ALU.mult, ALU.add), reads=[kb, rb], writes=[rb])
            S.op("dve", STT(r[:, :], kf[:, :], -CW3, r[:, :], ALU.mult, ALU.add), reads=[kb, rb], writes=[rb])
            _wrap(S, r[:, :], rb, m[:, :], mb)
            S.op("act", ACT(sn[:, :], r[:, :], AF.Sin), reads=[rb], writes=[snb])
            S.op("dve", TS(sn[:, :], sn[:, :], fr[:, 1:2], None, ALU.mult), reads=[snb, fb], writes=[snb])
            S.op("dve", TS(r[:, :], r[:, :], float(np.pi / 2), None, ALU.add), reads=[rb, snb], writes=[rb])
            _wrap(S, r[:, :], rb, m[:, :], mb)
            S.op("act", ACT(cs[:, :], r[:, :], AF.Sin), reads=[rb], writes=[csb])
            S.dma("sp", tab_d[i, 0, :, :], cs[:, :], S.dsem(), reads=[csb])
            S.dma("sp", tab_d[i, 1, :, :], sn[:, :], S.dsem(), reads=[snb])


def emit_rope_evac(S, p, pb, st_out, stb, ct, sn, tb_, q32, q32b, tA, tAb, tB, tBb, h):
    hi = 32 + h
    if p is not None:
        S.op("act", ACT(q32[:, :], p[:, :], AF.Copy), reads=[pb], writes=[q32b])
    S.op("pool", CP(st_out, q32[:, :]), reads=[q32b], writes=[stb])
    S.op("dve", TT(tA[0:hi, :], q32[0:hi, :], ct[0:hi, :], ALU.mult), reads=[q32b, tb_], writes=[tAb])
    S.op("dve", TT(tB[0:h, :], q32[32:hi, :], sn[32:hi, :], ALU.mult), reads=[q32b, tb_], writes=[tBb])
    S.op("dve", TT(tB[32:hi, :], q32[0:h, :], sn[0:h, :], ALU.mult), reads=[q32b, tb_], writes=[tBb])
    S.op("dve", TT(st_out[0:hi, :], tA[0:hi, :], tB[0:hi, :], ALU.add), reads=[tAb, tBb], writes=[stb])


def halo_off(dils):
    off = [0]
    for d in dils:
        off.append(off[-1] + d)
    return off


def emit_mixA_1(C, hT_d, gain_d, wqkv_r, tab_d, QT_d, KT_d, V_d, KTx_d, Vx_d, dils=A_DIL, H=A_HEADS,
                D=D_MODEL, T=TOK, NT=512, gather=None):
    S = C.S
    KC = D // 128
    HB = H // 4
    NB = T // 128
    off = halo_off(dils)
    with C.phase():
        gain = C.sb([128, KC], F32, "gain")
        gain_b = Buf()
        S.dma("sp", gain[:, :], gain_d[:, :], S.dsem(), writes=[gain_b])
        xn = C.sb([128, KC, T], BF16, "xnf")
        xb = bufs(KC)
        ws = [(C.sb([128, 4 * KC * 128], BF16, "wqkv"), Buf(), S.dsem("wqkv", sw=True)) for _ in range(2)]
        st = Stream(S, "pool", ws)
        for i in range(len(dils) * 3 * HB):
            st.add(lambda tt, i=i: (tt[:, :], wqkv_r[i, :, :], CAST_KW))
        stg = [(C.sb([128, 4, T], BF16, "stg"), Buf(), S.dsem("stg"), S.dsem("stgx")) for _ in range(2)]
        P = [(C.ps([128, NT], F32, "P"), Buf()) for _ in range(2)]
        ct = C.sb([128, T], F32, "ct")
        sn = C.sb([128, T], F32, "sn")
        tb_ = Buf()
        tA = [(C.sb([128, NT], F32, "tA"), Buf()) for _ in range(2)]
        tB = [(C.sb([128, NT], F32, "tB"), Buf()) for _ in range(2)]
        q32s = [(C.sb([128, NT], F32, "q32"), Buf()) for _ in range(2)]
        for (t_, b_) in tB:
            S.op("pool", MEMSET(t_[:, :], 0.0), writes=[b_])
        n = 0
        ns = 0
        wi = 0
        for gi, d in enumerate(dils):
            nb = NB // d
            with contextlib.ExitStack() as nst:
                old = C.stack
                C.stack = nst
                emit_norm_full(C, hT_d, gain, gain_b, xn, xb, KC, T, d=d)
                C.stack = old
                S.barrier(recycle=False)
            tds = S.dsem()
            fin = tds.cnt + 32
            S.dma("sp", ct[:, :], tab_d[gi, 0, :, :], tds, writes=[tb_], group_final=fin)
            S.dma("sp", sn[:, :], tab_d[gi, 1, :, :], tds, writes=[tb_], group_final=fin)
            for s in range(3):
                for hb in range(HB):
                    w, wb, _ = st.get(wi)
                    wi += 1
                    sg, sgb, sds, sxs = stg[ns % 2]
                    ns += 1
                    if s < 2:
                        for hh in range(4):
                            for t in range(T // NT):
                                tsl = slice(t * NT, (t + 1) * NT)
                                p, pb = P[n % 2]
                                a_, ab = tA[n % 2]
                                b2, bb = tB[n % 2]
                                n += 1
                                S.pe_group([MM(p[:, :], w[:, (hh * KC + k) * 128:(hh * KC + k + 1) * 128], xn[:, k, tsl],
                                               k == 0, k == KC - 1) for k in range(KC)], reads=[wb] + xb, writes=[pb])
                                q32, q32b = q32s[n % 2]
                                emit_rope_evac(S, p, pb, sg[:, hh, tsl], sgb, ct[:, tsl], sn[:, tsl], tb_,
                                               q32, q32b, a_, ab, b2, bb, 16)
                        dst = (QT_d if s == 0 else KT_d)[gi, 4 * hb:4 * hb + 4, :, :].rearrange("h p t -> p h t")
                        S.dma("sp", dst, sg[:, :, :], sds, reads=[sgb])
                        if s == 1:
                            fin = sxs.cnt + 16 * d
                            for r in range(d):
                                c0 = (r * nb + nb - 1) * 128
                                S.dma("sp", KTx_d[4 * hb:4 * hb + 4, :, (off[gi] + r) * 128:(off[gi] + r + 1) * 128]
                                      .rearrange("h p c -> p h c"), sg[:, :, c0:c0 + 128], sxs, reads=[sgb], group_final=fin)
                    else:
                        sg4 = sg[:, :, :].rearrange("p h (b c) -> p h b c", c=128)
                        for bi in range(NB):
                            p, pb = P[n % 2]
                            n += 1
                            S.pe_group([MM(p[:, :], xn[:, k, bi * 128:(bi + 1) * 128], w[:, k * 512:(k + 1) * 512],
                                           k == 0, k == KC - 1) for k in range(KC)], reads=[wb] + xb, writes=[pb])
                            src = p[:, :].rearrange("p (h c) -> p h c", h=4)
                            if n % 2 == 0:
                                S.op("act", ACT(sg4[:, :, bi, :], src, AF.Copy), reads=[pb], writes=[sgb])
                            else:
                                S.op("dve", CP(sg4[:, :, bi, :], src), reads=[pb], writes=[sgb])
                        S.dma("sp", V_d[gi, 4 * hb:4 * hb + 4, :, :].rearrange("h p x -> p h x"), sg[:, :, :], sds, reads=[sgb])
                        fin = sxs.cnt + 16 * d
                        for r in range(d):
                            bi = r * nb + nb - 1
                            S.dma("sp", Vx_d[4 * hb:4 * hb + 4, :, off[gi] + r, :].rearrange("h p c -> p h c"),
                                  sg4[:, :, bi, :], sxs, reads=[sgb], group_final=fin)
        if gather is not None:
            S.barrier(recycle=False)
            emit_gather(C, KTx_d, gather[0], H)
            emit_gather(C, Vx_d, gather[1], H)


def emit_mixA_2(C, hT_d, QT_d, KT_d, V_d, KTh_d, Vh_d, hmask_d, mask_d, wo_r, dils=A_DIL, H=A_HEADS,
                D=D_MODEL, T=TOK):
    S = C.S
    K = C.K
    KC = D // 128
    NB = T // 128
    off = halo_off(dils)
    nhmax = max(dils)
    scale = float(A_HEAD_DIM) ** -0.5
    with C.phase():
        hm = C.sb([128, 1], F32, "hmask")
        hmb = Buf()
        S.dma("sp", hm[:, :], hmask_d[:, :], S.dsem(), writes=[hmb])
        mk32 = C.sb([128, 256], F32, "mk32")
        mkb = Buf()
        S.dma("sp", mk32[:, :], mask_d[:, :], S.dsem(), writes=[mkb])
        M = C.sb([128, 256], BF16, "M")
        Mh = C.sb([128, 256], BF16, "Mh")
        Mb, Mhb = Buf(), Buf()
        S.op("dve", CP(M[:, :], mk32[:, :]), reads=[mkb], writes=[Mb])
        S.op("dve", CP(Mh[:, 128:256], mk32[:, 128:256]), reads=[mkb], writes=[Mhb])
        S.op("dve", TS(Mh[:, 0:128], mk32[:, 0:128], hm[:, 0:1], None, ALU.mult), reads=[mkb, hmb], writes=[Mhb])
        oT = C.sb([128, H, T], BF16, "oT")
        oTb = bufs(H)
        slots = []
        for _ in range(2):
            slots.append(dict(q=C.sb([128, T], BF16, "q"), k=C.sb([128, (nhmax + NB) * 128], BF16, "k"),
                              v=C.sb([128, nhmax + NB, 128], BF16, "v"), b=Buf(), ds=S.dsem("qkv")))
        Oa = C.sb([128, T], F32, "Oacc")
        La = C.sb([128, T], F32, "Lacc")
        Oab, Lab = Buf(), Buf()
        pe_ = [(C.sb([128, 256], BF16, "pe"), Buf()) for _ in range(3)]
        pm_ = [(C.sb([128, 256], BF16, "pm"), Buf()) for _ in range(SKEW + 2)]
        pS = [(C.ps([128, 512], F32, "pS"), Buf()) for _ in range(2)]
        pO = [(C.ps([128, 512], F32, "pO"), Buf()) for _ in range(2)]
        pL = [(C.ps([128, 512], F32, "pL"), Buf()) for _ in range(2)]
        items = [(h, gi) for h in range(H) for gi in range(len(dils))]

        def load(i):
            h, gi = items[i]
            d = dils[gi]
            sl = slots[i % 2]
            fin = sl["ds"].cnt + 16 * 5
            kw = dict(writes=[sl["b"]], group_final=fin)
            S.dma("sp", sl["q"][:, :], QT_d[gi, h, :, :], sl["ds"], **kw)
            S.dma("sp", sl["k"][:, 0:d * 128], KTh_d[h, :, off[gi] * 128:(off[gi] + d) * 128], sl["ds"], **kw)
            S.dma("sp", sl["k"][:, d * 128:(d + NB) * 128], KT_d[gi, h, :, :], sl["ds"], **kw)
            S.dma("sp", sl["v"][:, 0:d, :], Vh_d[h, :, off[gi]:off[gi] + d, :], sl["ds"], **kw)
            S.dma("sp", sl["v"][:, d:d + NB, :].rearrange("p b c -> p (b c)"), V_d[gi, h, :, :], sl["ds"], **kw)

        load(0)
        n = 0
        for i, (h, gi) in enumerate(items):
            if i + 1 < len(items):
                load(i + 1)
            d = dils[gi]
            nb = NB // d
            sl = slots[i % 2]
            q, k_, v, slb = sl["q"], sl["k"], sl["v"], sl["b"]
            pend = []

            def emit_pv(st_):
                bi, r, a, prev, cur, m_, mb, k2 = st_
                po, pob = pO[k2 % 2]
                pl, plb = pL[k2 % 2]
                S.pe_group([MM(po[:, 0:128], v[:, prev, :], m_[:, 0:128], True, False),
                            MM(po[:, 0:128], v[:, cur, :], m_[:, 128:256], False, True)],
                           reads=[slb, mb], writes=[pob])
                S.pe_group([MM(pl[:, 0:128], K["ones"][:, :], m_[:, 0:128], True, False),
                            MM(pl[:, 0:128], K["ones"][:, :], m_[:, 128:256], False, True)],
                           reads=[K["ones_b"], mb], writes=[plb])
                if d == 1:
                    oc = Oa[:, bi * 128:(bi + 1) * 128]
                    lc = La[:, bi * 128:(bi + 1) * 128]
                else:
                    oc = Oa[:, :].rearrange("p (a i r) -> p a r i", i=128, r=d)[:, a, r, :]
                    lc = La[:, :].rearrange("p (a i r) -> p a r i", i=128, r=d)[:, a, r, :]
                if gi == 0:
                    S.op("act", ACT(oc, po[:, 0:128], AF.Copy), reads=[pob], writes=[Oab])
                    S.op("dve", CP(lc, pl[:, 0:128]), reads=[plb], writes=[Lab])
                else:
                    S.op("dve", TT(oc, po[:, 0:128], oc, ALU.add), reads=[pob, Oab], writes=[Oab])
                    S.op("dve", TT(lc, pl[:, 0:128], lc, ALU.add), reads=[plb, Lab], writes=[Lab])

            for bi in range(NB):
                r, a = bi // nb, bi % nb
                cur = d + bi
                prev = (d + bi - 1) if a > 0 else r
                msk, mskb = (M, Mb) if a > 0 else (Mh, Mhb)
                ps, psb = pS[n % 2]
                e_, eb = pe_[n % len(pe_)]
                m_, mb = pm_[n % len(pm_)]
                qs = q[:, bi * 128:(bi + 1) * 128]
                S.pe_group([MM(ps[:, 0:128], k_[:, prev * 128:(prev + 1) * 128], qs, True, True),
                            MM(ps[:, 128:256], k_[:, cur * 128:(cur + 1) * 128], qs, True, True)],
                           reads=[slb], writes=[psb])
                S.op("act", ACT(e_[:, :], ps[:, 0:256], AF.Exp, scale=scale), reads=[psb], writes=[eb])
                S.op("pool", TT(m_[:, :], e_[:, :], msk[:, :], ALU.mult), reads=[eb, mskb], writes=[mb])
                pend.append((bi, r, a, prev, cur, m_, mb, n))
                n += 1
                if len(pend) > SKEW:
                    emit_pv(pend.pop(0))
            while pend:
                emit_pv(pend.pop(0))
            if gi == len(dils) - 1:
                S.op("dve", RECIP(La[:, :], La[:, :]), reads=[Lab], writes=[Lab])
                S.op("pool", TT(oT[:, h, :], Oa[:, :], La[:, :], ALU.mult), reads=[Oab, Lab], writes=[oTb[h]])
        emit_oproj(C, oT, oTb, wo_r, hT_d, H, T, KCout=KC)


def emit_mixB_1a(C, hT_d, gain_d, gq_d, gkv_d, winr, tab_d, cqn_d, kvx_d, krx_d,
                 QR=B_Q_RANK, KVR=B_KV_RANK, D=D_MODEL, T=TOK, NT=512, gather=None):
    S = C.S
    KC = D // 128
    NQ = QR // 128
    NKV = KVR // 128
    NCH = NQ + NKV + 1
    with C.phase():
        gain = C.sb([128, KC], F32, "gain")
        gq = C.sb([128, NQ], F32, "gq")
        gkv = C.sb([128, NKV], F32, "gkv")
        gain_b, gqb, gkvb = bufs(3)
        S.dma("sp", gain[:, :], gain_d[:, :], S.dsem(), writes=[gain_b])
        S.dma("sp", gq[:, :], gq_d[:, :], S.dsem(), writes=[gqb])
        S.dma("sp", gkv[:, :], gkv_d[:, :], S.dsem(), writes=[gkvb])
        xn = C.sb([128, KC, T], BF16, "xnf")
        xb = bufs(KC)
        with contextlib.ExitStack() as nst:
            old = C.stack
            C.stack = nst
            emit_norm_full(C, hT_d, gain, gain_b, xn, xb, KC, T)
            C.stack = old
            S.barrier(recycle=False)
        ct = C.sb([128, T], F32, "ct")
        sn = C.sb([128, T], F32, "sn")
        tb_ = Buf()
        tds = S.dsem()
        fin = tds.cnt + 32
        S.dma("sp", ct[:, :], tab_d[0, :, :], tds, writes=[tb_], group_final=fin)
        S.dma("sp", sn[:, :], tab_d[1, :, :], tds, writes=[tb_], group_final=fin)
        ws = [(C.sb([128, KC * 128], BF16, "win"), Buf(), S.dsem("win", sw=True)) for _ in range(3)]
        st = Stream(S, "pool", ws)
        for t in range(T // NT):
            for m in range(NCH):
                st.add(lambda tt, m=m: (tt[:, :], winr[m, :, :], CAST_KW))
        c32 = C.sb([128, NCH, NT], F32, "c32")
        cb = bufs(NCH)
        P = [(C.ps([128, NT], F32, "P"), Buf()) for _ in range(2)]
        Rq = norm_res(C, 1, NT, nslot=0)
        cq_st = [(C.sb([128, NQ, NT], BF16, "cqst"), Buf(), S.dsem("cqst")) for _ in range(2)]
        kv_st = [(C.sb([128, NKV, NT], BF16, "kvst"), Buf(), S.dsem("kvst")) for _ in range(2)]
        kr_st = [(C.sb([128, NT], BF16, "krst"), Buf(), S.dsem("krst")) for _ in range(2)]
        tA = (C.sb([128, NT], F32, "tA"), Buf())
        tB = (C.sb([128, NT], F32, "tB"), Buf())
        n = 0
        for t in range(T // NT):
            tsl = slice(t * NT, (t + 1) * NT)
            for m in range(NCH):
                w, wb, _ = st.get(t * NCH + m)
                p, pb = P[n % 2]
                n += 1
                S.pe_group([MM(p[:, :], w[:, k * 128:(k + 1) * 128], xn[:, k, tsl], k == 0, k == KC - 1)
                            for k in range(KC)], reads=[wb] + xb, writes=[pb])
                S.op("act", ACT(c32[:, m, :], p[:, :], AF.Copy), reads=[pb], writes=[cb[m]])
            cq, cqb, cqd = cq_st[t % 2]
            kv, kvb, kvd = kv_st[t % 2]
            kr, krb, krd = kr_st[t % 2]
            rs, rs_b = emit_rstd(C, Rq, lambda k: c32[:, k, :], cb[0:NQ], NQ, NT, QR)
            for k in range(NQ):
                S.op("dve", STT(cq[:, k, :], c32[:, k, :], gq[:, k:k + 1], rs[:, :], ALU.mult, ALU.mult),
                     reads=[cb[k], rs_b, gqb], writes=[cqb])
            S.dma("sp", cqn_d[:, :, tsl].rearrange("k p n -> p k n"), cq[:, :, :], cqd, reads=[cqb])
            rs, rs_b = emit_rstd(C, Rq, lambda k: c32[:, NQ + k, :], cb[NQ:NQ + NKV], NKV, NT, KVR)
            for k in range(NKV):
                S.op("dve", STT(kv[:, k, :], c32[:, NQ + k, :], gkv[:, k:k + 1], rs[:, :], ALU.mult, ALU.mult),
                     reads=[cb[NQ + k], rs_b, gkvb], writes=[kvb])
            S.dma("sp", kvx_d[:, :, tsl].rearrange("k p n -> p k n"), kv[:, :, :], kvd, reads=[kvb])
            emit_rope_evac(S, None, None, kr[:, :], krb, ct[:, tsl], sn[:, tsl], tb_,
                           c32[:, NCH - 1, :], cb[NCH - 1], tA[0], tA[1], tB[0], tB[1], 32)
            S.dma("sp", krx_d[0][:, tsl], kr[:, :], krd, reads=[krb])
        if gather is not None:
            S.barrier(recycle=False)
            emit_gather(C, kvx_d, gather[0], NKV)
            emit_gather(C, krx_d, gather[1], 1)


def emit_mixB_1b(C, cqn_d, wuq_r, tab_d, QB_d, H=B_HEADS, QR=B_Q_RANK, T=TOK, NT=512):
    S = C.S
    NQ = QR // 128
    with C.phase():
        cq = C.sb([128, NQ, T], BF16, "cqn")
        cqb = Buf()
        S.dma("sp", cq[:, :, :], cqn_d[:, :, :].rearrange("k p n -> p k n"), S.dsem(), writes=[cqb])
        ct = C.sb([128, T], F32, "ct")
        sn = C.sb([128, T], F32, "sn")
        tb_ = Buf()
        tds = S.dsem()
        fin = tds.cnt + 32
        S.dma("sp", ct[:, :], tab_d[0, :, :], tds, writes=[tb_], group_final=fin)
        S.dma("sp", sn[:, :], tab_d[1, :, :], tds, writes=[tb_], group_final=fin)
        ws = [(C.sb([128, 2 * NQ * 128], BF16, "wuq"), Buf(), S.dsem("wuq", sw=True)) for _ in range(2)]
        st = Stream(S, "pool", ws)
        for h in range(H):
            st.add(lambda tt, h=h: (tt[:, :], wuq_r[h, :, :], CAST_KW))
        stg = [(C.sb([128, 2, T], BF16, "qst"), Buf(), S.dsem("qst")) for _ in range(2)]
        P = [(C.ps([128, NT], F32, "P"), Buf()) for _ in range(2)]
        tA = [(C.sb([128, NT], F32, "tA"), Buf()) for _ in range(2)]
        tB = [(C.sb([128, NT], F32, "tB"), Buf()) for _ in range(2)]
        q32s = [(C.sb([128, NT], F32, "q32"), Buf()) for _ in range(2)]
        n = 0
        for h in range(H):
            w, wb, _ = st.get(h)
            sg, sgb, sds = stg[h % 2]
            for t in range(T // NT):
                tsl = slice(t * NT, (t + 1) * NT)
                for part in range(2):
                    p, pb = P[n % 2]
                    a_, ab = tA[n % 2]
                    b2, bb = tB[n % 2]
                    q32, q32b = q32s[n % 2]
                    n += 1
                    S.pe_group([MM(p[:, :], w[:, (part * NQ + k) * 128:(part * NQ + k + 1) * 128], cq[:, k, tsl],
                                   k == 0, k == NQ - 1) for k in range(NQ)], reads=[wb, cqb], writes=[pb])
                    if part == 0:
                        S.op("act", ACT(sg[:, 0, tsl], p[:, :], AF.Copy), reads=[pb], writes=[sgb])
                    else:
                        emit_rope_evac(S, p, pb, sg[:, 1, tsl], sgb, ct[:, tsl], sn[:, tsl], tb_,
                                       q32, q32b, a_, ab, b2, bb, 32)
            S.dma("sp", QB_d[h, :, :, :].rearrange("s p t -> p s t"), sg[:, :, :], sds, reads=[sgb])


def emit_mixB_2(C, QB_d, kvg_d, krg_d, wukv_r, qidx_d, kidx_d, oT_d, H=B_HEADS, KVR=B_KV_RANK,
                T=TOK, SK=SEQ, NT=512, gathered=False):
    S = C.S
    K = C.K
    NKV = KVR // 128
    NKT = SK // 128
    scale = float(B_NOPE + B_ROPE) ** -0.5
    with C.phase():
        lat = C.sb([128, NKV, SK], BF16, "lat")
        kr = C.sb([128, SK], BF16, "kr")
        latb, krb = Buf(), Buf()
        if gathered:
            lds = S.dsem()
            fin = lds.cnt + 16 * NKV
            for k in range(NKV):
                S.dma("sp", lat[:, k, :].rearrange("p (s n) -> p s n", n=T), kvg_d[k].rearrange("s p n -> p s n"), lds,
                      writes=[latb], group_final=fin)
            S.dma("sp", kr[:, :].rearrange("p (s n) -> p s n", n=T), krg_d[0].rearrange("s p n -> p s n"), S.dsem(), writes=[krb])
        else:
            S.dma("sp", lat[:, :, :], kvg_d[:, :, :].rearrange("k p n -> p k n"), S.dsem(), writes=[latb])
            S.dma("sp", kr[:, :], krg_d[:, :], S.dsem(), writes=[krb])
        qi = C.sb([128, T], F32, "qidx")
        ki = C.sb([128, NKT], F32, "kidx")
        qib, kib = Buf(), Buf()
        S.dma("sp", qi[:, :], qidx_d[0:1, :].partition_broadcast(128), S.dsem(), writes=[qib])
        S.dma("sp", ki[:, :], kidx_d[:, :], S.dsem(), writes=[kib])
        ws = [(C.sb([128, 2 * NKV * 128], BF16, "wukv"), Buf(), S.dsem("wukv", sw=True)) for _ in range(2)]
        st = Stream(S, "pool", ws)
        for h in range(H):
            st.add(lambda tt, h=h: (tt[:, :], wukv_r[h, :, :], CAST_KW))
        Kn = C.sb([128, SK], BF16, "Kn")
        Knb = Buf()
        V = C.sb([128, NKT, 128], BF16, "V")
        Vb = Buf()
        qs = [(C.sb([128, 2, T], BF16, "q"), Buf(), S.dsem("q")) for _ in range(2)]
        es = [(C.sb([128, NT], BF16, "e"), Buf()) for _ in range(3)]
        pms = [(C.sb([128, NT], BF16, "pm"), Buf()) for _ in range(SKEW + 2)]
        ost = [(C.sb([128, T], BF16, "ost"), Buf(), S.dsem("ost")) for _ in range(2)]
        rl = (C.sb([128, NT], F32, "rl"), Buf())
        pP = [(C.ps([128, NT], F32, "pP"), Buf()) for _ in range(2)]
        pS = [(C.ps([128, NT], F32, "pS"), Buf()) for _ in range(2)]
        pO = [(C.ps([128, NT], F32, "pO"), Buf()) for _ in range(2)]
        pL = [(C.ps([128, NT], F32, "pL"), Buf()) for _ in range(2)]

        def loadq(h):
            q, qb, qd = qs[h % 2]
            S.dma("sp", q[:, :, :], QB_d[h, :, :, :].rearrange("s p t -> p s t"), qd, writes=[qb])

        loadq(0)
        n = 0
        ne = 0
        for h in range(H):
            if h + 1 < H:
                loadq(h + 1)
            w, wb, _ = st.get(h)
            q, qb, _ = qs[h % 2]
            o_, ob, od = ost[h % 2]
            for tt in range(SK // NT):
                p, pb = pP[n % 2]
                n += 1
                S.pe_group([MM(p[:, :], w[:, k * 128:(k + 1) * 128], lat[:, k, tt * NT:(tt + 1) * NT], k == 0, k == NKV - 1)
                            for k in range(NKV)], reads=[wb, latb], writes=[pb])
                if n % 2 == 0:
                    S.op("act", ACT(Kn[:, tt * NT:(tt + 1) * NT], p[:, :], AF.Copy), reads=[pb], writes=[Knb])
                else:
                    S.op("dve", CP(Kn[:, tt * NT:(tt + 1) * NT], p[:, :]), reads=[pb], writes=[Knb])
            for k4 in range(NKT // 4):
                p, pb = pP[n % 2]
                n += 1
                fns = []
                for j in range(4):
                    kt = k4 * 4 + j
                    for k in range(NKV):
                        fns.append(MM(p[:, j * 128:(j + 1) * 128], lat[:, k, kt * 128:(kt + 1) * 128],
                                      w[:, (NKV + k) * 128:(NKV + k + 1) * 128], k == 0, k == NKV - 1))
                S.pe_group(fns, reads=[wb, latb], writes=[pb])
                dst = V[:, k4 * 4:k4 * 4 + 4, :].rearrange("p b c -> p (b c)")
                if n % 2 == 0:
                    S.op("act", ACT(dst, p[:, :], AF.Copy), reads=[pb], writes=[Vb])
                else:
                    S.op("dve", CP(dst, p[:, :]), reads=[pb], writes=[Vb])
            pend = []

            def emit_pv(st_):
                qg, kt, m_, mb = st_
                po, pob = pO[qg % 2]
                pl, plb = pL[qg % 2]
                qsl = slice(qg * NT, (qg + 1) * NT)
                S.pe_group([MM(po[:, :], V[:, kt, :], m_[:, :], kt == 0, kt == NKT - 1)],
                           reads=[Vb, mb], writes=[pob])
                S.pe_group([MM(pl[:, :], K["ones"][:, :], m_[:, :], kt == 0, kt == NKT - 1)],
                           reads=[K["ones_b"], mb], writes=[plb])
                if kt == NKT - 1:
                    r_, rb = rl
                    S.op("dve", RECIP(r_[:, :], pl[:, :]), reads=[plb], writes=[rb])
                    S.op("dve", TT(o_[:, qsl], po[:, :], r_[:, :], ALU.mult), reads=[pob, rb], writes=[ob])

            for qg in range(T // NT):
                qsl = slice(qg * NT, (qg + 1) * NT)
                for kt in range(NKT):
                    ps, psb = pS[ne % 2]
                    e_, eb = es[ne % len(es)]
                    m_, mb = pms[ne % len(pms)]
                    ne += 1
                    ksl = slice(kt * 128, (kt + 1) * 128)
                    S.pe_group([MM(ps[:, :], Kn[:, ksl], q[:, 0, qsl], True, False),
                                MM(ps[:, :], kr[0:64, ksl], q[0:64, 1, qsl], False, True)],
                               reads=[Knb, krb, qb], writes=[psb])
                    S.op("act", ACT(e_[:, :], ps[:, :], AF.Exp, scale=scale), reads=[psb], writes=[eb])
                    S.op("dve", STT(m_[:, :], qi[:, qsl], ki[:, kt:kt + 1], e_[:, :], ALU.is_ge, ALU.mult),
                         reads=[eb, qib, kib], writes=[mb])
                    pend.append((qg, kt, m_, mb))
                    if len(pend) > SKEW:
                        emit_pv(pend.pop(0))
            while pend:
                emit_pv(pend.pop(0))
            S.dma("sp", oT_d[h, :, :], o_[:, :], od, reads=[ob])


def emit_mixB_3(C, hT_d, oT_d, wo_r, H=B_HEADS, D=D_MODEL, T=TOK):
    S = C.S
    with C.phase():
        oT = C.sb([128, H, T], BF16, "oT")
        ob = Buf()
        S.dma("sp", oT[:, :, :], oT_d[:, :, :].rearrange("h p t -> p h t"), S.dsem(), writes=[ob])
        emit_oproj(C, oT, [ob], wo_r, hT_d, H, T, KCout=D // 128)


A_PERM = np.array(list(range(0, 16)) + list(range(32, 48)) + list(range(16, 32)) + list(range(48, 128)))


def fm_w(W, chunks, group=1):
    Din = W.shape[0]
    KC = Din // 128
    M = len(chunks[0])
    nb = len(chunks) // group
    out = np.empty((nb, 128, group, KC, M), np.float32)
    Wp = np.concatenate([W, np.zeros((Din, 1), W.dtype)], axis=1)
    for b in range(nb):
        for g in range(group):
            cols = np.asarray(chunks[b * group + g])
            out[b, :, g] = Wp[:, cols].reshape(KC, 128, M).transpose(1, 0, 2)
    return out.reshape(nb, 128, group * KC * M)


def tm_w(W, blocks):
    KC = W.shape[0] // 128
    out = np.empty((len(blocks), 128, KC, len(blocks[0])), np.float32)
    for b, cols in enumerate(blocks):
        out[b] = W[:, cols].reshape(KC, 128, len(cols)).transpose(1, 0, 2)
    return out.reshape(len(blocks), 128, -1)


def chunks_of(n0, n):
    return [n0 + np.arange(m * 128, (m + 1) * 128) for m in range(n // 128)]


def gain_l(g):
    return np.ascontiguousarray(np.asarray(g, np.float32).reshape(-1, 128).T)


def lay_w13(w13):
    D, F2 = w13.shape
    DFF = F2 // 2
    KC, NJ = D // 128, DFF // 128
    a = w13.reshape(KC, 128, 2, NJ, 128)
    return np.ascontiguousarray(a.transpose(3, 1, 0, 2, 4)).reshape(NJ, 128, KC * 256)


def lay_w2(w2):
    DFF, D = w2.shape
    KC, NJ = D // 128, DFF // 128
    a = w2.reshape(NJ, 128, KC, 128)
    return np.ascontiguousarray(a.transpose(2, 1, 0, 3)).reshape(KC, 128, NJ * 128)


def lay_sq(w):
    return fm_w(w, chunks_of(0, w.shape[1]))


def lay_wqkv(W, ng, H):
    HB = H // 4
    blks = []
    for g in range(ng):
        for s in range(3):
            for hb in range(HB):
                base = lambda h: ((g * 3 + s) * H + h) * 128
                if s < 2:
                    blks.append(fm_w(W, [base(4 * hb + hh) + A_PERM for hh in range(4)], group=4)[0])
                else:
                    blks.append(tm_w(W, [base(4 * hb) + np.arange(512)])[0])
    return np.stack(blks)


def lay_win_c(w, D):
    ch = []
    for m in range(D // 128):
        for s in range(3):
            ch.append(s * D + np.arange(m * 128, (m + 1) * 128))
    return fm_w(w, ch, group=3)


def _pad128(a):
    return np.concatenate([a, -np.ones(128 - len(a), int)])


def lay_win_b(w, QR, KVR):
    return fm_w(w, chunks_of(0, QR + KVR) + [_pad128(QR + KVR + np.arange(64))])


def lay_wuq(w, H):
    return fm_w(w, sum([[h * 192 + np.arange(128), _pad128(h * 192 + 128 + np.arange(64))] for h in range(H)], []), group=2)


def lay_wukv(w, H):
    return fm_w(w, sum([[h * 256 + np.arange(128), h * 256 + 128 + np.arange(128)] for h in range(H)], []), group=2)


def rope_frq(half, rows_a, rows_b):
    f = (np.float32(ROPE_THETA) ** (-np.arange(half, dtype=np.float32) / np.float32(half))).astype(np.float32)
    o = np.zeros((128, 2), np.float32)
    o[rows_a, 0] = f
    o[rows_a, 1] = 1.0
    o[rows_b, 0] = f
    o[rows_b, 1] = -1.0
    return o


def band_mask():
    k = np.arange(128)[:, None]
    q = np.arange(128)[None, :]
    return np.concatenate([(k >= q), (k <= q)], axis=1).astype(np.float32)


T_ = TOK
KC_ = D_MODEL // 128
NJ_ = D_FF // 128
NQ_ = B_Q_RANK // 128
NKV_ = B_KV_RANK // 128
NHALO = sum(A_DIL)
TAB_DILS = A_DIL + (1,)

SPEC = {
    "ident": ((128, 128), F32), "x": ((T_, D_MODEL), F32), "pos": ((1, T_), I32), "frq": ((4, 128, 2), F32),
    "mask": ((128, 256), F32), "hmask": ((128, 1), F32), "qidx": ((1, T_), F32), "kidx": ((128, SEQ // 128), F32),
    "gf": ((128, KC_), F32), "out": ((T_, D_MODEL), F32),
    "hT": ((KC_, 128, T_), F32), "hT_in": ((KC_, 128, T_), F32), "tab": ((4, 2, 128, T_), F32),
    "QT": ((3, A_HEADS, 128, T_), BF16), "KT": ((3, A_HEADS, 128, T_), BF16), "V": ((3, A_HEADS, 128, T_), BF16),
    "KTx": ((A_HEADS, 128, NHALO * 128), BF16), "Vx": ((A_HEADS, 128, NHALO, 128), BF16),
    "KTh": ((A_HEADS, 128, NHALO * 128), BF16), "Vh": ((A_HEADS, 128, NHALO, 128), BF16),
    "cqn": ((NQ_, 128, T_), BF16), "kvx": ((NKV_, 128, T_), BF16), "krx": ((128, T_), BF16),
    "kvg": ((NKV_, 128, SEQ), BF16), "krg": ((128, SEQ), BF16), "QB": ((B_HEADS, 2, 128, T_), BF16),
    "oT": ((B_HEADS, 128, T_), BF16),
    "cuT": ((KC_, 128, T_), F32), "bT": ((KC_, 128, T_), F32), "cux": ((KC_, 128, 2), F32), "cuh": ((KC_, 128, 2), F32),
}
for _l in range(DEPTH):
    for _s in ("1", "2"):
        SPEC["l%d_g%s" % (_l, _s)] = ((128, KC_), F32)
        SPEC["l%d_w13_%s" % (_l, _s)] = ((NJ_, 128, KC_ * 256), F32)
        SPEC["l%d_w2_%s" % (_l, _s)] = ((KC_, 128, NJ_ * 128), F32)
    SPEC["l%d_gm" % _l] = ((128, KC_), F32)
for _l in (0, 3):
    SPEC["l%d_wqkv" % _l] = ((36, 128, 4 * KC_ * 128), F32)
    SPEC["l%d_wo" % _l] = ((KC_, 128, A_HEADS * 128), F32)
SPEC["l1_gq"] = ((128, NQ_), F32)
SPEC["l1_gkv"] = ((128, NKV_), F32)
SPEC["l1_win"] = ((NQ_ + NKV_ + 1, 128, KC_ * 128), F32)
SPEC["l1_wuq"] = ((B_HEADS, 128, 2 * NQ_ * 128), F32)
SPEC["l1_wukv"] = ((B_HEADS, 128, 2 * NKV_ * 128), F32)
SPEC["l1_wo"] = ((KC_, 128, B_HEADS * 128), F32)
SPEC["l2_win"] = ((KC_, 128, 3 * KC_ * 128), F32)
SPEC["l2_convw"] = ((128, KC_, 3), F32)
SPEC["l2_wout"] = ((KC_, 128, KC_ * 128), F32)


def ffn_names(l, s):
    return ["l%d_g%s" % (l, s), "l%d_w13_%s" % (l, s), "l%d_w2_%s" % (l, s)]


def do_ffn(C, d, l, s):
    emit_ffn(C, d["hT"], d["l%d_g%s" % (l, s)], d["l%d_w13_%s" % (l, s)], d["l%d_w2_%s" % (l, s)])


def do_A1(C, d, l):
    emit_mixA_1(C, d["hT"], d["l%d_gm" % l], d["l%d_wqkv" % l], d["tab"], d["QT"], d["KT"], d["V"], d["KTx"], d["Vx"])


def do_A2(C, d, l):
    emit_mixA_2(C, d["hT"], d["QT"], d["KT"], d["V"], d["KTh"], d["Vh"], d["hmask"], d["mask"], d["l%d_wo" % l])


def do_B1(C, d):
    emit_mixB_1a(C, d["hT"], d["l1_gm"], d["l1_gq"], d["l1_gkv"], d["l1_win"], d["tab"][3], d["cqn"], d["kvx"], d["krx"])
    emit_mixB_1b(C, d["cqn"], d["l1_wuq"], d["tab"][3], d["QB"])


def do_B2(C, d):
    emit_mixB_2(C, d["QB"], d["kvg"], d["krg"], d["l1_wukv"], d["qidx"], d["kidx"], d["oT"])
    emit_mixB_3(C, d["hT"], d["oT"], d["l1_wo"])


def do_tables(C, d):
    emit_tables(C, d["pos"], d["frq"], d["tab"], TAB_DILS)


def L1(C, d):
    do_tables(C, d)
    emit_pin(C, d["x"], d["hT"])
    do_ffn(C, d, 0, "1")
    do_A1(C, d, 0)


def L2(C, d):
    emit_dram_copy(C, d["hT"], d["hT_in"])
    do_A2(C, d, 0)
    do_ffn(C, d, 0, "2")
    do_ffn(C, d, 1, "1")
    do_tables(C, d)
    do_B1(C, d)


def L3(C, d):
    emit_dram_copy(C, d["hT"], d["hT_in"])
    do_B2(C, d)
    do_ffn(C, d, 1, "2")
    do_ffn(C, d, 2, "1")
    emit_mixC_1(C, d["hT"], d["l2_gm"], d["l2_win"], d["cuT"], d["bT"], d["cux"])


def L4(C, d):
    emit_dram_copy(C, d["hT"], d["hT_in"])
    emit_mixC_2(C, d["hT"], d["cuT"], d["bT"], d["cuh"], d["l2_convw"], d["hmask"], d["l2_wout"])
    do_ffn(C, d, 2, "2")
    do_ffn(C, d, 3, "1")
    do_tables(C, d)
    do_A1(C, d, 3)


def L5(C, d):
    emit_dram_copy(C, d["hT"], d["hT_in"])
    do_A2(C, d, 3)
    do_ffn(C, d, 3, "2")
    emit_final(C, d["hT"], d["gf"], d["out"])


LAUNCHES = [
    ("L1", L1, ["ident", "x", "pos", "frq"] + ffn_names(0, "1") + ["l0_gm", "l0_wqkv"],
     ["hT", "QT", "KT", "V", "KTx", "Vx"], ["tab"]),
    ("L2", L2, ["ident", "hT_in", "QT", "KT", "V", "KTh", "Vh", "hmask", "mask", "l0_wo"] + ffn_names(0, "2") + ffn_names(1, "1")
     + ["pos", "frq", "l1_gm", "l1_gq", "l1_gkv", "l1_win", "l1_wuq"],
     ["hT", "kvx", "krx", "QB"], ["tab", "cqn"]),
    ("L3", L3, ["ident", "hT_in", "QB", "kvg", "krg", "l1_wukv", "qidx", "kidx", "l1_wo"] + ffn_names(1, "2") + ffn_names(2, "1")
     + ["l2_gm", "l2_win"],
     ["hT", "cuT", "bT", "cux"], ["oT"]),
    ("L4", L4, ["ident", "hT_in", "cuT", "bT", "cuh", "l2_convw", "hmask", "l2_wout"] + ffn_names(2, "2") + ffn_names(3, "1")
     + ["pos", "frq", "l3_gm", "l3_wqkv"],
     ["hT", "QT", "KT", "V", "KTx", "Vx"], ["tab"]),
    ("L5", L5, ["ident", "hT_in", "QT", "KT", "V", "KTh", "Vh", "hmask", "mask", "l3_wo"] + ffn_names(3, "2") + ["gf"],
     ["out"], ["hT"]),
]

_PROGS = {}


def build_prog(name, fn, ins, outs, scratch):
    if name in _PROGS:
        return _PROGS[name]
    nc = bass.Bass("TRN2", target_bir_lowering=False)
    d = {}
    for n in ins:
        d[n] = nc.dram_tensor(n, list(SPEC[n][0]), SPEC[n][1], kind="ExternalInput").ap()
    for n in outs:
        d[n] = nc.dram_tensor(n, list(SPEC[n][0]), SPEC[n][1], kind="ExternalOutput").ap()
    for n in scratch:
        d[n] = nc.dram_tensor(n, list(SPEC[n][0]), SPEC[n][1]).ap()
    with contextlib.ExitStack() as st:
        C = Ctx(nc, st)
        emit_consts(C, d["ident"])
        fn(C, d)
        C.S.emit()
    _PROGS[name] = nc
    return nc


INPUT_NAMES = (
    "x", "positions",
    "l0_ffn1_norm", "l0_ffn1_w13", "l0_ffn1_w2", "l0_mix_norm", "l0_a_w_qkv", "l0_a_w_o",
    "l0_ffn2_norm", "l0_ffn2_w13", "l0_ffn2_w2",
    "l1_ffn1_norm", "l1_ffn1_w13", "l1_ffn1_w2", "l1_mix_norm", "l1_b_w_in", "l1_b_q_norm",
    "l1_b_w_uq", "l1_b_kv_norm", "l1_b_w_ukv", "l1_b_w_o", "l1_ffn2_norm", "l1_ffn2_w13", "l1_ffn2_w2",
    "l2_ffn1_norm", "l2_ffn1_w13", "l2_ffn1_w2", "l2_mix_norm", "l2_c_w_in", "l2_c_conv_w", "l2_c_w_out",
    "l2_ffn2_norm", "l2_ffn2_w13", "l2_ffn2_w2",
    "l3_ffn1_norm", "l3_ffn1_w13", "l3_ffn1_w2", "l3_mix_norm", "l3_a_w_qkv", "l3_a_w_o",
    "l3_ffn2_norm", "l3_ffn2_w13", "l3_ffn2_w2",
    "final_norm",
)


def host_layout(inp):
    missing = [n for n in INPUT_NAMES if n not in inp]
    assert not missing, missing
    W = {}
    f = lambda k: np.asarray(inp[k], np.float32)
    for l in range(DEPTH):
        p = "l%d_" % l
        for s, nm in (("1", "ffn1"), ("2", "ffn2")):
            W[p + "g" + s] = gain_l(f(p + nm + "_norm"))
            W[p + "w13_" + s] = lay_w13(f(p + nm + "_w13"))
            W[p + "w2_" + s] = lay_w2(f(p + nm + "_w2"))
        W[p + "gm"] = gain_l(f(p + "mix_norm"))
    for l in (0, 3):
        p = "l%d_" % l
        W[p + "wqkv"] = lay_wqkv(f(p + "a_w_qkv"), 3, A_HEADS)
        W[p + "wo"] = lay_sq(f(p + "a_w_o"))
    W["l1_gq"] = gain_l(f("l1_b_q_norm"))
    W["l1_gkv"] = gain_l(f("l1_b_kv_norm"))
    W["l1_win"] = lay_win_b(f("l1_b_w_in"), B_Q_RANK, B_KV_RANK)
    W["l1_wuq"] = lay_wuq(f("l1_b_w_uq"), B_HEADS)
    W["l1_wukv"] = lay_wukv(f("l1_b_w_ukv"), B_HEADS)
    W["l1_wo"] = lay_sq(f("l1_b_w_o"))
    W["l2_win"] = lay_win_c(f("l2_c_w_in"), D_MODEL)
    W["l2_convw"] = np.ascontiguousarray(f("l2_c_conv_w").T.reshape(KC_, 128, 3).transpose(1, 0, 2))
    W["l2_wout"] = lay_sq(f("l2_c_w_out"))
    W["gf"] = gain_l(f("final_norm"))
    W["ident"] = np.eye(128, dtype=np.float32)
    fa = rope_frq(A_ROT // 2, np.arange(0, 16), np.arange(32, 48))
    fb = rope_frq(B_ROPE // 2, np.arange(0, 32), np.arange(32, 64))
    W["frq"] = np.stack([fa, fa, fa, fb])
    W["mask"] = band_mask()
    W["kidx"] = (np.arange(SEQ // 128)[None, :] * 128 + np.arange(128)[:, None]).astype(np.float32)
    x = f("x")
    pos = np.asarray(inp["positions"], np.int32)
    per_core = []
    for c in range(NCORES):
        b, j = c // CPB, c % CPB
        per_core.append({
            "x": np.ascontiguousarray(x[b, j * T_:(j + 1) * T_]),
            "pos": np.ascontiguousarray(pos[b:b + 1, j * T_:(j + 1) * T_]),
            "hmask": np.full((128, 1), float(j > 0), np.float32),
            "qidx": (j * T_ + np.arange(T_, dtype=np.float32))[None],
        })
    return W, per_core


def kernel_unfused(**inp):
    import ml_dtypes
    W, pc = host_layout(inp)
    state = [dict(p) for p in pc]
    out = None
    for (name, fn, ins, outs, scratch) in LAUNCHES:
        nc = build_prog(name, fn, ins, outs, scratch)
        in_maps = []
        for c in range(NCORES):
            im = {}
            for n in ins:
                im[n] = state[c][n] if n in state[c] else W[n]
            in_maps.append(im)
        res = run_bass_kernel_spmd(nc, in_maps, core_ids=list(range(NCORES))).results
        for c in range(NCORES):
            for n in outs:
                state[c]["hT_in" if n == "hT" else n] = np.asarray(res[c][n])
        for c in range(NCORES):
            j = c % CPB
            if "KTx" in outs:
                state[c]["KTh"] = state[c - 1]["KTx"] if j > 0 else np.zeros_like(state[c]["KTx"])
                state[c]["Vh"] = state[c - 1]["Vx"] if j > 0 else np.zeros_like(state[c]["Vx"])
            if "kvx" in outs:
                b0 = (c // CPB) * CPB
                state[c]["kvg"] = np.concatenate([state[b0 + i]["kvx"] for i in range(CPB)], axis=2)
                state[c]["krg"] = np.concatenate([state[b0 + i]["krx"][0] for i in range(CPB)], axis=1)
            if "cux" in outs:
                state[c]["cuh"] = state[c - 1]["cux"] if j > 0 else np.zeros_like(state[c]["cux"])
    out = np.stack([np.concatenate([np.asarray(state[b * CPB + j]["out"]) for j in range(CPB)], axis=0) for b in range(BATCH)])
    return out.astype(np.float32)


def kernel(**inputs):
    return kernel_fused(**inputs)


GROUPS = [list(range(b * CPB, (b + 1) * CPB)) for b in range(BATCH)]


def emit_gather(C, src, dst, n):
    S = C.S
    cs = S.dsem("cc", sw=True)
    for i in range(n):
        S.coll("AllGather", GROUPS, src[i], dst[i], cs)


def emit_select(C, src_g, dst, oh_d, n, X, dt):
    S = C.S
    with C.phase():
        oh = C.sb([128, CPB], F32, "oh")
        ohb = Buf()
        S.dma("sp", oh[:, :], oh_d[:, :], S.dsem(), writes=[ohb])
        cand = [(C.sb([128, CPB, X], dt, "cand"), Buf(), S.dsem("cand")) for _ in range(2)]
        acc = [(C.sb([128, X], F32, "acc"), Buf()) for _ in range(2)]
        outs = [(C.sb([128, X], dt, "sel"), Buf(), S.dsem("sel")) for _ in range(2)]
        for i in range(n):
            c, cb, cds = cand[i % 2]
            a, ab = acc[i % 2]
            o, ob, ods = outs[i % 2]
            S.dma("sp", c[:, :, :], src_g[i].rearrange("s p x -> p s x"), cds, writes=[cb])
            S.op("dve", TS(a[:, :], c[:, 0, :], oh[:, 0:1], None, ALU.mult), reads=[cb, ohb], writes=[ab])
            for s in range(1, CPB):
                last = s == CPB - 1
                S.op("dve", STT(o[:, :] if last else a[:, :], c[:, s, :], oh[:, s:s + 1], a[:, :], ALU.mult, ALU.add),
                     reads=[cb, ohb, ab], writes=[ob] if last else [ab])
            S.dma("sp", dst[i], o[:, :], ods, reads=[ob])


def emit_select_c(C, cug, cuh, oh_d, KC=KC_):
    S = C.S
    X = 2 * KC
    with C.phase():
        oh = C.sb([128, CPB], F32, "oh")
        ohb = Buf()
        S.dma("sp", oh[:, :], oh_d[:, :], S.dsem(), writes=[ohb])
        c = C.sb([128, CPB, X], F32, "cand")
        cb = Buf()
        S.dma("sp", c[:, :, :], cug[:, :, 0:X].rearrange("s p x -> p s x"), S.dsem(), writes=[cb])
        a = C.sb([128, X], F32, "acc")
        ab = Buf()
        S.op("dve", TS(a[:, :], c[:, 0, :], oh[:, 0:1], None, ALU.mult), reads=[cb, ohb], writes=[ab])
        for s in range(1, CPB):
            S.op("dve", STT(a[:, :], c[:, s, :], oh[:, s:s + 1], a[:, :], ALU.mult, ALU.add),
                 reads=[cb, ohb, ab], writes=[ab])
        S.dma("sp", cuh[:, 0:X], a[:, :], S.dsem(), reads=[ab])


def FUSED(C, d):
    def A_layer(l):
        sfx = str(l)
        emit_mixA_1(C, d["hT"], d["l%d_gm" % l], d["l%d_wqkv" % l], d["tab"], d["QT" + sfx], d["KT" + sfx], d["V" + sfx],
                    d["KTx" + sfx], d["Vx" + sfx], gather=(d["KTg" + sfx], d["Vg" + sfx]))
        emit_select(C, d["KTg" + sfx], d["KTh" + sfx], d["oh"], A_HEADS, NHALO * 128, BF16)
        emit_select(C, d["Vg" + sfx].rearrange("h s p b c -> h s p (b c)"), d["Vh" + sfx].rearrange("h p b c -> h p (b c)"),
                    d["oh"], A_HEADS, NHALO * 128, BF16)
        emit_mixA_2(C, d["hT"], d["QT" + sfx], d["KT" + sfx], d["V" + sfx], d["KTh" + sfx], d["Vh" + sfx], d["hmask"], d["mask"],
                    d["l%d_wo" % l])

    do_tables(C, d)
    emit_pin(C, d["x"], d["hT"])
    do_ffn(C, d, 0, "1")
    A_layer(0)
    do_ffn(C, d, 0, "2")
    do_ffn(C, d, 1, "1")
    emit_mixB_1a(C, d["hT"], d["l1_gm"], d["l1_gq"], d["l1_gkv"], d["l1_win"], d["tab"][3], d["cqn"], d["kvx"], d["krx"],
                 gather=(d["kvgs"], d["krgs"]))
    emit_mixB_1b(C, d["cqn"], d["l1_wuq"], d["tab"][3], d["QB"])
    emit_mixB_2(C, d["QB"], d["kvgs"], d["krgs"], d["l1_wukv"], d["qidx"], d["kidx"], d["oT"], gathered=True)
    emit_mixB_3(C, d["hT"], d["oT"], d["l1_wo"])
    do_ffn(C, d, 1, "2")
    do_ffn(C, d, 2, "1")
    emit_mixC_1(C, d["hT"], d["l2_gm"], d["l2_win"], d["cuT"], d["bT"], d["cux"], gather=d["cug"])
    emit_select_c(C, d["cug"][0], d["cuh"][0], d["oh"])
    emit_mixC_2(C, d["hT"], d["cuT"], d["bT"], d["cuh"], d["l2_convw"], d["hmask"], d["l2_wout"])
    do_ffn(C, d, 2, "2")
    do_ffn(C, d, 3, "1")
    A_layer(3)
    do_ffn(C, d, 3, "2")
    emit_final(C, d["hT"], d["gf"], d["out"])


for _l in ("0", "3"):
    for _n in ("QT", "KT", "V", "KTx", "Vx", "KTh", "Vh"):
        SPEC[_n + _l] = SPEC[_n]
    SPEC["KTg" + _l] = ((A_HEADS, CPB, 128, NHALO * 128), BF16)
    SPEC["Vg" + _l] = ((A_HEADS, CPB, 128, NHALO, 128), BF16)
SPEC["kvgs"] = ((NKV_, CPB, 128, T_), BF16)
SPEC["krgs"] = ((1, CPB, 128, T_), BF16)
SPEC["cux"] = ((1, 128, 512), F32)
SPEC["cug"] = ((1, CPB, 128, 512), F32)
SPEC["cuh"] = ((1, 128, 512), F32)
SPEC["oh"] = ((128, CPB), F32)
SPEC["krx"] = ((1, 128, T_), BF16)

FUSED_INS = (["ident", "x", "pos", "frq", "mask", "hmask", "oh", "qidx", "kidx", "gf"]
             + sum([ffn_names(l, s) for l in range(DEPTH) for s in ("1", "2")], [])
             + ["l%d_gm" % l for l in range(DEPTH)]
             + ["l0_wqkv", "l0_wo", "l3_wqkv", "l3_wo", "l1_gq", "l1_gkv", "l1_win", "l1_wuq", "l1_wukv", "l1_wo",
                "l2_win", "l2_convw", "l2_wout"])
FUSED_SCRATCH = (["hT", "tab", "cqn", "kvx", "krx", "kvgs", "krgs", "QB", "oT", "cuT", "bT", "cux", "cug", "cuh"]
                 + [n + l for l in ("0", "3") for n in ("QT", "KT", "V", "KTx", "Vx", "KTh", "Vh", "KTg", "Vg")])


def kernel_fused(**inp):
    W, pc = host_layout(inp)
    nc = build_prog("FUSED", FUSED, FUSED_INS, ["out"], FUSED_SCRATCH)
    in_maps = []
    for c in range(NCORES):
        j = c % CPB
        oh = np.zeros((128, CPB), np.float32)
        if j > 0:
            oh[:, j - 1] = 1.0
        pcc = dict(pc[c], oh=oh)
        in_maps.append({n: (pcc[n] if n in pcc else W[n]) for n in FUSED_INS})
    res = run_bass_kernel_spmd(nc, in_maps, core_ids=list(range(NCORES))).results
    out = np.stack([np.concatenate([np.asarray(res[b * CPB + j]["out"]) for j in range(CPB)], axis=0) for b in range(BATCH)])
    return out.astype(np.float32)
```

```python
import contextlib
import numpy as np
import concourse.bass as bass
import concourse.mybir as mybir
from concourse.bass_utils import run_bass_kernel_spmd

F32 = mybir.dt.float32
BF16 = mybir.dt.bfloat16
I32 = mybir.dt.int32
AF = mybir.ActivationFunctionType
ALU = mybir.AluOpType

D_MODEL = 2048
BATCH = 2
SEQ = 8192
DEPTH = 4
D_FF = 5632
ROPE_THETA = 500000.0
A_HEADS = 16
A_HEAD_DIM = 128
A_ROT = 32
A_DIL = (1, 4, 16)
B_HEADS = 16
B_Q_RANK = 1536
B_KV_RANK = 512
B_NOPE = 128
B_ROPE = 64
B_V = 128
NCORES = 8
TOK = BATCH * SEQ // NCORES
CPB = NCORES // BATCH
EPS = 1e-6


class Buf:
    __slots__ = ("w", "r")

    def __init__(self):
        self.w = None
        self.r = {}


def bufs(n):
    return [Buf() for _ in range(n)]


class DSem:
    __slots__ = ("sem", "cnt")

    def __init__(self, sem):
        self.sem = sem
        self.cnt = 0


class Sched:
    ENG = ("pe", "act", "dve", "pool", "sp")

    def __init__(self, nc, stack):
        self.nc = nc
        self.stack = stack
        self.q = {e: [] for e in self.ENG}
        self.cur = {}
        self.cnt = {}
        self.known = {e: {} for e in self.ENG}
        self.nsem = 0
        self.ds_all = []
        self.ds_hw = []
        self.ds_sw = []
        self.ds_free = []
        self.ds_free_sw = []
        for e in self.ENG:
            self.cur[e] = self._new_sem("e_" + e)
            self.cnt[e] = 0

    def _new_sem(self, name):
        self.nsem += 1
        return self.stack.enter_context(self.nc.semaphore("s%d_%s" % (self.nsem, name)))

    def dsem(self, name="d", sw=False):
        free = self.ds_free_sw if sw else self.ds_free
        if free:
            return free.pop()
        d = DSem(self._new_sem(name))
        (self.ds_sw if sw else self.ds_hw).append(d)
        self.ds_all.append(d)
        return d

    def _deps(self, reads, writes):
        deps = []
        for b in reads:
            if b.w is not None:
                deps.append(b.w)
        for b in writes:
            if b.w is not None:
                deps.append(b.w)
            deps.extend(b.r.values())
        return deps

    def _waits(self, eng, deps, skip=None):
        kn = self.known[eng]
        best = {}
        for (sem, val) in deps:
            if sem is skip:
                continue
            k = id(sem)
            if kn.get(k, 0) >= val:
                continue
            if k not in best or best[k][1] < val:
                best[k] = (sem, val)
        for k, (sem, val) in best.items():
            kn[k] = val
            self.q[eng].append(("wait", sem, val))

    def _mark(self, tok, reads, writes):
        k = id(tok[0])
        for b in reads:
            if k not in b.r or b.r[k][1] < tok[1]:
                b.r[k] = tok
        for b in writes:
            b.w = tok
            b.r = {}

    def _sig(self, eng):
        self.cnt[eng] += 1
        return (self.cur[eng], self.cnt[eng])

    def op(self, eng, fn, reads=(), writes=()):
        self._waits(eng, self._deps(reads, writes))
        tok = self._sig(eng)
        self.q[eng].append(("op", fn, tok[0]))
        self._mark(tok, reads, writes)
        return tok

    def pe_group(self, fns, reads=(), writes=()):
        self._waits("pe", self._deps(reads, writes), skip=self.cur["pe"])
        tok = self._sig("pe")
        for f in fns[:-1]:
            self.q["pe"].append(("op", f, None))
        self.q["pe"].append(("op", fns[-1], tok[0]))
        self._mark(tok, reads, writes)
        return tok

    def dma(self, queue, out, in_, dsem, reads=(), writes=(), group_final=None, **kw):
        deps = self._deps(reads, writes)
        if group_final is not None:
            deps = [d_ for d_ in deps if not (d_[0] is dsem.sem and d_[1] == group_final)]
        self._waits(queue, deps)
        dsem.cnt += 16
        tok = (dsem.sem, dsem.cnt if group_final is None else group_final)
        self.q[queue].append(("dma", out, in_, dsem.sem, kw))
        self._mark(tok, reads, writes)
        return tok

    def coll(self, kind, groups, src, dst, dsem, reads=(), writes=()):
        self._waits("pool", self._deps(reads, writes))
        dsem.cnt += 1
        tok = (dsem.sem, dsem.cnt)
        self.q["pool"].append(("coll", kind, groups, src, dst, dsem.sem))
        self._mark(tok, reads, writes)
        return tok

    def call(self, eng, fn, reads=(), writes=()):
        self._waits(eng, self._deps(reads, writes))
        self.q[eng].append(("call", fn))

    def wait_tok(self, eng, tok):
        self._waits(eng, [tok])

    def barrier(self, recycle=True):
        for e in self.ENG:
            if e != "sp" and self.cnt[e] > 0:
                self.wait_tok("sp", (self.cur[e], self.cnt[e]))
        for d in self.ds_all:
            if d.cnt > 0:
                self.wait_tok("sp", (d.sem, d.cnt))
        tok = self._sig("sp")
        self.q["sp"].append(("inc", tok[0]))
        for e in self.ENG:
            if e != "sp":
                self.wait_tok(e, tok)
        if recycle:
            self.ds_free = list(self.ds_hw)
            self.ds_free_sw = list(self.ds_sw)

    def emit(self):
        def run(e, items):
            for it in items:
                if it[0] == "wait":
                    e.wait_ge(it[1], it[2])
                elif it[0] == "op":
                    ins = it[1](e)
                    if it[2] is not None:
                        ins.then_inc(it[2], 1)
                elif it[0] == "inc":
                    e.sem_inc(it[1], 1)
                elif it[0] == "call":
                    it[1](e)
                elif it[0] == "coll":
                    _, kind, groups, src, dst, sem = it
                    e.collective_compute(kind, ALU.bypass, replica_groups=groups, ins=[src.opt()], outs=[dst.opt()]).then_inc(sem, 1)
                else:
                    _, out, in_, sem, kw = it
                    if callable(in_):
                        in_ = in_()
                    e.dma_start(out=out, in_=in_, **kw).then_inc(sem, 16)

        with self.nc.Block() as block:
            @block.tensor
            def _(e):
                run(e, self.q["pe"])

            @block.scalar
            def _(e):
                run(e, self.q["act"])

            @block.vector
            def _(e):
                run(e, self.q["dve"])

            @block.gpsimd
            def _(e):
                run(e, self.q["pool"])

            @block.sync
            def _(e):
                run(e, self.q["sp"])


class Stream:
    def __init__(self, S, queue, slots):
        self.S = S
        self.queue = queue
        self.slots = slots
        self.items = []
        self.issued = 0

    def add(self, fn):
        self.items.append(fn)
        return len(self.items) - 1

    def ensure(self, upto):
        upto = min(upto, len(self.items) - 1)
        while self.issued <= upto:
            i = self.issued
            t, buf, ds = self.slots[i % len(self.slots)]
            o, a, kw = self.items[i](t)
            self.S.dma(self.queue, o, a, ds, writes=[buf], **kw)
            self.issued += 1

    def get(self, i):
        self.ensure(i + len(self.slots) - 1)
        return self.slots[i % len(self.slots)]


class Ctx:
    def __init__(self, nc, gstack):
        self.nc = nc
        self.gstack = gstack
        self.stack = gstack
        self.S = Sched(nc, gstack)
        self.n = 0
        self.K = {}

    def sb(self, shape, dt, name="sb"):
        self.n += 1
        return self.stack.enter_context(self.nc.sbuf_tensor("%s_%d" % (name, self.n), list(shape), dt))

    def ps(self, shape, dt=F32, name="ps"):
        self.n += 1
        return self.stack.enter_context(self.nc.psum_tensor("%s_%d" % (name, self.n), list(shape), dt))

    @contextlib.contextmanager
    def phase(self):
        with contextlib.ExitStack() as st:
            self.stack = st
            yield
            self.S.barrier()
        self.stack = self.gstack


def MM(out, lhsT, rhs, start, stop):
    return lambda e: e.matmul(out, lhsT, rhs, start=start, stop=stop)


def ACT(out, in_, func, **kw):
    return lambda e: e.activation(out=out, in_=in_, func=func, **kw)


def TT(out, in0, in1, op):
    return lambda e: e.tensor_tensor(out=out, in0=in0, in1=in1, op=op)


def TS(out, in0, s1, s2, op0, op1=None):
    if op1 is None:
        return lambda e: e.tensor_scalar(out=out, in0=in0, scalar1=s1, scalar2=None, op0=op0)
    return lambda e: e.tensor_scalar(out=out, in0=in0, scalar1=s1, scalar2=s2, op0=op0, op1=op1)


def STT(out, in0, scalar, in1, op0, op1):
    return lambda e: e.scalar_tensor_tensor(out=out, in0=in0, scalar=scalar, in1=in1, op0=op0, op1=op1)


def CP(out, in_):
    return lambda e: e.tensor_copy(out=out, in_=in_)


def MEMSET(ap, v):
    return lambda e: e.memset(ap, v)


def RECIP(out, in_):
    return lambda e: e.reciprocal(out, in_)


CAST_KW = dict(max_dma_last_dim=4096)
SKEW = 2


def emit_consts(C, ident_d):
    S = C.S
    K = C.K
    K["ones"] = C.sb([128, 128], BF16, "ones")
    K["ones_b"] = Buf()
    S.op("pool", MEMSET(K["ones"][:, :], 1.0), writes=[K["ones_b"]])
    K["eps"] = C.sb([128, 1], F32, "eps")
    K["eps_b"] = Buf()
    S.op("pool", MEMSET(K["eps"][:, :], EPS), writes=[K["eps_b"]])
    K["ident"] = C.sb([128, 128], F32, "ident")
    K["ident_b"] = Buf()
    ds = S.dsem()
    S.dma("sp", K["ident"][:, :], ident_d[:, :], ds, writes=[K["ident_b"]])


def norm_res(C, KC, NT, nslot=2):
    S = C.S
    R = {}
    R["h"] = [(C.sb([128, KC, NT], F32, "h"), bufs(KC), S.dsem("h")) for _ in range(nslot)]
    R["ssum"] = (C.ps([128, NT], F32, "ssum"), Buf())
    R["sq"] = [(C.sb([128, NT], BF16, "sq"), Buf()) for _ in range(3)]
    R["rs"] = (C.sb([128, NT], F32, "rs"), Buf())
    return R


def emit_rstd(C, R, src, srcb, KC, NT, Dn):
    S = C.S
    K = C.K
    ssum, ssum_b = R["ssum"]
    for k in range(KC):
        sq, sq_b = R["sq"][k % len(R["sq"])]
        S.op("act", ACT(sq[:, :NT], src(k), AF.Square), reads=[srcb[k]], writes=[sq_b])
        S.pe_group([MM(ssum[:, :NT], K["ones"][:, :], sq[:, :NT], k == 0, k == KC - 1)],
                   reads=[sq_b, K["ones_b"]], writes=[ssum_b])
    rs, rs_b = R["rs"]
    S.op("act", ACT(rs[:, :NT], ssum[:, :NT], AF.Sqrt, scale=1.0 / Dn, bias=K["eps"][:, :]),
         reads=[ssum_b, K["eps_b"]], writes=[rs_b])
    S.op("dve", RECIP(rs[:, :NT], rs[:, :NT]), reads=[rs_b], writes=[rs_b])
    return rs, rs_b


def emit_norm_full(C, hT_d, gain, gain_b, xn, xnb, KC, T, d=1, NT=256):
    S = C.S
    R = norm_res(C, KC, NT)
    for t in range(T // NT):
        h, hb, hds = R["h"][t % 2]
        tsl = slice(t * NT, (t + 1) * NT)
        S.dma("sp", h[:, :, :], hT_d[:, :, tsl].rearrange("k p n -> p k n"), hds, writes=hb)
        rs, rs_b = emit_rstd(C, R, lambda k, h=h: h[:, k, :], hb, KC, NT, KC * 128)
        for k in range(KC):
            if d == 1:
                o = xn[:, k, tsl]
                i0 = h[:, k, :]
                i1 = rs[:, :]
            else:
                m0 = t * NT // d
                o = xn[:, k, :].rearrange("p (r m) -> p m r", r=d)[:, m0:m0 + NT // d, :]
                i0 = h[:, k, :].rearrange("p (m r) -> p m r", r=d)
                i1 = rs[:, :].rearrange("p (m r) -> p m r", r=d)
            S.op("dve", STT(o, i0, gain[:, k:k + 1], i1, ALU.mult, ALU.mult),
                 reads=[hb[k], rs_b, gain_b], writes=[xnb[k]])


def emit_ffn(C, hT_d, gain_d, w13r, w2r, D=D_MODEL, DFF=D_FF, T=TOK, NT=512):
    S = C.S
    K = C.K
    KC = D // 128
    NJ = DFF // 128
    NTILE = T // NT
    with C.phase():
        w13s = [(C.sb([128, KC * 256], BF16, "w13"), Buf(), S.dsem("w13", sw=True)) for _ in range(2)]
        w2s = [(C.sb([128, NJ * 128], BF16, "w2"), Buf(), S.dsem("w2", sw=True)) for _ in range(2)]
        gain = C.sb([128, KC], F32, "gain")
        gain_b = Buf()
        S.dma("sp", gain[:, :], gain_d[:, :], S.dsem(), writes=[gain_b])
        R = norm_res(C, KC, NT)
        hst = [S.dsem("hst") for _ in range(2)]
        xn = C.sb([128, KC, NT], BF16, "xn")
        xb = bufs(KC)
        hid = C.sb([128, NJ, NT], BF16, "hid")
        hidb = bufs(NJ)
        G = [(C.ps([128, NT], F32, "G"), Buf()) for _ in range(2)]
        U = [(C.ps([128, NT], F32, "U"), Buf()) for _ in range(2)]
        Y = [(C.ps([128, NT], F32, "Y"), Buf()) for _ in range(2)]
        sgs = [(C.sb([128, NT], F32, "sg"), Buf()) for _ in range(2)]

        st13 = Stream(S, "pool", w13s)
        st2 = Stream(S, "pool", w2s)
        for t in range(NTILE):
            for j in range(NJ):
                st13.add(lambda tt, j=j: (tt[:, :], w13r[j, :, :], CAST_KW))
            for m in range(KC):
                st2.add(lambda tt, m=m: (tt[:, :], w2r[m, :, :], CAST_KW))

        for t in range(NTILE):
            hs = t % 2
            h, hb, hds = R["h"][hs]
            tsl = slice(t * NT, (t + 1) * NT)
            S.dma("sp", h[:, :, :], hT_d[:, :, tsl].rearrange("k p n -> p k n"), hds, writes=hb)
            st13.ensure(t * NJ + 1)
            rs, rs_b = emit_rstd(C, R, lambda k, h=h: h[:, k, :], hb, KC, NT, D)
            for k in range(KC):
                S.op("dve", STT(xn[:, k, :], h[:, k, :], gain[:, k:k + 1], rs[:, :], ALU.mult, ALU.mult),
                     reads=[hb[k], rs_b, gain_b], writes=[xb[k]])
            for j in range(NJ):
                w, wb, _ = st13.get(t * NJ + j)
                if j == NJ - 4:
                    st2.ensure(t * KC + 1)
                g_, gb = G[j % 2]
                u_, ub = U[j % 2]
                S.pe_group([MM(g_[:, :], w[:, (2 * k) * 128:(2 * k + 1) * 128], xn[:, k, :], k == 0, k == KC - 1)
                            for k in range(KC)], reads=[wb] + xb, writes=[gb])
                S.pe_group([MM(u_[:, :], w[:, (2 * k + 1) * 128:(2 * k + 2) * 128], xn[:, k, :], k == 0, k == KC - 1)
                            for k in range(KC)], reads=[wb] + xb, writes=[ub])
                sg, sgb = sgs[j % 2]
                S.op("act", ACT(sg[:, :], g_[:, :], AF.Silu), reads=[gb], writes=[sgb])
                S.op("dve", TT(hid[:, j, :], sg[:, :], u_[:, :], ALU.mult), reads=[sgb, ub], writes=[hidb[j]])
            for m in range(KC):
                w, wb, _ = st2.get(t * KC + m)
                if m == KC - 2 and t + 1 < NTILE:
                    st13.ensure((t + 1) * NJ + 1)
                y_, yb = Y[m % 2]
                S.pe_group([MM(y_[:, :], w[:, j * 128:(j + 1) * 128], hid[:, j, :], j == 0, j == NJ - 1)
                            for j in range(NJ)], reads=[wb] + hidb, writes=[yb])
                S.op("dve", STT(h[:, m, :], y_[:, :], 0.5, h[:, m, :], ALU.mult, ALU.add),
                     reads=[yb, hb[m]], writes=[hb[m]])
                if m == 0:
                    fin = hst[hs].cnt + 16 * KC
                S.dma("sp", hT_d[m, :, tsl], h[:, m, :], hst[hs], reads=[hb[m]], group_final=fin)


def emit_pin(C, x_d, hT_d, D=D_MODEL, T=TOK):
    S = C.S
    K = C.K
    KC = D // 128
    with C.phase():
        xs = [(C.sb([128, D], F32, "x"), Buf(), S.dsem("x")) for _ in range(8)]
        hst = [(C.sb([128, KC, 512], F32, "hst"), bufs(KC), S.dsem("hst")) for _ in range(2)]
        ps = [(C.ps([128, 512], F32, "tp"), Buf()) for _ in range(2)]
        n = 0
        for grp in range(T // 512):
            xt = []
            for i in range(4):
                x, xb, xds = xs[(grp % 2) * 4 + i]
                t0 = grp * 512 + i * 128
                S.dma("sp", x[:, :], x_d[t0:t0 + 128, :], xds, writes=[xb])
                xt.append((x, xb))
            hs, hsb, hds = hst[grp % 2]
            for k in range(KC):
                p, pb = ps[n % 2]
                S.pe_group([lambda e, p=p, x=xt[i][0], i=i, k=k: e.transpose(
                    p[:, i * 128:(i + 1) * 128], x[:, k * 128:(k + 1) * 128], K["ident"][:, :]) for i in range(4)],
                    reads=[b for (_, b) in xt] + [K["ident_b"]], writes=[pb])
                eng = "act" if n % 2 == 0 else "dve"
                if eng == "act":
                    S.op("act", ACT(hs[:, k, :], p[:, :], AF.Copy), reads=[pb], writes=[hsb[k]])
                else:
                    S.op("dve", CP(hs[:, k, :], p[:, :]), reads=[pb], writes=[hsb[k]])
                n += 1
            S.dma("sp", hT_d[:, :, grp * 512:(grp + 1) * 512].rearrange("k p n -> p k n"), hs[:, :, :], hds, reads=hsb)


def emit_final(C, hT_d, gain_d, out_d, D=D_MODEL, T=TOK, NT=512):
    S = C.S
    K = C.K
    KC = D // 128
    with C.phase():
        gain = C.sb([128, KC], F32, "gain")
        gain_b = Buf()
        S.dma("sp", gain[:, :], gain_d[:, :], S.dsem(), writes=[gain_b])
        R = norm_res(C, KC, NT)
        xn = C.sb([128, KC, NT], F32, "xn32")
        xb = bufs(KC)
        osb = [(C.sb([128, D], F32, "osb"), Buf(), S.dsem("o")) for _ in range(2)]
        ps = [(C.ps([128, 512], F32, "tp"), Buf()) for _ in range(2)]
        n = 0
        no = 0
        last = []
        for t in range(T // NT):
            h, hb, hds = R["h"][t % 2]
            tsl = slice(t * NT, (t + 1) * NT)
            S.dma("sp", h[:, :, :], hT_d[:, :, tsl].rearrange("k p n -> p k n"), hds, writes=hb)
            rs, rs_b = emit_rstd(C, R, lambda k, h=h: h[:, k, :], hb, KC, NT, D)
            for k in range(KC):
                S.op("dve", STT(xn[:, k, :], h[:, k, :], gain[:, k:k + 1], rs[:, :], ALU.mult, ALU.mult),
                     reads=[hb[k], rs_b, gain_b], writes=[xb[k]])
            for i in range(NT // 128):
                o, ob, ods = osb[no % 2]
                no += 1
                for k0 in range(0, KC, 4):
                    p, pb = ps[n % 2]
                    S.pe_group([lambda e, p=p, kk=kk, k0=k0, i=i: e.transpose(
                        p[:, kk * 128:(kk + 1) * 128], xn[:, k0 + kk, i * 128:(i + 1) * 128], K["ident"][:, :])
                        for kk in range(4)], reads=xb[k0:k0 + 4] + [K["ident_b"]], writes=[pb])
                    if n % 2 == 0:
                        S.op("act", ACT(o[:, k0 * 128:(k0 + 4) * 128], p[:, :], AF.Copy), reads=[pb], writes=[ob])
                    else:
                        S.op("dve", CP(o[:, k0 * 128:(k0 + 4) * 128], p[:, :]), reads=[pb], writes=[ob])
                    n += 1
                t0 = t * NT + i * 128
                last.append(S.dma("sp", out_d[t0:t0 + 128, :], o[:, :], ods, reads=[ob]))
        for tk in last[-2:]:
            S.wait_tok("sp", tk)


def emit_oproj(C, g, gb, wr_d, hT_d, KCin, T=TOK, KCout=D_MODEL // 128, NT=512, scale=1.0):
    S = C.S
    ws = [(C.sb([128, KCin * 128], BF16, "wo"), Buf(), S.dsem("wo", sw=True)) for _ in range(2)]
    hr = [(C.sb([128, T], F32, "hrow"), Buf(), S.dsem("hr"), S.dsem("hrs")) for _ in range(2)]
    Y = [(C.ps([128, NT], F32, "Y"), Buf()) for _ in range(2)]
    st = Stream(S, "pool", ws)
    for m in range(KCout):
        st.add(lambda tt, m=m: (tt[:, :], wr_d[m, :, :], CAST_KW))
    n = 0
    for m in range(KCout):
        w, wb, _ = st.get(m)
        h, hb, hds, hss = hr[m % 2]
        S.dma("sp", h[:, :], hT_d[m, :, :], hds, writes=[hb])
        for t in range(T // NT):
            y_, yb = Y[n % 2]
            n += 1
            tsl = slice(t * NT, (t + 1) * NT)
            S.pe_group([MM(y_[:, :], w[:, k * 128:(k + 1) * 128], g[:, k, tsl], k == 0, k == KCin - 1)
                        for k in range(KCin)], reads=[wb] + list(gb), writes=[yb])
            S.op("dve", STT(h[:, tsl], y_[:, :], scale, h[:, tsl], ALU.mult, ALU.add), reads=[yb, hb], writes=[hb])
        S.dma("sp", hT_d[m, :, :], h[:, :], hss, reads=[hb])


def emit_mixC_1(C, hT_d, gain_d, winr, cuT_d, bT_d, cux_d, D=D_MODEL, T=TOK, NT=512, gather=None):
    S = C.S
    KC = D // 128
    with C.phase():
        gain = C.sb([128, KC], F32, "gain")
        gain_b = Buf()
        S.dma("sp", gain[:, :], gain_d[:, :], S.dsem(), writes=[gain_b])
        xn = C.sb([128, KC, T], BF16, "xnf")
        xb = bufs(KC)
        emit_norm_full(C, hT_d, gain, gain_b, xn, xb, KC, T)
        ws = [(C.sb([128, 3 * KC * 128], BF16, "win"), Buf(), S.dsem("win", sw=True)) for _ in range(2)]
        st = Stream(S, "pool", ws)
        for m in range(KC):
            st.add(lambda tt, m=m: (tt[:, :], winr[m, :, :], CAST_KW))
        P = [[(C.ps([128, NT], F32, "P"), Buf()) for _ in range(2)] for _ in range(3)]
        csb = [(C.sb([128, NT], F32, "c"), Buf()) for _ in range(2)]
        cur = [(C.sb([128, T], F32, "cu"), Buf(), S.dsem("cu")) for _ in range(2)]
        br = [(C.sb([128, T], F32, "b"), Buf(), S.dsem("b")) for _ in range(2)]
        xds = [S.dsem("cux") for _ in range(2)]
        n = 0
        for m in range(KC):
            w, wb, _ = st.get(m)
            cu, cub, cuds = cur[m % 2]
            b_, bb, bds = br[m % 2]
            for t in range(T // NT):
                tsl = slice(t * NT, (t + 1) * NT)
                pp = [P[s][n % 2] for s in range(3)]
                for s in range(3):
                    S.pe_group([MM(pp[s][0][:, :], w[:, (s * KC + k) * 128:(s * KC + k + 1) * 128], xn[:, k, tsl],
                                   k == 0, k == KC - 1) for k in range(KC)], reads=[wb] + xb, writes=[pp[s][1]])
                c_, cb = csb[n % 2]
                n += 1
                S.op("act", ACT(b_[:, tsl], pp[0][0][:, :], AF.Copy), reads=[pp[0][1]], writes=[bb])
                S.op("act", ACT(c_[:, :], pp[1][0][:, :], AF.Copy), reads=[pp[1][1]], writes=[cb])
                S.op("dve", TT(cu[:, tsl], c_[:, :], pp[2][0][:, :], ALU.mult), reads=[cb, pp[2][1]], writes=[cub])
            S.dma("sp", cuT_d[m, :, :], cu[:, :], cuds, reads=[cub])
            S.dma("sp", bT_d[m, :, :], b_[:, :], bds, reads=[bb])
            S.dma("sp", cux_d[0][:, 2 * m:2 * m + 2], cu[:, T - 2:T], xds[m % 2], reads=[cub])
        if gather is not None:
            S.barrier(recycle=False)
            emit_gather(C, cux_d, gather, 1)


def emit_mixC_2(C, hT_d, cuT_d, bT_d, cuh_d, convw_d, hmask_d, woutr, D=D_MODEL, T=TOK):
    S = C.S
    KC = D // 128
    with C.phase():
        cw = C.sb([128, KC, 3], F32, "convw")
        cwb = Buf()
        S.dma("sp", cw[:, :, :], convw_d[:, :, :], S.dsem(), writes=[cwb])
        hm = C.sb([128, 1], F32, "hmask")
        hmb = Buf()
        S.dma("sp", hm[:, :], hmask_d[:, :], S.dsem(), writes=[hmb])
        g = C.sb([128, KC, T], BF16, "g")
        gb = bufs(KC)
        cur = [(C.sb([128, T + 2], F32, "cu"), Buf(), Buf(), S.dsem("cu"), S.dsem("cuh")) for _ in range(2)]
        br = [(C.sb([128, T], F32, "b"), Buf(), S.dsem("b")) for _ in range(2)]
        zs = [(C.sb([128, T], F32, "z"), Buf()) for _ in range(2)]
        for m in range(KC):
            cu, cub, chb, cuds, chds = cur[m % 2]
            b_, bb, bds = br[m % 2]
            z, zb = zs[m % 2]
            S.dma("sp", cu[:, 2:T + 2], cuT_d[m, :, :], cuds, writes=[cub])
            S.dma("sp", cu[:, 0:2], cuh_d[0][:, 2 * m:2 * m + 2], chds, writes=[chb])
            S.dma("sp", b_[:, :], bT_d[m, :, :], bds, writes=[bb])
            S.op("dve", TS(cu[:, 0:2], cu[:, 0:2], hm[:, 0:1], None, ALU.mult), reads=[hmb], writes=[chb])
            S.op("dve", TS(z[:, :], cu[:, 2:T + 2], cw[:, m, 2:3], None, ALU.mult), reads=[cub, cwb], writes=[zb])
            S.op("dve", STT(z[:, :], cu[:, 1:T + 1], cw[:, m, 1:2], z[:, :], ALU.mult, ALU.add),
                 reads=[cub, chb, cwb, zb], writes=[zb])
            S.op("dve", STT(z[:, :], cu[:, 0:T], cw[:, m, 0:1], z[:, :], ALU.mult, ALU.add),
                 reads=[cub, chb, cwb, zb], writes=[zb])
            S.op("pool", TT(g[:, m, :], z[:, :], b_[:, :], ALU.mult), reads=[zb, bb], writes=[gb[m]])
        emit_oproj(C, g, gb, woutr, hT_d, KC, T, KCout=KC)


def emit_dram_copy(C, dst, src):
    S = C.S
    ds = S.dsem("cp")
    fin = ds.cnt + 16 * dst.shape[0]
    for k in range(dst.shape[0]):
        S.dma("sp", dst[k, :, :], src[k, :, :], ds, group_final=fin)
    S.barrier()


TWO_PI = 6.283185307179586
CW1 = 6.28125
CW2 = float(np.float32(np.frombuffer(np.uint32(np.frombuffer(np.float32(TWO_PI - CW1).tobytes(), np.uint32)[0] & 0xFFFFF000).tobytes(), np.float32)[0]))
CW3 = float(np.float32(TWO_PI - CW1 - CW2))
PI_LO = 3.1415925


def _wrap(S, r, rb, m, mb):
    S.op("dve", TS(m, r, float(np.pi), -TWO_PI, ALU.is_gt, ALU.mult), reads=[rb], writes=[mb])
    S.op("dve", TT(r, r, m, ALU.add), reads=[rb, mb], writes=[rb])
    S.op("dve", TS(m, r, -float(np.pi), TWO_PI, ALU.is_lt, ALU.mult), reads=[rb], writes=[mb])
    S.op("dve", TT(r, r, m, ALU.add), reads=[rb, mb], writes=[rb])
    S.op("dve", TS(r, r, PI_LO, -PI_LO, ALU.min, ALU.max), reads=[rb], writes=[rb])


def emit_tables(C, pos_d, frq_d, tab_d, dils, T=TOK):
    S = C.S
    with C.phase():
        pos_i = C.sb([128, T], I32, "posi")
        pos_f = C.sb([128, T], F32, "posf")
        pb = Buf()
        S.dma("sp", pos_i[:, :], pos_d[0:1, :].partition_broadcast(128), S.dsem(), writes=[pb])
        S.op("dve", CP(pos_f[:, :], pos_i[:, :]), reads=[pb], writes=[pb])
        x = C.sb([128, T], F32, "ang")
        r = C.sb([128, T], F32, "red")
        m = C.sb([128, T], F32, "msk")
        kf = C.sb([128, T], F32, "kf")
        ki = C.sb([128, T], I32, "ki")
        sn = C.sb([128, T], F32, "sin")
        cs = C.sb([128, T], F32, "cos")
        xb, rb, mb, kb, snb, csb = bufs(6)
        for i, d in enumerate(dils):
            fr = C.sb([128, 2], F32, "frq")
            fb = Buf()
            S.dma("sp", fr[:, :], frq_d[i, :, :], S.dsem(), writes=[fb])
            if d == 1:
                S.op("dve", TS(x[:, :], pos_f[:, :], fr[:, 0:1], None, ALU.mult), reads=[pb, fb], writes=[xb])
            else:
                S.op("dve", TS(x[:, :].rearrange("p (r m) -> p m r", r=d), pos_f[:, :].rearrange("p (m r) -> p m r", r=d),
                               fr[:, 0:1], None, ALU.mult), reads=[pb, fb], writes=[xb])
            S.op("dve", TS(ki[:, :], x[:, :], 1.0 / TWO_PI, None, ALU.mult), reads=[xb], writes=[kb])
            S.op("dve", CP(kf[:, :], ki[:, :]), reads=[kb], writes=[kb])
            S.op("dve", STT(r[:, :], kf[:, :], -CW1, x[:, :], ALU.mult, ALU.add), reads=[kb, xb], writes=[rb])
            S.op("dve", STT(r[:, :], kf[:, :], -CW2, r[:, :], ALU.mult, ALU.add), reads=[kb, rb], writes=[rb])
            S.op("dve", STT(r[:, :], kf[:, :], -CW3, r[:, :], ALU.mult, ALU.add), reads=[kb, rb], writes=[rb])
            _wrap(S, r[:, :], rb, m[:, :], mb)
            S.op("act", ACT(sn[:, :], r[:, :], AF.Sin), reads=[rb], writes=[snb])
            S.op("dve", TS(sn[:, :], sn[:, :], fr[:, 1:2], None, ALU.mult), reads=[snb, fb], writes=[snb])
            S.op("dve", TS(r[:, :], r[:, :], float(np.pi / 2), None, ALU.add), reads=[rb, snb], writes=[rb])
            _wrap(S, r[:, :], rb, m[:, :], mb)
            S.op("act", ACT(cs[:, :], r[:, :], AF.Sin), reads=[rb], writes=[csb])
            S.dma("sp", tab_d[i, 0, :, :], cs[:, :], S.dsem(), reads=[csb])
            S.dma("sp", tab_d[i, 1, :, :], sn[:, :], S.dsem(), reads=[snb])


def emit_rope_evac(S, p, pb, st_out, stb, ct, sn, tb_, q32, q32b, tA, tAb, tB, tBb, h):
    hi = 32 + h
    if p is not None:
        S.op("act", ACT(q32[:, :], p[:, :], AF.Copy), reads=[pb], writes=[q32b])
    S.op("pool", CP(st_out, q32[:, :]), reads=[q32b], writes=[stb])
    S.op("dve", TT(tA[0:hi, :], q32[0:hi, :], ct[0:hi, :], ALU.mult), reads=[q32b, tb_], writes=[tAb])
    S.op("dve", TT(tB[0:h, :], q32[32:hi, :], sn[32:hi, :], ALU.mult), reads=[q32b, tb_], writes=[tBb])
    S.op("dve", TT(tB[32:hi, :], q32[0:h, :], sn[0:h, :], ALU.mult), reads=[q32b, tb_], writes=[tBb])
    S.op("dve", TT(st_out[0:hi, :], tA[0:hi, :], tB[0:hi, :], ALU.add), reads=[tAb, tBb], writes=[stb])


def halo_off(dils):
    off = [0]
    for d in dils:
        off.append(off[-1] + d)
    return off


def emit_mixA_1(C, hT_d, gain_d, wqkv_r, tab_d, QT_d, KT_d, V_d, KTx_d, Vx_d, dils=A_DIL, H=A_HEADS,
                D=D_MODEL, T=TOK, NT=512, gather=None):
    S = C.S
    KC = D // 128
    HB = H // 4
    NB = T // 128
    off = halo_off(dils)
    with C.phase():
        gain = C.sb([128, KC], F32, "gain")
        gain_b = Buf()
        S.dma("sp", gain[:, :], gain_d[:, :], S.dsem(), writes=[gain_b])
        xn = C.sb([128, KC, T], BF16, "xnf")
        xb = bufs(KC)
        ws = [(C.sb([128, 4 * KC * 128], BF16, "wqkv"), Buf(), S.dsem("wqkv", sw=True)) for _ in range(2)]
        st = Stream(S, "pool", ws)
        for i in range(len(dils) * 3 * HB):
            st.add(lambda tt, i=i: (tt[:, :], wqkv_r[i, :, :], CAST_KW))
        stg = [(C.sb([128, 4, T], BF16, "stg"), Buf(), S.dsem("stg"), S.dsem("stgx")) for _ in range(2)]
        P = [(C.ps([128, NT], F32, "P"), Buf()) for _ in range(2)]
        ct = C.sb([128, T], F32, "ct")
        sn = C.sb([128, T], F32, "sn")
        tb_ = Buf()
        tA = [(C.sb([128, NT], F32, "tA"), Buf()) for _ in range(2)]
        tB = [(C.sb([128, NT], F32, "tB"), Buf()) for _ in range(2)]
        q32s = [(C.sb([128, NT], F32, "q32"), Buf()) for _ in range(2)]
        for (t_, b_) in tB:
            S.op("pool", MEMSET(t_[:, :], 0.0), writes=[b_])
        n = 0
        ns = 0
        wi = 0
        for gi, d in enumerate(dils):
            nb = NB // d
            with contextlib.ExitStack() as nst:
                old = C.stack
                C.stack = nst
                emit_norm_full(C, hT_d, gain, gain_b, xn, xb, KC, T, d=d)
                C.stack = old
                S.barrier(recycle=False)
            tds = S.dsem()
            fin = tds.cnt + 32
            S.dma("sp", ct[:, :], tab_d[gi, 0, :, :], tds, writes=[tb_], group_final=fin)
            S.dma("sp", sn[:, :], tab_d[gi, 1, :, :], tds, writes=[tb_], group_final=fin)
            for s in range(3):
                for hb in range(HB):
                    w, wb, _ = st.get(wi)
                    wi += 1
                    sg, sgb, sds, sxs = stg[ns % 2]
                    ns += 1
                    if s < 2:
                        for hh in range(4):
                            for t in range(T // NT):
                                tsl = slice(t * NT, (t + 1) * NT)
                                p, pb = P[n % 2]
                                a_, ab = tA[n % 2]
                                b2, bb = tB[n % 2]
                                n += 1
                                S.pe_group([MM(p[:, :], w[:, (hh * KC + k) * 128:(hh * KC + k + 1) * 128], xn[:, k, tsl],
                                               k == 0, k == KC - 1) for k in range(KC)], reads=[wb] + xb, writes=[pb])
                                q32, q32b = q32s[n % 2]
                                emit_rope_evac(S, p, pb, sg[:, hh, tsl], sgb, ct[:, tsl], sn[:, tsl], tb_,
                                               q32, q32b, a_, ab, b2, bb, 16)
                        dst = (QT_d if s == 0 else KT_d)[gi, 4 * hb:4 * hb + 4, :, :].rearrange("h p t -> p h t")
                        S.dma("sp", dst, sg[:, :, :], sds, reads=[sgb])
                        if s == 1:
                            fin = sxs.cnt + 16 * d
                            for r in range(d):
                                c0 = (r * nb + nb - 1) * 128
                                S.dma("sp", KTx_d[4 * hb:4 * hb + 4, :, (off[gi] + r) * 128:(off[gi] + r + 1) * 128]
                                      .rearrange("h p c -> p h c"), sg[:, :, c0:c0 + 128], sxs, reads=[sgb], group_final=fin)
                    else:
                        sg4 = sg[:, :, :].rearrange("p h (b c) -> p h b c", c=128)
                        for bi in range(NB):
                            p, pb = P[n % 2]
                            n += 1
                            S.pe_group([MM(p[:, :], xn[:, k, bi * 128:(bi + 1) * 128], w[:, k * 512:(k + 1) * 512],
                                           k == 0, k == KC - 1) for k in range(KC)], reads=[wb] + xb, writes=[pb])
                            src = p[:, :].rearrange("p (h c) -> p h c", h=4)
                            if n % 2 == 0:
                                S.op("act", ACT(sg4[:, :, bi, :], src, AF.Copy), reads=[pb], writes=[sgb])
                            else:
                                S.op("dve", CP(sg4[:, :, bi, :], src), reads=[pb], writes=[sgb])
                        S.dma("sp", V_d[gi, 4 * hb:4 * hb + 4, :, :].rearrange("h p x -> p h x"), sg[:, :, :], sds, reads=[sgb])
                        fin = sxs.cnt + 16 * d
                        for r in range(d):
                            bi = r * nb + nb - 1
                            S.dma("sp", Vx_d[4 * hb:4 * hb + 4, :, off[gi] + r, :].rearrange("h p c -> p h c"),
                                  sg4[:, :, bi, :], sxs, reads=[sgb], group_final=fin)
        if gather is not None:
            S.barrier(recycle=False)
            emit_gather(C, KTx_d, gather[0], H)
            emit_gather(C, Vx_d, gather[1], H)


def emit_mixA_2(C, hT_d, QT_d, KT_d, V_d, KTh_d, Vh_d, hmask_d, mask_d, wo_r, dils=A_DIL, H=A_HEADS,
                D=D_MODEL, T=TOK, xchg=None):
    S = C.S
    K = C.K
    KC = D // 128
    NB = T // 128
    off = halo_off(dils)
    nhmax = max(dils)
    scale = float(A_HEAD_DIM) ** -0.5
    with C.phase():
        hm = C.sb([128, 1], F32, "hmask")
        hmb = Buf()
        S.dma("sp", hm[:, :], hmask_d[:, :], S.dsem(), writes=[hmb])
        mk32 = C.sb([128, 256], F32, "mk32")
        mkb = Buf()
        S.dma("sp", mk32[:, :], mask_d[:, :], S.dsem(), writes=[mkb])
        M = C.sb([128, 256], BF16, "M")
        Mh = C.sb([128, 256], BF16, "Mh")
        Mb, Mhb = Buf(), Buf()
        S.op("dve", CP(M[:, :], mk32[:, :]), reads=[mkb], writes=[Mb])
        S.op("dve", CP(Mh[:, 128:256], mk32[:, 128:256]), reads=[mkb], writes=[Mhb])
        S.op("dve", TS(Mh[:, 0:128], mk32[:, 0:128], hm[:, 0:1], None, ALU.mult), reads=[mkb, hmb], writes=[Mhb])
        oT = C.sb([128, H, T], BF16, "oT")
        oTb = bufs(H)
        slots = []
        for _ in range(2):
            slots.append(dict(q=C.sb([128, T], BF16, "q"), k=C.sb([128, (nhmax + NB) * 128], BF16, "k"),
                              v=C.sb([128, nhmax + NB, 128], BF16, "v"), b=Buf(), ds=S.dsem("qkv")))
        Oa = C.sb([128, T], F32, "Oacc")
        La = C.sb([128, T], F32, "Lacc")
        Oab, Lab = Buf(), Buf()
        pe_ = [(C.sb([128, 256], BF16, "pe"), Buf()) for _ in range(3)]
        pm_ = [(C.sb([128, 256], BF16, "pm"), Buf()) for _ in range(SKEW + 2)]
        pS = [(C.ps([128, 512], F32, "pS"), Buf()) for _ in range(2)]
        pO = [(C.ps([128, 512], F32, "pO"), Buf()) for _ in range(2)]
        pL = [(C.ps([128, 512], F32, "pL"), Buf()) for _ in range(2)]
        items = [(h, gi) for h in range(H) for gi in range(len(dils))]
        hK = bufs(H)
        hV = bufs(H)
        if xchg is not None:
            KTx_d, Vx_d, KTg_d, Vg_d, oh_d, do_coll = xchg
            XH = off[-1] * 128
            gK = bufs(H)
            gV = bufs(H)
            if do_coll:
                cs = S.dsem("cc", sw=True)
                for h in range(H):
                    S.coll("AllGather", GROUPS, KTx_d[h], KTg_d[h], cs, writes=[gK[h]])
                    S.coll("AllGather", GROUPS, Vx_d[h], Vg_d[h], cs, writes=[gV[h]])
            oh = C.sb([128, CPB], F32, "oh")
            ohb = Buf()
            S.dma("sp", oh[:, :], oh_d[:, :], S.dsem(), writes=[ohb])
            cand = (C.sb([128, CPB, XH], BF16, "cand"), Buf(), S.dsem("cand"))
            sacc = (C.sb([128, XH], BF16, "sacc"), Buf(), S.dsem("sacc"))
            selected = set()

            def select(h):
                if h in selected:
                    return
                selected.add(h)
                c, cb, cds = cand
                a, ab, ads = sacc
                for (src, gb_, dst, hb_) in ((KTg_d[h], gK[h], KTh_d[h], hK[h]),
                                              (Vg_d[h].rearrange("s p b c -> s p (b c)"), gV[h],
                                               Vh_d[h].rearrange("p b c -> p (b c)"), hV[h])):
                    S.dma("sp", c[:, :, :], src.rearrange("s p x -> p s x"), cds, reads=[gb_], writes=[cb])
                    S.op("dve", TS(a[:, :], c[:, 0, :], oh[:, 0:1], None, ALU.mult), reads=[cb, ohb], writes=[ab])
                    for s_ in range(1, CPB):
                        S.op("dve", STT(a[:, :], c[:, s_, :], oh[:, s_:s_ + 1], a[:, :], ALU.mult, ALU.add),
                             reads=[cb, ohb, ab], writes=[ab])
                    S.dma("sp", dst, a[:, :], ads, reads=[ab], writes=[hb_])

        def load(i):
            h, gi = items[i]
            if xchg is not None:
                select(h)
            d = dils[gi]
            sl = slots[i % 2]
            fin = sl["ds"].cnt + 16 * 5
            kw = dict(writes=[sl["b"]], group_final=fin)
            S.dma("sp", sl["q"][:, :], QT_d[gi, h, :, :], sl["ds"], **kw)
            S.dma("sp", sl["k"][:, 0:d * 128], KTh_d[h, :, off[gi] * 128:(off[gi] + d) * 128], sl["ds"], reads=[hK[h]], **kw)
            S.dma("sp", sl["k"][:, d * 128:(d + NB) * 128], KT_d[gi, h, :, :], sl["ds"], **kw)
            S.dma("sp", sl["v"][:, 0:d, :], Vh_d[h, :, off[gi]:off[gi] + d, :], sl["ds"], reads=[hV[h]], **kw)
            S.dma("sp", sl["v"][:, d:d + NB, :].rearrange("p b c -> p (b c)"), V_d[gi, h, :, :], sl["ds"], **kw)

        load(0)
        n = 0
        for i, (h, gi) in enumerate(items):
            if i + 1 < len(items):
                load(i + 1)
            d = dils[gi]
            nb = NB // d
            sl = slots[i % 2]
            q, k_, v, slb = sl["q"], sl["k"], sl["v"], sl["b"]
            pend = []

            def emit_pv(st_):
                bi, r, a, prev, cur, m_, mb, k2 = st_
                po, pob = pO[k2 % 2]
                pl, plb = pL[k2 % 2]
                S.pe_group([MM(po[:, 0:128], v[:, prev, :], m_[:, 0:128], True, False),
                            MM(po[:, 0:128], v[:, cur, :], m_[:, 128:256], False, True)],
                           reads=[slb, mb], writes=[pob])
                S.pe_group([MM(pl[:, 0:128], K["ones"][:, :], m_[:, 0:128], True, False),
                            MM(pl[:, 0:128], K["ones"][:, :], m_[:, 128:256], False, True)],
                           reads=[K["ones_b"], mb], writes=[plb])
                if d == 1:
                    oc = Oa[:, bi * 128:(bi + 1) * 128]
                    lc = La[:, bi * 128:(bi + 1) * 128]
                else:
                    oc = Oa[:, :].rearrange("p (a i r) -> p a r i", i=128, r=d)[:, a, r, :]
                    lc = La[:, :].rearrange("p (a i r) -> p a r i", i=128, r=d)[:, a, r, :]
                if gi == 0:
                    S.op("act", ACT(oc, po[:, 0:128], AF.Copy), reads=[pob], writes=[Oab])
                    S.op("dve", CP(lc, pl[:, 0:128]), reads=[plb], writes=[Lab])
                else:
                    S.op("dve", TT(oc, po[:, 0:128], oc, ALU.add), reads=[pob, Oab], writes=[Oab])
                    S.op("dve", TT(lc, pl[:, 0:128], lc, ALU.add), reads=[plb, Lab], writes=[Lab])

            for bi in range(NB):
                r, a = bi // nb, bi % nb
                cur = d + bi
                prev = (d + bi - 1) if a > 0 else r
                msk, mskb = (M, Mb) if a > 0 else (Mh, Mhb)
                ps, psb = pS[n % 2]
                e_, eb = pe_[n % len(pe_)]
                m_, mb = pm_[n % len(pm_)]
                qs = q[:, bi * 128:(bi + 1) * 128]
                S.pe_group([MM(ps[:, 0:128], k_[:, prev * 128:(prev + 1) * 128], qs, True, True),
                            MM(ps[:, 128:256], k_[:, cur * 128:(cur + 1) * 128], qs, True, True)],
                           reads=[slb], writes=[psb])
                S.op("act", ACT(e_[:, :], ps[:, 0:256], AF.Exp, scale=scale), reads=[psb], writes=[eb])
                S.op("pool", TT(m_[:, :], e_[:, :], msk[:, :], ALU.mult), reads=[eb, mskb], writes=[mb])
                pend.append((bi, r, a, prev, cur, m_, mb, n))
                n += 1
                if len(pend) > SKEW:
                    emit_pv(pend.pop(0))
            while pend:
                emit_pv(pend.pop(0))
            if gi == len(dils) - 1:
                S.op("dve", RECIP(La[:, :], La[:, :]), reads=[Lab], writes=[Lab])
                S.op("pool", TT(oT[:, h, :], Oa[:, :], La[:, :], ALU.mult), reads=[Oab, Lab], writes=[oTb[h]])
        emit_oproj(C, oT, oTb, wo_r, hT_d, H, T, KCout=KC)


def emit_mixB_1a(C, hT_d, gain_d, gq_d, gkv_d, winr, tab_d, cqn_d, kvx_d, krx_d,
                 QR=B_Q_RANK, KVR=B_KV_RANK, D=D_MODEL, T=TOK, NT=512, gather=None):
    S = C.S
    KC = D // 128
    NQ = QR // 128
    NKV = KVR // 128
    NCH = NQ + NKV + 1
    with C.phase():
        gain = C.sb([128, KC], F32, "gain")
        gq = C.sb([128, NQ], F32, "gq")
        gkv = C.sb([128, NKV], F32, "gkv")
        gain_b, gqb, gkvb = bufs(3)
        S.dma("sp", gain[:, :], gain_d[:, :], S.dsem(), writes=[gain_b])
        S.dma("sp", gq[:, :], gq_d[:, :], S.dsem(), writes=[gqb])
        S.dma("sp", gkv[:, :], gkv_d[:, :], S.dsem(), writes=[gkvb])
        xn = C.sb([128, KC, T], BF16, "xnf")
        xb = bufs(KC)
        with contextlib.ExitStack() as nst:
            old = C.stack
            C.stack = nst
            emit_norm_full(C, hT_d, gain, gain_b, xn, xb, KC, T)
            C.stack = old
            S.barrier(recycle=False)
        ct = C.sb([128, T], F32, "ct")
        sn = C.sb([128, T], F32, "sn")
        tb_ = Buf()
        tds = S.dsem()
        fin = tds.cnt + 32
        S.dma("sp", ct[:, :], tab_d[0, :, :], tds, writes=[tb_], group_final=fin)
        S.dma("sp", sn[:, :], tab_d[1, :, :], tds, writes=[tb_], group_final=fin)
        ws = [(C.sb([128, KC * 128], BF16, "win"), Buf(), S.dsem("win", sw=True)) for _ in range(3)]
        st = Stream(S, "pool", ws)
        for t in range(T // NT):
            for m in range(NCH):
                st.add(lambda tt, m=m: (tt[:, :], winr[m, :, :], CAST_KW))
        c32 = C.sb([128, NCH, NT], F32, "c32")
        cb = bufs(NCH)
        P = [(C.ps([128, NT], F32, "P"), Buf()) for _ in range(2)]
        Rq = norm_res(C, 1, NT, nslot=0)
        cq_st = [(C.sb([128, NQ, NT], BF16, "cqst"), Buf(), S.dsem("cqst")) for _ in range(2)]
        kv_st = [(C.sb([128, NKV, NT], BF16, "kvst"), Buf(), S.dsem("kvst")) for _ in range(2)]
        kr_st = [(C.sb([128, NT], BF16, "krst"), Buf(), S.dsem("krst")) for _ in range(2)]
        tA = (C.sb([128, NT], F32, "tA"), Buf())
        tB = (C.sb([128, NT], F32, "tB"), Buf())
        n = 0
        for t in range(T // NT):
            tsl = slice(t * NT, (t + 1) * NT)
            for m in range(NCH):
                w, wb, _ = st.get(t * NCH + m)
                p, pb = P[n % 2]
                n += 1
                S.pe_group([MM(p[:, :], w[:, k * 128:(k + 1) * 128], xn[:, k, tsl], k == 0, k == KC - 1)
                            for k in range(KC)], reads=[wb] + xb, writes=[pb])
                S.op("act", ACT(c32[:, m, :], p[:, :], AF.Copy), reads=[pb], writes=[cb[m]])
            cq, cqb, cqd = cq_st[t % 2]
            kv, kvb, kvd = kv_st[t % 2]
            kr, krb, krd = kr_st[t % 2]
            rs, rs_b = emit_rstd(C, Rq, lambda k: c32[:, k, :], cb[0:NQ], NQ, NT, QR)
            for k in range(NQ):
                S.op("dve", STT(cq[:, k, :], c32[:, k, :], gq[:, k:k + 1], rs[:, :], ALU.mult, ALU.mult),
                     reads=[cb[k], rs_b, gqb], writes=[cqb])
            S.dma("sp", cqn_d[:, :, tsl].rearrange("k p n -> p k n"), cq[:, :, :], cqd, reads=[cqb])
            rs, rs_b = emit_rstd(C, Rq, lambda k: c32[:, NQ + k, :], cb[NQ:NQ + NKV], NKV, NT, KVR)
            for k in range(NKV):
                S.op("dve", STT(kv[:, k, :], c32[:, NQ + k, :], gkv[:, k:k + 1], rs[:, :], ALU.mult, ALU.mult),
                     reads=[cb[NQ + k], rs_b, gkvb], writes=[kvb])
            S.dma("sp", kvx_d[:, :, tsl].rearrange("k p n -> p k n"), kv[:, :, :], kvd, reads=[kvb])
            emit_rope_evac(S, None, None, kr[:, :], krb, ct[:, tsl], sn[:, tsl], tb_,
                           c32[:, NCH - 1, :], cb[NCH - 1], tA[0], tA[1], tB[0], tB[1], 32)
            S.dma("sp", krx_d[0][:, tsl], kr[:, :], krd, reads=[krb])
        if gather is not None:
            S.barrier(recycle=False)
            emit_gather(C, kvx_d, gather[0], NKV)
            emit_gather(C, krx_d, gather[1], 1)


def emit_mixB_1b(C, cqn_d, wuq_r, tab_d, QB_d, H=B_HEADS, QR=B_Q_RANK, T=TOK, NT=512):
    S = C.S
    NQ = QR // 128
    with C.phase():
        cq = C.sb([128, NQ, T], BF16, "cqn")
        cqb = Buf()
        S.dma("sp", cq[:, :, :], cqn_d[:, :, :].rearrange("k p n -> p k n"), S.dsem(), writes=[cqb])
        ct = C.sb([128, T], F32, "ct")
        sn = C.sb([128, T], F32, "sn")
        tb_ = Buf()
        tds = S.dsem()
        fin = tds.cnt + 32
        S.dma("sp", ct[:, :], tab_d[0, :, :], tds, writes=[tb_], group_final=fin)
        S.dma("sp", sn[:, :], tab_d[1, :, :], tds, writes=[tb_], group_final=fin)
        ws = [(C.sb([128, 2 * NQ * 128], BF16, "wuq"), Buf(), S.dsem("wuq", sw=True)) for _ in range(2)]
        st = Stream(S, "pool", ws)
        for h in range(H):
            st.add(lambda tt, h=h: (tt[:, :], wuq_r[h, :, :], CAST_KW))
        stg = [(C.sb([128, 2, T], BF16, "qst"), Buf(), S.dsem("qst")) for _ in range(2)]
        P = [(C.ps([128, NT], F32, "P"), Buf()) for _ in range(2)]
        tA = [(C.sb([128, NT], F32, "tA"), Buf()) for _ in range(2)]
        tB = [(C.sb([128, NT], F32, "tB"), Buf()) for _ in range(2)]
        q32s = [(C.sb([128, NT], F32, "q32"), Buf()) for _ in range(2)]
        n = 0
        for h in range(H):
            w, wb, _ = st.get(h)
            sg, sgb, sds = stg[h % 2]
            for t in range(T // NT):
                tsl = slice(t * NT, (t + 1) * NT)
                for part in range(2):
                    p, pb = P[n % 2]
                    a_, ab = tA[n % 2]
                    b2, bb = tB[n % 2]
                    q32, q32b = q32s[n % 2]
                    n += 1
                    S.pe_group([MM(p[:, :], w[:, (part * NQ + k) * 128:(part * NQ + k + 1) * 128], cq[:, k, tsl],
                                   k == 0, k == NQ - 1) for k in range(NQ)], reads=[wb, cqb], writes=[pb])
                    if part == 0:
                        S.op("act", ACT(sg[:, 0, tsl], p[:, :], AF.Copy), reads=[pb], writes=[sgb])
                    else:
                        emit_rope_evac(S, p, pb, sg[:, 1, tsl], sgb, ct[:, tsl], sn[:, tsl], tb_,
                                       q32, q32b, a_, ab, b2, bb, 32)
            S.dma("sp", QB_d[h, :, :, :].rearrange("s p t -> p s t"), sg[:, :, :], sds, reads=[sgb])


def emit_mixB_2(C, QB_d, kvg_d, krg_d, wukv_r, qidx_d, kidx_d, oT_d, H=B_HEADS, KVR=B_KV_RANK,
                T=TOK, SK=SEQ, NT=512, gathered=False):
    S = C.S
    K = C.K
    NKV = KVR // 128
    NKT = SK // 128
    scale = float(B_NOPE + B_ROPE) ** -0.5
    with C.phase():
        lat = C.sb([128, NKV, SK], BF16, "lat")
        kr = C.sb([128, SK], BF16, "kr")
        latb, krb = Buf(), Buf()
        if gathered:
            lds = S.dsem()
            fin = lds.cnt + 16 * NKV
            for k in range(NKV):
                S.dma("sp", lat[:, k, :].rearrange("p (s n) -> p s n", n=T), kvg_d[k].rearrange("s p n -> p s n"), lds,
                      writes=[latb], group_final=fin)
            S.dma("sp", kr[:, :].rearrange("p (s n) -> p s n", n=T), krg_d[0].rearrange("s p n -> p s n"), S.dsem(), writes=[krb])
        else:
            S.dma("sp", lat[:, :, :], kvg_d[:, :, :].rearrange("k p n -> p k n"), S.dsem(), writes=[latb])
            S.dma("sp", kr[:, :], krg_d[:, :], S.dsem(), writes=[krb])
        qi = C.sb([128, T], F32, "qidx")
        ki = C.sb([128, NKT], F32, "kidx")
        qib, kib = Buf(), Buf()
        S.dma("sp", qi[:, :], qidx_d[0:1, :].partition_broadcast(128), S.dsem(), writes=[qib])
        S.dma("sp", ki[:, :], kidx_d[:, :], S.dsem(), writes=[kib])
        ws = [(C.sb([128, 2 * NKV * 128], BF16, "wukv"), Buf(), S.dsem("wukv", sw=True)) for _ in range(2)]
        st = Stream(S, "pool", ws)
        for h in range(H):
            st.add(lambda tt, h=h: (tt[:, :], wukv_r[h, :, :], CAST_KW))
        Kn = C.sb([128, SK], BF16, "Kn")
        Knb = Buf()
        V = C.sb([128, NKT, 128], BF16, "V")
        Vb = Buf()
        qs = [(C.sb([128, 2, T], BF16, "q"), Buf(), S.dsem("q")) for _ in range(2)]
        es = [(C.sb([128, NT], BF16, "e"), Buf()) for _ in range(3)]
        pms = [(C.sb([128, NT], BF16, "pm"), Buf()) for _ in range(SKEW + 2)]
        ost = [(C.sb([128, T], BF16, "ost"), Buf(), S.dsem("ost")) for _ in range(2)]
        rl = (C.sb([128, NT], F32, "rl"), Buf())
        pP = [(C.ps([128, NT], F32, "pP"), Buf()) for _ in range(2)]
        pS = [(C.ps([128, NT], F32, "pS"), Buf()) for _ in range(2)]
        pO = [(C.ps([128, NT], F32, "pO"), Buf()) for _ in range(2)]
        pL = [(C.ps([128, NT], F32, "pL"), Buf()) for _ in range(2)]

        def loadq(h):
            q, qb, qd = qs[h % 2]
            S.dma("sp", q[:, :, :], QB_d[h, :, :, :].rearrange("s p t -> p s t"), qd, writes=[qb])

        loadq(0)
        n = 0
        ne = 0
        for h in range(H):
            if h + 1 < H:
                loadq(h + 1)
            w, wb, _ = st.get(h)
            q, qb, _ = qs[h % 2]
            o_, ob, od = ost[h % 2]
            for tt in range(SK // NT):
                p, pb = pP[n % 2]
                n += 1
                S.pe_group([MM(p[:, :], w[:, k * 128:(k + 1) * 128], lat[:, k, tt * NT:(tt + 1) * NT], k == 0, k == NKV - 1)
                            for k in range(NKV)], reads=[wb, latb], writes=[pb])
                if n % 2 == 0:
                    S.op("act", ACT(Kn[:, tt * NT:(tt + 1) * NT], p[:, :], AF.Copy), reads=[pb], writes=[Knb])
                else:
                    S.op("dve", CP(Kn[:, tt * NT:(tt + 1) * NT], p[:, :]), reads=[pb], writes=[Knb])
            for k4 in range(NKT // 4):
                p, pb = pP[n % 2]
                n += 1
                fns = []
                for j in range(4):
                    kt = k4 * 4 + j
                    for k in range(NKV):
                        fns.append(MM(p[:, j * 128:(j + 1) * 128], lat[:, k, kt * 128:(kt + 1) * 128],
                                      w[:, (NKV + k) * 128:(NKV + k + 1) * 128], k == 0, k == NKV - 1))
                S.pe_group(fns, reads=[wb, latb], writes=[pb])
                dst = V[:, k4 * 4:k4 * 4 + 4, :].rearrange("p b c -> p (b c)")
                if n % 2 == 0:
                    S.op("act", ACT(dst, p[:, :], AF.Copy), reads=[pb], writes=[Vb])
                else:
                    S.op("dve", CP(dst, p[:, :]), reads=[pb], writes=[Vb])
            pend = []

            def emit_pv(st_):
                qg, kt, m_, mb = st_
                po, pob = pO[qg % 2]
                pl, plb = pL[qg % 2]
                qsl = slice(qg * NT, (qg + 1) * NT)
                S.pe_group([MM(po[:, :], V[:, kt, :], m_[:, :], kt == 0, kt == NKT - 1)],
                           reads=[Vb, mb], writes=[pob])
                S.pe_group([MM(pl[:, :], K["ones"][:, :], m_[:, :], kt == 0, kt == NKT - 1)],
                           reads=[K["ones_b"], mb], writes=[plb])
                if kt == NKT - 1:
                    r_, rb = rl
                    S.op("dve", RECIP(r_[:, :], pl[:, :]), reads=[plb], writes=[rb])
                    S.op("dve", TT(o_[:, qsl], po[:, :], r_[:, :], ALU.mult), reads=[pob, rb], writes=[ob])

            for qg in range(T // NT):
                qsl = slice(qg * NT, (qg + 1) * NT)
                for kt in range(NKT):
                    ps, psb = pS[ne % 2]
                    e_, eb = es[ne % len(es)]
                    m_, mb = pms[ne % len(pms)]
                    ne += 1
                    ksl = slice(kt * 128, (kt + 1) * 128)
                    S.pe_group([MM(ps[:, :], Kn[:, ksl], q[:, 0, qsl], True, False),
                                MM(ps[:, :], kr[0:64, ksl], q[0:64, 1, qsl], False, True)],
                               reads=[Knb, krb, qb], writes=[psb])
                    S.op("act", ACT(e_[:, :], ps[:, :], AF.Exp, scale=scale), reads=[psb], writes=[eb])
                    S.op("dve", STT(m_[:, :], qi[:, qsl], ki[:, kt:kt + 1], e_[:, :], ALU.is_ge, ALU.mult),
                         reads=[eb, qib, kib], writes=[mb])
                    pend.append((qg, kt, m_, mb))
                    if len(pend) > SKEW:
                        emit_pv(pend.pop(0))
            while pend:
                emit_pv(pend.pop(0))
            S.dma("sp", oT_d[h, :, :], o_[:, :], od, reads=[ob])


def emit_mixB_3(C, hT_d, oT_d, wo_r, H=B_HEADS, D=D_MODEL, T=TOK):
    S = C.S
    with C.phase():
        oT = C.sb([128, H, T], BF16, "oT")
        ob = Buf()
        S.dma("sp", oT[:, :, :], oT_d[:, :, :].rearrange("h p t -> p h t"), S.dsem(), writes=[ob])
        emit_oproj(C, oT, [ob], wo_r, hT_d, H, T, KCout=D // 128)


A_PERM = np.array(list(range(0, 16)) + list(range(32, 48)) + list(range(16, 32)) + list(range(48, 128)))


def fm_w(W, chunks, group=1):
    Din = W.shape[0]
    KC = Din // 128
    M = len(chunks[0])
    nb = len(chunks) // group
    out = np.empty((nb, 128, group, KC, M), np.float32)
    Wp = np.concatenate([W, np.zeros((Din, 1), W.dtype)], axis=1)
    for b in range(nb):
        for g in range(group):
            cols = np.asarray(chunks[b * group + g])
            out[b, :, g] = Wp[:, cols].reshape(KC, 128, M).transpose(1, 0, 2)
    return out.reshape(nb, 128, group * KC * M)


def tm_w(W, blocks):
    KC = W.shape[0] // 128
    out = np.empty((len(blocks), 128, KC, len(blocks[0])), np.float32)
    for b, cols in enumerate(blocks):
        out[b] = W[:, cols].reshape(KC, 128, len(cols)).transpose(1, 0, 2)
    return out.reshape(len(blocks), 128, -1)


def chunks_of(n0, n):
    return [n0 + np.arange(m * 128, (m + 1) * 128) for m in range(n // 128)]


def gain_l(g):
    return np.ascontiguousarray(np.asarray(g, np.float32).reshape(-1, 128).T)


def lay_w13(w13):
    D, F2 = w13.shape
    DFF = F2 // 2
    KC, NJ = D // 128, DFF // 128
    a = w13.reshape(KC, 128, 2, NJ, 128)
    return np.ascontiguousarray(a.transpose(3, 1, 0, 2, 4)).reshape(NJ, 128, KC * 256)


def lay_w2(w2):
    DFF, D = w2.shape
    KC, NJ = D // 128, DFF // 128
    a = w2.reshape(NJ, 128, KC, 128)
    return np.ascontiguousarray(a.transpose(2, 1, 0, 3)).reshape(KC, 128, NJ * 128)


def lay_sq(w):
    return fm_w(w, chunks_of(0, w.shape[1]))


def lay_wqkv(W, ng, H):
    HB = H // 4
    blks = []
    for g in range(ng):
        for s in range(3):
            for hb in range(HB):
                base = lambda h: ((g * 3 + s) * H + h) * 128
                if s < 2:
                    blks.append(fm_w(W, [base(4 * hb + hh) + A_PERM for hh in range(4)], group=4)[0])
                else:
                    blks.append(tm_w(W, [base(4 * hb) + np.arange(512)])[0])
    return np.stack(blks)


def lay_win_c(w, D):
    ch = []
    for m in range(D // 128):
        for s in range(3):
            ch.append(s * D + np.arange(m * 128, (m + 1) * 128))
    return fm_w(w, ch, group=3)


def _pad128(a):
    return np.concatenate([a, -np.ones(128 - len(a), int)])


def lay_win_b(w, QR, KVR):
    return fm_w(w, chunks_of(0, QR + KVR) + [_pad128(QR + KVR + np.arange(64))])


def lay_wuq(w, H):
    return fm_w(w, sum([[h * 192 + np.arange(128), _pad128(h * 192 + 128 + np.arange(64))] for h in range(H)], []), group=2)


def lay_wukv(w, H):
    return fm_w(w, sum([[h * 256 + np.arange(128), h * 256 + 128 + np.arange(128)] for h in range(H)], []), group=2)


def rope_frq(half, rows_a, rows_b):
    f = (np.float32(ROPE_THETA) ** (-np.arange(half, dtype=np.float32) / np.float32(half))).astype(np.float32)
    o = np.zeros((128, 2), np.float32)
    o[rows_a, 0] = f
    o[rows_a, 1] = 1.0
    o[rows_b, 0] = f
    o[rows_b, 1] = -1.0
    return o


def band_mask():
    k = np.arange(128)[:, None]
    q = np.arange(128)[None, :]
    return np.concatenate([(k >= q), (k <= q)], axis=1).astype(np.float32)


T_ = TOK
KC_ = D_MODEL // 128
NJ_ = D_FF // 128
NQ_ = B_Q_RANK // 128
NKV_ = B_KV_RANK // 128
NHALO = sum(A_DIL)
TAB_DILS = A_DIL + (1,)

SPEC = {
    "ident": ((128, 128), F32), "x": ((T_, D_MODEL), F32), "pos": ((1, T_), I32), "frq": ((4, 128, 2), F32),
    "mask": ((128, 256), F32), "hmask": ((128, 1), F32), "qidx": ((1, T_), F32), "kidx": ((128, SEQ // 128), F32),
    "gf": ((128, KC_), F32), "out": ((T_, D_MODEL), F32),
    "hT": ((KC_, 128, T_), F32), "hT_in": ((KC_, 128, T_), F32), "tab": ((4, 2, 128, T_), F32),
    "QT": ((3, A_HEADS, 128, T_), BF16), "KT": ((3, A_HEADS, 128, T_), BF16), "V": ((3, A_HEADS, 128, T_), BF16),
    "KTx": ((A_HEADS, 128, NHALO * 128), BF16), "Vx": ((A_HEADS, 128, NHALO, 128), BF16),
    "KTh": ((A_HEADS, 128, NHALO * 128), BF16), "Vh": ((A_HEADS, 128, NHALO, 128), BF16),
    "cqn": ((NQ_, 128, T_), BF16), "kvx": ((NKV_, 128, T_), BF16), "krx": ((128, T_), BF16),
    "kvg": ((NKV_, 128, SEQ), BF16), "krg": ((128, SEQ), BF16), "QB": ((B_HEADS, 2, 128, T_), BF16),
    "oT": ((B_HEADS, 128, T_), BF16),
    "cuT": ((KC_, 128, T_), F32), "bT": ((KC_, 128, T_), F32), "cux": ((KC_, 128, 2), F32), "cuh": ((KC_, 128, 2), F32),
}
for _l in range(DEPTH):
    for _s in ("1", "2"):
        SPEC["l%d_g%s" % (_l, _s)] = ((128, KC_), F32)
        SPEC["l%d_w13_%s" % (_l, _s)] = ((NJ_, 128, KC_ * 256), F32)
        SPEC["l%d_w2_%s" % (_l, _s)] = ((KC_, 128, NJ_ * 128), F32)
    SPEC["l%d_gm" % _l] = ((128, KC_), F32)
for _l in (0, 3):
    SPEC["l%d_wqkv" % _l] = ((36, 128, 4 * KC_ * 128), F32)
    SPEC["l%d_wo" % _l] = ((KC_, 128, A_HEADS * 128), F32)
SPEC["l1_gq"] = ((128, NQ_), F32)
SPEC["l1_gkv"] = ((128, NKV_), F32)
SPEC["l1_win"] = ((NQ_ + NKV_ + 1, 128, KC_ * 128), F32)
SPEC["l1_wuq"] = ((B_HEADS, 128, 2 * NQ_ * 128), F32)
SPEC["l1_wukv"] = ((B_HEADS, 128, 2 * NKV_ * 128), F32)
SPEC["l1_wo"] = ((KC_, 128, B_HEADS * 128), F32)
SPEC["l2_win"] = ((KC_, 128, 3 * KC_ * 128), F32)
SPEC["l2_convw"] = ((128, KC_, 3), F32)
SPEC["l2_wout"] = ((KC_, 128, KC_ * 128), F32)


def ffn_names(l, s):
    return ["l%d_g%s" % (l, s), "l%d_w13_%s" % (l, s), "l%d_w2_%s" % (l, s)]


def do_ffn(C, d, l, s):
    emit_ffn(C, d["hT"], d["l%d_g%s" % (l, s)], d["l%d_w13_%s" % (l, s)], d["l%d_w2_%s" % (l, s)])


def do_A1(C, d, l):
    emit_mixA_1(C, d["hT"], d["l%d_gm" % l], d["l%d_wqkv" % l], d["tab"], d["QT"], d["KT"], d["V"], d["KTx"], d["Vx"])


def do_A2(C, d, l):
    emit_mixA_2(C, d["hT"], d["QT"], d["KT"], d["V"], d["KTh"], d["Vh"], d["hmask"], d["mask"], d["l%d_wo" % l])


def do_B1(C, d):
    emit_mixB_1a(C, d["hT"], d["l1_gm"], d["l1_gq"], d["l1_gkv"], d["l1_win"], d["tab"][3], d["cqn"], d["kvx"], d["krx"])
    emit_mixB_1b(C, d["cqn"], d["l1_wuq"], d["tab"][3], d["QB"])


def do_B2(C, d):
    emit_mixB_2(C, d["QB"], d["kvg"], d["krg"], d["l1_wukv"], d["qidx"], d["kidx"], d["oT"])
    emit_mixB_3(C, d["hT"], d["oT"], d["l1_wo"])


def do_tables(C, d):
    emit_tables(C, d["pos"], d["frq"], d["tab"], TAB_DILS)


def L1(C, d):
    do_tables(C, d)
    emit_pin(C, d["x"], d["hT"])
    do_ffn(C, d, 0, "1")
    do_A1(C, d, 0)


def L2(C, d):
    emit_dram_copy(C, d["hT"], d["hT_in"])
    do_A2(C, d, 0)
    do_ffn(C, d, 0, "2")
    do_ffn(C, d, 1, "1")
    do_tables(C, d)
    do_B1(C, d)


def L3(C, d):
    emit_dram_copy(C, d["hT"], d["hT_in"])
    do_B2(C, d)
    do_ffn(C, d, 1, "2")
    do_ffn(C, d, 2, "1")
    emit_mixC_1(C, d["hT"], d["l2_gm"], d["l2_win"], d["cuT"], d["bT"], d["cux"])


def L4(C, d):
    emit_dram_copy(C, d["hT"], d["hT_in"])
    emit_mixC_2(C, d["hT"], d["cuT"], d["bT"], d["cuh"], d["l2_convw"], d["hmask"], d["l2_wout"])
    do_ffn(C, d, 2, "2")
    do_ffn(C, d, 3, "1")
    do_tables(C, d)
    do_A1(C, d, 3)


def L5(C, d):
    emit_dram_copy(C, d["hT"], d["hT_in"])
    do_A2(C, d, 3)
    do_ffn(C, d, 3, "2")
    emit_final(C, d["hT"], d["gf"], d["out"])


LAUNCHES = [
    ("L1", L1, ["ident", "x", "pos", "frq"] + ffn_names(0, "1") + ["l0_gm", "l0_wqkv"],
     ["hT", "QT", "KT", "V", "KTx", "Vx"], ["tab"]),
    ("L2", L2, ["ident", "hT_in", "QT", "KT", "V", "KTh", "Vh", "hmask", "mask", "l0_wo"] + ffn_names(0, "2") + ffn_names(1, "1")
     + ["pos", "frq", "l1_gm", "l1_gq", "l1_gkv", "l1_win", "l1_wuq"],
     ["hT", "kvx", "krx", "QB"], ["tab", "cqn"]),
    ("L3", L3, ["ident", "hT_in", "QB", "kvg", "krg", "l1_wukv", "qidx", "kidx", "l1_wo"] + ffn_names(1, "2") + ffn_names(2, "1")
     + ["l2_gm", "l2_win"],
     ["hT", "cuT", "bT", "cux"], ["oT"]),
    ("L4", L4, ["ident", "hT_in", "cuT", "bT", "cuh", "l2_convw", "hmask", "l2_wout"] + ffn_names(2, "2") + ffn_names(3, "1")
     + ["pos", "frq", "l3_gm", "l3_wqkv"],
     ["hT", "QT", "KT", "V", "KTx", "Vx"], ["tab"]),
    ("L5", L5, ["ident", "hT_in", "QT", "KT", "V", "KTh", "Vh", "hmask", "mask", "l3_wo"] + ffn_names(3, "2") + ["gf"],
     ["out"], ["hT"]),
]

_PROGS = {}


def build_prog(name, fn, ins, outs, scratch):
    if name in _PROGS:
        return _PROGS[name]
    nc = bass.Bass("TRN2", target_bir_lowering=False)
    d = {}
    for n in ins:
        d[n] = nc.dram_tensor(n, list(SPEC[n][0]), SPEC[n][1], kind="ExternalInput").ap()
    for n in outs:
        d[n] = nc.dram_tensor(n, list(SPEC[n][0]), SPEC[n][1], kind="ExternalOutput").ap()
    for n in scratch:
        d[n] = nc.dram_tensor(n, list(SPEC[n][0]), SPEC[n][1]).ap()
    with contextlib.ExitStack() as st:
        C = Ctx(nc, st)
        emit_consts(C, d["ident"])
        fn(C, d)
        C.S.emit()
    _PROGS[name] = nc
    return nc


INPUT_NAMES = (
    "x", "positions",
    "l0_ffn1_norm", "l0_ffn1_w13", "l0_ffn1_w2", "l0_mix_norm", "l0_a_w_qkv", "l0_a_w_o",
    "l0_ffn2_norm", "l0_ffn2_w13", "l0_ffn2_w2",
    "l1_ffn1_norm", "l1_ffn1_w13", "l1_ffn1_w2", "l1_mix_norm", "l1_b_w_in", "l1_b_q_norm",
    "l1_b_w_uq", "l1_b_kv_norm", "l1_b_w_ukv", "l1_b_w_o", "l1_ffn2_norm", "l1_ffn2_w13", "l1_ffn2_w2",
    "l2_ffn1_norm", "l2_ffn1_w13", "l2_ffn1_w2", "l2_mix_norm", "l2_c_w_in", "l2_c_conv_w", "l2_c_w_out",
    "l2_ffn2_norm", "l2_ffn2_w13", "l2_ffn2_w2",
    "l3_ffn1_norm", "l3_ffn1_w13", "l3_ffn1_w2", "l3_mix_norm", "l3_a_w_qkv", "l3_a_w_o",
    "l3_ffn2_norm", "l3_ffn2_w13", "l3_ffn2_w2",
    "final_norm",
)


def host_layout(inp):
    missing = [n for n in INPUT_NAMES if n not in inp]
    assert not missing, missing
    W = {}
    f = lambda k: np.asarray(inp[k], np.float32)
    for l in range(DEPTH):
        p = "l%d_" % l
        for s, nm in (("1", "ffn1"), ("2", "ffn2")):
            W[p + "g" + s] = gain_l(f(p + nm + "_norm"))
            W[p + "w13_" + s] = lay_w13(f(p + nm + "_w13"))
            W[p + "w2_" + s] = lay_w2(f(p + nm + "_w2"))
        W[p + "gm"] = gain_l(f(p + "mix_norm"))
    for l in (0, 3):
        p = "l%d_" % l
        W[p + "wqkv"] = lay_wqkv(f(p + "a_w_qkv"), 3, A_HEADS)
        W[p + "wo"] = lay_sq(f(p + "a_w_o"))
    W["l1_gq"] = gain_l(f("l1_b_q_norm"))
    W["l1_gkv"] = gain_l(f("l1_b_kv_norm"))
    W["l1_win"] = lay_win_b(f("l1_b_w_in"), B_Q_RANK, B_KV_RANK)
    W["l1_wuq"] = lay_wuq(f("l1_b_w_uq"), B_HEADS)
    W["l1_wukv"] = lay_wukv(f("l1_b_w_ukv"), B_HEADS)
    W["l1_wo"] = lay_sq(f("l1_b_w_o"))
    W["l2_win"] = lay_win_c(f("l2_c_w_in"), D_MODEL)
    W["l2_convw"] = np.ascontiguousarray(f("l2_c_conv_w").T.reshape(KC_, 128, 3).transpose(1, 0, 2))
    W["l2_wout"] = lay_sq(f("l2_c_w_out"))
    W["gf"] = gain_l(f("final_norm"))
    W["ident"] = np.eye(128, dtype=np.float32)
    fa = rope_frq(A_ROT // 2, np.arange(0, 16), np.arange(32, 48))
    fb = rope_frq(B_ROPE // 2, np.arange(0, 32), np.arange(32, 64))
    W["frq"] = np.stack([fa, fa, fa, fb])
    W["mask"] = band_mask()
    W["kidx"] = (np.arange(SEQ // 128)[None, :] * 128 + np.arange(128)[:, None]).astype(np.float32)
    x = f("x")
    pos = np.asarray(inp["positions"], np.int32)
    per_core = []
    for c in range(NCORES):
        b, j = c // CPB, c % CPB
        per_core.append({
            "x": np.ascontiguousarray(x[b, j * T_:(j + 1) * T_]),
            "pos": np.ascontiguousarray(pos[b:b + 1, j * T_:(j + 1) * T_]),
            "hmask": np.full((128, 1), float(j > 0), np.float32),
            "qidx": (j * T_ + np.arange(T_, dtype=np.float32))[None],
        })
    return W, per_core


def kernel_unfused(**inp):
    import ml_dtypes
    W, pc = host_layout(inp)
    state = [dict(p) for p in pc]
    out = None
    for (name, fn, ins, outs, scratch) in LAUNCHES:
        nc = build_prog(name, fn, ins, outs, scratch)
        in_maps = []
        for c in range(NCORES):
            im = {}
            for n in ins:
                im[n] = state[c][n] if n in state[c] else W[n]
            in_maps.append(im)
        res = run_bass_kernel_spmd(nc, in_maps, core_ids=list(range(NCORES))).results
        for c in range(NCORES):
            for n in outs:
                state[c]["hT_in" if n == "hT" else n] = np.asarray(res[c][n])
        for c in range(NCORES):
            j = c % CPB
            if "KTx" in outs:
                state[c]["KTh"] = state[c - 1]["KTx"] if j > 0 else np.zeros_like(state[c]["KTx"])
                state[c]["Vh"] = state[c - 1]["Vx"] if j > 0 else np.zeros_like(state[c]["Vx"])
            if "kvx" in outs:
                b0 = (c // CPB) * CPB
                state[c]["kvg"] = np.concatenate([state[b0 + i]["kvx"] for i in range(CPB)], axis=2)
                state[c]["krg"] = np.concatenate([state[b0 + i]["krx"][0] for i in range(CPB)], axis=1)
            if "cux" in outs:
                state[c]["cuh"] = state[c - 1]["cux"] if j > 0 else np.zeros_like(state[c]["cux"])
    out = np.stack([np.concatenate([np.asarray(state[b * CPB + j]["out"]) for j in range(CPB)], axis=0) for b in range(BATCH)])
    return out.astype(np.float32)


def kernel(**inputs):
    return kernel_fused(**inputs)


GROUPS = [list(range(b * CPB, (b + 1) * CPB)) for b in range(BATCH)]


def emit_gather(C, src, dst, n):
    S = C.S
    cs = S.dsem("cc", sw=True)
    for i in range(n):
        S.coll("AllGather", GROUPS, src[i], dst[i], cs)


def emit_select(C, src_g, dst, oh_d, n, X, dt):
    S = C.S
    with C.phase():
        oh = C.sb([128, CPB], F32, "oh")
        ohb = Buf()
        S.dma("sp", oh[:, :], oh_d[:, :], S.dsem(), writes=[ohb])
        cand = [(C.sb([128, CPB, X], dt, "cand"), Buf(), S.dsem("cand")) for _ in range(2)]
        acc = [(C.sb([128, X], F32, "acc"), Buf()) for _ in range(2)]
        outs = [(C.sb([128, X], dt, "sel"), Buf(), S.dsem("sel")) for _ in range(2)]
        for i in range(n):
            c, cb, cds = cand[i % 2]
            a, ab = acc[i % 2]
            o, ob, ods = outs[i % 2]
            S.dma("sp", c[:, :, :], src_g[i].rearrange("s p x -> p s x"), cds, writes=[cb])
            S.op("dve", TS(a[:, :], c[:, 0, :], oh[:, 0:1], None, ALU.mult), reads=[cb, ohb], writes=[ab])
            for s in range(1, CPB):
                last = s == CPB - 1
                S.op("dve", STT(o[:, :] if last else a[:, :], c[:, s, :], oh[:, s:s + 1], a[:, :], ALU.mult, ALU.add),
                     reads=[cb, ohb, ab], writes=[ob] if last else [ab])
            S.dma("sp", dst[i], o[:, :], ods, reads=[ob])


def emit_select_c(C, cug, cuh, oh_d, KC=KC_):
    S = C.S
    X = 2 * KC
    with C.phase():
        oh = C.sb([128, CPB], F32, "oh")
        ohb = Buf()
        S.dma("sp", oh[:, :], oh_d[:, :], S.dsem(), writes=[ohb])
        c = C.sb([128, CPB, X], F32, "cand")
        cb = Buf()
        S.dma("sp", c[:, :, :], cug[:, :, 0:X].rearrange("s p x -> p s x"), S.dsem(), writes=[cb])
        a = C.sb([128, X], F32, "acc")
        ab = Buf()
        S.op("dve", TS(a[:, :], c[:, 0, :], oh[:, 0:1], None, ALU.mult), reads=[cb, ohb], writes=[ab])
        for s in range(1, CPB):
            S.op("dve", STT(a[:, :], c[:, s, :], oh[:, s:s + 1], a[:, :], ALU.mult, ALU.add),
                 reads=[cb, ohb, ab], writes=[ab])
        S.dma("sp", cuh[:, 0:X], a[:, :], S.dsem(), reads=[ab])


def FUSED(C, d):
    def A_layer(l):
        sfx = str(l)
        emit_mixA_1(C, d["hT"], d["l%d_gm" % l], d["l%d_wqkv" % l], d["tab"], d["QT" + sfx], d["KT" + sfx], d["V" + sfx],
                    d["KTx" + sfx], d["Vx" + sfx])
        emit_mixA_2(C, d["hT"], d["QT" + sfx], d["KT" + sfx], d["V" + sfx], d["KTh" + sfx], d["Vh" + sfx], d["hmask"], d["mask"],
                    d["l%d_wo" % l], xchg=(d["KTx" + sfx], d["Vx" + sfx], d["KTg" + sfx], d["Vg" + sfx], d["oh"], True))

    do_tables(C, d)
    emit_pin(C, d["x"], d["hT"])
    do_ffn(C, d, 0, "1")
    A_layer(0)
    do_ffn(C, d, 0, "2")
    do_ffn(C, d, 1, "1")
    emit_mixB_1a(C, d["hT"], d["l1_gm"], d["l1_gq"], d["l1_gkv"], d["l1_win"], d["tab"][3], d["cqn"], d["kvx"], d["krx"],
                 gather=(d["kvgs"], d["krgs"]))
    emit_mixB_1b(C, d["cqn"], d["l1_wuq"], d["tab"][3], d["QB"])
    emit_mixB_2(C, d["QB"], d["kvgs"], d["krgs"], d["l1_wukv"], d["qidx"], d["kidx"], d["oT"], gathered=True)
    emit_mixB_3(C, d["hT"], d["oT"], d["l1_wo"])
    do_ffn(C, d, 1, "2")
    do_ffn(C, d, 2, "1")
    emit_mixC_1(C, d["hT"], d["l2_gm"], d["l2_win"], d["cuT"], d["bT"], d["cux"], gather=d["cug"])
    emit_select_c(C, d["cug"][0], d["cuh"][0], d["oh"])
    emit_mixC_2(C, d["hT"], d["cuT"], d["bT"], d["cuh"], d["l2_convw"], d["hmask"], d["l2_wout"])
    do_ffn(C, d, 2, "2")
    do_ffn(C, d, 3, "1")
    A_layer(3)
    do_ffn(C, d, 3, "2")
    emit_final(C, d["hT"], d["gf"], d["out"])


for _l in ("0", "3"):
    for _n in ("QT", "KT", "V", "KTx", "Vx", "KTh", "Vh"):
        SPEC[_n + _l] = SPEC[_n]
    SPEC["KTg" + _l] = ((A_HEADS, CPB, 128, NHALO * 128), BF16)
    SPEC["Vg" + _l] = ((A_HEADS, CPB, 128, NHALO, 128), BF16)
SPEC["kvgs"] = ((NKV_, CPB, 128, T_), BF16)
SPEC["krgs"] = ((1, CPB, 128, T_), BF16)
SPEC["cux"] = ((1, 128, 512), F32)
SPEC["cug"] = ((1, CPB, 128, 512), F32)
SPEC["cuh"] = ((1, 128, 512), F32)
SPEC["oh"] = ((128, CPB), F32)
SPEC["krx"] = ((1, 128, T_), BF16)

FUSED_INS = (["ident", "x", "pos", "frq", "mask", "hmask", "oh", "qidx", "kidx", "gf"]
             + sum([ffn_names(l, s) for l in range(DEPTH) for s in ("1", "2")], [])
             + ["l%d_gm" % l for l in range(DEPTH)]
             + ["l0_wqkv", "l0_wo", "l3_wqkv", "l3_wo", "l1_gq", "l1_gkv", "l1_win", "l1_wuq", "l1_wukv", "l1_wo",
                "l2_win", "l2_convw", "l2_wout"])
FUSED_SCRATCH = (["hT", "tab", "cqn", "kvx", "krx", "kvgs", "krgs", "QB", "oT", "cuT", "bT", "cux", "cug", "cuh"]
                 + [n + l for l in ("0", "3") for n in ("QT", "KT", "V", "KTx", "Vx", "KTh", "Vh", "KTg", "Vg")])


def kernel_fused(**inp):
    W, pc = host_layout(inp)
    nc = build_prog("FUSED", FUSED, FUSED_INS, ["out"], FUSED_SCRATCH)
    in_maps = []
    for c in range(NCORES):
        j = c % CPB
        oh = np.zeros((128, CPB), np.float32)
        if j > 0:
            oh[:, j - 1] = 1.0
        pcc = dict(pc[c], oh=oh)
        in_maps.append({n: (pcc[n] if n in pcc else W[n]) for n in FUSED_INS})
    res = run_bass_kernel_spmd(nc, in_maps, core_ids=list(range(NCORES))).results
    out = np.stack([np.concatenate([np.asarray(res[b * CPB + j]["out"]) for j in range(CPB)], axis=0) for b in range(BATCH)])
    return out.astype(np.float32)
```

```python
import contextlib
import numpy as np
import concourse.bass as bass
import concourse.mybir as mybir
from concourse.bass_utils import run_bass_kernel_spmd

F32 = mybir.dt.float32
BF16 = mybir.dt.bfloat16
I32 = mybir.dt.int32
AF = mybir.ActivationFunctionType
ALU = mybir.AluOpType

D_MODEL = 2048
BATCH = 2
SEQ = 8192
DEPTH = 4
D_FF = 5632
ROPE_THETA = 500000.0
A_HEADS = 16
A_HEAD_DIM = 128
A_ROT = 32
A_DIL = (1, 4, 16)
B_HEADS = 16
B_Q_RANK = 1536
B_KV_RANK = 512
B_NOPE = 128
B_ROPE = 64
B_V = 128
NCORES = 8
TOK = BATCH * SEQ // NCORES
CPB = NCORES // BATCH
EPS = 1e-6


class Buf:
    __slots__ = ("w", "r")

    def __init__(self):
        self.w = None
        self.r = {}


def bufs(n):
    return [Buf() for _ in range(n)]


class DSem:
    __slots__ = ("sem", "cnt")

    def __init__(self, sem):
        self.sem = sem
        self.cnt = 0


class Sched:
    ENG = ("pe", "act", "dve", "pool", "sp")

    def __init__(self, nc, stack):
        self.nc = nc
        self.stack = stack
        self.q = {e: [] for e in self.ENG}
        self.cur = {}
        self.cnt = {}
        self.known = {e: {} for e in self.ENG}
        self.nsem = 0
        self.ds_all = []
        self.ds_hw = []
        self.ds_sw = []
        self.ds_free = []
        self.ds_free_sw = []
        for e in self.ENG:
            self.cur[e] = self._new_sem("e_" + e)
            self.cnt[e] = 0

    def _new_sem(self, name):
        self.nsem += 1
        return self.stack.enter_context(self.nc.semaphore("s%d_%s" % (self.nsem, name)))

    def dsem(self, name="d", sw=False):
        free = self.ds_free_sw if sw else self.ds_free
        if free:
            return free.pop()
        d = DSem(self._new_sem(name))
        (self.ds_sw if sw else self.ds_hw).append(d)
        self.ds_all.append(d)
        return d

    def _deps(self, reads, writes):
        deps = []
        for b in reads:
            if b.w is not None:
                deps.append(b.w)
        for b in writes:
            if b.w is not None:
                deps.append(b.w)
            deps.extend(b.r.values())
        return deps

    def _waits(self, eng, deps, skip=None):
        kn = self.known[eng]
        best = {}
        for (sem, val) in deps:
            if sem is skip:
                continue
            k = id(sem)
            if kn.get(k, 0) >= val:
                continue
            if k not in best or best[k][1] < val:
                best[k] = (sem, val)
        for k, (sem, val) in best.items():
            kn[k] = val
            self.q[eng].append(("wait", sem, val))

    def _mark(self, tok, reads, writes):
        k = id(tok[0])
        for b in reads:
            if k not in b.r or b.r[k][1] < tok[1]:
                b.r[k] = tok
        for b in writes:
            b.w = tok
            b.r = {}

    def _sig(self, eng):
        self.cnt[eng] += 1
        return (self.cur[eng], self.cnt[eng])

    def op(self, eng, fn, reads=(), writes=()):
        self._waits(eng, self._deps(reads, writes))
        tok = self._sig(eng)
        self.q[eng].append(("op", fn, tok[0]))
        self._mark(tok, reads, writes)
        return tok

    def pe_group(self, fns, reads=(), writes=()):
        self._waits("pe", self._deps(reads, writes), skip=self.cur["pe"])
        tok = self._sig("pe")
        for f in fns[:-1]:
            self.q["pe"].append(("op", f, None))
        self.q["pe"].append(("op", fns[-1], tok[0]))
        self._mark(tok, reads, writes)
        return tok

    def dma(self, queue, out, in_, dsem, reads=(), writes=(), group_final=None, **kw):
        deps = self._deps(reads, writes)
        if group_final is not None:
            deps = [d_ for d_ in deps if not (d_[0] is dsem.sem and d_[1] == group_final)]
        self._waits(queue, deps)
        dsem.cnt += 16
        tok = (dsem.sem, dsem.cnt if group_final is None else group_final)
        self.q[queue].append(("dma", out, in_, dsem.sem, kw))
        self._mark(tok, reads, writes)
        return tok

    def coll(self, kind, groups, src, dst, dsem, reads=(), writes=()):
        self._waits("pool", self._deps(reads, writes))
        dsem.cnt += 1
        tok = (dsem.sem, dsem.cnt)
        self.q["pool"].append(("coll", kind, groups, src, dst, dsem.sem))
        self._mark(tok, reads, writes)
        return tok

    def call(self, eng, fn, reads=(), writes=()):
        self._waits(eng, self._deps(reads, writes))
        self.q[eng].append(("call", fn))

    def wait_tok(self, eng, tok):
        self._waits(eng, [tok])

    def barrier(self, recycle=True):
        for e in self.ENG:
            if e != "sp" and self.cnt[e] > 0:
                self.wait_tok("sp", (self.cur[e], self.cnt[e]))
        for d in self.ds_all:
            if d.cnt > 0:
                self.wait_tok("sp", (d.sem, d.cnt))
        tok = self._sig("sp")
        self.q["sp"].append(("inc", tok[0]))
        for e in self.ENG:
            if e != "sp":
                self.wait_tok(e, tok)
        if recycle:
            self.ds_free = list(self.ds_hw)
            self.ds_free_sw = list(self.ds_sw)

    def emit(self):
        def run(e, items):
            for it in items:
                if it[0] == "wait":
                    e.wait_ge(it[1], it[2])
                elif it[0] == "op":
                    ins = it[1](e)
                    if it[2] is not None:
                        ins.then_inc(it[2], 1)
                elif it[0] == "inc":
                    e.sem_inc(it[1], 1)
                elif it[0] == "call":
                    it[1](e)
                elif it[0] == "coll":
                    _, kind, groups, src, dst, sem = it
                    e.collective_compute(kind, ALU.bypass, replica_groups=groups, ins=[src.opt()], outs=[dst.opt()]).then_inc(sem, 1)
                else:
                    _, out, in_, sem, kw = it
                    if callable(in_):
                        in_ = in_()
                    e.dma_start(out=out, in_=in_, **kw).then_inc(sem, 16)

        with self.nc.Block() as block:
            @block.tensor
            def _(e):
                run(e, self.q["pe"])

            @block.scalar
            def _(e):
                run(e, self.q["act"])

            @block.vector
            def _(e):
                run(e, self.q["dve"])

            @block.gpsimd
            def _(e):
                run(e, self.q["pool"])

            @block.sync
            def _(e):
                run(e, self.q["sp"])


class Stream:
    def __init__(self, S, queue, slots):
        self.S = S
        self.queue = queue
        self.slots = slots
        self.items = []
        self.issued = 0

    def add(self, fn):
        self.items.append(fn)
        return len(self.items) - 1

    def ensure(self, upto):
        upto = min(upto, len(self.items) - 1)
        while self.issued <= upto:
            i = self.issued
            t, buf, ds = self.slots[i % len(self.slots)]
            o, a, kw = self.items[i](t)
            self.S.dma(self.queue, o, a, ds, writes=[buf], **kw)
            self.issued += 1

    def get(self, i):
        self.ensure(i + len(self.slots) - 1)
        return self.slots[i % len(self.slots)]


class Ctx:
    def __init__(self, nc, gstack):
        self.nc = nc
        self.gstack = gstack
        self.stack = gstack
        self.S = Sched(nc, gstack)
        self.n = 0
        self.K = {}

    def sb(self, shape, dt, name="sb"):
        self.n += 1
        return self.stack.enter_context(self.nc.sbuf_tensor("%s_%d" % (name, self.n), list(shape), dt))

    def ps(self, shape, dt=F32, name="ps"):
        self.n += 1
        return self.stack.enter_context(self.nc.psum_tensor("%s_%d" % (name, self.n), list(shape), dt))

    @contextlib.contextmanager
    def phase(self):
        with contextlib.ExitStack() as st:
            self.stack = st
            yield
            self.S.barrier()
        self.stack = self.gstack


def MM(out, lhsT, rhs, start, stop):
    return lambda e: e.matmul(out, lhsT, rhs, start=start, stop=stop)


def ACT(out, in_, func, **kw):
    return lambda e: e.activation(out=out, in_=in_, func=func, **kw)


def TT(out, in0, in1, op):
    return lambda e: e.tensor_tensor(out=out, in0=in0, in1=in1, op=op)


def TS(out, in0, s1, s2, op0, op1=None):
    if op1 is None:
        return lambda e: e.tensor_scalar(out=out, in0=in0, scalar1=s1, scalar2=None, op0=op0)
    return lambda e: e.tensor_scalar(out=out, in0=in0, scalar1=s1, scalar2=s2, op0=op0, op1=op1)


def STT(out, in0, scalar, in1, op0, op1):
    return lambda e: e.scalar_tensor_tensor(out=out, in0=in0, scalar=scalar, in1=in1, op0=op0, op1=op1)


def CP(out, in_):
    return lambda e: e.tensor_copy(out=out, in_=in_)


def MEMSET(ap, v):
    return lambda e: e.memset(ap, v)


def RECIP(out, in_):
    return lambda e: e.reciprocal(out, in_)


CAST_KW = dict(max_dma_last_dim=4096)
SKEW = 2


def emit_consts(C, ident_d):
    S = C.S
    K = C.K
    K["ones"] = C.sb([128, 128], BF16, "ones")
    K["ones_b"] = Buf()
    S.op("pool", MEMSET(K["ones"][:, :], 1.0), writes=[K["ones_b"]])
    K["eps"] = C.sb([128, 1], F32, "eps")
    K["eps_b"] = Buf()
    S.op("pool", MEMSET(K["eps"][:, :], EPS), writes=[K["eps_b"]])
    K["ident"] = C.sb([128, 128], F32, "ident")
    K["ident_b"] = Buf()
    ds = S.dsem()
    S.dma("sp", K["ident"][:, :], ident_d[:, :], ds, writes=[K["ident_b"]])


def norm_res(C, KC, NT, nslot=2):
    S = C.S
    R = {}
    R["h"] = [(C.sb([128, KC, NT], F32, "h"), bufs(KC), S.dsem("h")) for _ in range(nslot)]
    R["ssum"] = (C.ps([128, NT], F32, "ssum"), Buf())
    R["sq"] = [(C.sb([128, NT], BF16, "sq"), Buf()) for _ in range(3)]
    R["rs"] = (C.sb([128, NT], F32, "rs"), Buf())
    return R


def emit_rstd(C, R, src, srcb, KC, NT, Dn):
    S = C.S
    K = C.K
    ssum, ssum_b = R["ssum"]
    for k in range(KC):
        sq, sq_b = R["sq"][k % len(R["sq"])]
        S.op("act", ACT(sq[:, :NT], src(k), AF.Square), reads=[srcb[k]], writes=[sq_b])
        S.pe_group([MM(ssum[:, :NT], K["ones"][:, :], sq[:, :NT], k == 0, k == KC - 1)],
                   reads=[sq_b, K["ones_b"]], writes=[ssum_b])
    rs, rs_b = R["rs"]
    S.op("act", ACT(rs[:, :NT], ssum[:, :NT], AF.Sqrt, scale=1.0 / Dn, bias=K["eps"][:, :]),
         reads=[ssum_b, K["eps_b"]], writes=[rs_b])
    S.op("dve", RECIP(rs[:, :NT], rs[:, :NT]), reads=[rs_b], writes=[rs_b])
    return rs, rs_b


def emit_norm_full(C, hT_d, gain, gain_b, xn, xnb, KC, T, d=1, NT=256):
    S = C.S
    R = norm_res(C, KC, NT)
    for t in range(T // NT):
        h, hb, hds = R["h"][t % 2]
        tsl = slice(t * NT, (t + 1) * NT)
        S.dma("sp", h[:, :, :], hT_d[:, :, tsl].rearrange("k p n -> p k n"), hds, writes=hb)
        rs, rs_b = emit_rstd(C, R, lambda k, h=h: h[:, k, :], hb, KC, NT, KC * 128)
        for k in range(KC):
            if d == 1:
                o = xn[:, k, tsl]
                i0 = h[:, k, :]
                i1 = rs[:, :]
            else:
                m0 = t * NT // d
                o = xn[:, k, :].rearrange("p (r m) -> p m r", r=d)[:, m0:m0 + NT // d, :]
                i0 = h[:, k, :].rearrange("p (m r) -> p m r", r=d)
                i1 = rs[:, :].rearrange("p (m r) -> p m r", r=d)
            S.op("dve", STT(o, i0, gain[:, k:k + 1], i1, ALU.mult, ALU.mult),
                 reads=[hb[k], rs_b, gain_b], writes=[xnb[k]])


def emit_ffn(C, hT_d, gain_d, w13r, w2r, D=D_MODEL, DFF=D_FF, T=TOK, NT=512):
    S = C.S
    K = C.K
    KC = D // 128
    NJ = DFF // 128
    NTILE = T // NT
    with C.phase():
        w13s = [(C.sb([128, KC * 256], BF16, "w13"), Buf(), S.dsem("w13", sw=True)) for _ in range(2)]
        w2s = [(C.sb([128, NJ * 128], BF16, "w2"), Buf(), S.dsem("w2", sw=True)) for _ in range(2)]
        gain = C.sb([128, KC], F32, "gain")
        gain_b = Buf()
        S.dma("sp", gain[:, :], gain_d[:, :], S.dsem(), writes=[gain_b])
        R = norm_res(C, KC, NT)
        hst = [S.dsem("hst") for _ in range(2)]
        xns = [(C.sb([128, KC, NT], BF16, "xn"), bufs(KC)) for _ in range(2)]
        hid = C.sb([128, NJ, NT], BF16, "hid")
        hidb = bufs(NJ)
        G = [(C.ps([128, NT], F32, "G"), Buf()) for _ in range(2)]
        U = [(C.ps([128, NT], F32, "U"), Buf()) for _ in range(2)]
        Y = [(C.ps([128, NT], F32, "Y"), Buf()) for _ in range(2)]
        sgs = [(C.sb([128, NT], F32, "sg"), Buf()) for _ in range(2)]

        st13 = Stream(S, "pool", w13s)
        st2 = Stream(S, "pool", w2s)
        for t in range(NTILE):
            for j in range(NJ):
                st13.add(lambda tt, j=j: (tt[:, :], w13r[j, :, :], CAST_KW))
            for m in range(KC):
                st2.add(lambda tt, m=m: (tt[:, :], w2r[m, :, :], CAST_KW))

        def load_h(t):
            h, hb, hds = R["h"][t % 2]
            S.dma("sp", h[:, :, :], hT_d[:, :, t * NT:(t + 1) * NT].rearrange("k p n -> p k n"), hds, writes=hb)

        def do_norm(t):
            h, hb, hds = R["h"][t % 2]
            xn, xb = xns[t % 2]
            rs, rs_b = emit_rstd(C, R, lambda k, h=h: h[:, k, :], hb, KC, NT, D)
            for k in range(KC):
                S.op("dve", STT(xn[:, k, :], h[:, k, :], gain[:, k:k + 1], rs[:, :], ALU.mult, ALU.mult),
                     reads=[hb[k], rs_b, gain_b], writes=[xb[k]])

        load_h(0)
        st13.ensure(1)
        do_norm(0)
        for t in range(NTILE):
            hs = t % 2
            h, hb, hds = R["h"][hs]
            xn, xb = xns[t % 2]
            tsl = slice(t * NT, (t + 1) * NT)
            if t + 1 < NTILE:
                load_h(t + 1)
            st13.ensure(t * NJ + 1)
            for j in range(NJ):
                w, wb, _ = st13.get(t * NJ + j)
                if j == NJ - 4:
                    st2.ensure(t * KC + 1)
                g_, gb = G[j % 2]
                u_, ub = U[j % 2]
                S.pe_group([MM(g_[:, :], w[:, (2 * k) * 128:(2 * k + 1) * 128], xn[:, k, :], k == 0, k == KC - 1)
                            for k in range(KC)], reads=[wb] + xb, writes=[gb])
                S.pe_group([MM(u_[:, :], w[:, (2 * k + 1) * 128:(2 * k + 2) * 128], xn[:, k, :], k == 0, k == KC - 1)
                            for k in range(KC)], reads=[wb] + xb, writes=[ub])
                sg, sgb = sgs[j % 2]
                S.op("act", ACT(sg[:, :], g_[:, :], AF.Silu), reads=[gb], writes=[sgb])
                S.op("dve", TT(hid[:, j, :], sg[:, :], u_[:, :], ALU.mult), reads=[sgb, ub], writes=[hidb[j]])
            for m in range(KC):
                w, wb, _ = st2.get(t * KC + m)
                if m == 2 and t + 1 < NTILE:
                    do_norm(t + 1)
                if m == KC - 2 and t + 1 < NTILE:
                    st13.ensure((t + 1) * NJ + 1)
                y_, yb = Y[m % 2]
                S.pe_group([MM(y_[:, :], w[:, j * 128:(j + 1) * 128], hid[:, j, :], j == 0, j == NJ - 1)
                            for j in range(NJ)], reads=[wb] + hidb, writes=[yb])
                S.op("dve", STT(h[:, m, :], y_[:, :], 0.5, h[:, m, :], ALU.mult, ALU.add),
                     reads=[yb, hb[m]], writes=[hb[m]])
                if m == 0:
                    fin = hst[hs].cnt + 16 * KC
                S.dma("sp", hT_d[m, :, tsl], h[:, m, :], hst[hs], reads=[hb[m]], group_final=fin)


def emit_pin(C, x_d, hT_d, D=D_MODEL, T=TOK):
    S = C.S
    K = C.K
    KC = D // 128
    with C.phase():
        xs = [(C.sb([128, D], F32, "x"), Buf(), S.dsem("x")) for _ in range(8)]
        hst = [(C.sb([128, KC, 512], F32, "hst"), bufs(KC), S.dsem("hst")) for _ in range(2)]
        ps = [(C.ps([128, 512], F32, "tp"), Buf()) for _ in range(2)]
        n = 0
        for grp in range(T // 512):
            xt = []
            for i in range(4):
                x, xb, xds = xs[(grp % 2) * 4 + i]
                t0 = grp * 512 + i * 128
                S.dma("sp", x[:, :], x_d[t0:t0 + 128, :], xds, writes=[xb])
                xt.append((x, xb))
            hs, hsb, hds = hst[grp % 2]
            for k in range(KC):
                p, pb = ps[n % 2]
                S.pe_group([lambda e, p=p, x=xt[i][0], i=i, k=k: e.transpose(
                    p[:, i * 128:(i + 1) * 128], x[:, k * 128:(k + 1) * 128], K["ident"][:, :]) for i in range(4)],
                    reads=[b for (_, b) in xt] + [K["ident_b"]], writes=[pb])
                eng = "act" if n % 2 == 0 else "dve"
                if eng == "act":
                    S.op("act", ACT(hs[:, k, :], p[:, :], AF.Copy), reads=[pb], writes=[hsb[k]])
                else:
                    S.op("dve", CP(hs[:, k, :], p[:, :]), reads=[pb], writes=[hsb[k]])
                n += 1
            S.dma("sp", hT_d[:, :, grp * 512:(grp + 1) * 512].rearrange("k p n -> p k n"), hs[:, :, :], hds, reads=hsb)


def emit_final(C, hT_d, gain_d, out_d, D=D_MODEL, T=TOK, NT=512):
    S = C.S
    K = C.K
    KC = D // 128
    with C.phase():
        gain = C.sb([128, KC], F32, "gain")
        gain_b = Buf()
        S.dma("sp", gain[:, :], gain_d[:, :], S.dsem(), writes=[gain_b])
        R = norm_res(C, KC, NT)
        xn = C.sb([128, KC, NT], F32, "xn32")
        xb = bufs(KC)
        osb = [(C.sb([128, D], F32, "osb"), Buf(), S.dsem("o")) for _ in range(2)]
        ps = [(C.ps([128, 512], F32, "tp"), Buf()) for _ in range(2)]
        n = 0
        no = 0
        last = []
        for t in range(T // NT):
            h, hb, hds = R["h"][t % 2]
            tsl = slice(t * NT, (t + 1) * NT)
            S.dma("sp", h[:, :, :], hT_d[:, :, tsl].rearrange("k p n -> p k n"), hds, writes=hb)
            rs, rs_b = emit_rstd(C, R, lambda k, h=h: h[:, k, :], hb, KC, NT, D)
            for k in range(KC):
                S.op("dve", STT(xn[:, k, :], h[:, k, :], gain[:, k:k + 1], rs[:, :], ALU.mult, ALU.mult),
                     reads=[hb[k], rs_b, gain_b], writes=[xb[k]])
            for i in range(NT // 128):
                o, ob, ods = osb[no % 2]
                no += 1
                for k0 in range(0, KC, 4):
                    p, pb = ps[n % 2]
                    S.pe_group([lambda e, p=p, kk=kk, k0=k0, i=i: e.transpose(
                        p[:, kk * 128:(kk + 1) * 128], xn[:, k0 + kk, i * 128:(i + 1) * 128], K["ident"][:, :])
                        for kk in range(4)], reads=xb[k0:k0 + 4] + [K["ident_b"]], writes=[pb])
                    if n % 2 == 0:
                        S.op("act", ACT(o[:, k0 * 128:(k0 + 4) * 128], p[:, :], AF.Copy), reads=[pb], writes=[ob])
                    else:
                        S.op("dve", CP(o[:, k0 * 128:(k0 + 4) * 128], p[:, :]), reads=[pb], writes=[ob])
                    n += 1
                t0 = t * NT + i * 128
                last.append(S.dma("sp", out_d[t0:t0 + 128, :], o[:, :], ods, reads=[ob]))
        for tk in last[-2:]:
            S.wait_tok("sp", tk)


def emit_oproj(C, g, gb, wr_d, hT_d, KCin, T=TOK, KCout=D_MODEL // 128, NT=512, scale=1.0):
    S = C.S
    ws = [(C.sb([128, KCin * 128], BF16, "wo"), Buf(), S.dsem("wo", sw=True)) for _ in range(2)]
    hr = [(C.sb([128, T], F32, "hrow"), Buf(), S.dsem("hr"), S.dsem("hrs")) for _ in range(2)]
    Y = [(C.ps([128, NT], F32, "Y"), Buf()) for _ in range(2)]
    st = Stream(S, "pool", ws)
    for m in range(KCout):
        st.add(lambda tt, m=m: (tt[:, :], wr_d[m, :, :], CAST_KW))
    n = 0
    for m in range(KCout):
        w, wb, _ = st.get(m)
        h, hb, hds, hss = hr[m % 2]
        S.dma("sp", h[:, :], hT_d[m, :, :], hds, writes=[hb])
        for t in range(T // NT):
            y_, yb = Y[n % 2]
            n += 1
            tsl = slice(t * NT, (t + 1) * NT)
            S.pe_group([MM(y_[:, :], w[:, k * 128:(k + 1) * 128], g[:, k, tsl], k == 0, k == KCin - 1)
                        for k in range(KCin)], reads=[wb] + list(gb), writes=[yb])
            S.op("dve", STT(h[:, tsl], y_[:, :], scale, h[:, tsl], ALU.mult, ALU.add), reads=[yb, hb], writes=[hb])
        S.dma("sp", hT_d[m, :, :], h[:, :], hss, reads=[hb])


def emit_mixC_1(C, hT_d, gain_d, winr, cuT_d, bT_d, cux_d, D=D_MODEL, T=TOK, NT=512, gather=None):
    S = C.S
    KC = D // 128
    with C.phase():
        gain = C.sb([128, KC], F32, "gain")
        gain_b = Buf()
        S.dma("sp", gain[:, :], gain_d[:, :], S.dsem(), writes=[gain_b])
        xn = C.sb([128, KC, T], BF16, "xnf")
        xb = bufs(KC)
        emit_norm_full(C, hT_d, gain, gain_b, xn, xb, KC, T)
        ws = [(C.sb([128, 3 * KC * 128], BF16, "win"), Buf(), S.dsem("win", sw=True)) for _ in range(2)]
        st = Stream(S, "pool", ws)
        for m in range(KC):
            st.add(lambda tt, m=m: (tt[:, :], winr[m, :, :], CAST_KW))
        P = [[(C.ps([128, NT], F32, "P"), Buf()) for _ in range(2)] for _ in range(3)]
        csb = [(C.sb([128, NT], F32, "c"), Buf()) for _ in range(2)]
        cur = [(C.sb([128, T], F32, "cu"), Buf(), S.dsem("cu")) for _ in range(2)]
        br = [(C.sb([128, T], F32, "b"), Buf(), S.dsem("b")) for _ in range(2)]
        xds = [S.dsem("cux") for _ in range(2)]
        n = 0
        for m in range(KC):
            w, wb, _ = st.get(m)
            cu, cub, cuds = cur[m % 2]
            b_, bb, bds = br[m % 2]
            for t in range(T // NT):
                tsl = slice(t * NT, (t + 1) * NT)
                pp = [P[s][n % 2] for s in range(3)]
                for s in range(3):
                    S.pe_group([MM(pp[s][0][:, :], w[:, (s * KC + k) * 128:(s * KC + k + 1) * 128], xn[:, k, tsl],
                                   k == 0, k == KC - 1) for k in range(KC)], reads=[wb] + xb, writes=[pp[s][1]])
                c_, cb = csb[n % 2]
                n += 1
                S.op("act", ACT(b_[:, tsl], pp[0][0][:, :], AF.Copy), reads=[pp[0][1]], writes=[bb])
                S.op("act", ACT(c_[:, :], pp[1][0][:, :], AF.Copy), reads=[pp[1][1]], writes=[cb])
                S.op("dve", TT(cu[:, tsl], c_[:, :], pp[2][0][:, :], ALU.mult), reads=[cb, pp[2][1]], writes=[cub])
            S.dma("sp", cuT_d[m, :, :], cu[:, :], cuds, reads=[cub])
            S.dma("sp", bT_d[m, :, :], b_[:, :], bds, reads=[bb])
            S.dma("sp", cux_d[0][:, 2 * m:2 * m + 2], cu[:, T - 2:T], xds[m % 2], reads=[cub])
        if gather is not None:
            S.barrier(recycle=False)
            emit_gather(C, cux_d, gather, 1)


def emit_mixC_2(C, hT_d, cuT_d, bT_d, cuh_d, convw_d, hmask_d, woutr, D=D_MODEL, T=TOK):
    S = C.S
    KC = D // 128
    with C.phase():
        cw = C.sb([128, KC, 3], F32, "convw")
        cwb = Buf()
        S.dma("sp", cw[:, :, :], convw_d[:, :, :], S.dsem(), writes=[cwb])
        hm = C.sb([128, 1], F32, "hmask")
        hmb = Buf()
        S.dma("sp", hm[:, :], hmask_d[:, :], S.dsem(), writes=[hmb])
        g = C.sb([128, KC, T], BF16, "g")
        gb = bufs(KC)
        cur = [(C.sb([128, T + 2], F32, "cu"), Buf(), Buf(), S.dsem("cu"), S.dsem("cuh")) for _ in range(2)]
        br = [(C.sb([128, T], F32, "b"), Buf(), S.dsem("b")) for _ in range(2)]
        zs = [(C.sb([128, T], F32, "z"), Buf()) for _ in range(2)]
        for m in range(KC):
            cu, cub, chb, cuds, chds = cur[m % 2]
            b_, bb, bds = br[m % 2]
            z, zb = zs[m % 2]
            S.dma("sp", cu[:, 2:T + 2], cuT_d[m, :, :], cuds, writes=[cub])
            S.dma("sp", cu[:, 0:2], cuh_d[0][:, 2 * m:2 * m + 2], chds, writes=[chb])
            S.dma("sp", b_[:, :], bT_d[m, :, :], bds, writes=[bb])
            S.op("dve", TS(cu[:, 0:2], cu[:, 0:2], hm[:, 0:1], None, ALU.mult), reads=[hmb], writes=[chb])
            S.op("dve", TS(z[:, :], cu[:, 2:T + 2], cw[:, m, 2:3], None, ALU.mult), reads=[cub, cwb], writes=[zb])
            S.op("dve", STT(z[:, :], cu[:, 1:T + 1], cw[:, m, 1:2], z[:, :], ALU.mult, ALU.add),
                 reads=[cub, chb, cwb, zb], writes=[zb])
            S.op("dve", STT(z[:, :], cu[:, 0:T], cw[:, m, 0:1], z[:, :], ALU.mult, ALU.add),
                 reads=[cub, chb, cwb, zb], writes=[zb])
            S.op("pool", TT(g[:, m, :], z[:, :], b_[:, :], ALU.mult), reads=[zb, bb], writes=[gb[m]])
        emit_oproj(C, g, gb, woutr, hT_d, KC, T, KCout=KC)


def emit_dram_copy(C, dst, src):
    S = C.S
    ds = S.dsem("cp")
    fin = ds.cnt + 16 * dst.shape[0]
    for k in range(dst.shape[0]):
        S.dma("sp", dst[k, :, :], src[k, :, :], ds, group_final=fin)
    S.barrier()


TWO_PI = 6.283185307179586
CW1 = 6.28125
CW2 = float(np.float32(np.frombuffer(np.uint32(np.frombuffer(np.float32(TWO_PI - CW1).tobytes(), np.uint32)[0] & 0xFFFFF000).tobytes(), np.float32)[0]))
CW3 = float(np.float32(TWO_PI - CW1 - CW2))
PI_LO = 3.1415925


def _wrap(S, r, rb, m, mb):
    S.op("dve", TS(m, r, float(np.pi), -TWO_PI, ALU.is_gt, ALU.mult), reads=[rb], writes=[mb])
    S.op("dve", TT(r, r, m, ALU.add), reads=[rb, mb], writes=[rb])
    S.op("dve", TS(m, r, -float(np.pi), TWO_PI, ALU.is_lt, ALU.mult), reads=[rb], writes=[mb])
    S.op("dve", TT(r, r, m, ALU.add), reads=[rb, mb], writes=[rb])
    S.op("dve", TS(r, r, PI_LO, -PI_LO, ALU.min, ALU.max), reads=[rb], writes=[rb])


def emit_tables(C, pos_d, frq_d, tab_d, dils, T=TOK):
    S = C.S
    with C.phase():
        pos_i = C.sb([128, T], I32, "posi")
        pos_f = C.sb([128, T], F32, "posf")
        pb = Buf()
        S.dma("sp", pos_i[:, :], pos_d[0:1, :].partition_broadcast(128), S.dsem(), writes=[pb])
        S.op("dve", CP(pos_f[:, :], pos_i[:, :]), reads=[pb], writes=[pb])
        x = C.sb([128, T], F32, "ang")
        r = C.sb([128, T], F32, "red")
        m = C.sb([128, T], F32, "msk")
        kf = C.sb([128, T], F32, "kf")
        ki = C.sb([128, T], I32, "ki")
        sn = C.sb([128, T], F32, "sin")
        cs = C.sb([128, T], F32, "cos")
        xb, rb, mb, kb, snb, csb = bufs(6)
        for i, d in enumerate(dils):
            fr = C.sb([128, 2], F32, "frq")
            fb = Buf()
            S.dma("sp", fr[:, :], frq_d[i, :, :], S.dsem(), writes=[fb])
            if d == 1:
                S.op("dve", TS(x[:, :], pos_f[:, :], fr[:, 0:1], None, ALU.mult), reads=[pb, fb], writes=[xb])
            else:
                S.op("dve", TS(x[:, :].rearrange("p (r m) -> p m r", r=d), pos_f[:, :].rearrange("p (m r) -> p m r", r=d),
                               fr[:, 0:1], None, ALU.mult), reads=[pb, fb], writes=[xb])
            S.op("dve", TS(ki[:, :], x[:, :], 1.0 / TWO_PI, None, ALU.mult), reads=[xb], writes=[kb])
            S.op("dve", CP(kf[:, :], ki[:, :]), reads=[kb], writes=[kb])
            S.op("dve", STT(r[:, :], kf[:, :], -CW1, x[:, :], ALU.mult, ALU.add), reads=[kb, xb], writes=[rb])
            S.op("dve", STT(r[:, :], kf[:, :], -CW2, r[:, :], ALU.mult, ALU.add), reads=[kb, rb], writes=[rb])
            S.op("dve", STT(r[:, :], kf[:, :], -CW3, r[:, :], ALU.mult, ALU.add), reads=[kb, rb], writes=[rb])
            _wrap(S, r[:, :], rb, m[:, :], mb)
            S.op("act", ACT(sn[:, :], r[:, :], AF.Sin), reads=[rb], writes=[snb])
            S.op("dve", TS(sn[:, :], sn[:, :], fr[:, 1:2], None, ALU.mult), reads=[snb, fb], writes=[snb])
            S.op("dve", TS(r[:, :], r[:, :], float(np.pi / 2), None, ALU.add), reads=[rb, snb], writes=[rb])
            _wrap(S, r[:, :], rb, m[:, :], mb)
            S.op("act", ACT(cs[:, :], r[:, :], AF.Sin), reads=[rb], writes=[csb])
            S.dma("sp", tab_d[i, 0, :, :], cs[:, :], S.dsem(), reads=[csb])
            S.dma("sp", tab_d[i, 1, :, :], sn[:, :], S.dsem(), reads=[snb])


def emit_rope_evac(S, p, pb, st_out, stb, ct, sn, tb_, q32, q32b, tA, tAb, tB, tBb, h):
    hi = 32 + h
    if p is not None:
        S.op("act", ACT(q32[:, :], p[:, :], AF.Copy), reads=[pb], writes=[q32b])
    S.op("pool", CP(st_out, q32[:, :]), reads=[q32b], writes=[stb])
    S.op("dve", TT(tA[0:hi, :], q32[0:hi, :], ct[0:hi, :], ALU.mult), reads=[q32b, tb_], writes=[tAb])
    S.op("dve", TT(tB[0:h, :], q32[32:hi, :], sn[32:hi, :], ALU.mult), reads=[q32b, tb_], writes=[tBb])
    S.op("dve", TT(tB[32:hi, :], q32[0:h, :], sn[0:h, :], ALU.mult), reads=[q32b, tb_], writes=[tBb])
    S.op("dve", TT(st_out[0:hi, :], tA[0:hi, :], tB[0:hi, :], ALU.add), reads=[tAb, tBb], writes=[stb])


def halo_off(dils):
    off = [0]
    for d in dils:
        off.append(off[-1] + d)
    return off


def emit_mixA_1(C, hT_d, gain_d, wqkv_r, tab_d, QT_d, KT_d, V_d, KTx_d, Vx_d, dils=A_DIL, H=A_HEADS,
                D=D_MODEL, T=TOK, NT=512, gather=None):
    S = C.S
    KC = D // 128
    HB = H // 4
    NB = T // 128
    off = halo_off(dils)
    with C.phase():
        gain = C.sb([128, KC], F32, "gain")
        gain_b = Buf()
        S.dma("sp", gain[:, :], gain_d[:, :], S.dsem(), writes=[gain_b])
        xn = C.sb([128, KC, T], BF16, "xnf")
        xb = bufs(KC)
        ws = [(C.sb([128, 4 * KC * 128], BF16, "wqkv"), Buf(), S.dsem("wqkv", sw=True)) for _ in range(2)]
        st = Stream(S, "pool", ws)
        for i in range(len(dils) * 3 * HB):
            st.add(lambda tt, i=i: (tt[:, :], wqkv_r[i, :, :], CAST_KW))
        stg = [(C.sb([128, 4, T], BF16, "stg"), Buf(), S.dsem("stg"), S.dsem("stgx")) for _ in range(2)]
        P = [(C.ps([128, NT], F32, "P"), Buf()) for _ in range(2)]
        ct = C.sb([128, T], F32, "ct")
        sn = C.sb([128, T], F32, "sn")
        tb_ = Buf()
        tA = [(C.sb([128, NT], F32, "tA"), Buf()) for _ in range(2)]
        tB = [(C.sb([128, NT], F32, "tB"), Buf()) for _ in range(2)]
        q32s = [(C.sb([128, NT], F32, "q32"), Buf()) for _ in range(2)]
        for (t_, b_) in tB:
            S.op("pool", MEMSET(t_[:, :], 0.0), writes=[b_])
        n = 0
        ns = 0
        wi = 0
        for gi, d in enumerate(dils):
            nb = NB // d
            with contextlib.ExitStack() as nst:
                old = C.stack
                C.stack = nst
                emit_norm_full(C, hT_d, gain, gain_b, xn, xb, KC, T, d=d)
                C.stack = old
                S.barrier(recycle=False)
            tds = S.dsem()
            fin = tds.cnt + 32
            S.dma("sp", ct[:, :], tab_d[gi, 0, :, :], tds, writes=[tb_], group_final=fin)
            S.dma("sp", sn[:, :], tab_d[gi, 1, :, :], tds, writes=[tb_], group_final=fin)
            for s in range(3):
                for hb in range(HB):
                    w, wb, _ = st.get(wi)
                    wi += 1
                    sg, sgb, sds, sxs = stg[ns % 2]
                    ns += 1
                    if s < 2:
                        for hh in range(4):
                            for t in range(T // NT):
                                tsl = slice(t * NT, (t + 1) * NT)
                                p, pb = P[n % 2]
                                a_, ab = tA[n % 2]
                                b2, bb = tB[n % 2]
                                n += 1
                                S.pe_group([MM(p[:, :], w[:, (hh * KC + k) * 128:(hh * KC + k + 1) * 128], xn[:, k, tsl],
                                               k == 0, k == KC - 1) for k in range(KC)], reads=[wb] + xb, writes=[pb])
                                q32, q32b = q32s[n % 2]
                                emit_rope_evac(S, p, pb, sg[:, hh, tsl], sgb, ct[:, tsl], sn[:, tsl], tb_,
                                               q32, q32b, a_, ab, b2, bb, 16)
                        dst = (QT_d if s == 0 else KT_d)[gi, 4 * hb:4 * hb + 4, :, :].rearrange("h p t -> p h t")
                        S.dma("sp", dst, sg[:, :, :], sds, reads=[sgb])
                        if s == 1:
                            fin = sxs.cnt + 16 * d
                            for r in range(d):
                                c0 = (r * nb + nb - 1) * 128
                                S.dma("sp", KTx_d[4 * hb:4 * hb + 4, :, (off[gi] + r) * 128:(off[gi] + r + 1) * 128]
                                      .rearrange("h p c -> p h c"), sg[:, :, c0:c0 + 128], sxs, reads=[sgb], group_final=fin)
                    else:
                        sg4 = sg[:, :, :].rearrange("p h (b c) -> p h b c", c=128)
                        for bi in range(NB):
                            p, pb = P[n % 2]
                            n += 1
                            S.pe_group([MM(p[:, :], xn[:, k, bi * 128:(bi + 1) * 128], w[:, k * 512:(k + 1) * 512],
                                           k == 0, k == KC - 1) for k in range(KC)], reads=[wb] + xb, writes=[pb])
                            src = p[:, :].rearrange("p (h c) -> p h c", h=4)
                            if n % 2 == 0:
                                S.op("act", ACT(sg4[:, :, bi, :], src, AF.Copy), reads=[pb], writes=[sgb])
                            else:
                                S.op("dve", CP(sg4[:, :, bi, :], src), reads=[pb], writes=[sgb])
                        S.dma("sp", V_d[gi, 4 * hb:4 * hb + 4, :, :].rearrange("h p x -> p h x"), sg[:, :, :], sds, reads=[sgb])
                        fin = sxs.cnt + 16 * d
                        for r in range(d):
                            bi = r * nb + nb - 1
                            S.dma("sp", Vx_d[4 * hb:4 * hb + 4, :, off[gi] + r, :].rearrange("h p c -> p h c"),
                                  sg4[:, :, bi, :], sxs, reads=[sgb], group_final=fin)
        if gather is not None:
            S.barrier(recycle=False)
            emit_gather(C, KTx_d, gather[0], H)
            emit_gather(C, Vx_d, gather[1], H)


def emit_mixA_2(C, hT_d, QT_d, KT_d, V_d, KTh_d, Vh_d, hmask_d, mask_d, wo_r, dils=A_DIL, H=A_HEADS,
                D=D_MODEL, T=TOK, xchg=None):
    S = C.S
    K = C.K
    KC = D // 128
    NB = T // 128
    off = halo_off(dils)
    nhmax = max(dils)
    scale = float(A_HEAD_DIM) ** -0.5
    with C.phase():
        hm = C.sb([128, 1], F32, "hmask")
        hmb = Buf()
        S.dma("sp", hm[:, :], hmask_d[:, :], S.dsem(), writes=[hmb])
        mk32 = C.sb([128, 256], F32, "mk32")
        mkb = Buf()
        S.dma("sp", mk32[:, :], mask_d[:, :], S.dsem(), writes=[mkb])
        M = C.sb([128, 256], BF16, "M")
        Mh = C.sb([128, 256], BF16, "Mh")
        Mb, Mhb = Buf(), Buf()
        S.op("dve", CP(M[:, :], mk32[:, :]), reads=[mkb], writes=[Mb])
        S.op("dve", CP(Mh[:, 128:256], mk32[:, 128:256]), reads=[mkb], writes=[Mhb])
        S.op("dve", TS(Mh[:, 0:128], mk32[:, 0:128], hm[:, 0:1], None, ALU.mult), reads=[mkb, hmb], writes=[Mhb])
        oT = C.sb([128, H, T], BF16, "oT")
        oTb = bufs(H)
        slots = []
        for _ in range(2):
            slots.append(dict(q=C.sb([128, T], BF16, "q"), k=C.sb([128, (nhmax + NB) * 128], BF16, "k"),
                              v=C.sb([128, nhmax + NB, 128], BF16, "v"), b=Buf(), ds=S.dsem("qkv")))
        Oa = C.sb([128, T], F32, "Oacc")
        La = C.sb([128, T], F32, "Lacc")
        Oab, Lab = Buf(), Buf()
        pe_ = [(C.sb([128, 256], BF16, "pe"), Buf()) for _ in range(3)]
        pm_ = [(C.sb([128, 256], BF16, "pm"), Buf()) for _ in range(SKEW + 2)]
        pS = [(C.ps([128, 512], F32, "pS"), Buf()) for _ in range(2)]
        pO = [(C.ps([128, 512], F32, "pO"), Buf()) for _ in range(2)]
        pL = [(C.ps([128, 512], F32, "pL"), Buf()) for _ in range(2)]
        items = [(h, gi) for h in range(H) for gi in range(len(dils))]
        hK = bufs(H)
        hV = bufs(H)
        if xchg is not None:
            KTx_d, Vx_d, KTg_d, Vg_d, oh_d, do_coll = xchg
            XH = off[-1] * 128
            gK = bufs(H)
            gV = bufs(H)
            if do_coll:
                cs = S.dsem("cc", sw=True)
                for h in range(H):
                    S.coll("AllGather", GROUPS, KTx_d[h], KTg_d[h], cs, writes=[gK[h]])
                    S.coll("AllGather", GROUPS, Vx_d[h], Vg_d[h], cs, writes=[gV[h]])
            oh = C.sb([128, CPB], F32, "oh")
            ohb = Buf()
            S.dma("sp", oh[:, :], oh_d[:, :], S.dsem(), writes=[ohb])
            cand = (C.sb([128, CPB, XH], BF16, "cand"), Buf(), S.dsem("cand"))
            sacc = (C.sb([128, XH], BF16, "sacc"), Buf(), S.dsem("sacc"))
            selected = set()

            def select(h):
                if h in selected:
                    return
                selected.add(h)
                c, cb, cds = cand
                a, ab, ads = sacc
                for (src, gb_, dst, hb_) in ((KTg_d[h], gK[h], KTh_d[h], hK[h]),
                                              (Vg_d[h].rearrange("s p b c -> s p (b c)"), gV[h],
                                               Vh_d[h].rearrange("p b c -> p (b c)"), hV[h])):
                    S.dma("sp", c[:, :, :], src.rearrange("s p x -> p s x"), cds, reads=[gb_], writes=[cb])
                    S.op("dve", TS(a[:, :], c[:, 0, :], oh[:, 0:1], None, ALU.mult), reads=[cb, ohb], writes=[ab])
                    for s_ in range(1, CPB):
                        S.op("dve", STT(a[:, :], c[:, s_, :], oh[:, s_:s_ + 1], a[:, :], ALU.mult, ALU.add),
                             reads=[cb, ohb, ab], writes=[ab])
                    S.dma("sp", dst, a[:, :], ads, reads=[ab], writes=[hb_])

        def load(i):
            h, gi = items[i]
            if xchg is not None:
                select(h)
            d = dils[gi]
            sl = slots[i % 2]
            fin = sl["ds"].cnt + 16 * 5
            kw = dict(writes=[sl["b"]], group_final=fin)
            S.dma("sp", sl["q"][:, :], QT_d[gi, h, :, :], sl["ds"], **kw)
            S.dma("sp", sl["k"][:, 0:d * 128], KTh_d[h, :, off[gi] * 128:(off[gi] + d) * 128], sl["ds"], reads=[hK[h]], **kw)
            S.dma("sp", sl["k"][:, d * 128:(d + NB) * 128], KT_d[gi, h, :, :], sl["ds"], **kw)
            S.dma("sp", sl["v"][:, 0:d, :], Vh_d[h, :, off[gi]:off[gi] + d, :], sl["ds"], reads=[hV[h]], **kw)
            S.dma("sp", sl["v"][:, d:d + NB, :].rearrange("p b c -> p (b c)"), V_d[gi, h, :, :], sl["ds"], **kw)

        load(0)
        n = 0
        for i, (h, gi) in enumerate(items):
            if i + 1 < len(items):
                load(i + 1)
            d = dils[gi]
            nb = NB // d
            sl = slots[i % 2]
            q, k_, v, slb = sl["q"], sl["k"], sl["v"], sl["b"]
            pend = []

            def emit_pv(st_):
                bi, r, a, prev, cur, m_, mb, k2 = st_
                po, pob = pO[k2 % 2]
                pl, plb = pL[k2 % 2]
                S.pe_group([MM(po[:, 0:128], v[:, prev, :], m_[:, 0:128], True, False),
                            MM(po[:, 0:128], v[:, cur, :], m_[:, 128:256], False, True)],
                           reads=[slb, mb], writes=[pob])
                S.pe_group([MM(pl[:, 0:128], K["ones"][:, :], m_[:, 0:128], True, False),
                            MM(pl[:, 0:128], K["ones"][:, :], m_[:, 128:256], False, True)],
                           reads=[K["ones_b"], mb], writes=[plb])
                if d == 1:
                    oc = Oa[:, bi * 128:(bi + 1) * 128]
                    lc = La[:, bi * 128:(bi + 1) * 128]
                else:
                    oc = Oa[:, :].rearrange("p (a i r) -> p a r i", i=128, r=d)[:, a, r, :]
                    lc = La[:, :].rearrange("p (a i r) -> p a r i", i=128, r=d)[:, a, r, :]
                if gi == 0:
                    S.op("act", ACT(oc, po[:, 0:128], AF.Copy), reads=[pob], writes=[Oab])
                    S.op("dve", CP(lc, pl[:, 0:128]), reads=[plb], writes=[Lab])
                else:
                    S.op("dve", TT(oc, po[:, 0:128], oc, ALU.add), reads=[pob, Oab], writes=[Oab])
                    S.op("dve", TT(lc, pl[:, 0:128], lc, ALU.add), reads=[plb, Lab], writes=[Lab])

            for bi in range(NB):
                r, a = bi // nb, bi % nb
                cur = d + bi
                prev = (d + bi - 1) if a > 0 else r
                msk, mskb = (M, Mb) if a > 0 else (Mh, Mhb)
                ps, psb = pS[n % 2]
                e_, eb = pe_[n % len(pe_)]
                m_, mb = pm_[n % len(pm_)]
                qs = q[:, bi * 128:(bi + 1) * 128]
                S.pe_group([MM(ps[:, 0:128], k_[:, prev * 128:(prev + 1) * 128], qs, True, True),
                            MM(ps[:, 128:256], k_[:, cur * 128:(cur + 1) * 128], qs, True, True)],
                           reads=[slb], writes=[psb])
                S.op("act", ACT(e_[:, :], ps[:, 0:256], AF.Exp, scale=scale), reads=[psb], writes=[eb])
                S.op("pool", TT(m_[:, :], e_[:, :], msk[:, :], ALU.mult), reads=[eb, mskb], writes=[mb])
                pend.append((bi, r, a, prev, cur, m_, mb, n))
                n += 1
                if len(pend) > SKEW:
                    emit_pv(pend.pop(0))
            while pend:
                emit_pv(pend.pop(0))
            if gi == len(dils) - 1:
                S.op("dve", RECIP(La[:, :], La[:, :]), reads=[Lab], writes=[Lab])
                S.op("pool", TT(oT[:, h, :], Oa[:, :], La[:, :], ALU.mult), reads=[Oab, Lab], writes=[oTb[h]])
        emit_oproj(C, oT, oTb, wo_r, hT_d, H, T, KCout=KC)


def emit_mixB_1a(C, hT_d, gain_d, gq_d, gkv_d, winr, tab_d, cqn_d, kvx_d, krx_d,
                 QR=B_Q_RANK, KVR=B_KV_RANK, D=D_MODEL, T=TOK, NT=512, gather=None):
    S = C.S
    KC = D // 128
    NQ = QR // 128
    NKV = KVR // 128
    NCH = NQ + NKV + 1
    with C.phase():
        gain = C.sb([128, KC], F32, "gain")
        gq = C.sb([128, NQ], F32, "gq")
        gkv = C.sb([128, NKV], F32, "gkv")
        gain_b, gqb, gkvb = bufs(3)
        S.dma("sp", gain[:, :], gain_d[:, :], S.dsem(), writes=[gain_b])
        S.dma("sp", gq[:, :], gq_d[:, :], S.dsem(), writes=[gqb])
        S.dma("sp", gkv[:, :], gkv_d[:, :], S.dsem(), writes=[gkvb])
        xn = C.sb([128, KC, T], BF16, "xnf")
        xb = bufs(KC)
        with contextlib.ExitStack() as nst:
            old = C.stack
            C.stack = nst
            emit_norm_full(C, hT_d, gain, gain_b, xn, xb, KC, T)
            C.stack = old
            S.barrier(recycle=False)
        ct = C.sb([128, T], F32, "ct")
        sn = C.sb([128, T], F32, "sn")
        tb_ = Buf()
        tds = S.dsem()
        fin = tds.cnt + 32
        S.dma("sp", ct[:, :], tab_d[0, :, :], tds, writes=[tb_], group_final=fin)
        S.dma("sp", sn[:, :], tab_d[1, :, :], tds, writes=[tb_], group_final=fin)
        ws = [(C.sb([128, KC * 128], BF16, "win"), Buf(), S.dsem("win", sw=True)) for _ in range(3)]
        st = Stream(S, "pool", ws)
        for t in range(T // NT):
            for m in range(NCH):
                st.add(lambda tt, m=m: (tt[:, :], winr[m, :, :], CAST_KW))
        c32 = C.sb([128, NCH, NT], F32, "c32")
        cb = bufs(NCH)
        P = [(C.ps([128, NT], F32, "P"), Buf()) for _ in range(2)]
        Rq = norm_res(C, 1, NT, nslot=0)
        cq_st = [(C.sb([128, NQ, NT], BF16, "cqst"), Buf(), S.dsem("cqst")) for _ in range(2)]
        kv_st = [(C.sb([128, NKV, NT], BF16, "kvst"), Buf(), S.dsem("kvst")) for _ in range(2)]
        kr_st = [(C.sb([128, NT], BF16, "krst"), Buf(), S.dsem("krst")) for _ in range(2)]
        tA = (C.sb([128, NT], F32, "tA"), Buf())
        tB = (C.sb([128, NT], F32, "tB"), Buf())
        n = 0
        for t in range(T // NT):
            tsl = slice(t * NT, (t + 1) * NT)
            for m in range(NCH):
                w, wb, _ = st.get(t * NCH + m)
                p, pb = P[n % 2]
                n += 1
                S.pe_group([MM(p[:, :], w[:, k * 128:(k + 1) * 128], xn[:, k, tsl], k == 0, k == KC - 1)
                            for k in range(KC)], reads=[wb] + xb, writes=[pb])
                S.op("act", ACT(c32[:, m, :], p[:, :], AF.Copy), reads=[pb], writes=[cb[m]])
            cq, cqb, cqd = cq_st[t % 2]
            kv, kvb, kvd = kv_st[t % 2]
            kr, krb, krd = kr_st[t % 2]
            rs, rs_b = emit_rstd(C, Rq, lambda k: c32[:, k, :], cb[0:NQ], NQ, NT, QR)
            for k in range(NQ):
                S.op("dve", STT(cq[:, k, :], c32[:, k, :], gq[:, k:k + 1], rs[:, :], ALU.mult, ALU.mult),
                     reads=[cb[k], rs_b, gqb], writes=[cqb])
            S.dma("sp", cqn_d[:, :, tsl].rearrange("k p n -> p k n"), cq[:, :, :], cqd, reads=[cqb])
            rs, rs_b = emit_rstd(C, Rq, lambda k: c32[:, NQ + k, :], cb[NQ:NQ + NKV], NKV, NT, KVR)
            for k in range(NKV):
                S.op("dve", STT(kv[:, k, :], c32[:, NQ + k, :], gkv[:, k:k + 1], rs[:, :], ALU.mult, ALU.mult),
                     reads=[cb[NQ + k], rs_b, gkvb], writes=[kvb])
            S.dma("sp", kvx_d[:, :, tsl].rearrange("k p n -> p k n"), kv[:, :, :], kvd, reads=[kvb])
            emit_rope_evac(S, None, None, kr[:, :], krb, ct[:, tsl], sn[:, tsl], tb_,
                           c32[:, NCH - 1, :], cb[NCH - 1], tA[0], tA[1], tB[0], tB[1], 32)
            S.dma("sp", krx_d[0][:, tsl], kr[:, :], krd, reads=[krb])
        if gather is not None:
            S.barrier(recycle=False)
            emit_gather(C, kvx_d, gather[0], NKV)
            emit_gather(C, krx_d, gather[1], 1)


def emit_mixB_1b(C, cqn_d, wuq_r, tab_d, QB_d, H=B_HEADS, QR=B_Q_RANK, T=TOK, NT=512):
    S = C.S
    NQ = QR // 128
    with C.phase():
        cq = C.sb([128, NQ, T], BF16, "cqn")
        cqb = Buf()
        S.dma("sp", cq[:, :, :], cqn_d[:, :, :].rearrange("k p n -> p k n"), S.dsem(), writes=[cqb])
        ct = C.sb([128, T], F32, "ct")
        sn = C.sb([128, T], F32, "sn")
        tb_ = Buf()
        tds = S.dsem()
        fin = tds.cnt + 32
        S.dma("sp", ct[:, :], tab_d[0, :, :], tds, writes=[tb_], group_final=fin)
        S.dma("sp", sn[:, :], tab_d[1, :, :], tds, writes=[tb_], group_final=fin)
        ws = [(C.sb([128, 2 * NQ * 128], BF16, "wuq"), Buf(), S.dsem("wuq", sw=True)) for _ in range(2)]
        st = Stream(S, "pool", ws)
        for h in range(H):
            st.add(lambda tt, h=h: (tt[:, :], wuq_r[h, :, :], CAST_KW))
        stg = [(C.sb([128, 2, T], BF16, "qst"), Buf(), S.dsem("qst")) for _ in range(2)]
        P = [(C.ps([128, NT], F32, "P"), Buf()) for _ in range(2)]
        tA = [(C.sb([128, NT], F32, "tA"), Buf()) for _ in range(2)]
        tB = [(C.sb([128, NT], F32, "tB"), Buf()) for _ in range(2)]
        q32s = [(C.sb([128, NT], F32, "q32"), Buf()) for _ in range(2)]
        n = 0
        for h in range(H):
            w, wb, _ = st.get(h)
            sg, sgb, sds = stg[h % 2]
            for t in range(T // NT):
                tsl = slice(t * NT, (t + 1) * NT)
                for part in range(2):
                    p, pb = P[n % 2]
                    a_, ab = tA[n % 2]
                    b2, bb = tB[n % 2]
                    q32, q32b = q32s[n % 2]
                    n += 1
                    S.pe_group([MM(p[:, :], w[:, (part * NQ + k) * 128:(part * NQ + k + 1) * 128], cq[:, k, tsl],
                                   k == 0, k == NQ - 1) for k in range(NQ)], reads=[wb, cqb], writes=[pb])
                    if part == 0:
                        S.op("act", ACT(sg[:, 0, tsl], p[:, :], AF.Copy), reads=[pb], writes=[sgb])
                    else:
                        emit_rope_evac(S, p, pb, sg[:, 1, tsl], sgb, ct[:, tsl], sn[:, tsl], tb_,
                                       q32, q32b, a_, ab, b2, bb, 32)
            S.dma("sp", QB_d[h, :, :, :].rearrange("s p t -> p s t"), sg[:, :, :], sds, reads=[sgb])


def emit_mixB_2(C, QB_d, kvg_d, krg_d, wukv_r, qidx_d, kidx_d, oT_d, H=B_HEADS, KVR=B_KV_RANK,
                T=TOK, SK=SEQ, NT=512, gathered=False):
    S = C.S
    K = C.K
    NKV = KVR // 128
    NKT = SK // 128
    scale = float(B_NOPE + B_ROPE) ** -0.5
    with C.phase():
        lat = C.sb([128, NKV, SK], BF16, "lat")
        kr = C.sb([128, SK], BF16, "kr")
        latb, krb = Buf(), Buf()
        if gathered:
            lds = S.dsem()
            fin = lds.cnt + 16 * NKV
            for k in range(NKV):
                S.dma("sp", lat[:, k, :].rearrange("p (s n) -> p s n", n=T), kvg_d[k].rearrange("s p n -> p s n"), lds,
                      writes=[latb], group_final=fin)
            S.dma("sp", kr[:, :].rearrange("p (s n) -> p s n", n=T), krg_d[0].rearrange("s p n -> p s n"), S.dsem(), writes=[krb])
        else:
            S.dma("sp", lat[:, :, :], kvg_d[:, :, :].rearrange("k p n -> p k n"), S.dsem(), writes=[latb])
            S.dma("sp", kr[:, :], krg_d[:, :], S.dsem(), writes=[krb])
        qi = C.sb([128, T], F32, "qidx")
        ki = C.sb([128, NKT], F32, "kidx")
        qib, kib = Buf(), Buf()
        S.dma("sp", qi[:, :], qidx_d[0:1, :].partition_broadcast(128), S.dsem(), writes=[qib])
        S.dma("sp", ki[:, :], kidx_d[:, :], S.dsem(), writes=[kib])
        ws = [(C.sb([128, 2 * NKV * 128], BF16, "wukv"), Buf(), S.dsem("wukv", sw=True)) for _ in range(2)]
        st = Stream(S, "pool", ws)
        for h in range(H):
            st.add(lambda tt, h=h: (tt[:, :], wukv_r[h, :, :], CAST_KW))
        Kn = C.sb([128, SK], BF16, "Kn")
        Knb = Buf()
        V = C.sb([128, NKT, 128], BF16, "V")
        Vb = Buf()
        qs = [(C.sb([128, 2, T], BF16, "q"), Buf(), S.dsem("q")) for _ in range(2)]
        es = [(C.sb([128, NT], BF16, "e"), Buf()) for _ in range(3)]
        pms = [(C.sb([128, NT], BF16, "pm"), Buf()) for _ in range(SKEW + 2)]
        ost = [(C.sb([128, T], BF16, "ost"), Buf(), S.dsem("ost")) for _ in range(2)]
        rl = (C.sb([128, NT], F32, "rl"), Buf())
        pP = [(C.ps([128, NT], F32, "pP"), Buf()) for _ in range(2)]
        pS = [(C.ps([128, NT], F32, "pS"), Buf()) for _ in range(2)]
        pO = [(C.ps([128, NT], F32, "pO"), Buf()) for _ in range(2)]
        pL = [(C.ps([128, NT], F32, "pL"), Buf()) for _ in range(2)]

        def loadq(h):
            q, qb, qd = qs[h % 2]
            S.dma("sp", q[:, :, :], QB_d[h, :, :, :].rearrange("s p t -> p s t"), qd, writes=[qb])

        loadq(0)
        n = 0
        ne = 0
        for h in range(H):
            if h + 1 < H:
                loadq(h + 1)
            w, wb, _ = st.get(h)
            q, qb, _ = qs[h % 2]
            o_, ob, od = ost[h % 2]
            for tt in range(SK // NT):
                p, pb = pP[n % 2]
                n += 1
                S.pe_group([MM(p[:, :], w[:, k * 128:(k + 1) * 128], lat[:, k, tt * NT:(tt + 1) * NT], k == 0, k == NKV - 1)
                            for k in range(NKV)], reads=[wb, latb], writes=[pb])
                if n % 2 == 0:
                    S.op("act", ACT(Kn[:, tt * NT:(tt + 1) * NT], p[:, :], AF.Copy), reads=[pb], writes=[Knb])
                else:
                    S.op("dve", CP(Kn[:, tt * NT:(tt + 1) * NT], p[:, :]), reads=[pb], writes=[Knb])
            for k4 in range(NKT // 4):
                p, pb = pP[n % 2]
                n += 1
                fns = []
                for j in range(4):
                    kt = k4 * 4 + j
                    for k in range(NKV):
                        fns.append(MM(p[:, j * 128:(j + 1) * 128], lat[:, k, kt * 128:(kt + 1) * 128],
                                      w[:, (NKV + k) * 128:(NKV + k + 1) * 128], k == 0, k == NKV - 1))
                S.pe_group(fns, reads=[wb, latb], writes=[pb])
                dst = V[:, k4 * 4:k4 * 4 + 4, :].rearrange("p b c -> p (b c)")
                if n % 2 == 0:
                    S.op("act", ACT(dst, p[:, :], AF.Copy), reads=[pb], writes=[Vb])
                else:
                    S.op("dve", CP(dst, p[:, :]), reads=[pb], writes=[Vb])
            pend = []

            def emit_pv(st_):
                qg, kt, m_, mb = st_
                po, pob = pO[qg % 2]
                pl, plb = pL[qg % 2]
                qsl = slice(qg * NT, (qg + 1) * NT)
                S.pe_group([MM(po[:, :], V[:, kt, :], m_[:, :], kt == 0, kt == NKT - 1)],
                           reads=[Vb, mb], writes=[pob])
                S.pe_group([MM(pl[:, :], K["ones"][:, :], m_[:, :], kt == 0, kt == NKT - 1)],
                           reads=[K["ones_b"], mb], writes=[plb])
                if kt == NKT - 1:
                    r_, rb = rl
                    S.op("dve", RECIP(r_[:, :], pl[:, :]), reads=[plb], writes=[rb])
                    S.op("dve", TT(o_[:, qsl], po[:, :], r_[:, :], ALU.mult), reads=[pob, rb], writes=[ob])

            for qg in range(T // NT):
                qsl = slice(qg * NT, (qg + 1) * NT)
                for kt in range(NKT):
                    ps, psb = pS[ne % 2]
                    e_, eb = es[ne % len(es)]
                    m_, mb = pms[ne % len(pms)]
                    ne += 1
                    ksl = slice(kt * 128, (kt + 1) * 128)
                    S.pe_group([MM(ps[:, :], Kn[:, ksl], q[:, 0, qsl], True, False),
                                MM(ps[:, :], kr[0:64, ksl], q[0:64, 1, qsl], False, True)],
                               reads=[Knb, krb, qb], writes=[psb])
                    S.op("act", ACT(e_[:, :], ps[:, :], AF.Exp, scale=scale), reads=[psb], writes=[eb])
                    S.op("dve", STT(m_[:, :], qi[:, qsl], ki[:, kt:kt + 1], e_[:, :], ALU.is_ge, ALU.mult),
                         reads=[eb, qib, kib], writes=[mb])
                    pend.append((qg, kt, m_, mb))
                    if len(pend) > SKEW:
                        emit_pv(pend.pop(0))
            while pend:
                emit_pv(pend.pop(0))
            S.dma("sp", oT_d[h, :, :], o_[:, :], od, reads=[ob])


def emit_mixB_3(C, hT_d, oT_d, wo_r, H=B_HEADS, D=D_MODEL, T=TOK):
    S = C.S
    with C.phase():
        oT = C.sb([128, H, T], BF16, "oT")
        ob = Buf()
        S.dma("sp", oT[:, :, :], oT_d[:, :, :].rearrange("h p t -> p h t"), S.dsem(), writes=[ob])
        emit_oproj(C, oT, [ob], wo_r, hT_d, H, T, KCout=D // 128)


A_PERM = np.array(list(range(0, 16)) + list(range(32, 48)) + list(range(16, 32)) + list(range(48, 128)))


def fm_w(W, chunks, group=1):
    Din = W.shape[0]
    KC = Din // 128
    M = len(chunks[0])
    nb = len(chunks) // group
    out = np.empty((nb, 128, group, KC, M), np.float32)
    Wp = np.concatenate([W, np.zeros((Din, 1), W.dtype)], axis=1)
    for b in range(nb):
        for g in range(group):
            cols = np.asarray(chunks[b * group + g])
            out[b, :, g] = Wp[:, cols].reshape(KC, 128, M).transpose(1, 0, 2)
    return out.reshape(nb, 128, group * KC * M)


def tm_w(W, blocks):
    KC = W.shape[0] // 128
    out = np.empty((len(blocks), 128, KC, len(blocks[0])), np.float32)
    for b, cols in enumerate(blocks):
        out[b] = W[:, cols].reshape(KC, 128, len(cols)).transpose(1, 0, 2)
    return out.reshape(len(blocks), 128, -1)


def chunks_of(n0, n):
    return [n0 + np.arange(m * 128, (m + 1) * 128) for m in range(n // 128)]


def gain_l(g):
    return np.ascontiguousarray(np.asarray(g, np.float32).reshape(-1, 128).T)


def lay_w13(w13):
    D, F2 = w13.shape
    DFF = F2 // 2
    KC, NJ = D // 128, DFF // 128
    a = w13.reshape(KC, 128, 2, NJ, 128)
    return np.ascontiguousarray(a.transpose(3, 1, 0, 2, 4)).reshape(NJ, 128, KC * 256)


def lay_w2(w2):
    DFF, D = w2.shape
    KC, NJ = D // 128, DFF // 128
    a = w2.reshape(NJ, 128, KC, 128)
    return np.ascontiguousarray(a.transpose(2, 1, 0, 3)).reshape(KC, 128, NJ * 128)


def lay_sq(w):
    return fm_w(w, chunks_of(0, w.shape[1]))


def lay_wqkv(W, ng, H):
    HB = H // 4
    blks = []
    for g in range(ng):
        for s in range(3):
            for hb in range(HB):
                base = lambda h: ((g * 3 + s) * H + h) * 128
                if s < 2:
                    blks.append(fm_w(W, [base(4 * hb + hh) + A_PERM for hh in range(4)], group=4)[0])
                else:
                    blks.append(tm_w(W, [base(4 * hb) + np.arange(512)])[0])
    return np.stack(blks)


def lay_win_c(w, D):
    ch = []
    for m in range(D // 128):
        for s in range(3):
            ch.append(s * D + np.arange(m * 128, (m + 1) * 128))
    return fm_w(w, ch, group=3)


def _pad128(a):
    return np.concatenate([a, -np.ones(128 - len(a), int)])


def lay_win_b(w, QR, KVR):
    return fm_w(w, chunks_of(0, QR + KVR) + [_pad128(QR + KVR + np.arange(64))])


def lay_wuq(w, H):
    return fm_w(w, sum([[h * 192 + np.arange(128), _pad128(h * 192 + 128 + np.arange(64))] for h in range(H)], []), group=2)


def lay_wukv(w, H):
    return fm_w(w, sum([[h * 256 + np.arange(128), h * 256 + 128 + np.arange(128)] for h in range(H)], []), group=2)


def rope_frq(half, rows_a, rows_b):
    f = (np.float32(ROPE_THETA) ** (-np.arange(half, dtype=np.float32) / np.float32(half))).astype(np.float32)
    o = np.zeros((128, 2), np.float32)
    o[rows_a, 0] = f
    o[rows_a, 1] = 1.0
    o[rows_b, 0] = f
    o[rows_b, 1] = -1.0
    return o


def band_mask():
    k = np.arange(128)[:, None]
    q = np.arange(128)[None, :]
    return np.concatenate([(k >= q), (k <= q)], axis=1).astype(np.float32)


T_ = TOK
KC_ = D_MODEL // 128
NJ_ = D_FF // 128
NQ_ = B_Q_RANK // 128
NKV_ = B_KV_RANK // 128
NHALO = sum(A_DIL)
TAB_DILS = A_DIL + (1,)

SPEC = {
    "ident": ((128, 128), F32), "x": ((T_, D_MODEL), F32), "pos": ((1, T_), I32), "frq": ((4, 128, 2), F32),
    "mask": ((128, 256), F32), "hmask": ((128, 1), F32), "qidx": ((1, T_), F32), "kidx": ((128, SEQ // 128), F32),
    "gf": ((128, KC_), F32), "out": ((T_, D_MODEL), F32),
    "hT": ((KC_, 128, T_), F32), "hT_in": ((KC_, 128, T_), F32), "tab": ((4, 2, 128, T_), F32),
    "QT": ((3, A_HEADS, 128, T_), BF16), "KT": ((3, A_HEADS, 128, T_), BF16), "V": ((3, A_HEADS, 128, T_), BF16),
    "KTx": ((A_HEADS, 128, NHALO * 128), BF16), "Vx": ((A_HEADS, 128, NHALO, 128), BF16),
    "KTh": ((A_HEADS, 128, NHALO * 128), BF16), "Vh": ((A_HEADS, 128, NHALO, 128), BF16),
    "cqn": ((NQ_, 128, T_), BF16), "kvx": ((NKV_, 128, T_), BF16), "krx": ((128, T_), BF16),
    "kvg": ((NKV_, 128, SEQ), BF16), "krg": ((128, SEQ), BF16), "QB": ((B_HEADS, 2, 128, T_), BF16),
    "oT": ((B_HEADS, 128, T_), BF16),
    "cuT": ((KC_, 128, T_), F32), "bT": ((KC_, 128, T_), F32), "cux": ((KC_, 128, 2), F32), "cuh": ((KC_, 128, 2), F32),
}
for _l in range(DEPTH):
    for _s in ("1", "2"):
        SPEC["l%d_g%s" % (_l, _s)] = ((128, KC_), F32)
        SPEC["l%d_w13_%s" % (_l, _s)] = ((NJ_, 128, KC_ * 256), F32)
        SPEC["l%d_w2_%s" % (_l, _s)] = ((KC_, 128, NJ_ * 128), F32)
    SPEC["l%d_gm" % _l] = ((128, KC_), F32)
for _l in (0, 3):
    SPEC["l%d_wqkv" % _l] = ((36, 128, 4 * KC_ * 128), F32)
    SPEC["l%d_wo" % _l] = ((KC_, 128, A_HEADS * 128), F32)
SPEC["l1_gq"] = ((128, NQ_), F32)
SPEC["l1_gkv"] = ((128, NKV_), F32)
SPEC["l1_win"] = ((NQ_ + NKV_ + 1, 128, KC_ * 128), F32)
SPEC["l1_wuq"] = ((B_HEADS, 128, 2 * NQ_ * 128), F32)
SPEC["l1_wukv"] = ((B_HEADS, 128, 2 * NKV_ * 128), F32)
SPEC["l1_wo"] = ((KC_, 128, B_HEADS * 128), F32)
SPEC["l2_win"] = ((KC_, 128, 3 * KC_ * 128), F32)
SPEC["l2_convw"] = ((128, KC_, 3), F32)
SPEC["l2_wout"] = ((KC_, 128, KC_ * 128), F32)


def ffn_names(l, s):
    return ["l%d_g%s" % (l, s), "l%d_w13_%s" % (l, s), "l%d_w2_%s" % (l, s)]


def do_ffn(C, d, l, s):
    emit_ffn(C, d["hT"], d["l%d_g%s" % (l, s)], d["l%d_w13_%s" % (l, s)], d["l%d_w2_%s" % (l, s)])


def do_A1(C, d, l):
    emit_mixA_1(C, d["hT"], d["l%d_gm" % l], d["l%d_wqkv" % l], d["tab"], d["QT"], d["KT"], d["V"], d["KTx"], d["Vx"])


def do_A2(C, d, l):
    emit_mixA_2(C, d["hT"], d["QT"], d["KT"], d["V"], d["KTh"], d["Vh"], d["hmask"], d["mask"], d["l%d_wo" % l])


def do_B1(C, d):
    emit_mixB_1a(C, d["hT"], d["l1_gm"], d["l1_gq"], d["l1_gkv"], d["l1_win"], d["tab"][3], d["cqn"], d["kvx"], d["krx"])
    emit_mixB_1b(C, d["cqn"], d["l1_wuq"], d["tab"][3], d["QB"])


def do_B2(C, d):
    emit_mixB_2(C, d["QB"], d["kvg"], d["krg"], d["l1_wukv"], d["qidx"], d["kidx"], d["oT"])
    emit_mixB_3(C, d["hT"], d["oT"], d["l1_wo"])


def do_tables(C, d):
    emit_tables(C, d["pos"], d["frq"], d["tab"], TAB_DILS)


def L1(C, d):
    do_tables(C, d)
    emit_pin(C, d["x"], d["hT"])
    do_ffn(C, d, 0, "1")
    do_A1(C, d, 0)


def L2(C, d):
    emit_dram_copy(C, d["hT"], d["hT_in"])
    do_A2(C, d, 0)
    do_ffn(C, d, 0, "2")
    do_ffn(C, d, 1, "1")
    do_tables(C, d)
    do_B1(C, d)


def L3(C, d):
    emit_dram_copy(C, d["hT"], d["hT_in"])
    do_B2(C, d)
    do_ffn(C, d, 1, "2")
    do_ffn(C, d, 2, "1")
    emit_mixC_1(C, d["hT"], d["l2_gm"], d["l2_win"], d["cuT"], d["bT"], d["cux"])


def L4(C, d):
    emit_dram_copy(C, d["hT"], d["hT_in"])
    emit_mixC_2(C, d["hT"], d["cuT"], d["bT"], d["cuh"], d["l2_convw"], d["hmask"], d["l2_wout"])
    do_ffn(C, d, 2, "2")
    do_ffn(C, d, 3, "1")
    do_tables(C, d)
    do_A1(C, d, 3)


def L5(C, d):
    emit_dram_copy(C, d["hT"], d["hT_in"])
    do_A2(C, d, 3)
    do_ffn(C, d, 3, "2")
    emit_final(C, d["hT"], d["gf"], d["out"])


LAUNCHES = [
    ("L1", L1, ["ident", "x", "pos", "frq"] + ffn_names(0, "1") + ["l0_gm", "l0_wqkv"],
     ["hT", "QT", "KT", "V", "KTx", "Vx"], ["tab"]),
    ("L2", L2, ["ident", "hT_in", "QT", "KT", "V", "KTh", "Vh", "hmask", "mask", "l0_wo"] + ffn_names(0, "2") + ffn_names(1, "1")
     + ["pos", "frq", "l1_gm", "l1_gq", "l1_gkv", "l1_win", "l1_wuq"],
     ["hT", "kvx", "krx", "QB"], ["tab", "cqn"]),
    ("L3", L3, ["ident", "hT_in", "QB", "kvg", "krg", "l1_wukv", "qidx", "kidx", "l1_wo"] + ffn_names(1, "2") + ffn_names(2, "1")
     + ["l2_gm", "l2_win"],
     ["hT", "cuT", "bT", "cux"], ["oT"]),
    ("L4", L4, ["ident", "hT_in", "cuT", "bT", "cuh", "l2_convw", "hmask", "l2_wout"] + ffn_names(2, "2") + ffn_names(3, "1")
     + ["pos", "frq", "l3_gm", "l3_wqkv"],
     ["hT", "QT", "KT", "V", "KTx", "Vx"], ["tab"]),
    ("L5", L5, ["ident", "hT_in", "QT", "KT", "V", "KTh", "Vh", "hmask", "mask", "l3_wo"] + ffn_names(3, "2") + ["gf"],
     ["out"], ["hT"]),
]

_PROGS = {}


def build_prog(name, fn, ins, outs, scratch):
    if name in _PROGS:
        return _PROGS[name]
    nc = bass.Bass("TRN2", target_bir_lowering=False)
    d = {}
    for n in ins:
        d[n] = nc.dram_tensor(n, list(SPEC[n][0]), SPEC[n][1], kind="ExternalInput").ap()
    for n in outs:
        d[n] = nc.dram_tensor(n, list(SPEC[n][0]), SPEC[n][1], kind="ExternalOutput").ap()
    for n in scratch:
        d[n] = nc.dram_tensor(n, list(SPEC[n][0]), SPEC[n][1]).ap()
    with contextlib.ExitStack() as st:
        C = Ctx(nc, st)
        emit_consts(C, d["ident"])
        fn(C, d)
        C.S.emit()
    _PROGS[name] = nc
    return nc


INPUT_NAMES = (
    "x", "positions",
    "l0_ffn1_norm", "l0_ffn1_w13", "l0_ffn1_w2", "l0_mix_norm", "l0_a_w_qkv", "l0_a_w_o",
    "l0_ffn2_norm", "l0_ffn2_w13", "l0_ffn2_w2",
    "l1_ffn1_norm", "l1_ffn1_w13", "l1_ffn1_w2", "l1_mix_norm", "l1_b_w_in", "l1_b_q_norm",
    "l1_b_w_uq", "l1_b_kv_norm", "l1_b_w_ukv", "l1_b_w_o", "l1_ffn2_norm", "l1_ffn2_w13", "l1_ffn2_w2",
    "l2_ffn1_norm", "l2_ffn1_w13", "l2_ffn1_w2", "l2_mix_norm", "l2_c_w_in", "l2_c_conv_w", "l2_c_w_out",
    "l2_ffn2_norm", "l2_ffn2_w13", "l2_ffn2_w2",
    "l3_ffn1_norm", "l3_ffn1_w13", "l3_ffn1_w2", "l3_mix_norm", "l3_a_w_qkv", "l3_a_w_o",
    "l3_ffn2_norm", "l3_ffn2_w13", "l3_ffn2_w2",
    "final_norm",
)


def host_layout(inp):
    missing = [n for n in INPUT_NAMES if n not in inp]
    assert not missing, missing
    W = {}
    f = lambda k: np.asarray(inp[k], np.float32)
    for l in range(DEPTH):
        p = "l%d_" % l
        for s, nm in (("1", "ffn1"), ("2", "ffn2")):
            W[p + "g" + s] = gain_l(f(p + nm + "_norm"))
            W[p + "w13_" + s] = lay_w13(f(p + nm + "_w13"))
            W[p + "w2_" + s] = lay_w2(f(p + nm + "_w2"))
        W[p + "gm"] = gain_l(f(p + "mix_norm"))
    for l in (0, 3):
        p = "l%d_" % l
        W[p + "wqkv"] = lay_wqkv(f(p + "a_w_qkv"), 3, A_HEADS)
        W[p + "wo"] = lay_sq(f(p + "a_w_o"))
    W["l1_gq"] = gain_l(f("l1_b_q_norm"))
    W["l1_gkv"] = gain_l(f("l1_b_kv_norm"))
    W["l1_win"] = lay_win_b(f("l1_b_w_in"), B_Q_RANK, B_KV_RANK)
    W["l1_wuq"] = lay_wuq(f("l1_b_w_uq"), B_HEADS)
    W["l1_wukv"] = lay_wukv(f("l1_b_w_ukv"), B_HEADS)
    W["l1_wo"] = lay_sq(f("l1_b_w_o"))
    W["l2_win"] = lay_win_c(f("l2_c_w_in"), D_MODEL)
    W["l2_convw"] = np.ascontiguousarray(f("l2_c_conv_w").T.reshape(KC_, 128, 3).transpose(1, 0, 2))
    W["l2_wout"] = lay_sq(f("l2_c_w_out"))
    W["gf"] = gain_l(f("final_norm"))
    W["ident"] = np.eye(128, dtype=np.float32)
    fa = rope_frq(A_ROT // 2, np.arange(0, 16), np.arange(32, 48))
    fb = rope_frq(B_ROPE // 2, np.arange(0, 32), np.arange(32, 64))
    W["frq"] = np.stack([fa, fa, fa, fb])
    W["mask"] = band_mask()
    W["kidx"] = (np.arange(SEQ // 128)[None, :] * 128 + np.arange(128)[:, None]).astype(np.float32)
    x = f("x")
    pos = np.asarray(inp["positions"], np.int32)
    per_core = []
    for c in range(NCORES):
        b, j = c // CPB, c % CPB
        per_core.append({
            "x": np.ascontiguousarray(x[b, j * T_:(j + 1) * T_]),
            "pos": np.ascontiguousarray(pos[b:b + 1, j * T_:(j + 1) * T_]),
            "hmask": np.full((128, 1), float(j > 0), np.float32),
            "qidx": (j * T_ + np.arange(T_, dtype=np.float32))[None],
        })
    return W, per_core


def kernel_unfused(**inp):
    import ml_dtypes
    W, pc = host_layout(inp)
    state = [dict(p) for p in pc]
    out = None
    for (name, fn, ins, outs, scratch) in LAUNCHES:
        nc = build_prog(name, fn, ins, outs, scratch)
        in_maps = []
        for c in range(NCORES):
            im = {}
            for n in ins:
                im[n] = state[c][n] if n in state[c] else W[n]
            in_maps.append(im)
        res = run_bass_kernel_spmd(nc, in_maps, core_ids=list(range(NCORES))).results
        for c in range(NCORES):
            for n in outs:
                state[c]["hT_in" if n == "hT" else n] = np.asarray(res[c][n])
        for c in range(NCORES):
            j = c % CPB
            if "KTx" in outs:
                state[c]["KTh"] = state[c - 1]["KTx"] if j > 0 else np.zeros_like(state[c]["KTx"])
                state[c]["Vh"] = state[c - 1]["Vx"] if j > 0 else np.zeros_like(state[c]["Vx"])
            if "kvx" in outs:
                b0 = (c // CPB) * CPB
                state[c]["kvg"] = np.concatenate([state[b0 + i]["kvx"] for i in range(CPB)], axis=2)
                state[c]["krg"] = np.concatenate([state[b0 + i]["krx"][0] for i in range(CPB)], axis=1)
            if "cux" in outs:
                state[c]["cuh"] = state[c - 1]["cux"] if j > 0 else np.zeros_like(state[c]["cux"])
    out = np.stack([np.concatenate([np.asarray(state[b * CPB + j]["out"]) for j in range(CPB)], axis=0) for b in range(BATCH)])
    return out.astype(np.float32)


def kernel(**inputs):
    return kernel_fused(**inputs)


GROUPS = [list(range(b * CPB, (b + 1) * CPB)) for b in range(BATCH)]


def emit_gather(C, src, dst, n):
    S = C.S
    cs = S.dsem("cc", sw=True)
    for i in range(n):
        S.coll("AllGather", GROUPS, src[i], dst[i], cs)


def emit_select(C, src_g, dst, oh_d, n, X, dt):
    S = C.S
    with C.phase():
        oh = C.sb([128, CPB], F32, "oh")
        ohb = Buf()
        S.dma("sp", oh[:, :], oh_d[:, :], S.dsem(), writes=[ohb])
        cand = [(C.sb([128, CPB, X], dt, "cand"), Buf(), S.dsem("cand")) for _ in range(2)]
        acc = [(C.sb([128, X], F32, "acc"), Buf()) for _ in range(2)]
        outs = [(C.sb([128, X], dt, "sel"), Buf(), S.dsem("sel")) for _ in range(2)]
        for i in range(n):
            c, cb, cds = cand[i % 2]
            a, ab = acc[i % 2]
            o, ob, ods = outs[i % 2]
            S.dma("sp", c[:, :, :], src_g[i].rearrange("s p x -> p s x"), cds, writes=[cb])
            S.op("dve", TS(a[:, :], c[:, 0, :], oh[:, 0:1], None, ALU.mult), reads=[cb, ohb], writes=[ab])
            for s in range(1, CPB):
                last = s == CPB - 1
                S.op("dve", STT(o[:, :] if last else a[:, :], c[:, s, :], oh[:, s:s + 1], a[:, :], ALU.mult, ALU.add),
                     reads=[cb, ohb, ab], writes=[ob] if last else [ab])
            S.dma("sp", dst[i], o[:, :], ods, reads=[ob])


def emit_select_c(C, cug, cuh, oh_d, KC=KC_):
    S = C.S
    X = 2 * KC
    with C.phase():
        oh = C.sb([128, CPB], F32, "oh")
        ohb = Buf()
        S.dma("sp", oh[:, :], oh_d[:, :], S.dsem(), writes=[ohb])
        c = C.sb([128, CPB, X], F32, "cand")
        cb = Buf()
        S.dma("sp", c[:, :, :], cug[:, :, 0:X].rearrange("s p x -> p s x"), S.dsem(), writes=[cb])
        a = C.sb([128, X], F32, "acc")
        ab = Buf()
        S.op("dve", TS(a[:, :], c[:, 0, :], oh[:, 0:1], None, ALU.mult), reads=[cb, ohb], writes=[ab])
        for s in range(1, CPB):
            S.op("dve", STT(a[:, :], c[:, s, :], oh[:, s:s + 1], a[:, :], ALU.mult, ALU.add),
                 reads=[cb, ohb, ab], writes=[ab])
        S.dma("sp", cuh[:, 0:X], a[:, :], S.dsem(), reads=[ab])


def FUSED(C, d):
    def A_layer(l):
        sfx = str(l)
        emit_mixA_1(C, d["hT"], d["l%d_gm" % l], d["l%d_wqkv" % l], d["tab"], d["QT" + sfx], d["KT" + sfx], d["V" + sfx],
                    d["KTx" + sfx], d["Vx" + sfx])
        emit_mixA_2(C, d["hT"], d["QT" + sfx], d["KT" + sfx], d["V" + sfx], d["KTh" + sfx], d["Vh" + sfx], d["hmask"], d["mask"],
                    d["l%d_wo" % l], xchg=(d["KTx" + sfx], d["Vx" + sfx], d["KTg" + sfx], d["Vg" + sfx], d["oh"], True))

    do_tables(C, d)
    emit_pin(C, d["x"], d["hT"])
    do_ffn(C, d, 0, "1")
    A_layer(0)
    do_ffn(C, d, 0, "2")
    do_ffn(C, d, 1, "1")
    emit_mixB_1a(C, d["hT"], d["l1_gm"], d["l1_gq"], d["l1_gkv"], d["l1_win"], d["tab"][3], d["cqn"], d["kvx"], d["krx"],
                 gather=(d["kvgs"], d["krgs"]))
    emit_mixB_1b(C, d["cqn"], d["l1_wuq"], d["tab"][3], d["QB"])
    emit_mixB_2(C, d["QB"], d["kvgs"], d["krgs"], d["l1_wukv"], d["qidx"], d["kidx"], d["oT"], gathered=True)
    emit_mixB_3(C, d["hT"], d["oT"], d["l1_wo"])
    do_ffn(C, d, 1, "2")
    do_ffn(C, d, 2, "1")
    emit_mixC_1(C, d["hT"], d["l2_gm"], d["l2_win"], d["cuT"], d["bT"], d["cux"], gather=d["cug"])
    emit_select_c(C, d["cug"][0], d["cuh"][0], d["oh"])
    emit_mixC_2(C, d["hT"], d["cuT"], d["bT"], d["cuh"], d["l2_convw"], d["hmask"], d["l2_wout"])
    do_ffn(C, d, 2, "2")
    do_ffn(C, d, 3, "1")
    A_layer(3)
    do_ffn(C, d, 3, "2")
    emit_final(C, d["hT"], d["gf"], d["out"])


for _l in ("0", "3"):
    for _n in ("QT", "KT", "V", "KTx", "Vx", "KTh", "Vh"):
        SPEC[_n + _l] = SPEC[_n]
    SPEC["KTg" + _l] = ((A_HEADS, CPB, 128, NHALO * 128), BF16)
    SPEC["Vg" + _l] = ((A_HEADS, CPB, 128, NHALO, 128), BF16)
SPEC["kvgs"] = ((NKV_, CPB, 128, T_), BF16)
SPEC["krgs"] = ((1, CPB, 128, T_), BF16)
SPEC["cux"] = ((1, 128, 512), F32)
SPEC["cug"] = ((1, CPB, 128, 512), F32)
SPEC["cuh"] = ((1, 128, 512), F32)
SPEC["oh"] = ((128, CPB), F32)
SPEC["krx"] = ((1, 128, T_), BF16)

FUSED_INS = (["ident", "x", "pos", "frq", "mask", "hmask", "oh", "qidx", "kidx", "gf"]
             + sum([ffn_names(l, s) for l in range(DEPTH) for s in ("1", "2")], [])
             + ["l%d_gm" % l for l in range(DEPTH)]
             + ["l0_wqkv", "l0_wo", "l3_wqkv", "l3_wo", "l1_gq", "l1_gkv", "l1_win", "l1_wuq", "l1_wukv", "l1_wo",
                "l2_win", "l2_convw", "l2_wout"])
FUSED_SCRATCH = (["hT", "tab", "cqn", "kvx", "krx", "kvgs", "krgs", "QB", "oT", "cuT", "bT", "cux", "cug", "cuh"]
                 + [n + l for l in ("0", "3") for n in ("QT", "KT", "V", "KTx", "Vx", "KTh", "Vh", "KTg", "Vg")])


def kernel_fused(**inp):
    W, pc = host_layout(inp)
    nc = build_prog("FUSED", FUSED, FUSED_INS, ["out"], FUSED_SCRATCH)
    in_maps = []
    for c in range(NCORES):
        j = c % CPB
        oh = np.zeros((128, CPB), np.float32)
        if j > 0:
            oh[:, j - 1] = 1.0
        pcc = dict(pc[c], oh=oh)
        in_maps.append({n: (pcc[n] if n in pcc else W[n]) for n in FUSED_INS})
    res = run_bass_kernel_spmd(nc, in_maps, core_ids=list(range(NCORES))).results
    out = np.stack([np.concatenate([np.asarray(res[b * CPB + j]["out"]) for j in range(CPB)], axis=0) for b in range(BATCH)])
    return out.astype(np.float32)
```

```python
import contextlib
import numpy as np
import concourse.bass as bass
import concourse.mybir as mybir
from concourse.bass_utils import run_bass_kernel_spmd

F32 = mybir.dt.float32
BF16 = mybir.dt.bfloat16
I32 = mybir.dt.int32
AF = mybir.ActivationFunctionType
ALU = mybir.AluOpType

D_MODEL = 2048
BATCH = 2
SEQ = 8192
DEPTH = 4
D_FF = 5632
ROPE_THETA = 500000.0
A_HEADS = 16
A_HEAD_DIM = 128
A_ROT = 32
A_DIL = (1, 4, 16)
B_HEADS = 16
B_Q_RANK = 1536
B_KV_RANK = 512
B_NOPE = 128
B_ROPE = 64
B_V = 128
NCORES = 8
TOK = BATCH * SEQ // NCORES
CPB = NCORES // BATCH
EPS = 1e-6


class Buf:
    __slots__ = ("w", "r")

    def __init__(self):
        self.w = None
        self.r = {}


def bufs(n):
    return [Buf() for _ in range(n)]


class DSem:
    __slots__ = ("sem", "cnt")

    def __init__(self, sem):
        self.sem = sem
        self.cnt = 0


class Sched:
    ENG = ("pe", "act", "dve", "pool", "sp")

    def __init__(self, nc, stack):
        self.nc = nc
        self.stack = stack
        self.q = {e: [] for e in self.ENG}
        self.cur = {}
        self.cnt = {}
        self.known = {e: {} for e in self.ENG}
        self.nsem = 0
        self.ds_all = []
        self.ds_hw = []
        self.ds_sw = []
        self.ds_free = []
        self.ds_free_sw = []
        for e in self.ENG:
            self.cur[e] = self._new_sem("e_" + e)
            self.cnt[e] = 0

    def _new_sem(self, name):
        self.nsem += 1
        return self.stack.enter_context(self.nc.semaphore("s%d_%s" % (self.nsem, name)))

    def dsem(self, name="d", sw=False):
        free = self.ds_free_sw if sw else self.ds_free
        if free:
            return free.pop()
        d = DSem(self._new_sem(name))
        (self.ds_sw if sw else self.ds_hw).append(d)
        self.ds_all.append(d)
        return d

    def _deps(self, reads, writes):
        deps = []
        for b in reads:
            if b.w is not None:
                deps.append(b.w)
        for b in writes:
            if b.w is not None:
                deps.append(b.w)
            deps.extend(b.r.values())
        return deps

    def _waits(self, eng, deps, skip=None):
        kn = self.known[eng]
        best = {}
        for (sem, val) in deps:
            if sem is skip:
                continue
            k = id(sem)
            if kn.get(k, 0) >= val:
                continue
            if k not in best or best[k][1] < val:
                best[k] = (sem, val)
        for k, (sem, val) in best.items():
            kn[k] = val
            self.q[eng].append(("wait", sem, val))

    def _mark(self, tok, reads, writes):
        k = id(tok[0])
        for b in reads:
            if k not in b.r or b.r[k][1] < tok[1]:
                b.r[k] = tok
        for b in writes:
            b.w = tok
            b.r = {}

    def _sig(self, eng):
        self.cnt[eng] += 1
        return (self.cur[eng], self.cnt[eng])

    def op(self, eng, fn, reads=(), writes=()):
        self._waits(eng, self._deps(reads, writes))
        tok = self._sig(eng)
        self.q[eng].append(("op", fn, tok[0]))
        self._mark(tok, reads, writes)
        return tok

    def pe_group(self, fns, reads=(), writes=()):
        self._waits("pe", self._deps(reads, writes), skip=self.cur["pe"])
        tok = self._sig("pe")
        for f in fns[:-1]:
            self.q["pe"].append(("op", f, None))
        self.q["pe"].append(("op", fns[-1], tok[0]))
        self._mark(tok, reads, writes)
        return tok

    def dma(self, queue, out, in_, dsem, reads=(), writes=(), group_final=None, **kw):
        deps = self._deps(reads, writes)
        if group_final is not None:
            deps = [d_ for d_ in deps if not (d_[0] is dsem.sem and d_[1] == group_final)]
        self._waits(queue, deps)
        dsem.cnt += 16
        tok = (dsem.sem, dsem.cnt if group_final is None else group_final)
        self.q[queue].append(("dma", out, in_, dsem.sem, kw))
        self._mark(tok, reads, writes)
        return tok

    def coll(self, kind, groups, src, dst, dsem, reads=(), writes=()):
        self._waits("pool", self._deps(reads, writes))
        dsem.cnt += 1
        tok = (dsem.sem, dsem.cnt)
        self.q["pool"].append(("coll", kind, groups, src, dst, dsem.sem))
        self._mark(tok, reads, writes)
        return tok

    def call(self, eng, fn, reads=(), writes=()):
        self._waits(eng, self._deps(reads, writes))
        self.q[eng].append(("call", fn))

    def wait_tok(self, eng, tok):
        self._waits(eng, [tok])

    def barrier(self, recycle=True):
        for e in self.ENG:
            if e != "sp" and self.cnt[e] > 0:
                self.wait_tok("sp", (self.cur[e], self.cnt[e]))
        for d in self.ds_all:
            if d.cnt > 0:
                self.wait_tok("sp", (d.sem, d.cnt))
        tok = self._sig("sp")
        self.q["sp"].append(("inc", tok[0]))
        for e in self.ENG:
            if e != "sp":
                self.wait_tok(e, tok)
        if recycle:
            self.ds_free = list(self.ds_hw)
            self.ds_free_sw = list(self.ds_sw)

    def emit(self):
        def run(e, items):
            for it in items:
                if it[0] == "wait":
                    e.wait_ge(it[1], it[2])
                elif it[0] == "op":
                    ins = it[1](e)
                    if it[2] is not None:
                        ins.then_inc(it[2], 1)
                elif it[0] == "inc":
                    e.sem_inc(it[1], 1)
                elif it[0] == "call":
                    it[1](e)
                elif it[0] == "coll":
                    _, kind, groups, src, dst, sem = it
                    e.collective_compute(kind, ALU.bypass, replica_groups=groups, ins=[src.opt()], outs=[dst.opt()]).then_inc(sem, 1)
                else:
                    _, out, in_, sem, kw = it
                    if callable(in_):
                        in_ = in_()
                    e.dma_start(out=out, in_=in_, **kw).then_inc(sem, 16)

        with self.nc.Block() as block:
            @block.tensor
            def _(e):
                run(e, self.q["pe"])

            @block.scalar
            def _(e):
                run(e, self.q["act"])

            @block.vector
            def _(e):
                run(e, self.q["dve"])

            @block.gpsimd
            def _(e):
                run(e, self.q["pool"])

            @block.sync
            def _(e):
                run(e, self.q["sp"])


class Stream:
    def __init__(self, S, queue, slots):
        self.S = S
        self.queue = queue
        self.slots = slots
        self.items = []
        self.issued = 0

    def add(self, fn):
        self.items.append(fn)
        return len(self.items) - 1

    def ensure(self, upto):
        upto = min(upto, len(self.items) - 1)
        while self.issued <= upto:
            i = self.issued
            t, buf, ds = self.slots[i % len(self.slots)]
            o, a, kw = self.items[i](t)
            self.S.dma(self.queue, o, a, ds, writes=[buf], **kw)
            self.issued += 1

    def get(self, i):
        self.ensure(i + len(self.slots) - 1)
        return self.slots[i % len(self.slots)]


class Ctx:
    def __init__(self, nc, gstack):
        self.nc = nc
        self.gstack = gstack
        self.stack = gstack
        self.S = Sched(nc, gstack)
        self.n = 0
        self.K = {}

    def sb(self, shape, dt, name="sb"):
        self.n += 1
        return self.stack.enter_context(self.nc.sbuf_tensor("%s_%d" % (name, self.n), list(shape), dt))

    def ps(self, shape, dt=F32, name="ps"):
        self.n += 1
        return self.stack.enter_context(self.nc.psum_tensor("%s_%d" % (name, self.n), list(shape), dt))

    @contextlib.contextmanager
    def phase(self):
        with contextlib.ExitStack() as st:
            self.stack = st
            yield
            self.S.barrier()
        self.stack = self.gstack


def MM(out, lhsT, rhs, start, stop):
    return lambda e: e.matmul(out, lhsT, rhs, start=start, stop=stop)


def ACT(out, in_, func, **kw):
    return lambda e: e.activation(out=out, in_=in_, func=func, **kw)


def TT(out, in0, in1, op):
    return lambda e: e.tensor_tensor(out=out, in0=in0, in1=in1, op=op)


def TS(out, in0, s1, s2, op0, op1=None):
    if op1 is None:
        return lambda e: e.tensor_scalar(out=out, in0=in0, scalar1=s1, scalar2=None, op0=op0)
    return lambda e: e.tensor_scalar(out=out, in0=in0, scalar1=s1, scalar2=s2, op0=op0, op1=op1)


def STT(out, in0, scalar, in1, op0, op1):
    return lambda e: e.scalar_tensor_tensor(out=out, in0=in0, scalar=scalar, in1=in1, op0=op0, op1=op1)


def CP(out, in_):
    return lambda e: e.tensor_copy(out=out, in_=in_)


def MEMSET(ap, v):
    return lambda e: e.memset(ap, v)


def RECIP(out, in_):
    return lambda e: e.reciprocal(out, in_)


CAST_KW = dict(max_dma_last_dim=4096)
SKEW = 2


def emit_consts(C, ident_d):
    S = C.S
    K = C.K
    K["ones"] = C.sb([128, 128], BF16, "ones")
    K["ones_b"] = Buf()
    S.op("pool", MEMSET(K["ones"][:, :], 1.0), writes=[K["ones_b"]])
    K["eps"] = C.sb([128, 1], F32, "eps")
    K["eps_b"] = Buf()
    S.op("pool", MEMSET(K["eps"][:, :], EPS), writes=[K["eps_b"]])
    K["ident"] = C.sb([128, 128], F32, "ident")
    K["ident_b"] = Buf()
    ds = S.dsem()
    S.dma("sp", K["ident"][:, :], ident_d[:, :], ds, writes=[K["ident_b"]])


def norm_res(C, KC, NT, nslot=2):
    S = C.S
    R = {}
    R["h"] = [(C.sb([128, KC, NT], F32, "h"), bufs(KC), S.dsem("h")) for _ in range(nslot)]
    R["ssum"] = (C.ps([128, NT], F32, "ssum"), Buf())
    R["sq"] = [(C.sb([128, NT], BF16, "sq"), Buf()) for _ in range(3)]
    R["rs"] = (C.sb([128, NT], F32, "rs"), Buf())
    return R


def emit_rstd(C, R, src, srcb, KC, NT, Dn):
    S = C.S
    K = C.K
    ssum, ssum_b = R["ssum"]
    for k in range(KC):
        sq, sq_b = R["sq"][k % len(R["sq"])]
        S.op("act", ACT(sq[:, :NT], src(k), AF.Square), reads=[srcb[k]], writes=[sq_b])
        S.pe_group([MM(ssum[:, :NT], K["ones"][:, :], sq[:, :NT], k == 0, k == KC - 1)],
                   reads=[sq_b, K["ones_b"]], writes=[ssum_b])
    rs, rs_b = R["rs"]
    S.op("act", ACT(rs[:, :NT], ssum[:, :NT], AF.Sqrt, scale=1.0 / Dn, bias=K["eps"][:, :]),
         reads=[ssum_b, K["eps_b"]], writes=[rs_b])
    S.op("dve", RECIP(rs[:, :NT], rs[:, :NT]), reads=[rs_b], writes=[rs_b])
    return rs, rs_b


def emit_norm_full(C, hT_d, gain, gain_b, xn, xnb, KC, T, d=1, NT=256, R=None):
    S = C.S
    if R is None:
        R = norm_res(C, KC, NT)
    for t in range(T // NT):
        h, hb, hds = R["h"][t % 2]
        tsl = slice(t * NT, (t + 1) * NT)
        S.dma("sp", h[:, :, :], hT_d[:, :, tsl].rearrange("k p n -> p k n"), hds, writes=hb)
        rs, rs_b = emit_rstd(C, R, lambda k, h=h: h[:, k, :], hb, KC, NT, KC * 128)
        for k in range(KC):
            if d == 1:
                o = xn[:, k, tsl]
                i0 = h[:, k, :]
                i1 = rs[:, :]
            else:
                m0 = t * NT // d
                o = xn[:, k, :].rearrange("p (r m) -> p m r", r=d)[:, m0:m0 + NT // d, :]
                i0 = h[:, k, :].rearrange("p (m r) -> p m r", r=d)
                i1 = rs[:, :].rearrange("p (m r) -> p m r", r=d)
            S.op("dve", STT(o, i0, gain[:, k:k + 1], i1, ALU.mult, ALU.mult),
                 reads=[hb[k], rs_b, gain_b], writes=[xnb[k]])


def emit_ffn(C, hT_d, gain_d, w13r, w2r, D=D_MODEL, DFF=D_FF, T=TOK, NT=512):
    S = C.S
    K = C.K
    KC = D // 128
    NJ = DFF // 128
    NTILE = T // NT
    with C.phase():
        w13s = [(C.sb([128, KC * 256], BF16, "w13"), Buf(), S.dsem("w13", sw=True)) for _ in range(2)]
        w2s = [(C.sb([128, NJ * 128], BF16, "w2"), Buf(), S.dsem("w2", sw=True)) for _ in range(2)]
        gain = C.sb([128, KC], F32, "gain")
        gain_b = Buf()
        S.dma("sp", gain[:, :], gain_d[:, :], S.dsem(), writes=[gain_b])
        R = norm_res(C, KC, NT)
        hst = [S.dsem("hst") for _ in range(2)]
        xns = [(C.sb([128, KC, NT], BF16, "xn"), bufs(KC)) for _ in range(2)]
        hid = C.sb([128, NJ, NT], BF16, "hid")
        hidb = bufs(NJ)
        G = [(C.ps([128, NT], F32, "G"), Buf()) for _ in range(2)]
        U = [(C.ps([128, NT], F32, "U"), Buf()) for _ in range(2)]
        Y = [(C.ps([128, NT], F32, "Y"), Buf()) for _ in range(2)]
        sgs = [(C.sb([128, NT], F32, "sg"), Buf()) for _ in range(2)]

        st13 = Stream(S, "pool", w13s)
        st2 = Stream(S, "pool", w2s)
        for t in range(NTILE):
            for j in range(NJ):
                st13.add(lambda tt, j=j: (tt[:, :], w13r[j, :, :], CAST_KW))
            for m in range(KC):
                st2.add(lambda tt, m=m: (tt[:, :], w2r[m, :, :], CAST_KW))

        def load_h(t):
            h, hb, hds = R["h"][t % 2]
            S.dma("sp", h[:, :, :], hT_d[:, :, t * NT:(t + 1) * NT].rearrange("k p n -> p k n"), hds, writes=hb)

        def do_norm(t):
            h, hb, hds = R["h"][t % 2]
            xn, xb = xns[t % 2]
            rs, rs_b = emit_rstd(C, R, lambda k, h=h: h[:, k, :], hb, KC, NT, D)
            for k in range(KC):
                S.op("dve", STT(xn[:, k, :], h[:, k, :], gain[:, k:k + 1], rs[:, :], ALU.mult, ALU.mult),
                     reads=[hb[k], rs_b, gain_b], writes=[xb[k]])

        load_h(0)
        st13.ensure(1)
        do_norm(0)
        for t in range(NTILE):
            hs = t % 2
            h, hb, hds = R["h"][hs]
            xn, xb = xns[t % 2]
            tsl = slice(t * NT, (t + 1) * NT)
            if t + 1 < NTILE:
                load_h(t + 1)
            st13.ensure(t * NJ + 1)
            for j in range(NJ):
                w, wb, _ = st13.get(t * NJ + j)
                if j == NJ - 4:
                    st2.ensure(t * KC + 1)
                g_, gb = G[j % 2]
                u_, ub = U[j % 2]
                S.pe_group([MM(g_[:, :], w[:, (2 * k) * 128:(2 * k + 1) * 128], xn[:, k, :], k == 0, k == KC - 1)
                            for k in range(KC)], reads=[wb] + xb, writes=[gb])
                S.pe_group([MM(u_[:, :], w[:, (2 * k + 1) * 128:(2 * k + 2) * 128], xn[:, k, :], k == 0, k == KC - 1)
                            for k in range(KC)], reads=[wb] + xb, writes=[ub])
                sg, sgb = sgs[j % 2]
                S.op("act", ACT(sg[:, :], g_[:, :], AF.Silu), reads=[gb], writes=[sgb])
                S.op("dve", TT(hid[:, j, :], sg[:, :], u_[:, :], ALU.mult), reads=[sgb, ub], writes=[hidb[j]])
            for m in range(KC):
                w, wb, _ = st2.get(t * KC + m)
                if m == 2 and t + 1 < NTILE:
                    do_norm(t + 1)
                if m == KC - 2 and t + 1 < NTILE:
                    st13.ensure((t + 1) * NJ + 1)
                y_, yb = Y[m % 2]
                S.pe_group([MM(y_[:, :], w[:, j * 128:(j + 1) * 128], hid[:, j, :], j == 0, j == NJ - 1)
                            for j in range(NJ)], reads=[wb] + hidb, writes=[yb])
                S.op("dve", STT(h[:, m, :], y_[:, :], 0.5, h[:, m, :], ALU.mult, ALU.add),
                     reads=[yb, hb[m]], writes=[hb[m]])
                if m == 0:
                    fin = hst[hs].cnt + 16 * KC
                S.dma("sp", hT_d[m, :, tsl], h[:, m, :], hst[hs], reads=[hb[m]], group_final=fin)


def emit_pin(C, x_d, hT_d, D=D_MODEL, T=TOK):
    S = C.S
    K = C.K
    KC = D // 128
    with C.phase():
        xs = [(C.sb([128, D], F32, "x"), Buf(), S.dsem("x")) for _ in range(8)]
        hst = [(C.sb([128, KC, 512], F32, "hst"), bufs(KC), S.dsem("hst")) for _ in range(2)]
        ps = [(C.ps([128, 512], F32, "tp"), Buf()) for _ in range(2)]
        n = 0
        for grp in range(T // 512):
            xt = []
            for i in range(4):
                x, xb, xds = xs[(grp % 2) * 4 + i]
                t0 = grp * 512 + i * 128
                S.dma("sp", x[:, :], x_d[t0:t0 + 128, :], xds, writes=[xb])
                xt.append((x, xb))
            hs, hsb, hds = hst[grp % 2]
            for k in range(KC):
                p, pb = ps[n % 2]
                S.pe_group([lambda e, p=p, x=xt[i][0], i=i, k=k: e.transpose(
                    p[:, i * 128:(i + 1) * 128], x[:, k * 128:(k + 1) * 128], K["ident"][:, :]) for i in range(4)],
                    reads=[b for (_, b) in xt] + [K["ident_b"]], writes=[pb])
                eng = "act" if n % 2 == 0 else "dve"
                if eng == "act":
                    S.op("act", ACT(hs[:, k, :], p[:, :], AF.Copy), reads=[pb], writes=[hsb[k]])
                else:
                    S.op("dve", CP(hs[:, k, :], p[:, :]), reads=[pb], writes=[hsb[k]])
                n += 1
            S.dma("sp", hT_d[:, :, grp * 512:(grp + 1) * 512].rearrange("k p n -> p k n"), hs[:, :, :], hds, reads=hsb)


def emit_final(C, hT_d, gain_d, out_d, D=D_MODEL, T=TOK, NT=512):
    S = C.S
    K = C.K
    KC = D // 128
    with C.phase():
        gain = C.sb([128, KC], F32, "gain")
        gain_b = Buf()
        S.dma("sp", gain[:, :], gain_d[:, :], S.dsem(), writes=[gain_b])
        R = norm_res(C, KC, NT)
        xn = C.sb([128, KC, NT], F32, "xn32")
        xb = bufs(KC)
        osb = [(C.sb([128, D], F32, "osb"), Buf(), S.dsem("o")) for _ in range(2)]
        ps = [(C.ps([128, 512], F32, "tp"), Buf()) for _ in range(2)]
        n = 0
        no = 0
        last = []
        for t in range(T // NT):
            h, hb, hds = R["h"][t % 2]
            tsl = slice(t * NT, (t + 1) * NT)
            S.dma("sp", h[:, :, :], hT_d[:, :, tsl].rearrange("k p n -> p k n"), hds, writes=hb)
            rs, rs_b = emit_rstd(C, R, lambda k, h=h: h[:, k, :], hb, KC, NT, D)
            for k in range(KC):
                S.op("dve", STT(xn[:, k, :], h[:, k, :], gain[:, k:k + 1], rs[:, :], ALU.mult, ALU.mult),
                     reads=[hb[k], rs_b, gain_b], writes=[xb[k]])
            for i in range(NT // 128):
                o, ob, ods = osb[no % 2]
                no += 1
                for k0 in range(0, KC, 4):
                    p, pb = ps[n % 2]
                    S.pe_group([lambda e, p=p, kk=kk, k0=k0, i=i: e.transpose(
                        p[:, kk * 128:(kk + 1) * 128], xn[:, k0 + kk, i * 128:(i + 1) * 128], K["ident"][:, :])
                        for kk in range(4)], reads=xb[k0:k0 + 4] + [K["ident_b"]], writes=[pb])
                    if n % 2 == 0:
                        S.op("act", ACT(o[:, k0 * 128:(k0 + 4) * 128], p[:, :], AF.Copy), reads=[pb], writes=[ob])
                    else:
                        S.op("dve", CP(o[:, k0 * 128:(k0 + 4) * 128], p[:, :]), reads=[pb], writes=[ob])
                    n += 1
                t0 = t * NT + i * 128
                last.append(S.dma("sp", out_d[t0:t0 + 128, :], o[:, :], ods, reads=[ob]))
        for tk in last[-2:]:
            S.wait_tok("sp", tk)


def emit_oproj(C, g, gb, wr_d, hT_d, KCin, T=TOK, KCout=D_MODEL // 128, NT=512, scale=1.0):
    S = C.S
    ws = [(C.sb([128, KCin * 128], BF16, "wo"), Buf(), S.dsem("wo", sw=True)) for _ in range(2)]
    hr = [(C.sb([128, T], F32, "hrow"), Buf(), S.dsem("hr"), S.dsem("hrs")) for _ in range(2)]
    Y = [(C.ps([128, NT], F32, "Y"), Buf()) for _ in range(2)]
    st = Stream(S, "pool", ws)
    for m in range(KCout):
        st.add(lambda tt, m=m: (tt[:, :], wr_d[m, :, :], CAST_KW))
    n = 0
    for m in range(KCout):
        w, wb, _ = st.get(m)
        h, hb, hds, hss = hr[m % 2]
        S.dma("sp", h[:, :], hT_d[m, :, :], hds, writes=[hb])
        for t in range(T // NT):
            y_, yb = Y[n % 2]
            n += 1
            tsl = slice(t * NT, (t + 1) * NT)
            S.pe_group([MM(y_[:, :], w[:, k * 128:(k + 1) * 128], g[:, k, tsl], k == 0, k == KCin - 1)
                        for k in range(KCin)], reads=[wb] + list(gb), writes=[yb])
            S.op("dve", STT(h[:, tsl], y_[:, :], scale, h[:, tsl], ALU.mult, ALU.add), reads=[yb, hb], writes=[hb])
        S.dma("sp", hT_d[m, :, :], h[:, :], hss, reads=[hb])


def emit_mixC_1(C, hT_d, gain_d, winr, cuT_d, bT_d, cux_d, D=D_MODEL, T=TOK, NT=512, gather=None):
    S = C.S
    KC = D // 128
    with C.phase():
        gain = C.sb([128, KC], F32, "gain")
        gain_b = Buf()
        S.dma("sp", gain[:, :], gain_d[:, :], S.dsem(), writes=[gain_b])
        xn = C.sb([128, KC, T], BF16, "xnf")
        xb = bufs(KC)
        emit_norm_full(C, hT_d, gain, gain_b, xn, xb, KC, T)
        ws = [(C.sb([128, 3 * KC * 128], BF16, "win"), Buf(), S.dsem("win", sw=True)) for _ in range(2)]
        st = Stream(S, "pool", ws)
        for m in range(KC):
            st.add(lambda tt, m=m: (tt[:, :], winr[m, :, :], CAST_KW))
        P = [[(C.ps([128, NT], F32, "P"), Buf()) for _ in range(2)] for _ in range(3)]
        csb = [(C.sb([128, NT], F32, "c"), Buf()) for _ in range(2)]
        cur = [(C.sb([128, T], F32, "cu"), Buf(), S.dsem("cu")) for _ in range(2)]
        br = [(C.sb([128, T], F32, "b"), Buf(), S.dsem("b")) for _ in range(2)]
        xds = [S.dsem("cux") for _ in range(2)]
        n = 0
        for m in range(KC):
            w, wb, _ = st.get(m)
            cu, cub, cuds = cur[m % 2]
            b_, bb, bds = br[m % 2]
            for t in range(T // NT):
                tsl = slice(t * NT, (t + 1) * NT)
                pp = [P[s][n % 2] for s in range(3)]
                for s in range(3):
                    S.pe_group([MM(pp[s][0][:, :], w[:, (s * KC + k) * 128:(s * KC + k + 1) * 128], xn[:, k, tsl],
                                   k == 0, k == KC - 1) for k in range(KC)], reads=[wb] + xb, writes=[pp[s][1]])
                c_, cb = csb[n % 2]
                n += 1
                S.op("act", ACT(b_[:, tsl], pp[0][0][:, :], AF.Copy), reads=[pp[0][1]], writes=[bb])
                S.op("act", ACT(c_[:, :], pp[1][0][:, :], AF.Copy), reads=[pp[1][1]], writes=[cb])
                S.op("dve", TT(cu[:, tsl], c_[:, :], pp[2][0][:, :], ALU.mult), reads=[cb, pp[2][1]], writes=[cub])
            S.dma("sp", cuT_d[m, :, :], cu[:, :], cuds, reads=[cub])
            S.dma("sp", bT_d[m, :, :], b_[:, :], bds, reads=[bb])
            S.dma("sp", cux_d[0][:, 2 * m:2 * m + 2], cu[:, T - 2:T], xds[m % 2], reads=[cub])
        if gather is not None:
            S.barrier(recycle=False)
            emit_gather(C, cux_d, gather, 1)


def emit_mixC_2(C, hT_d, cuT_d, bT_d, cuh_d, convw_d, hmask_d, woutr, D=D_MODEL, T=TOK):
    S = C.S
    KC = D // 128
    with C.phase():
        cw = C.sb([128, KC, 3], F32, "convw")
        cwb = Buf()
        S.dma("sp", cw[:, :, :], convw_d[:, :, :], S.dsem(), writes=[cwb])
        hm = C.sb([128, 1], F32, "hmask")
        hmb = Buf()
        S.dma("sp", hm[:, :], hmask_d[:, :], S.dsem(), writes=[hmb])
        g = C.sb([128, KC, T], BF16, "g")
        gb = bufs(KC)
        cur = [(C.sb([128, T + 2], F32, "cu"), Buf(), Buf(), S.dsem("cu"), S.dsem("cuh")) for _ in range(2)]
        br = [(C.sb([128, T], F32, "b"), Buf(), S.dsem("b")) for _ in range(2)]
        zs = [(C.sb([128, T], F32, "z"), Buf()) for _ in range(2)]
        for m in range(KC):
            cu, cub, chb, cuds, chds = cur[m % 2]
            b_, bb, bds = br[m % 2]
            z, zb = zs[m % 2]
            S.dma("sp", cu[:, 2:T + 2], cuT_d[m, :, :], cuds, writes=[cub])
            S.dma("sp", cu[:, 0:2], cuh_d[0][:, 2 * m:2 * m + 2], chds, writes=[chb])
            S.dma("sp", b_[:, :], bT_d[m, :, :], bds, writes=[bb])
            S.op("dve", TS(cu[:, 0:2], cu[:, 0:2], hm[:, 0:1], None, ALU.mult), reads=[hmb], writes=[chb])
            S.op("dve", TS(z[:, :], cu[:, 2:T + 2], cw[:, m, 2:3], None, ALU.mult), reads=[cub, cwb], writes=[zb])
            S.op("dve", STT(z[:, :], cu[:, 1:T + 1], cw[:, m, 1:2], z[:, :], ALU.mult, ALU.add),
                 reads=[cub, chb, cwb, zb], writes=[zb])
            S.op("dve", STT(z[:, :], cu[:, 0:T], cw[:, m, 0:1], z[:, :], ALU.mult, ALU.add),
                 reads=[cub, chb, cwb, zb], writes=[zb])
            S.op("pool", TT(g[:, m, :], z[:, :], b_[:, :], ALU.mult), reads=[zb, bb], writes=[gb[m]])
        emit_oproj(C, g, gb, woutr, hT_d, KC, T, KCout=KC)


def emit_dram_copy(C, dst, src):
    S = C.S
    ds = S.dsem("cp")
    fin = ds.cnt + 16 * dst.shape[0]
    for k in range(dst.shape[0]):
        S.dma("sp", dst[k, :, :], src[k, :, :], ds, group_final=fin)
    S.barrier()


TWO_PI = 6.283185307179586
CW1 = 6.28125
CW2 = float(np.float32(np.frombuffer(np.uint32(np.frombuffer(np.float32(TWO_PI - CW1).tobytes(), np.uint32)[0] & 0xFFFFF000).tobytes(), np.float32)[0]))
CW3 = float(np.float32(TWO_PI - CW1 - CW2))
PI_LO = 3.1415925


def _wrap(S, r, rb, m, mb):
    S.op("dve", TS(m, r, float(np.pi), -TWO_PI, ALU.is_gt, ALU.mult), reads=[rb], writes=[mb])
    S.op("dve", TT(r, r, m, ALU.add), reads=[rb, mb], writes=[rb])
    S.op("dve", TS(m, r, -float(np.pi), TWO_PI, ALU.is_lt, ALU.mult), reads=[rb], writes=[mb])
    S.op("dve", TT(r, r, m, ALU.add), reads=[rb, mb], writes=[rb])
    S.op("dve", TS(r, r, PI_LO, -PI_LO, ALU.min, ALU.max), reads=[rb], writes=[rb])


def emit_tables(C, pos_d, frq_d, tab_d, dils, T=TOK):
    S = C.S
    with C.phase():
        pos_i = C.sb([128, T], I32, "posi")
        pos_f = C.sb([128, T], F32, "posf")
        pb = Buf()
        S.dma("sp", pos_i[:, :], pos_d[0:1, :].partition_broadcast(128), S.dsem(), writes=[pb])
        S.op("dve", CP(pos_f[:, :], pos_i[:, :]), reads=[pb], writes=[pb])
        x = C.sb([128, T], F32, "ang")
        r = C.sb([128, T], F32, "red")
        m = C.sb([128, T], F32, "msk")
        kf = C.sb([128, T], F32, "kf")
        ki = C.sb([128, T], I32, "ki")
        sn = C.sb([128, T], F32, "sin")
        cs = C.sb([128, T], F32, "cos")
        xb, rb, mb, kb, snb, csb = bufs(6)
        for i, d in enumerate(dils):
            fr = C.sb([128, 2], F32, "frq")
            fb = Buf()
            S.dma("sp", fr[:, :], frq_d[i, :, :], S.dsem(), writes=[fb])
            if d == 1:
                S.op("dve", TS(x[:, :], pos_f[:, :], fr[:, 0:1], None, ALU.mult), reads=[pb, fb], writes=[xb])
            else:
                S.op("dve", TS(x[:, :].rearrange("p (r m) -> p m r", r=d), pos_f[:, :].rearrange("p (m r) -> p m r", r=d),
                               fr[:, 0:1], None, ALU.mult), reads=[pb, fb], writes=[xb])
            S.op("dve", TS(ki[:, :], x[:, :], 1.0 / TWO_PI, None, ALU.mult), reads=[xb], writes=[kb])
            S.op("dve", CP(kf[:, :], ki[:, :]), reads=[kb], writes=[kb])
            S.op("dve", STT(r[:, :], kf[:, :], -CW1, x[:, :], ALU.mult, ALU.add), reads=[kb, xb], writes=[rb])
            S.op("dve", STT(r[:, :], kf[:, :], -CW2, r[:, :], ALU.mult, ALU.add), reads=[kb, rb], writes=[rb])
            S.op("dve", STT(r[:, :], kf[:, :], -CW3, r[:, :], ALU.mult, ALU.add), reads=[kb, rb], writes=[rb])
            _wrap(S, r[:, :], rb, m[:, :], mb)
            S.op("act", ACT(sn[:, :], r[:, :], AF.Sin), reads=[rb], writes=[snb])
            S.op("dve", TS(sn[:, :], sn[:, :], fr[:, 1:2], None, ALU.mult), reads=[snb, fb], writes=[snb])
            S.op("dve", TS(r[:, :], r[:, :], float(np.pi / 2), None, ALU.add), reads=[rb, snb], writes=[rb])
            _wrap(S, r[:, :], rb, m[:, :], mb)
            S.op("act", ACT(cs[:, :], r[:, :], AF.Sin), reads=[rb], writes=[csb])
            S.dma("sp", tab_d[i, 0, :, :], cs[:, :], S.dsem(), reads=[csb])
            S.dma("sp", tab_d[i, 1, :, :], sn[:, :], S.dsem(), reads=[snb])


def emit_rope_evac(S, p, pb, st_out, stb, ct, sn, tb_, q32, q32b, tA, tAb, tB, tBb, h):
    hi = 32 + h
    if p is not None:
        S.op("act", ACT(q32[:, :], p[:, :], AF.Copy), reads=[pb], writes=[q32b])
    S.op("pool", CP(st_out, q32[:, :]), reads=[q32b], writes=[stb])
    S.op("dve", TT(tA[0:hi, :], q32[0:hi, :], ct[0:hi, :], ALU.mult), reads=[q32b, tb_], writes=[tAb])
    S.op("dve", TT(tB[0:h, :], q32[32:hi, :], sn[32:hi, :], ALU.mult), reads=[q32b, tb_], writes=[tBb])
    S.op("dve", TT(tB[32:hi, :], q32[0:h, :], sn[0:h, :], ALU.mult), reads=[q32b, tb_], writes=[tBb])
    S.op("dve", TT(st_out[0:hi, :], tA[0:hi, :], tB[0:hi, :], ALU.add), reads=[tAb, tBb], writes=[stb])


def halo_off(dils):
    off = [0]
    for d in dils:
        off.append(off[-1] + d)
    return off


def emit_mixA_1(C, hT_d, gain_d, wqkv_r, tab_d, QT_d, KT_d, V_d, KTx_d, Vx_d, dils=A_DIL, H=A_HEADS,
                D=D_MODEL, T=TOK, NT=512, gather=None):
    S = C.S
    KC = D // 128
    HB = H // 4
    NB = T // 128
    off = halo_off(dils)
    with C.phase():
        gain = C.sb([128, KC], F32, "gain")
        gain_b = Buf()
        S.dma("sp", gain[:, :], gain_d[:, :], S.dsem(), writes=[gain_b])
        xn = C.sb([128, KC, T], BF16, "xnf")
        xb = bufs(KC)
        ws = [(C.sb([128, 4 * KC * 128], BF16, "wqkv"), Buf(), S.dsem("wqkv", sw=True)) for _ in range(2)]
        st = Stream(S, "pool", ws)
        for i in range(len(dils) * 3 * HB):
            st.add(lambda tt, i=i: (tt[:, :], wqkv_r[i, :, :], CAST_KW))
        stg = [(C.sb([128, 4, T], BF16, "stg"), Buf(), S.dsem("stg"), S.dsem("stgx")) for _ in range(2)]
        P = [(C.ps([128, NT], F32, "P"), Buf()) for _ in range(2)]
        ct = C.sb([128, T], F32, "ct")
        sn = C.sb([128, T], F32, "sn")
        tb_ = Buf()
        tA = [(C.sb([128, NT], F32, "tA"), Buf()) for _ in range(2)]
        tB = [(C.sb([128, NT], F32, "tB"), Buf()) for _ in range(2)]
        q32s = [(C.sb([128, NT], F32, "q32"), Buf()) for _ in range(2)]
        for (t_, b_) in tB:
            S.op("pool", MEMSET(t_[:, :], 0.0), writes=[b_])
        RN = norm_res(C, KC, 256)
        n = 0
        ns = 0
        wi = 0
        for gi, d in enumerate(dils):
            nb = NB // d
            emit_norm_full(C, hT_d, gain, gain_b, xn, xb, KC, T, d=d, R=RN)
            tds = S.dsem()
            fin = tds.cnt + 32
            S.dma("sp", ct[:, :], tab_d[gi, 0, :, :], tds, writes=[tb_], group_final=fin)
            S.dma("sp", sn[:, :], tab_d[gi, 1, :, :], tds, writes=[tb_], group_final=fin)
            for s in range(3):
                for hb in range(HB):
                    w, wb, _ = st.get(wi)
                    wi += 1
                    sg, sgb, sds, sxs = stg[ns % 2]
                    ns += 1
                    if s < 2:
                        for hh in range(4):
                            for t in range(T // NT):
                                tsl = slice(t * NT, (t + 1) * NT)
                                p, pb = P[n % 2]
                                a_, ab = tA[n % 2]
                                b2, bb = tB[n % 2]
                                n += 1
                                S.pe_group([MM(p[:, :], w[:, (hh * KC + k) * 128:(hh * KC + k + 1) * 128], xn[:, k, tsl],
                                               k == 0, k == KC - 1) for k in range(KC)], reads=[wb] + xb, writes=[pb])
                                q32, q32b = q32s[n % 2]
                                emit_rope_evac(S, p, pb, sg[:, hh, tsl], sgb, ct[:, tsl], sn[:, tsl], tb_,
                                               q32, q32b, a_, ab, b2, bb, 16)
                        dst = (QT_d if s == 0 else KT_d)[gi, 4 * hb:4 * hb + 4, :, :].rearrange("h p t -> p h t")
                        S.dma("sp", dst, sg[:, :, :], sds, reads=[sgb])
                        if s == 1:
                            fin = sxs.cnt + 16 * d
                            for r in range(d):
                                c0 = (r * nb + nb - 1) * 128
                                S.dma("sp", KTx_d[4 * hb:4 * hb + 4, :, (off[gi] + r) * 128:(off[gi] + r + 1) * 128]
                                      .rearrange("h p c -> p h c"), sg[:, :, c0:c0 + 128], sxs, reads=[sgb], group_final=fin)
                    else:
                        sg4 = sg[:, :, :].rearrange("p h (b c) -> p h b c", c=128)
                        for bi in range(NB):
                            p, pb = P[n % 2]
                            n += 1
                            S.pe_group([MM(p[:, :], xn[:, k, bi * 128:(bi + 1) * 128], w[:, k * 512:(k + 1) * 512],
                                           k == 0, k == KC - 1) for k in range(KC)], reads=[wb] + xb, writes=[pb])
                            src = p[:, :].rearrange("p (h c) -> p h c", h=4)
                            if n % 2 == 0:
                                S.op("act", ACT(sg4[:, :, bi, :], src, AF.Copy), reads=[pb], writes=[sgb])
                            else:
                                S.op("dve", CP(sg4[:, :, bi, :], src), reads=[pb], writes=[sgb])
                        S.dma("sp", V_d[gi, 4 * hb:4 * hb + 4, :, :].rearrange("h p x -> p h x"), sg[:, :, :], sds, reads=[sgb])
                        fin = sxs.cnt + 16 * d
                        for r in range(d):
                            bi = r * nb + nb - 1
                            S.dma("sp", Vx_d[4 * hb:4 * hb + 4, :, off[gi] + r, :].rearrange("h p c -> p h c"),
                                  sg4[:, :, bi, :], sxs, reads=[sgb], group_final=fin)
        if gather is not None:
            S.barrier(recycle=False)
            emit_gather(C, KTx_d, gather[0], H)
            emit_gather(C, Vx_d, gather[1], H)


def emit_mixA_2(C, hT_d, QT_d, KT_d, V_d, KTh_d, Vh_d, hmask_d, mask_d, wo_r, dils=A_DIL, H=A_HEADS,
                D=D_MODEL, T=TOK, xchg=None):
    S = C.S
    K = C.K
    KC = D // 128
    NB = T // 128
    off = halo_off(dils)
    nhmax = max(dils)
    scale = float(A_HEAD_DIM) ** -0.5
    with C.phase():
        hm = C.sb([128, 1], F32, "hmask")
        hmb = Buf()
        S.dma("sp", hm[:, :], hmask_d[:, :], S.dsem(), writes=[hmb])
        mk32 = C.sb([128, 256], F32, "mk32")
        mkb = Buf()
        S.dma("sp", mk32[:, :], mask_d[:, :], S.dsem(), writes=[mkb])
        M = C.sb([128, 256], BF16, "M")
        Mh = C.sb([128, 256], BF16, "Mh")
        Mb, Mhb = Buf(), Buf()
        S.op("dve", CP(M[:, :], mk32[:, :]), reads=[mkb], writes=[Mb])
        S.op("dve", CP(Mh[:, 128:256], mk32[:, 128:256]), reads=[mkb], writes=[Mhb])
        S.op("dve", TS(Mh[:, 0:128], mk32[:, 0:128], hm[:, 0:1], None, ALU.mult), reads=[mkb, hmb], writes=[Mhb])
        oT = C.sb([128, H, T], BF16, "oT")
        oTb = bufs(H)
        slots = []
        for _ in range(2):
            slots.append(dict(q=C.sb([128, T], BF16, "q"), k=C.sb([128, (nhmax + NB) * 128], BF16, "k"),
                              v=C.sb([128, nhmax + NB, 128], BF16, "v"), b=Buf(), ds=S.dsem("qkv")))
        Oa = C.sb([128, T], F32, "Oacc")
        La = C.sb([128, T], F32, "Lacc")
        Oab, Lab = Buf(), Buf()
        pe_ = [(C.sb([128, 256], BF16, "pe"), Buf()) for _ in range(3)]
        pm_ = [(C.sb([128, 256], BF16, "pm"), Buf()) for _ in range(SKEW + 2)]
        pS = [(C.ps([128, 512], F32, "pS"), Buf()) for _ in range(2)]
        pO = [(C.ps([128, 512], F32, "pO"), Buf()) for _ in range(2)]
        pL = [(C.ps([128, 512], F32, "pL"), Buf()) for _ in range(2)]
        items = [(h, gi) for h in range(H) for gi in range(len(dils))]
        hK = bufs(H)
        hV = bufs(H)
        if xchg is not None:
            KTx_d, Vx_d, KTg_d, Vg_d, oh_d, do_coll = xchg
            XH = off[-1] * 128
            gK = bufs(H)
            gV = bufs(H)
            if do_coll:
                cs = S.dsem("cc", sw=True)
                for h in range(H):
                    S.coll("AllGather", GROUPS, KTx_d[h], KTg_d[h], cs, writes=[gK[h]])
                    S.coll("AllGather", GROUPS, Vx_d[h], Vg_d[h], cs, writes=[gV[h]])
            oh = C.sb([128, CPB], F32, "oh")
            ohb = Buf()
            S.dma("sp", oh[:, :], oh_d[:, :], S.dsem(), writes=[ohb])
            cand = (C.sb([128, CPB, XH], BF16, "cand"), Buf(), S.dsem("cand"))
            sacc = (C.sb([128, XH], BF16, "sacc"), Buf(), S.dsem("sacc"))
            selected = set()

            def select(h):
                if h in selected:
                    return
                selected.add(h)
                c, cb, cds = cand
                a, ab, ads = sacc
                for (src, gb_, dst, hb_) in ((KTg_d[h], gK[h], KTh_d[h], hK[h]),
                                              (Vg_d[h].rearrange("s p b c -> s p (b c)"), gV[h],
                                               Vh_d[h].rearrange("p b c -> p (b c)"), hV[h])):
                    S.dma("sp", c[:, :, :], src.rearrange("s p x -> p s x"), cds, reads=[gb_], writes=[cb])
                    S.op("dve", TS(a[:, :], c[:, 0, :], oh[:, 0:1], None, ALU.mult), reads=[cb, ohb], writes=[ab])
                    for s_ in range(1, CPB):
                        S.op("dve", STT(a[:, :], c[:, s_, :], oh[:, s_:s_ + 1], a[:, :], ALU.mult, ALU.add),
                             reads=[cb, ohb, ab], writes=[ab])
                    S.dma("sp", dst, a[:, :], ads, reads=[ab], writes=[hb_])

        def load(i):
            h, gi = items[i]
            if xchg is not None:
                select(h)
            d = dils[gi]
            sl = slots[i % 2]
            fin = sl["ds"].cnt + 16 * 5
            kw = dict(writes=[sl["b"]], group_final=fin)
            S.dma("sp", sl["q"][:, :], QT_d[gi, h, :, :], sl["ds"], **kw)
            S.dma("sp", sl["k"][:, 0:d * 128], KTh_d[h, :, off[gi] * 128:(off[gi] + d) * 128], sl["ds"], reads=[hK[h]], **kw)
            S.dma("sp", sl["k"][:, d * 128:(d + NB) * 128], KT_d[gi, h, :, :], sl["ds"], **kw)
            S.dma("sp", sl["v"][:, 0:d, :], Vh_d[h, :, off[gi]:off[gi] + d, :], sl["ds"], reads=[hV[h]], **kw)
            S.dma("sp", sl["v"][:, d:d + NB, :].rearrange("p b c -> p (b c)"), V_d[gi, h, :, :], sl["ds"], **kw)

        load(0)
        n = 0
        for i, (h, gi) in enumerate(items):
            if i + 1 < len(items):
                load(i + 1)
            d = dils[gi]
            nb = NB // d
            sl = slots[i % 2]
            q, k_, v, slb = sl["q"], sl["k"], sl["v"], sl["b"]
            pend = []

            def emit_pv(st_):
                bi, r, a, prev, cur, m_, mb, k2 = st_
                po, pob = pO[k2 % 2]
                pl, plb = pL[k2 % 2]
                S.pe_group([MM(po[:, 0:128], v[:, prev, :], m_[:, 0:128], True, False),
                            MM(po[:, 0:128], v[:, cur, :], m_[:, 128:256], False, True)],
                           reads=[slb, mb], writes=[pob])
                S.pe_group([MM(pl[:, 0:128], K["ones"][:, :], m_[:, 0:128], True, False),
                            MM(pl[:, 0:128], K["ones"][:, :], m_[:, 128:256], False, True)],
                           reads=[K["ones_b"], mb], writes=[plb])
                if d == 1:
                    oc = Oa[:, bi * 128:(bi + 1) * 128]
                    lc = La[:, bi * 128:(bi + 1) * 128]
                else:
                    oc = Oa[:, :].rearrange("p (a i r) -> p a r i", i=128, r=d)[:, a, r, :]
                    lc = La[:, :].rearrange("p (a i r) -> p a r i", i=128, r=d)[:, a, r, :]
                if gi == 0:
                    S.op("act", ACT(oc, po[:, 0:128], AF.Copy), reads=[pob], writes=[Oab])
                    S.op("dve", CP(lc, pl[:, 0:128]), reads=[plb], writes=[Lab])
                else:
                    S.op("dve", TT(oc, po[:, 0:128], oc, ALU.add), reads=[pob, Oab], writes=[Oab])
                    S.op("dve", TT(lc, pl[:, 0:128], lc, ALU.add), reads=[plb, Lab], writes=[Lab])

            for bi in range(NB):
                r, a = bi // nb, bi % nb
                cur = d + bi
                prev = (d + bi - 1) if a > 0 else r
                msk, mskb = (M, Mb) if a > 0 else (Mh, Mhb)
                ps, psb = pS[n % 2]
                e_, eb = pe_[n % len(pe_)]
                m_, mb = pm_[n % len(pm_)]
                qs = q[:, bi * 128:(bi + 1) * 128]
                S.pe_group([MM(ps[:, 0:128], k_[:, prev * 128:(prev + 1) * 128], qs, True, True),
                            MM(ps[:, 128:256], k_[:, cur * 128:(cur + 1) * 128], qs, True, True)],
                           reads=[slb], writes=[psb])
                S.op("act", ACT(e_[:, :], ps[:, 0:256], AF.Exp, scale=scale), reads=[psb], writes=[eb])
                S.op("pool", TT(m_[:, :], e_[:, :], msk[:, :], ALU.mult), reads=[eb, mskb], writes=[mb])
                pend.append((bi, r, a, prev, cur, m_, mb, n))
                n += 1
                if len(pend) > SKEW:
                    emit_pv(pend.pop(0))
            while pend:
                emit_pv(pend.pop(0))
            if gi == len(dils) - 1:
                S.op("dve", RECIP(La[:, :], La[:, :]), reads=[Lab], writes=[Lab])
                S.op("pool", TT(oT[:, h, :], Oa[:, :], La[:, :], ALU.mult), reads=[Oab, Lab], writes=[oTb[h]])
        emit_oproj(C, oT, oTb, wo_r, hT_d, H, T, KCout=KC)


def emit_mixB_1a(C, hT_d, gain_d, gq_d, gkv_d, winr, tab_d, cqn_d, kvx_d, krx_d,
                 QR=B_Q_RANK, KVR=B_KV_RANK, D=D_MODEL, T=TOK, NT=512, gather=None):
    S = C.S
    KC = D // 128
    NQ = QR // 128
    NKV = KVR // 128
    NCH = NQ + NKV + 1
    with C.phase():
        gain = C.sb([128, KC], F32, "gain")
        gq = C.sb([128, NQ], F32, "gq")
        gkv = C.sb([128, NKV], F32, "gkv")
        gain_b, gqb, gkvb = bufs(3)
        S.dma("sp", gain[:, :], gain_d[:, :], S.dsem(), writes=[gain_b])
        S.dma("sp", gq[:, :], gq_d[:, :], S.dsem(), writes=[gqb])
        S.dma("sp", gkv[:, :], gkv_d[:, :], S.dsem(), writes=[gkvb])
        xn = C.sb([128, KC, T], BF16, "xnf")
        xb = bufs(KC)
        with contextlib.ExitStack() as nst:
            old = C.stack
            C.stack = nst
            emit_norm_full(C, hT_d, gain, gain_b, xn, xb, KC, T)
            C.stack = old
            S.barrier(recycle=False)
        ct = C.sb([128, T], F32, "ct")
        sn = C.sb([128, T], F32, "sn")
        tb_ = Buf()
        tds = S.dsem()
        fin = tds.cnt + 32
        S.dma("sp", ct[:, :], tab_d[0, :, :], tds, writes=[tb_], group_final=fin)
        S.dma("sp", sn[:, :], tab_d[1, :, :], tds, writes=[tb_], group_final=fin)
        ws = [(C.sb([128, KC * 128], BF16, "win"), Buf(), S.dsem("win", sw=True)) for _ in range(3)]
        st = Stream(S, "pool", ws)
        for t in range(T // NT):
            for m in range(NCH):
                st.add(lambda tt, m=m: (tt[:, :], winr[m, :, :], CAST_KW))
        c32 = C.sb([128, NCH, NT], F32, "c32")
        cb = bufs(NCH)
        P = [(C.ps([128, NT], F32, "P"), Buf()) for _ in range(2)]
        Rq = norm_res(C, 1, NT, nslot=0)
        cq_st = [(C.sb([128, NQ, NT], BF16, "cqst"), Buf(), S.dsem("cqst")) for _ in range(2)]
        kv_st = [(C.sb([128, NKV, NT], BF16, "kvst"), Buf(), S.dsem("kvst")) for _ in range(2)]
        kr_st = [(C.sb([128, NT], BF16, "krst"), Buf(), S.dsem("krst")) for _ in range(2)]
        tA = (C.sb([128, NT], F32, "tA"), Buf())
        tB = (C.sb([128, NT], F32, "tB"), Buf())
        n = 0
        for t in range(T // NT):
            tsl = slice(t * NT, (t + 1) * NT)
            for m in range(NCH):
                w, wb, _ = st.get(t * NCH + m)
                p, pb = P[n % 2]
                n += 1
                S.pe_group([MM(p[:, :], w[:, k * 128:(k + 1) * 128], xn[:, k, tsl], k == 0, k == KC - 1)
                            for k in range(KC)], reads=[wb] + xb, writes=[pb])
                S.op("act", ACT(c32[:, m, :], p[:, :], AF.Copy), reads=[pb], writes=[cb[m]])
            cq, cqb, cqd = cq_st[t % 2]
            kv, kvb, kvd = kv_st[t % 2]
            kr, krb, krd = kr_st[t % 2]
            rs, rs_b = emit_rstd(C, Rq, lambda k: c32[:, k, :], cb[0:NQ], NQ, NT, QR)
            for k in range(NQ):
                S.op("dve", STT(cq[:, k, :], c32[:, k, :], gq[:, k:k + 1], rs[:, :], ALU.mult, ALU.mult),
                     reads=[cb[k], rs_b, gqb], writes=[cqb])
            S.dma("sp", cqn_d[:, :, tsl].rearrange("k p n -> p k n"), cq[:, :, :], cqd, reads=[cqb])
            rs, rs_b = emit_rstd(C, Rq, lambda k: c32[:, NQ + k, :], cb[NQ:NQ + NKV], NKV, NT, KVR)
            for k in range(NKV):
                S.op("dve", STT(kv[:, k, :], c32[:, NQ + k, :], gkv[:, k:k + 1], rs[:, :], ALU.mult, ALU.mult),
                     reads=[cb[NQ + k], rs_b, gkvb], writes=[kvb])
            S.dma("sp", kvx_d[:, :, tsl].rearrange("k p n -> p k n"), kv[:, :, :], kvd, reads=[kvb])
            emit_rope_evac(S, None, None, kr[:, :], krb, ct[:, tsl], sn[:, tsl], tb_,
                           c32[:, NCH - 1, :], cb[NCH - 1], tA[0], tA[1], tB[0], tB[1], 32)
            S.dma("sp", krx_d[0][:, tsl], kr[:, :], krd, reads=[krb])
        if gather is not None:
            S.barrier(recycle=False)
            emit_gather(C, kvx_d, gather[0], NKV)
            emit_gather(C, krx_d, gather[1], 1)


def emit_mixB_1b(C, cqn_d, wuq_r, tab_d, QB_d, H=B_HEADS, QR=B_Q_RANK, T=TOK, NT=512):
    S = C.S
    NQ = QR // 128
    with C.phase():
        cq = C.sb([128, NQ, T], BF16, "cqn")
        cqb = Buf()
        S.dma("sp", cq[:, :, :], cqn_d[:, :, :].rearrange("k p n -> p k n"), S.dsem(), writes=[cqb])
        ct = C.sb([128, T], F32, "ct")
        sn = C.sb([128, T], F32, "sn")
        tb_ = Buf()
        tds = S.dsem()
        fin = tds.cnt + 32
        S.dma("sp", ct[:, :], tab_d[0, :, :], tds, writes=[tb_], group_final=fin)
        S.dma("sp", sn[:, :], tab_d[1, :, :], tds, writes=[tb_], group_final=fin)
        ws = [(C.sb([128, 2 * NQ * 128], BF16, "wuq"), Buf(), S.dsem("wuq", sw=True)) for _ in range(2)]
        st = Stream(S, "pool", ws)
        for h in range(H):
            st.add(lambda tt, h=h: (tt[:, :], wuq_r[h, :, :], CAST_KW))
        stg = [(C.sb([128, 2, T], BF16, "qst"), Buf(), S.dsem("qst")) for _ in range(2)]
        P = [(C.ps([128, NT], F32, "P"), Buf()) for _ in range(2)]
        tA = [(C.sb([128, NT], F32, "tA"), Buf()) for _ in range(2)]
        tB = [(C.sb([128, NT], F32, "tB"), Buf()) for _ in range(2)]
        q32s = [(C.sb([128, NT], F32, "q32"), Buf()) for _ in range(2)]
        n = 0
        for h in range(H):
            w, wb, _ = st.get(h)
            sg, sgb, sds = stg[h % 2]
            for t in range(T // NT):
                tsl = slice(t * NT, (t + 1) * NT)
                for part in range(2):
                    p, pb = P[n % 2]
                    a_, ab = tA[n % 2]
                    b2, bb = tB[n % 2]
                    q32, q32b = q32s[n % 2]
                    n += 1
                    S.pe_group([MM(p[:, :], w[:, (part * NQ + k) * 128:(part * NQ + k + 1) * 128], cq[:, k, tsl],
                                   k == 0, k == NQ - 1) for k in range(NQ)], reads=[wb, cqb], writes=[pb])
                    if part == 0:
                        S.op("act", ACT(sg[:, 0, tsl], p[:, :], AF.Copy), reads=[pb], writes=[sgb])
                    else:
                        emit_rope_evac(S, p, pb, sg[:, 1, tsl], sgb, ct[:, tsl], sn[:, tsl], tb_,
                                       q32, q32b, a_, ab, b2, bb, 32)
            S.dma("sp", QB_d[h, :, :, :].rearrange("s p t -> p s t"), sg[:, :, :], sds, reads=[sgb])


def emit_mixB_2(C, QB_d, kvg_d, krg_d, wukv_r, qidx_d, kidx_d, oT_d, H=B_HEADS, KVR=B_KV_RANK,
                T=TOK, SK=SEQ, NT=512, gathered=False):
    S = C.S
    K = C.K
    NKV = KVR // 128
    NKT = SK // 128
    scale = float(B_NOPE + B_ROPE) ** -0.5
    with C.phase():
        lat = C.sb([128, NKV, SK], BF16, "lat")
        kr = C.sb([128, SK], BF16, "kr")
        latb, krb = Buf(), Buf()
        if gathered:
            lds = S.dsem()
            fin = lds.cnt + 16 * NKV
            for k in range(NKV):
                S.dma("sp", lat[:, k, :].rearrange("p (s n) -> p s n", n=T), kvg_d[k].rearrange("s p n -> p s n"), lds,
                      writes=[latb], group_final=fin)
            S.dma("sp", kr[:, :].rearrange("p (s n) -> p s n", n=T), krg_d[0].rearrange("s p n -> p s n"), S.dsem(), writes=[krb])
        else:
            S.dma("sp", lat[:, :, :], kvg_d[:, :, :].rearrange("k p n -> p k n"), S.dsem(), writes=[latb])
            S.dma("sp", kr[:, :], krg_d[:, :], S.dsem(), writes=[krb])
        qi = C.sb([128, T], F32, "qidx")
        ki = C.sb([128, NKT], F32, "kidx")
        qib, kib = Buf(), Buf()
        S.dma("sp", qi[:, :], qidx_d[0:1, :].partition_broadcast(128), S.dsem(), writes=[qib])
        S.dma("sp", ki[:, :], kidx_d[:, :], S.dsem(), writes=[kib])
        ws = [(C.sb([128, 2 * NKV * 128], BF16, "wukv"), Buf(), S.dsem("wukv", sw=True)) for _ in range(2)]
        st = Stream(S, "pool", ws)
        for h in range(H):
            st.add(lambda tt, h=h: (tt[:, :], wukv_r[h, :, :], CAST_KW))
        Kn = C.sb([128, SK], BF16, "Kn")
        Knb = Buf()
        V = C.sb([128, NKT, 128], BF16, "V")
        Vb = Buf()
        qs = [(C.sb([128, 2, T], BF16, "q"), Buf(), S.dsem("q")) for _ in range(2)]
        es = [(C.sb([128, NT], BF16, "e"), Buf()) for _ in range(3)]
        pms = [(C.sb([128, NT], BF16, "pm"), Buf()) for _ in range(SKEW + 2)]
        ost = [(C.sb([128, T], BF16, "ost"), Buf(), S.dsem("ost")) for _ in range(2)]
        rl = (C.sb([128, NT], F32, "rl"), Buf())
        pP = [(C.ps([128, NT], F32, "pP"), Buf()) for _ in range(2)]
        pS = [(C.ps([128, NT], F32, "pS"), Buf()) for _ in range(2)]
        pO = [(C.ps([128, NT], F32, "pO"), Buf()) for _ in range(2)]
        pL = [(C.ps([128, NT], F32, "pL"), Buf()) for _ in range(2)]

        def loadq(h):
            q, qb, qd = qs[h % 2]
            S.dma("sp", q[:, :, :], QB_d[h, :, :, :].rearrange("s p t -> p s t"), qd, writes=[qb])

        loadq(0)
        n = 0
        ne = 0
        for h in range(H):
            if h + 1 < H:
                loadq(h + 1)
            w, wb, _ = st.get(h)
            q, qb, _ = qs[h % 2]
            o_, ob, od = ost[h % 2]
            for tt in range(SK // NT):
                p, pb = pP[n % 2]
                n += 1
                S.pe_group([MM(p[:, :], w[:, k * 128:(k + 1) * 128], lat[:, k, tt * NT:(tt + 1) * NT], k == 0, k == NKV - 1)
                            for k in range(NKV)], reads=[wb, latb], writes=[pb])
                if n % 2 == 0:
                    S.op("act", ACT(Kn[:, tt * NT:(tt + 1) * NT], p[:, :], AF.Copy), reads=[pb], writes=[Knb])
                else:
                    S.op("dve", CP(Kn[:, tt * NT:(tt + 1) * NT], p[:, :]), reads=[pb], writes=[Knb])
            for k4 in range(NKT // 4):
                p, pb = pP[n % 2]
                n += 1
                fns = []
                for j in range(4):
                    kt = k4 * 4 + j
                    for k in range(NKV):
                        fns.append(MM(p[:, j * 128:(j + 1) * 128], lat[:, k, kt * 128:(kt + 1) * 128],
                                      w[:, (NKV + k) * 128:(NKV + k + 1) * 128], k == 0, k == NKV - 1))
                S.pe_group(fns, reads=[wb, latb], writes=[pb])
                dst = V[:, k4 * 4:k4 * 4 + 4, :].rearrange("p b c -> p (b c)")
                if n % 2 == 0:
                    S.op("act", ACT(dst, p[:, :], AF.Copy), reads=[pb], writes=[Vb])
                else:
                    S.op("dve", CP(dst, p[:, :]), reads=[pb], writes=[Vb])
            pend = []

            def emit_pv(st_):
                qg, kt, m_, mb = st_
                po, pob = pO[qg % 2]
                pl, plb = pL[qg % 2]
                qsl = slice(qg * NT, (qg + 1) * NT)
                S.pe_group([MM(po[:, :], V[:, kt, :], m_[:, :], kt == 0, kt == NKT - 1)],
                           reads=[Vb, mb], writes=[pob])
                S.pe_group([MM(pl[:, :], K["ones"][:, :], m_[:, :], kt == 0, kt == NKT - 1)],
                           reads=[K["ones_b"], mb], writes=[plb])
                if kt == NKT - 1:
                    r_, rb = rl
                    S.op("dve", RECIP(r_[:, :], pl[:, :]), reads=[plb], writes=[rb])
                    S.op("dve", TT(o_[:, qsl], po[:, :], r_[:, :], ALU.mult), reads=[pob, rb], writes=[ob])

            for qg in range(T // NT):
                qsl = slice(qg * NT, (qg + 1) * NT)
                for kt in range(NKT):
                    ps, psb = pS[ne % 2]
                    e_, eb = es[ne % len(es)]
                    m_, mb = pms[ne % len(pms)]
                    ne += 1
                    ksl = slice(kt * 128, (kt + 1) * 128)
                    S.pe_group([MM(ps[:, :], Kn[:, ksl], q[:, 0, qsl], True, False),
                                MM(ps[:, :], kr[0:64, ksl], q[0:64, 1, qsl], False, True)],
                               reads=[Knb, krb, qb], writes=[psb])
                    S.op("act", ACT(e_[:, :], ps[:, :], AF.Exp, scale=scale), reads=[psb], writes=[eb])
                    S.op("dve", STT(m_[:, :], qi[:, qsl], ki[:, kt:kt + 1], e_[:, :], ALU.is_ge, ALU.mult),
                         reads=[eb, qib, kib], writes=[mb])
                    pend.append((qg, kt, m_, mb))
                    if len(pend) > SKEW:
                        emit_pv(pend.pop(0))
            while pend:
                emit_pv(pend.pop(0))
            S.dma("sp", oT_d[h, :, :], o_[:, :], od, reads=[ob])


def emit_mixB_3(C, hT_d, oT_d, wo_r, H=B_HEADS, D=D_MODEL, T=TOK):
    S = C.S
    with C.phase():
        oT = C.sb([128, H, T], BF16, "oT")
        ob = Buf()
        S.dma("sp", oT[:, :, :], oT_d[:, :, :].rearrange("h p t -> p h t"), S.dsem(), writes=[ob])
        emit_oproj(C, oT, [ob], wo_r, hT_d, H, T, KCout=D // 128)


A_PERM = np.array(list(range(0, 16)) + list(range(32, 48)) + list(range(16, 32)) + list(range(48, 128)))


def fm_w(W, chunks, group=1):
    Din = W.shape[0]
    KC = Din // 128
    M = len(chunks[0])
    nb = len(chunks) // group
    out = np.empty((nb, 128, group, KC, M), np.float32)
    Wp = np.concatenate([W, np.zeros((Din, 1), W.dtype)], axis=1)
    for b in range(nb):
        for g in range(group):
            cols = np.asarray(chunks[b * group + g])
            out[b, :, g] = Wp[:, cols].reshape(KC, 128, M).transpose(1, 0, 2)
    return out.reshape(nb, 128, group * KC * M)


def tm_w(W, blocks):
    KC = W.shape[0] // 128
    out = np.empty((len(blocks), 128, KC, len(blocks[0])), np.float32)
    for b, cols in enumerate(blocks):
        out[b] = W[:, cols].reshape(KC, 128, len(cols)).transpose(1, 0, 2)
    return out.reshape(len(blocks), 128, -1)


def chunks_of(n0, n):
    return [n0 + np.arange(m * 128, (m + 1) * 128) for m in range(n // 128)]


def gain_l(g):
    return np.ascontiguousarray(np.asarray(g, np.float32).reshape(-1, 128).T)


def lay_w13(w13):
    D, F2 = w13.shape
    DFF = F2 // 2
    KC, NJ = D // 128, DFF // 128
    a = w13.reshape(KC, 128, 2, NJ, 128)
    return np.ascontiguousarray(a.transpose(3, 1, 0, 2, 4)).reshape(NJ, 128, KC * 256)


def lay_w2(w2):
    DFF, D = w2.shape
    KC, NJ = D // 128, DFF // 128
    a = w2.reshape(NJ, 128, KC, 128)
    return np.ascontiguousarray(a.transpose(2, 1, 0, 3)).reshape(KC, 128, NJ * 128)


def lay_sq(w):
    return fm_w(w, chunks_of(0, w.shape[1]))


def lay_wqkv(W, ng, H):
    HB = H // 4
    blks = []
    for g in range(ng):
        for s in range(3):
            for hb in range(HB):
                base = lambda h: ((g * 3 + s) * H + h) * 128
                if s < 2:
                    blks.append(fm_w(W, [base(4 * hb + hh) + A_PERM for hh in range(4)], group=4)[0])
                else:
                    blks.append(tm_w(W, [base(4 * hb) + np.arange(512)])[0])
    return np.stack(blks)


def lay_win_c(w, D):
    ch = []
    for m in range(D // 128):
        for s in range(3):
            ch.append(s * D + np.arange(m * 128, (m + 1) * 128))
    return fm_w(w, ch, group=3)


def _pad128(a):
    return np.concatenate([a, -np.ones(128 - len(a), int)])


def lay_win_b(w, QR, KVR):
    return fm_w(w, chunks_of(0, QR + KVR) + [_pad128(QR + KVR + np.arange(64))])


def lay_wuq(w, H):
    return fm_w(w, sum([[h * 192 + np.arange(128), _pad128(h * 192 + 128 + np.arange(64))] for h in range(H)], []), group=2)


def lay_wukv(w, H):
    return fm_w(w, sum([[h * 256 + np.arange(128), h * 256 + 128 + np.arange(128)] for h in range(H)], []), group=2)


def rope_frq(half, rows_a, rows_b):
    f = (np.float32(ROPE_THETA) ** (-np.arange(half, dtype=np.float32) / np.float32(half))).astype(np.float32)
    o = np.zeros((128, 2), np.float32)
    o[rows_a, 0] = f
    o[rows_a, 1] = 1.0
    o[rows_b, 0] = f
    o[rows_b, 1] = -1.0
    return o


def band_mask():
    k = np.arange(128)[:, None]
    q = np.arange(128)[None, :]
    return np.concatenate([(k >= q), (k <= q)], axis=1).astype(np.float32)


T_ = TOK
KC_ = D_MODEL // 128
NJ_ = D_FF // 128
NQ_ = B_Q_RANK // 128
NKV_ = B_KV_RANK // 128
NHALO = sum(A_DIL)
TAB_DILS = A_DIL + (1,)

SPEC = {
    "ident": ((128, 128), F32), "x": ((T_, D_MODEL), F32), "pos": ((1, T_), I32), "frq": ((4, 128, 2), F32),
    "mask": ((128, 256), F32), "hmask": ((128, 1), F32), "qidx": ((1, T_), F32), "kidx": ((128, SEQ // 128), F32),
    "gf": ((128, KC_), F32), "out": ((T_, D_MODEL), F32),
    "hT": ((KC_, 128, T_), F32), "hT_in": ((KC_, 128, T_), F32), "tab": ((4, 2, 128, T_), F32),
    "QT": ((3, A_HEADS, 128, T_), BF16), "KT": ((3, A_HEADS, 128, T_), BF16), "V": ((3, A_HEADS, 128, T_), BF16),
    "KTx": ((A_HEADS, 128, NHALO * 128), BF16), "Vx": ((A_HEADS, 128, NHALO, 128), BF16),
    "KTh": ((A_HEADS, 128, NHALO * 128), BF16), "Vh": ((A_HEADS, 128, NHALO, 128), BF16),
    "cqn": ((NQ_, 128, T_), BF16), "kvx": ((NKV_, 128, T_), BF16), "krx": ((128, T_), BF16),
    "kvg": ((NKV_, 128, SEQ), BF16), "krg": ((128, SEQ), BF16), "QB": ((B_HEADS, 2, 128, T_), BF16),
    "oT": ((B_HEADS, 128, T_), BF16),
    "cuT": ((KC_, 128, T_), F32), "bT": ((KC_, 128, T_), F32), "cux": ((KC_, 128, 2), F32), "cuh": ((KC_, 128, 2), F32),
}
for _l in range(DEPTH):
    for _s in ("1", "2"):
        SPEC["l%d_g%s" % (_l, _s)] = ((128, KC_), F32)
        SPEC["l%d_w13_%s" % (_l, _s)] = ((NJ_, 128, KC_ * 256), F32)
        SPEC["l%d_w2_%s" % (_l, _s)] = ((KC_, 128, NJ_ * 128), F32)
    SPEC["l%d_gm" % _l] = ((128, KC_), F32)
for _l in (0, 3):
    SPEC["l%d_wqkv" % _l] = ((36, 128, 4 * KC_ * 128), F32)
    SPEC["l%d_wo" % _l] = ((KC_, 128, A_HEADS * 128), F32)
SPEC["l1_gq"] = ((128, NQ_), F32)
SPEC["l1_gkv"] = ((128, NKV_), F32)
SPEC["l1_win"] = ((NQ_ + NKV_ + 1, 128, KC_ * 128), F32)
SPEC["l1_wuq"] = ((B_HEADS, 128, 2 * NQ_ * 128), F32)
SPEC["l1_wukv"] = ((B_HEADS, 128, 2 * NKV_ * 128), F32)
SPEC["l1_wo"] = ((KC_, 128, B_HEADS * 128), F32)
SPEC["l2_win"] = ((KC_, 128, 3 * KC_ * 128), F32)
SPEC["l2_convw"] = ((128, KC_, 3), F32)
SPEC["l2_wout"] = ((KC_, 128, KC_ * 128), F32)


def ffn_names(l, s):
    return ["l%d_g%s" % (l, s), "l%d_w13_%s" % (l, s), "l%d_w2_%s" % (l, s)]


def do_ffn(C, d, l, s):
    emit_ffn(C, d["hT"], d["l%d_g%s" % (l, s)], d["l%d_w13_%s" % (l, s)], d["l%d_w2_%s" % (l, s)])


def do_A1(C, d, l):
    emit_mixA_1(C, d["hT"], d["l%d_gm" % l], d["l%d_wqkv" % l], d["tab"], d["QT"], d["KT"], d["V"], d["KTx"], d["Vx"])


def do_A2(C, d, l):
    emit_mixA_2(C, d["hT"], d["QT"], d["KT"], d["V"], d["KTh"], d["Vh"], d["hmask"], d["mask"], d["l%d_wo" % l])


def do_B1(C, d):
    emit_mixB_1a(C, d["hT"], d["l1_gm"], d["l1_gq"], d["l1_gkv"], d["l1_win"], d["tab"][3], d["cqn"], d["kvx"], d["krx"])
    emit_mixB_1b(C, d["cqn"], d["l1_wuq"], d["tab"][3], d["QB"])


def do_B2(C, d):
    emit_mixB_2(C, d["QB"], d["kvg"], d["krg"], d["l1_wukv"], d["qidx"], d["kidx"], d["oT"])
    emit_mixB_3(C, d["hT"], d["oT"], d["l1_wo"])


def do_tables(C, d):
    emit_tables(C, d["pos"], d["frq"], d["tab"], TAB_DILS)


def L1(C, d):
    do_tables(C, d)
    emit_pin(C, d["x"], d["hT"])
    do_ffn(C, d, 0, "1")
    do_A1(C, d, 0)


def L2(C, d):
    emit_dram_copy(C, d["hT"], d["hT_in"])
    do_A2(C, d, 0)
    do_ffn(C, d, 0, "2")
    do_ffn(C, d, 1, "1")
    do_tables(C, d)
    do_B1(C, d)


def L3(C, d):
    emit_dram_copy(C, d["hT"], d["hT_in"])
    do_B2(C, d)
    do_ffn(C, d, 1, "2")
    do_ffn(C, d, 2, "1")
    emit_mixC_1(C, d["hT"], d["l2_gm"], d["l2_win"], d["cuT"], d["bT"], d["cux"])


def L4(C, d):
    emit_dram_copy(C, d["hT"], d["hT_in"])
    emit_mixC_2(C, d["hT"], d["cuT"], d["bT"], d["cuh"], d["l2_convw"], d["hmask"], d["l2_wout"])
    do_ffn(C, d, 2, "2")
    do_ffn(C, d, 3, "1")
    do_tables(C, d)
    do_A1(C, d, 3)


def L5(C, d):
    emit_dram_copy(C, d["hT"], d["hT_in"])
    do_A2(C, d, 3)
    do_ffn(C, d, 3, "2")
    emit_final(C, d["hT"], d["gf"], d["out"])


LAUNCHES = [
    ("L1", L1, ["ident", "x", "pos", "frq"] + ffn_names(0, "1") + ["l0_gm", "l0_wqkv"],
     ["hT", "QT", "KT", "V", "KTx", "Vx"], ["tab"]),
    ("L2", L2, ["ident", "hT_in", "QT", "KT", "V", "KTh", "Vh", "hmask", "mask", "l0_wo"] + ffn_names(0, "2") + ffn_names(1, "1")
     + ["pos", "frq", "l1_gm", "l1_gq", "l1_gkv", "l1_win", "l1_wuq"],
     ["hT", "kvx", "krx", "QB"], ["tab", "cqn"]),
    ("L3", L3, ["ident", "hT_in", "QB", "kvg", "krg", "l1_wukv", "qidx", "kidx", "l1_wo"] + ffn_names(1, "2") + ffn_names(2, "1")
     + ["l2_gm", "l2_win"],
     ["hT", "cuT", "bT", "cux"], ["oT"]),
    ("L4", L4, ["ident", "hT_in", "cuT", "bT", "cuh", "l2_convw", "hmask", "l2_wout"] + ffn_names(2, "2") + ffn_names(3, "1")
     + ["pos", "frq", "l3_gm", "l3_wqkv"],
     ["hT", "QT", "KT", "V", "KTx", "Vx"], ["tab"]),
    ("L5", L5, ["ident", "hT_in", "QT", "KT", "V", "KTh", "Vh", "hmask", "mask", "l3_wo"] + ffn_names(3, "2") + ["gf"],
     ["out"], ["hT"]),
]

_PROGS = {}


def build_prog(name, fn, ins, outs, scratch):
    if name in _PROGS:
        return _PROGS[name]
    nc = bass.Bass("TRN2", target_bir_lowering=False)
    d = {}
    for n in ins:
        d[n] = nc.dram_tensor(n, list(SPEC[n][0]), SPEC[n][1], kind="ExternalInput").ap()
    for n in outs:
        d[n] = nc.dram_tensor(n, list(SPEC[n][0]), SPEC[n][1], kind="ExternalOutput").ap()
    for n in scratch:
        d[n] = nc.dram_tensor(n, list(SPEC[n][0]), SPEC[n][1]).ap()
    with contextlib.ExitStack() as st:
        C = Ctx(nc, st)
        emit_consts(C, d["ident"])
        fn(C, d)
        C.S.emit()
    _PROGS[name] = nc
    return nc


INPUT_NAMES = (
    "x", "positions",
    "l0_ffn1_norm", "l0_ffn1_w13", "l0_ffn1_w2", "l0_mix_norm", "l0_a_w_qkv", "l0_a_w_o",
    "l0_ffn2_norm", "l0_ffn2_w13", "l0_ffn2_w2",
    "l1_ffn1_norm", "l1_ffn1_w13", "l1_ffn1_w2", "l1_mix_norm", "l1_b_w_in", "l1_b_q_norm",
    "l1_b_w_uq", "l1_b_kv_norm", "l1_b_w_ukv", "l1_b_w_o", "l1_ffn2_norm", "l1_ffn2_w13", "l1_ffn2_w2",
    "l2_ffn1_norm", "l2_ffn1_w13", "l2_ffn1_w2", "l2_mix_norm", "l2_c_w_in", "l2_c_conv_w", "l2_c_w_out",
    "l2_ffn2_norm", "l2_ffn2_w13", "l2_ffn2_w2",
    "l3_ffn1_norm", "l3_ffn1_w13", "l3_ffn1_w2", "l3_mix_norm", "l3_a_w_qkv", "l3_a_w_o",
    "l3_ffn2_norm", "l3_ffn2_w13", "l3_ffn2_w2",
    "final_norm",
)


def host_layout(inp):
    missing = [n for n in INPUT_NAMES if n not in inp]
    assert not missing, missing
    W = {}
    f = lambda k: np.asarray(inp[k], np.float32)
    for l in range(DEPTH):
        p = "l%d_" % l
        for s, nm in (("1", "ffn1"), ("2", "ffn2")):
            W[p + "g" + s] = gain_l(f(p + nm + "_norm"))
            W[p + "w13_" + s] = lay_w13(f(p + nm + "_w13"))
            W[p + "w2_" + s] = lay_w2(f(p + nm + "_w2"))
        W[p + "gm"] = gain_l(f(p + "mix_norm"))
    for l in (0, 3):
        p = "l%d_" % l
        W[p + "wqkv"] = lay_wqkv(f(p + "a_w_qkv"), 3, A_HEADS)
        W[p + "wo"] = lay_sq(f(p + "a_w_o"))
    W["l1_gq"] = gain_l(f("l1_b_q_norm"))
    W["l1_gkv"] = gain_l(f("l1_b_kv_norm"))
    W["l1_win"] = lay_win_b(f("l1_b_w_in"), B_Q_RANK, B_KV_RANK)
    W["l1_wuq"] = lay_wuq(f("l1_b_w_uq"), B_HEADS)
    W["l1_wukv"] = lay_wukv(f("l1_b_w_ukv"), B_HEADS)
    W["l1_wo"] = lay_sq(f("l1_b_w_o"))
    W["l2_win"] = lay_win_c(f("l2_c_w_in"), D_MODEL)
    W["l2_convw"] = np.ascontiguousarray(f("l2_c_conv_w").T.reshape(KC_, 128, 3).transpose(1, 0, 2))
    W["l2_wout"] = lay_sq(f("l2_c_w_out"))
    W["gf"] = gain_l(f("final_norm"))
    W["ident"] = np.eye(128, dtype=np.float32)
    fa = rope_frq(A_ROT // 2, np.arange(0, 16), np.arange(32, 48))
    fb = rope_frq(B_ROPE // 2, np.arange(0, 32), np.arange(32, 64))
    W["frq"] = np.stack([fa, fa, fa, fb])
    W["mask"] = band_mask()
    W["kidx"] = (np.arange(SEQ // 128)[None, :] * 128 + np.arange(128)[:, None]).astype(np.float32)
    x = f("x")
    pos = np.asarray(inp["positions"], np.int32)
    per_core = []
    for c in range(NCORES):
        b, j = c // CPB, c % CPB
        per_core.append({
            "x": np.ascontiguousarray(x[b, j * T_:(j + 1) * T_]),
            "pos": np.ascontiguousarray(pos[b:b + 1, j * T_:(j + 1) * T_]),
            "hmask": np.full((128, 1), float(j > 0), np.float32),
            "qidx": (j * T_ + np.arange(T_, dtype=np.float32))[None],
        })
    return W, per_core


def kernel_unfused(**inp):
    import ml_dtypes
    W, pc = host_layout(inp)
    state = [dict(p) for p in pc]
    out = None
    for (name, fn, ins, outs, scratch) in LAUNCHES:
        nc = build_prog(name, fn, ins, outs, scratch)
        in_maps = []
        for c in range(NCORES):
            im = {}
            for n in ins:
                im[n] = state[c][n] if n in state[c] else W[n]
            in_maps.append(im)
        res = run_bass_kernel_spmd(nc, in_maps, core_ids=list(range(NCORES))).results
        for c in range(NCORES):
            for n in outs:
                state[c]["hT_in" if n == "hT" else n] = np.asarray(res[c][n])
        for c in range(NCORES):
            j = c % CPB
            if "KTx" in outs:
                state[c]["KTh"] = state[c - 1]["KTx"] if j > 0 else np.zeros_like(state[c]["KTx"])
                state[c]["Vh"] = state[c - 1]["Vx"] if j > 0 else np.zeros_like(state[c]["Vx"])
            if "kvx" in outs:
                b0 = (c // CPB) * CPB
                state[c]["kvg"] = np.concatenate([state[b0 + i]["kvx"] for i in range(CPB)], axis=2)
                state[c]["krg"] = np.concatenate([state[b0 + i]["krx"][0] for i in range(CPB)], axis=1)
            if "cux" in outs:
                state[c]["cuh"] = state[c - 1]["cux"] if j > 0 else np.zeros_like(state[c]["cux"])
    out = np.stack([np.concatenate([np.asarray(state[b * CPB + j]["out"]) for j in range(CPB)], axis=0) for b in range(BATCH)])
    return out.astype(np.float32)


def kernel(**inputs):
    return kernel_fused(**inputs)


GROUPS = [list(range(b * CPB, (b + 1) * CPB)) for b in range(BATCH)]


def emit_gather(C, src, dst, n):
    S = C.S
    cs = S.dsem("cc", sw=True)
    for i in range(n):
        S.coll("AllGather", GROUPS, src[i], dst[i], cs)


def emit_select(C, src_g, dst, oh_d, n, X, dt):
    S = C.S
    with C.phase():
        oh = C.sb([128, CPB], F32, "oh")
        ohb = Buf()
        S.dma("sp", oh[:, :], oh_d[:, :], S.dsem(), writes=[ohb])
        cand = [(C.sb([128, CPB, X], dt, "cand"), Buf(), S.dsem("cand")) for _ in range(2)]
        acc = [(C.sb([128, X], F32, "acc"), Buf()) for _ in range(2)]
        outs = [(C.sb([128, X], dt, "sel"), Buf(), S.dsem("sel")) for _ in range(2)]
        for i in range(n):
            c, cb, cds = cand[i % 2]
            a, ab = acc[i % 2]
            o, ob, ods = outs[i % 2]
            S.dma("sp", c[:, :, :], src_g[i].rearrange("s p x -> p s x"), cds, writes=[cb])
            S.op("dve", TS(a[:, :], c[:, 0, :], oh[:, 0:1], None, ALU.mult), reads=[cb, ohb], writes=[ab])
            for s in range(1, CPB):
                last = s == CPB - 1
                S.op("dve", STT(o[:, :] if last else a[:, :], c[:, s, :], oh[:, s:s + 1], a[:, :], ALU.mult, ALU.add),
                     reads=[cb, ohb, ab], writes=[ob] if last else [ab])
            S.dma("sp", dst[i], o[:, :], ods, reads=[ob])


def emit_select_c(C, cug, cuh, oh_d, KC=KC_):
    S = C.S
    X = 2 * KC
    with C.phase():
        oh = C.sb([128, CPB], F32, "oh")
        ohb = Buf()
        S.dma("sp", oh[:, :], oh_d[:, :], S.dsem(), writes=[ohb])
        c = C.sb([128, CPB, X], F32, "cand")
        cb = Buf()
        S.dma("sp", c[:, :, :], cug[:, :, 0:X].rearrange("s p x -> p s x"), S.dsem(), writes=[cb])
        a = C.sb([128, X], F32, "acc")
        ab = Buf()
        S.op("dve", TS(a[:, :], c[:, 0, :], oh[:, 0:1], None, ALU.mult), reads=[cb, ohb], writes=[ab])
        for s in range(1, CPB):
            S.op("dve", STT(a[:, :], c[:, s, :], oh[:, s:s + 1], a[:, :], ALU.mult, ALU.add),
                 reads=[cb, ohb, ab], writes=[ab])
        S.dma("sp", cuh[:, 0:X], a[:, :], S.dsem(), reads=[ab])


def FUSED(C, d):
    def A_layer(l):
        sfx = str(l)
        emit_mixA_1(C, d["hT"], d["l%d_gm" % l], d["l%d_wqkv" % l], d["tab"], d["QT" + sfx], d["KT" + sfx], d["V" + sfx],
                    d["KTx" + sfx], d["Vx" + sfx])
        emit_mixA_2(C, d["hT"], d["QT" + sfx], d["KT" + sfx], d["V" + sfx], d["KTh" + sfx], d["Vh" + sfx], d["hmask"], d["mask"],
                    d["l%d_wo" % l], xchg=(d["KTx" + sfx], d["Vx" + sfx], d["KTg" + sfx], d["Vg" + sfx], d["oh"], True))

    do_tables(C, d)
    emit_pin(C, d["x"], d["hT"])
    do_ffn(C, d, 0, "1")
    A_layer(0)
    do_ffn(C, d, 0, "2")
    do_ffn(C, d, 1, "1")
    emit_mixB_1a(C, d["hT"], d["l1_gm"], d["l1_gq"], d["l1_gkv"], d["l1_win"], d["tab"][3], d["cqn"], d["kvx"], d["krx"],
                 gather=(d["kvgs"], d["krgs"]))
    emit_mixB_1b(C, d["cqn"], d["l1_wuq"], d["tab"][3], d["QB"])
    emit_mixB_2(C, d["QB"], d["kvgs"], d["krgs"], d["l1_wukv"], d["qidx"], d["kidx"], d["oT"], gathered=True)
    emit_mixB_3(C, d["hT"], d["oT"], d["l1_wo"])
    do_ffn(C, d, 1, "2")
    do_ffn(C, d, 2, "1")
    emit_mixC_1(C, d["hT"], d["l2_gm"], d["l2_win"], d["cuT"], d["bT"], d["cux"], gather=d["cug"])
    emit_select_c(C, d["cug"][0], d["cuh"][0], d["oh"])
    emit_mixC_2(C, d["hT"], d["cuT"], d["bT"], d["cuh"], d["l2_convw"], d["hmask"], d["l2_wout"])
    do_ffn(C, d, 2, "2")
    do_ffn(C, d, 3, "1")
    A_layer(3)
    do_ffn(C, d, 3, "2")
    emit_final(C, d["hT"], d["gf"], d["out"])


for _l in ("0", "3"):
    for _n in ("QT", "KT", "V", "KTx", "Vx", "KTh", "Vh"):
        SPEC[_n + _l] = SPEC[_n]
    SPEC["KTg" + _l] = ((A_HEADS, CPB, 128, NHALO * 128), BF16)
    SPEC["Vg" + _l] = ((A_HEADS, CPB, 128, NHALO, 128), BF16)
SPEC["kvgs"] = ((NKV_, CPB, 128, T_), BF16)
SPEC["krgs"] = ((1, CPB, 128, T_), BF16)
SPEC["cux"] = ((1, 128, 512), F32)
SPEC["cug"] = ((1, CPB, 128, 512), F32)
SPEC["cuh"] = ((1, 128, 512), F32)
SPEC["oh"] = ((128, CPB), F32)
SPEC["krx"] = ((1, 128, T_), BF16)

FUSED_INS = (["ident", "x", "pos", "frq", "mask", "hmask", "oh", "qidx", "kidx", "gf"]
             + sum([ffn_names(l, s) for l in range(DEPTH) for s in ("1", "2")], [])
             + ["l%d_gm" % l for l in range(DEPTH)]
             + ["l0_wqkv", "l0_wo", "l3_wqkv", "l3_wo", "l1_gq", "l1_gkv", "l1_win", "l1_wuq", "l1_wukv", "l1_wo",
                "l2_win", "l2_convw", "l2_wout"])
FUSED_SCRATCH = (["hT", "tab", "cqn", "kvx", "krx", "kvgs", "krgs", "QB", "oT", "cuT", "bT", "cux", "cug", "cuh"]
                 + [n + l for l in ("0", "3") for n in ("QT", "KT", "V", "KTx", "Vx", "KTh", "Vh", "KTg", "Vg")])


def kernel_fused(**inp):
    W, pc = host_layout(inp)
    nc = build_prog("FUSED", FUSED, FUSED_INS, ["out"], FUSED_SCRATCH)
    in_maps = []
    for c in range(NCORES):
        j = c % CPB
        oh = np.zeros((128, CPB), np.float32)
        if j > 0:
            oh[:, j - 1] = 1.0
        pcc = dict(pc[c], oh=oh)
        in_maps.append({n: (pcc[n] if n in pcc else W[n]) for n in FUSED_INS})
    res = run_bass_kernel_spmd(nc, in_maps, core_ids=list(range(NCORES))).results
    out = np.stack([np.concatenate([np.asarray(res[b * CPB + j]["out"]) for j in range(CPB)], axis=0) for b in range(BATCH)])
    return out.astype(np.float32)
```
